# Optimizing a Trainium2 kernel written in Bass

```python
import math
import jax, jax.numpy as jnp
from jax import lax
import numpy as np

D_MODEL = 1024
BATCH = 8
SEQ = 4096
DEPTH = 1

CHUNK = 64
EPS = 1e-6

SSM_WIDTH = 512
SSM_GROUP = 16
SSM_GROUPS = SSM_WIDTH // SSM_GROUP
SSM_STATE = 64
DT_MIN = 1e-3
DT_MAX = 1e-1

ATT_HEADS = 8
HEAD_DIM = 128
ATT_WIDTH = ATT_HEADS * HEAD_DIM
Q_BLOCK = 128

N_BRANCHES = 2
IN_SPLITS = (
    SSM_WIDTH,
    SSM_WIDTH + ATT_WIDTH,
    SSM_WIDTH + 2 * ATT_WIDTH,
    SSM_WIDTH + 3 * ATT_WIDTH,
    SSM_WIDTH + 3 * ATT_WIDTH + ATT_HEADS,
    SSM_WIDTH + 3 * ATT_WIDTH + ATT_HEADS + D_MODEL,
)
IN_COLS = SSM_WIDTH + 3 * ATT_WIDTH + ATT_HEADS + N_BRANCHES * D_MODEL

PEER_HEADS = 8
PEER_KEYS = 128
PEER_EXPERTS = PEER_KEYS * PEER_KEYS
PEER_QUERY = 256
PEER_HALF = PEER_QUERY // 2
PEER_TOPK = 16
TOKEN_BLOCK = 128

kernel_name = "hybrid_s5_fox_peer_block"


def rms_norm(x, g):
    x32 = x.astype(jnp.float32)
    y = x32 * lax.rsqrt(jnp.mean(x32 * x32, axis=-1, keepdims=True) + EPS)
    return (y * g.astype(jnp.float32)).astype(x.dtype)


def _complex_affine_combine(earlier, later):
    ar1, ai1, br1, bi1 = earlier
    ar2, ai2, br2, bi2 = later
    return (ar2 * ar1 - ai2 * ai1,
            ar2 * ai1 + ai2 * ar1,
            ar2 * br1 - ai2 * bi1 + br2,
            ar2 * bi1 + ai2 * br1 + bi2)


def s5_branch(u, a_re, a_im, log_step, b_re, b_im, c_re, c_im, d_skip, w_glu):
    bsz, seq = u.shape[0], u.shape[1]
    f32 = jnp.float32
    u32 = u.astype(f32)
    ug = u32.reshape(bsz, seq, SSM_GROUPS, SSM_GROUP)
    step = jnp.exp(log_step.astype(f32))[:, None]
    ar = a_re.astype(f32)
    ai = a_im.astype(f32)
    mag = jnp.exp(ar * step)
    lam_re = mag * jnp.cos(ai * step)
    lam_im = mag * jnp.sin(ai * step)
    nr = lam_re - 1.0
    ni = lam_im
    den = ar * ar + ai * ai
    coef_re = (nr * ar + ni * ai) / den
    coef_im = (ni * ar - nr * ai) / den
    br = coef_re[..., None] * b_re.astype(f32) - coef_im[..., None] * b_im.astype(f32)
    bi = coef_re[..., None] * b_im.astype(f32) + coef_im[..., None] * b_re.astype(f32)
    xr = jnp.einsum('blgh,gph->blgp', ug, br)
    xi = jnp.einsum('blgh,gph->blgp', ug, bi)
    lr = jnp.broadcast_to(lam_re[None, None], (1, seq, SSM_GROUPS, SSM_STATE))
    li = jnp.broadcast_to(lam_im[None, None], (1, seq, SSM_GROUPS, SSM_STATE))
    _, _, sr, si = lax.associative_scan(_complex_affine_combine, (lr, li, xr, xi), axis=1)
    y = (jnp.einsum('blgp,ghp->blgh', sr, c_re.astype(f32))
         - jnp.einsum('blgp,ghp->blgh', si, c_im.astype(f32)))
    y = y.reshape(bsz, seq, SSM_WIDTH) + d_skip.astype(f32) * u32
    y = jax.nn.gelu(y).astype(u.dtype)
    z = y @ w_glu
    val, gate = jnp.split(z, 2, axis=-1)
    return val * jax.nn.sigmoid(gate)


def head_rms_norm(x, g):
    x32 = x.astype(jnp.float32)
    y = x32 * lax.rsqrt(jnp.mean(x32 * x32, axis=-1, keepdims=True) + EPS)
    return (y * g.astype(jnp.float32)).astype(x.dtype)


def forgetting_attention(q, k, v, f_logit, f_bias, q_g, k_g):
    bsz, seq = q.shape[0], q.shape[1]
    q = head_rms_norm(q.reshape(bsz, seq, ATT_HEADS, HEAD_DIM), q_g).transpose(0, 2, 1, 3)
    k = head_rms_norm(k.reshape(bsz, seq, ATT_HEADS, HEAD_DIM), k_g).transpose(0, 2, 1, 3)
    v = v.reshape(bsz, seq, ATT_HEADS, HEAD_DIM).transpose(0, 2, 1, 3)
    log_f = jax.nn.log_sigmoid((f_logit + f_bias).astype(jnp.float32)).transpose(0, 2, 1)
    cum = jnp.cumsum(log_f, axis=-1)
    n_blk = seq // Q_BLOCK
    q_blocks = q.reshape(bsz, ATT_HEADS, n_blk, Q_BLOCK, HEAD_DIM).transpose(2, 0, 1, 3, 4)
    c_blocks = cum.reshape(bsz, ATT_HEADS, n_blk, Q_BLOCK).transpose(2, 0, 1, 3)
    key_pos = jnp.arange(seq)
    scale = HEAD_DIM ** -0.5

    def one_block(args):
        blk, q_blk, c_blk = args
        s = jnp.einsum('bhqd,bhkd->bhqk', q_blk, k).astype(jnp.float32) * scale
        s = s + c_blk[..., None] - cum[:, :, None, :]
        q_pos = blk * Q_BLOCK + jnp.arange(Q_BLOCK)
        s = jnp.where(key_pos[None, :] <= q_pos[:, None], s, -jnp.inf)
        p = jax.nn.softmax(s, axis=-1)
        return jnp.einsum('bhqk,bhkd->bhqd', p.astype(v.dtype), v)

    o = lax.map(one_block, (jnp.arange(n_blk), q_blocks, c_blocks))
    return o.transpose(1, 0, 3, 2, 4).reshape(bsz, seq, ATT_WIDTH)


def peer_ffn(h, w_query, sub_keys, u_tab, v_tab):
    bsz, seq, d = h.shape
    q = (h @ w_query).reshape(bsz, seq, PEER_HEADS, 2, PEER_HALF).astype(jnp.float32)
    scores = jnp.einsum('blhcd,hckd->blhck', q, sub_keys.astype(jnp.float32))
    s_top, i_top = lax.top_k(scores, PEER_TOPK)
    cand_s = (s_top[..., 0, :, None] + s_top[..., 1, None, :]).reshape(bsz, seq, PEER_HEADS, PEER_TOPK * PEER_TOPK)
    cand_i = (i_top[..., 0, :, None] * PEER_KEYS + i_top[..., 1, None, :]).reshape(bsz, seq, PEER_HEADS, PEER_TOPK * PEER_TOPK)
    best_s, pos = lax.top_k(cand_s, PEER_TOPK)
    expert_idx = jnp.take_along_axis(cand_i, pos, axis=-1)
    gates = jax.nn.softmax(best_s, axis=-1)
    n_tok = bsz * seq
    n_tb = n_tok // TOKEN_BLOCK
    h_blocks = h.reshape(n_tb, TOKEN_BLOCK, d)
    i_blocks = expert_idx.reshape(n_tb, TOKEN_BLOCK, PEER_HEADS, PEER_TOPK)
    g_blocks = gates.reshape(n_tb, TOKEN_BLOCK, PEER_HEADS, PEER_TOPK)

    def one_block(args):
        h_blk, i_blk, g_blk = args
        u_sel = u_tab[i_blk]
        act = jax.nn.gelu(jnp.einsum('td,thkd->thk', h_blk, u_sel).astype(jnp.float32))
        w = (g_blk * act).astype(h.dtype)
        return jnp.einsum('thk,thkd->td', w, v_tab[i_blk])

    out = lax.map(one_block, (h_blocks, i_blocks, g_blocks))
    return out.reshape(bsz, seq, d)


def setup_inputs(seed: int = 0) -> dict:
    key = jax.random.key(seed)
    ks = jax.random.split(key, 21)
    f32 = jnp.float32
    nrm = lambda k, shape: jax.random.normal(k, shape, dtype=f32)
    x = nrm(ks[0], (BATCH, SEQ, D_MODEL))
    mix_norm_g = 1.0 + 0.02 * nrm(ks[1], (DEPTH, D_MODEL))
    w_in = nrm(ks[2], (DEPTH, D_MODEL, IN_COLS)) * D_MODEL ** -0.5
    ssm_a_re = -0.5 + 0.01 * nrm(ks[3], (DEPTH, SSM_GROUPS, SSM_STATE))
    ssm_a_im = math.pi * jnp.arange(SSM_STATE, dtype=f32) + 0.01 * nrm(ks[4], (DEPTH, SSM_GROUPS, SSM_STATE))
    ssm_log_step = jax.random.uniform(ks[5], (DEPTH, SSM_GROUPS), dtype=f32, minval=math.log(DT_MIN), maxval=math.log(DT_MAX))
    ssm_b_re = nrm(ks[6], (DEPTH, SSM_GROUPS, SSM_STATE, SSM_GROUP)) * (2.0 * SSM_GROUP) ** -0.5
    ssm_b_im = nrm(ks[7], (DEPTH, SSM_GROUPS, SSM_STATE, SSM_GROUP)) * (2.0 * SSM_GROUP) ** -0.5
    ssm_c_re = nrm(ks[8], (DEPTH, SSM_GROUPS, SSM_GROUP, SSM_STATE)) * (2.0 * SSM_STATE) ** -0.5
    ssm_c_im = nrm(ks[9], (DEPTH, SSM_GROUPS, SSM_GROUP, SSM_STATE)) * (2.0 * SSM_STATE) ** -0.5
    ssm_d = nrm(ks[10], (DEPTH, SSM_WIDTH))
    ssm_w_glu = nrm(ks[11], (DEPTH, SSM_WIDTH, 2 * D_MODEL)) * SSM_WIDTH ** -0.5
    fox_forget_bias = 2.0 + 0.5 * nrm(ks[12], (DEPTH, ATT_HEADS))
    q_norm_g = 1.0 + 0.02 * nrm(ks[13], (DEPTH, HEAD_DIM))
    k_norm_g = 1.0 + 0.02 * nrm(ks[14], (DEPTH, HEAD_DIM))
    w_out = nrm(ks[15], (DEPTH, D_MODEL, D_MODEL)) * D_MODEL ** -0.5
    ffn_norm_g = 1.0 + 0.02 * nrm(ks[16], (DEPTH, D_MODEL))
    peer_w_query = nrm(ks[17], (DEPTH, D_MODEL, PEER_HEADS * PEER_QUERY)) * D_MODEL ** -0.5
    peer_sub_keys = nrm(ks[18], (DEPTH, PEER_HEADS, 2, PEER_KEYS, PEER_HALF)) * PEER_HALF ** -0.5
    peer_u = nrm(ks[19], (DEPTH, PEER_EXPERTS, D_MODEL)) * D_MODEL ** -0.5
    peer_v = nrm(ks[20], (DEPTH, PEER_EXPERTS, D_MODEL)) * 0.1
    return {"x": x, "mix_norm_g": mix_norm_g, "w_in": w_in,
            "ssm_a_re": ssm_a_re, "ssm_a_im": ssm_a_im, "ssm_log_step": ssm_log_step,
            "ssm_b_re": ssm_b_re, "ssm_b_im": ssm_b_im, "ssm_c_re": ssm_c_re, "ssm_c_im": ssm_c_im,
            "ssm_d": ssm_d, "ssm_w_glu": ssm_w_glu, "fox_forget_bias": fox_forget_bias,
            "q_norm_g": q_norm_g, "k_norm_g": k_norm_g, "w_out": w_out, "ffn_norm_g": ffn_norm_g,
            "peer_w_query": peer_w_query, "peer_sub_keys": peer_sub_keys,
            "peer_u": peer_u, "peer_v": peer_v}


def reference(x, mix_norm_g, w_in, ssm_a_re, ssm_a_im, ssm_log_step, ssm_b_re, ssm_b_im,
              ssm_c_re, ssm_c_im, ssm_d, ssm_w_glu, fox_forget_bias, q_norm_g, k_norm_g,
              w_out, ffn_norm_g, peer_w_query, peer_sub_keys, peer_u, peer_v):
    for layer in range(DEPTH):
        h = rms_norm(x, mix_norm_g[layer])
        proj = h @ w_in[layer]
        u, q, k, v, f_logit, g_ssm, g_att = jnp.split(proj, IN_SPLITS, axis=-1)
        y_ssm = s5_branch(u, ssm_a_re[layer], ssm_a_im[layer], ssm_log_step[layer],
                          ssm_b_re[layer], ssm_b_im[layer], ssm_c_re[layer], ssm_c_im[layer],
                          ssm_d[layer], ssm_w_glu[layer])
        y_att = forgetting_attention(q, k, v, f_logit, fox_forget_bias[layer],
                                     q_norm_g[layer], k_norm_g[layer])
        merged = jax.nn.sigmoid(g_ssm) * y_ssm + jax.nn.sigmoid(g_att) * y_att
        x = x + (merged @ w_out[layer]).astype(x.dtype)
        h2 = rms_norm(x, ffn_norm_g[layer])
        x = x + peer_ffn(h2, peer_w_query[layer], peer_sub_keys[layer],
                         peer_u[layer], peer_v[layer]).astype(x.dtype)
    return x
```

```python
import math
from contextlib import ExitStack
import numpy as np
import concourse.bass as bass
import concourse.mybir as mybir
from concourse.bass_utils import run_bass_kernel_spmd

F32 = mybir.dt.float32; BF16 = mybir.dt.bfloat16; I32 = mybir.dt.int32; U32 = mybir.dt.uint32
ALU = mybir.AluOpType; AF = mybir.ActivationFunctionType; AX = mybir.AxisListType

D = 1024
KD = 8
INC = 5640
EPS = 1e-6
C_U, C_Q, C_K, C_V, C_F, C_GS, C_GA = 0, 512, 1536, 2560, 3584, 3592, 4616
NEG = -30000.0
TWO_PI = 2.0 * math.pi
NE = 16384


class Res:
    __slots__ = ("name", "w", "r")

    def __init__(self, name):
        self.name = name
        self.w = None
        self.r = {}


class Sched:
    ENGS = ("pe", "act", "dve", "pool", "sp")

    def __init__(self, nc, es, same_engine_sync=("act", "dve", "pool")):
        self.nc = nc
        self.es = es
        self.eng = {"pe": nc.tensor, "act": nc.scalar, "dve": nc.vector, "pool": nc.gpsimd, "sp": nc.sync}
        self.sem = {}
        self.cnt = {}
        for e in self.ENGS:
            self.sem[e] = es.enter_context(nc.semaphore("sem_" + e))
            self.cnt[e] = 0
        self.seen = {e: {} for e in self.ENGS}
        self.same = set(same_engine_sync)
        self.nwaits = 0
        self.nops = 0

    def res(self, name):
        return Res(name)

    def _chan(self, chan):
        if chan not in self.sem:
            self.sem[chan] = self.es.enter_context(self.nc.semaphore("semd_" + chan))
            self.cnt[chan] = 0
        return self.sem[chan]

    def _wait(self, e, tok):
        if tok is None:
            return
        key, val = tok
        if key == e and e not in self.same:
            return
        if self.seen[e].get(key, 0) >= val:
            return
        self.eng[e].wait_ge(self.sem[key], val)
        self.seen[e][key] = val
        self.nwaits += 1

    def _deps(self, e, reads, writes):
        for R in reads:
            self._wait(e, R.w)
        for W in writes:
            self._wait(e, W.w)
            for key, val in W.r.items():
                self._wait(e, (key, val))

    def _commit(self, tok, reads, writes):
        for R in reads:
            if R.r.get(tok[0], 0) < tok[1]:
                R.r[tok[0]] = tok[1]
        for W in writes:
            W.w = tok
            W.r = {}

    def op(self, e, fn, reads=(), writes=()):
        self._deps(e, reads, writes)
        ins = fn(self.eng[e])
        self.cnt[e] += 1
        ins.then_inc(self.sem[e], 1)
        tok = (e, self.cnt[e])
        self._commit(tok, reads, writes)
        self.nops += 1
        return tok

    def dma(self, e, chan, out, in_, reads=(), writes=(), **kw):
        sem = self._chan(chan)
        if self.cnt[chan] > 0:
            self._wait(e, (chan, self.cnt[chan]))
        self._deps(e, reads, writes)
        ins = self.eng[e].dma_start(out=out, in_=in_, **kw)
        self.cnt[chan] += 16
        ins.then_inc(sem, 16)
        tok = (chan, self.cnt[chan])
        self._commit(tok, reads, writes)
        return tok

    def barrier(self):
        for e in self.ENGS:
            for key in list(self.sem.keys()):
                if key != e and self.cnt[key] > 0:
                    self._wait(e, (key, self.cnt[key]))
                elif key == e and e in self.same and self.cnt[key] > 0:
                    self._wait(e, (key, self.cnt[key]))

    def wait_all_dma(self, e):
        for key in list(self.sem.keys()):
            if key not in self.ENGS and self.cnt[key] > 0:
                self._wait(e, (key, self.cnt[key]))


class Buf:
    def __init__(self, t, r):
        self.t = t
        self.r = r

    def __getitem__(self, k):
        return self.t[k]


def build_program(T, dbg=False, stop_after=None):
    NT = T // 128
    NCH = T // 512
    nc = bass.Bass("TRN2", target_bir_lowering=False)

    def din(name, shape, dt=F32):
        return nc.dram_tensor(name, shape, dt, kind="ExternalInput").ap()

    x_d = din("x", [T, D])
    w_in_d = din("w_in", [128, KD, INC])
    g1_d = din("g1", [128, KD])
    fb_d = din("fb", [128, 8])
    qg_d = din("qg", [128, 1])
    kg_d = din("kg", [128, 1])
    are_d = din("a_re_l", [128, 16]); aim_d = din("a_im_l", [128, 16]); ls_d = din("ls_l", [128, 16])
    bre_d = din("b_re_l", [128, 16, 16]); bim_d = din("b_im_l", [128, 16, 16])
    cre_d = din("c_re_l", [128, 16, 16]); cim_d = din("c_im_l", [128, 16, 16])
    dsk_d = din("d_l", [128, 4])
    wglu_d = din("w_glu", [128, 4, 2048])
    wout_d = din("w_out", [128, KD, D])
    g2_d = din("g2rep", [128, D])
    wqr_d = din("w_query", [128, KD, 2048])
    skT_d = din("skT", [128, 16, 128])
    UT_d = din("peer_uT", [128, KD, NE])
    V_d = din("peer_v", [128, 128, D])
    out_d = nc.dram_tensor("out", [T, D], F32, kind="ExternalOutput").ap()
    UTb_d = nc.dram_tensor("UTb", [128, KD, NE], BF16, kind="Internal").ap()
    Vb_d = nc.dram_tensor("Vb", [128, 128, D], BF16, kind="Internal").ap()
    rt_d = nc.dram_tensor("rt", [128, 3, T], F32, kind=("ExternalOutput" if dbg else "Internal")).ap()
    kind_scr = "ExternalOutput" if dbg else "Internal"
    mT_d = nc.dram_tensor("mT", [D, T], BF16, kind=kind_scr).ap()
    hT_d = nc.dram_tensor("hTd", [128, KD, T], BF16, kind="Internal").ap()
    h2T_d = nc.dram_tensor("h2T", [128, KD, T], BF16, kind=kind_scr).ap()
    m2T_d = nc.dram_tensor("m2T", [128, KD, T], BF16, kind=kind_scr).ap() if dbg else None

    with ExitStack() as es:
        S = Sched(nc, es)

        def sbuf(stack, name, shape, dt):
            t = stack.enter_context(nc.sbuf_tensor("sb_" + name, shape, dt))
            return Buf(t, S.res(name))

        banks = []
        for i in range(8):
            t = es.enter_context(nc.psum_tensor("bank%d" % i, [128, 512], F32))
            banks.append(Buf(t, S.res("bank%d" % i)))

        identf = sbuf(es, "identf", [128, 128], F32)
        identb = sbuf(es, "identb", [128, 128], BF16)
        onesb = sbuf(es, "onesb", [128, 128], BF16)
        onesf = sbuf(es, "onesf", [128, 128], F32)
        trif = sbuf(es, "trif", [128, 128], F32)
        maskneg = sbuf(es, "maskneg", [128, 128], BF16)
        S.op("pool", lambda e: e.memset(identf[:], 1.0), writes=[identf.r])
        S.op("pool", lambda e: e.affine_select(out=identf[:], in_=identf[:], pattern=[[-1, 128]], compare_op=ALU.is_equal,
                                               fill=0.0, base=0, channel_multiplier=1), reads=[identf.r], writes=[identf.r])
        S.op("pool", lambda e: e.tensor_copy(out=identb[:], in_=identf[:]), reads=[identf.r], writes=[identb.r])
        S.op("pool", lambda e: e.memset(onesb[:], 1.0), writes=[onesb.r])
        S.op("pool", lambda e: e.memset(onesf[:], 1.0), writes=[onesf.r])
        S.op("pool", lambda e: e.memset(trif[:], 1.0), writes=[trif.r])
        S.op("pool", lambda e: e.affine_select(out=trif[:], in_=trif[:], pattern=[[1, 128]], compare_op=ALU.is_ge,
                                               fill=0.0, base=0, channel_multiplier=-1), reads=[trif.r], writes=[trif.r])
        S.op("pool", lambda e: e.tensor_scalar(out=maskneg[:], in0=trif[:], scalar1=-1.0, scalar2=-NEG, op0=ALU.add, op1=ALU.mult),
             reads=[trif.r], writes=[maskneg.r])

        hpi = sbuf(es, "hpi", [128, 1], F32)
        S.op("pool", lambda e: e.memset(hpi[:], math.pi / 2.0), writes=[hpi.r])
        g1 = sbuf(es, "g1", [128, KD], F32)
        fb = sbuf(es, "fb", [128, 8], F32)
        qg = sbuf(es, "qg", [128, 1], F32)
        kg = sbuf(es, "kg", [128, 1], F32)
        S.dma("sp", "c_g1", g1[:], g1_d, writes=[g1.r])
        S.dma("sp", "c_fb", fb[:], fb_d, writes=[fb.r])
        S.dma("sp", "c_qg", qg[:], qg_d, writes=[qg.r])
        S.dma("sp", "c_kg", kg[:], kg_d, writes=[kg.r])
        S.op("dve", lambda e: e.tensor_scalar(out=qg[:], in0=qg[:], scalar1=128.0 ** -0.5, scalar2=None, op0=ALU.mult),
             reads=[qg.r], writes=[qg.r])

        cast_toks = []
        if stop_after is None:
            for q in range(16):
                cast_toks.append(S.dma("pool", "ucast%d" % q, UTb_d[:, :, q * 1024:(q + 1) * 1024], UT_d[:, :, q * 1024:(q + 1) * 1024]))
                cast_toks.append(S.dma("pool", "vcast%d" % q, Vb_d[:, q * 8:(q + 1) * 8, :], V_d[:, q * 8:(q + 1) * 8, :]))

        stg = [None, None]
        stg_i = [0]

        def alloc_stg(stack):
            for i in range(2):
                stg[i] = sbuf(stack, "wstg%d_%d" % (i, stg_i[0]), [128, KD, 512], F32)

        def load_w(dst, src_d, c0, ncols, gain, nk=KD):
            s = stg[stg_i[0] % 2]
            stg_i[0] += 1
            S.dma("sp", "wld%d" % (stg_i[0] % 2), s[:, 0:nk, 0:ncols], src_d[:, :, c0:c0 + ncols], writes=[s.r])
            if gain is not None:
                S.op("pool", lambda e: e.tensor_tensor(out=dst[:], in0=s[:, 0:nk, 0:ncols],
                                                       in1=gain[:, 0:nk].unsqueeze(2).to_broadcast([128, nk, ncols]), op=ALU.mult),
                     reads=[s.r, gain.r], writes=[dst.r])
            else:
                S.op("pool", lambda e: e.tensor_copy(out=dst[:], in_=s[:, 0:nk, 0:ncols]), reads=[s.r], writes=[dst.r])

        scopeAB = ExitStack()
        alloc_stg(scopeAB)
        hT = sbuf(scopeAB, "hT", [128, KD, T], BF16)
        hT_r = [S.res("hT_%d" % i) for i in range(NT)]
        cum = sbuf(scopeAB, "cum", [128, NT, 8], F32)
        cend = sbuf(scopeAB, "cend", [128, NT + 1, 8], F32)
        with ExitStack() as pa:
            xts = [sbuf(pa, "xt%d" % i, [128, D], F32) for i in range(2)]
            xs = [sbuf(pa, "xs%d" % i, [128, D], BF16) for i in range(2)]
            junk = sbuf(pa, "junkA", [128, D], BF16)
            ss = sbuf(pa, "ss", [128, NT], F32)
            rs = sbuf(pa, "rs", [128, NT], F32)
            wf = sbuf(pa, "wf", [128, KD, 8], BF16)
            zf = sbuf(pa, "zf", [128, NT, 8], F32)
            spf = sbuf(pa, "spf", [128, NT, 8], F32)
            load_w(wf, w_in_d, C_F, 8, g1)
            fbank = banks[7]
            for i in range(NT):
                xt = xts[i % 2]; xb = xs[i % 2]
                S.dma("sp", "xld%d" % (i % 2), xt[:], x_d[i * 128:(i + 1) * 128, :], writes=[xt.r])
                S.op("act", lambda e: e.activation(out=junk[:], in_=xt[:], func=AF.Square, accum_out=ss[:, i:i + 1]),
                     reads=[xt.r], writes=[junk.r, ss.r])
                S.op("act", lambda e: e.activation(out=rs[:, i:i + 1], in_=ss[:, i:i + 1], func=AF.Sqrt, bias=EPS, scale=1.0 / D),
                     reads=[ss.r], writes=[rs.r])
                S.op("dve", lambda e: e.reciprocal(out=rs[:, i:i + 1], in_=rs[:, i:i + 1]), reads=[rs.r], writes=[rs.r])
                S.op("dve", lambda e: e.tensor_scalar(out=xb[:], in0=xt[:], scalar1=rs[:, i:i + 1], scalar2=None, op0=ALU.mult),
                     reads=[xt.r, rs.r], writes=[xb.r])
                bk = banks[i % 2]
                bv = bk[:].bitcast(BF16)
                for k in range(KD):
                    S.op("pe", lambda e: e.transpose(out=bv[:, k * 128:(k + 1) * 128], in_=xb[:, k * 128:(k + 1) * 128], identity=identb[:]),
                         reads=[xb.r, identb.r], writes=[bk.r])
                S.op("act", lambda e: e.copy(out=hT[:, :, i * 128:(i + 1) * 128], in_=bv[:, 0:1024].rearrange("p (k t) -> p k t", k=KD)),
                     reads=[bk.r], writes=[hT_r[i]])
                for k in range(KD):
                    S.op("pe", lambda e: e.matmul(fbank[:, i * 8:(i + 1) * 8], lhsT=hT[:, k, i * 128:(i + 1) * 128], rhs=wf[:, k, :],
                                                  start=(k == 0), stop=(k == KD - 1)),
                         reads=[hT_r[i], wf.r], writes=[fbank.r])
            S.op("dve", lambda e: e.tensor_tensor(out=zf[:], in0=fbank[:, 0:NT * 8].rearrange("p (i h) -> p i h", h=8),
                                                  in1=fb[:].unsqueeze(1).to_broadcast([128, NT, 8]), op=ALU.add),
                 reads=[fbank.r, fb.r], writes=[zf.r])
            S.op("act", lambda e: e.activation(out=spf[:], in_=zf[:], func=AF.Exp, scale=-1.0), reads=[zf.r], writes=[spf.r])
            S.op("act", lambda e: e.activation(out=spf[:], in_=spf[:], func=AF.Ln, bias=1.0, scale=1.0), reads=[spf.r], writes=[spf.r])
            cb1 = banks[5]; cb2 = banks[6]
            for i in range(NT):
                S.op("pe", lambda e: e.matmul(cb1[:, i * 8:(i + 1) * 8], lhsT=trif[:], rhs=spf[:, i, :], start=True, stop=True),
                     reads=[trif.r, spf.r], writes=[cb1.r])
                S.op("pe", lambda e: e.matmul(cb2[:, i * 8:(i + 1) * 8], lhsT=onesf[:], rhs=spf[:, i, :], start=True, stop=True),
                     reads=[onesf.r, spf.r], writes=[cb2.r])
            S.op("dve", lambda e: e.memset(cend[:, 0, :], 0.0), writes=[cend.r])
            for i in range(NT):
                S.op("dve", lambda e: e.tensor_tensor(out=cend[:, i + 1, :], in0=cend[:, i, :], in1=cb2[:, i * 8:(i + 1) * 8], op=ALU.add),
                     reads=[cend.r, cb2.r], writes=[cend.r])
            S.op("dve", lambda e: e.tensor_tensor(out=cum[:], in0=cb1[:, 0:NT * 8].rearrange("p (i h) -> p i h", h=8),
                                                  in1=cend[:, 0:NT, :], op=ALU.add),
                 reads=[cb1.r, cend.r], writes=[cum.r])

        for c in range(NCH):
            S.dma("sp", "hsp%d" % (c % 2), hT_d[:, :, c * 512:(c + 1) * 512], hT[:, :, c * 512:(c + 1) * 512], reads=hT_r[c * 4:(c + 1) * 4])
        S.barrier()
        if stop_after == "A":
            return nc
        with ExitStack() as pb:
            wq = sbuf(pb, "wq", [128, KD, 128], BF16)
            wk = sbuf(pb, "wk", [128, KD, 128], BF16)
            wv = sbuf(pb, "wv", [128, KD, 128], BF16)
            wg = sbuf(pb, "wg", [128, KD, 128], BF16)
            qT = sbuf(pb, "qT", [128, T], BF16)
            kT = sbuf(pb, "kT", [128, T], BF16)
            vv = sbuf(pb, "vv", [128, NT, 128], BF16)
            sgT = sbuf(pb, "sgT", [128, T], BF16)
            nb = sbuf(pb, "nb", [128, NT, NT], F32)
            sq = [sbuf(pb, "sq%d" % i, [128, 512], BF16) for i in range(2)]
            rrep = [sbuf(pb, "rrep%d" % i, [128, 512], F32) for i in range(2)]
            Pt = [sbuf(pb, "Pt%d" % i, [128, 512], BF16) for i in range(3)]
            rec = sbuf(pb, "rec", [128, 512], F32)
            yv = sbuf(pb, "yv", [128, 512], F32)
            ym = [sbuf(pb, "ym%d" % i, [128, 512], BF16) for i in range(2)]
            hT_all = hT_r
            pcount = [0]
            print("SBUF remaining in phase B:", nc.sbuf_bytes_remaining)
            for h in range(8):
                load_w(wq, w_in_d, C_Q + h * 128, 128, g1)
                load_w(wk, w_in_d, C_K + h * 128, 128, g1)
                load_w(wv, w_in_d, C_V + h * 128, 128, g1)
                load_w(wg, w_in_d, C_GA + h * 128, 128, g1)
                for qs in range(NT):
                    S.op("dve", lambda e: e.tensor_scalar(out=nb[:, qs, 0:qs + 1], in0=cum[:, 0:qs + 1, h], scalar1=cend[:, qs + 1, h:h + 1],
                                                          scalar2=None, op0=ALU.subtract),
                         reads=[cum.r, cend.r], writes=[nb.r])
                for which, (wt, dstT, gvec) in enumerate(((wq, qT, qg), (wk, kT, kg))):
                    for c in range(NCH):
                        pj = banks[(2 * c) % 4]; pn = banks[(2 * c + 1) % 4]
                        for k in range(KD):
                            S.op("pe", lambda e: e.matmul(pj[:], lhsT=wt[:, k, :], rhs=hT[:, k, c * 512:(c + 1) * 512], start=(k == 0), stop=(k == KD - 1)),
                                 reads=[wt.r] + hT_all[c * 4:(c + 1) * 4], writes=[pj.r])
                        sqb = sq[c % 2]; rr = rrep[c % 2]
                        S.op("act", lambda e: e.activation(out=sqb[:], in_=pj[:], func=AF.Square), reads=[pj.r], writes=[sqb.r])
                        S.op("pe", lambda e: e.matmul(pn[:], lhsT=onesb[:], rhs=sqb[:], start=True, stop=True), reads=[onesb.r, sqb.r], writes=[pn.r])
                        S.op("act", lambda e: e.activation(out=rr[:], in_=pn[:], func=AF.Sqrt, bias=EPS, scale=1.0 / 128.0), reads=[pn.r], writes=[rr.r])
                        S.op("dve", lambda e: e.reciprocal(out=rr[:], in_=rr[:]), reads=[rr.r], writes=[rr.r])
                        S.op("dve", lambda e: e.scalar_tensor_tensor(out=dstT[:, c * 512:(c + 1) * 512], in0=pj[:], scalar=gvec[:, 0:1], in1=rr[:],
                                                                     op0=ALU.mult, op1=ALU.mult),
                             reads=[pj.r, gvec.r, rr.r], writes=[dstT.r])
                for i4 in range(NT // 4):
                    pv = banks[i4 % 2]
                    for ii in range(4):
                        i = i4 * 4 + ii
                        for k in range(KD):
                            S.op("pe", lambda e: e.matmul(pv[:, ii * 128:(ii + 1) * 128], lhsT=hT[:, k, i * 128:(i + 1) * 128], rhs=wv[:, k, :],
                                                          start=(k == 0), stop=(k == KD - 1)),
                                 reads=[wv.r, hT_all[i]], writes=[pv.r])
                    S.op("act", lambda e: e.copy(out=vv[:, i4 * 4:(i4 + 1) * 4, :], in_=pv[:].rearrange("p (a b) -> p a b", a=4)),
                         reads=[pv.r], writes=[vv.r])
                for c in range(NCH):
                    pg = banks[2 + (c % 2)]
                    for k in range(KD):
                        S.op("pe", lambda e: e.matmul(pg[:], lhsT=wg[:, k, :], rhs=hT[:, k, c * 512:(c + 1) * 512], start=(k == 0), stop=(k == KD - 1)),
                             reads=[wg.r] + hT_all[c * 4:(c + 1) * 4], writes=[pg.r])
                    S.op("act", lambda e: e.activation(out=sgT[:, c * 512:(c + 1) * 512], in_=pg[:], func=AF.Sigmoid), reads=[pg.r], writes=[sgT.r])
                for qc in range(NCH):
                    pO = banks[4 + (qc % 2)]; pL = banks[6 + (qc % 2)]
                    nkb = (qc + 1) * 4
                    def emit_S(kb):
                        q_lo = max(qc * 4, kb)
                        off = (q_lo - qc * 4) * 128
                        pS = banks[pcount[0] % 4]
                        Pb = Pt[pcount[0] % 3]
                        pcount[0] += 1
                        diag = kb >= qc * 4
                        S.op("pe", lambda e: e.matmul(pS[:, off:512], lhsT=kT[:, kb * 128:(kb + 1) * 128], rhs=qT[:, q_lo * 128:(qc + 1) * 512],
                                                      start=True, stop=not diag),
                             reads=[kT.r, qT.r], writes=[pS.r])
                        if diag:
                            S.op("pe", lambda e: e.matmul(pS[:, off:off + 128], lhsT=identb[:], rhs=maskneg[:], start=False, stop=True),
                                 reads=[identb.r, maskneg.r], writes=[pS.r])
                        for qs in range(q_lo, qc * 4 + 4):
                            o2 = (qs - qc * 4) * 128
                            S.op("act", lambda e: e.activation(out=Pb[:, o2:o2 + 128], in_=pS[:, o2:o2 + 128], func=AF.Exp,
                                                               bias=nb[:, qs, kb:kb + 1], scale=1.0),
                                 reads=[pS.r, nb.r], writes=[Pb.r])
                        return (Pb, off)

                    def emit_PV(kb, Pb, off):
                        S.op("pe", lambda e: e.matmul(pO[:, off:512], lhsT=vv[:, kb, :], rhs=Pb[:, off:512], start=(kb == 0), stop=(kb == nkb - 1)),
                             reads=[vv.r, Pb.r], writes=[pO.r])
                        S.op("pe", lambda e: e.matmul(pL[:, off:512], lhsT=onesb[:], rhs=Pb[:, off:512], start=(kb == 0), stop=(kb == nkb - 1)),
                             reads=[onesb.r, Pb.r], writes=[pL.r])

                    prev = None
                    for kb in range(nkb):
                        cur = emit_S(kb)
                        if prev is not None:
                            emit_PV(kb - 1, *prev)
                        prev = cur
                    emit_PV(nkb - 1, *prev)
                    S.op("dve", lambda e: e.reciprocal(out=rec[:], in_=pL[:]), reads=[pL.r], writes=[rec.r])
                    S.op("dve", lambda e: e.tensor_tensor(out=yv[:], in0=pO[:], in1=rec[:], op=ALU.mult), reads=[pO.r, rec.r], writes=[yv.r])
                    ymb = ym[qc % 2]
                    S.op("dve", lambda e: e.tensor_tensor(out=ymb[:], in0=yv[:], in1=sgT[:, qc * 512:(qc + 1) * 512], op=ALU.mult),
                         reads=[yv.r, sgT.r], writes=[ymb.r])
                    S.dma("sp", "mst%d" % (qc % 2), mT_d[h * 128:(h + 1) * 128, qc * 512:(qc + 1) * 512], ymb[:], reads=[ymb.r])


        S.barrier()
        scopeAB.close()
        if stop_after == "B":
            S.wait_all_dma("sp")
            return nc

        LT = 128
        with ExitStack() as pc:
            def small(name, shape=(128, 16)):
                return sbuf(pc, name, list(shape), F32)
            a_re = small("a_re"); a_im = small("a_im"); lsl = small("lsl")
            S.dma("sp", "c_are", a_re[:], are_d, writes=[a_re.r])
            S.dma("sp", "c_aim", a_im[:], aim_d, writes=[a_im.r])
            S.dma("sp", "c_ls", lsl[:], ls_d, writes=[lsl.r])
            dsk = sbuf(pc, "dsk", [128, 4], F32); g2 = sbuf(pc, "g2", [128, D], F32)
            S.dma("sp", "c_dsk", dsk[:], dsk_d, writes=[dsk.r]); S.dma("sp", "c_g2", g2[:], g2_d, writes=[g2.r])
            Cpad = sbuf(pc, "Cpad", [128, 16, 2, 128], F32)
            Bpad = sbuf(pc, "Bpad", [128, 16, 2, 128], F32)
            Tc = sbuf(pc, "Tc", [128, 16, LT], F32); Ts = sbuf(pc, "Ts", [128, 16, LT], F32)
            wu = sbuf(pc, "wu", [128, KD, 512], BF16)
            wgs = sbuf(pc, "wgs", [128, KD, D], BF16)
            wglu = sbuf(pc, "wglu", [128, 4, 2048], BF16)
            wout = sbuf(pc, "wout", [128, KD, D], BF16)
            step = small("step"); th = small("th"); mag = small("mag"); cs = small("cs"); sn = small("sn")
            ki = sbuf(pc, "ki", [128, 16], I32); kf = small("kf"); rr_ = small("rr_"); ab = small("ab")
            lre = small("lre"); lim = small("lim"); den = small("den"); t1 = small("t1"); t2 = small("t2")
            cfr = small("cfr"); cfi = small("cfi")
            wr_ = small("wr_"); wi_ = small("wi_"); wt1 = small("wt1"); wt2 = small("wt2")
            pset = ExitStack()
            alloc_stg(pset)
            bre = sbuf(pset, "bre", [128, 16, 16], F32); bim = sbuf(pset, "bim", [128, 16, 16], F32)
            cre = sbuf(pset, "cre", [128, 16, 16], F32); cim = sbuf(pset, "cim", [128, 16, 16], F32)
            S.dma("sp", "c_bre", bre[:], bre_d, writes=[bre.r]); S.dma("sp", "c_bim", bim[:], bim_d, writes=[bim.r])
            S.dma("sp", "c_cre", cre[:], cre_d, writes=[cre.r]); S.dma("sp", "c_cim", cim[:], cim_d, writes=[cim.r])
            bbr = sbuf(pset, "bbr", [128, 16, 16], F32); bbi = sbuf(pset, "bbi", [128, 16, 16], F32)
            u1 = sbuf(pset, "u1", [128, 16, 16], F32); u2 = sbuf(pset, "u2", [128, 16, 16], F32)
            BZ = sbuf(pset, "BZ", [128, 16, 2, 128], F32)
            p1 = sbuf(pset, "p1", [128, 16, LT // 2], F32); p2 = sbuf(pset, "p2", [128, 16, LT // 2], F32)

            def tt(eng, out, a, b, op, rd, wr):
                S.op(eng, lambda e: e.tensor_tensor(out=out, in0=a, in1=b, op=op), reads=rd, writes=wr)

            S.op("act", lambda e: e.activation(out=step[:], in_=lsl[:], func=AF.Exp), reads=[lsl.r], writes=[step.r])
            tt("dve", th[:], a_im[:], step[:], ALU.mult, [a_im.r, step.r], [th.r])
            tt("dve", mag[:], a_re[:], step[:], ALU.mult, [a_re.r, step.r], [mag.r])
            S.op("act", lambda e: e.activation(out=mag[:], in_=mag[:], func=AF.Exp), reads=[mag.r], writes=[mag.r])
            S.op("dve", lambda e: e.tensor_scalar(out=ki[:], in0=th[:], scalar1=1.0 / TWO_PI, scalar2=None, op0=ALU.mult), reads=[th.r], writes=[ki.r])
            S.op("dve", lambda e: e.tensor_copy(out=kf[:], in_=ki[:]), reads=[ki.r], writes=[kf.r])
            S.op("dve", lambda e: e.scalar_tensor_tensor(out=rr_[:], in0=kf[:], scalar=-TWO_PI, in1=th[:], op0=ALU.mult, op1=ALU.add),
                 reads=[kf.r, th.r], writes=[rr_.r])
            PI_LO = 3.1415925
            S.op("dve", lambda e: e.tensor_scalar(out=rr_[:], in0=rr_[:], scalar1=PI_LO, scalar2=-PI_LO, op0=ALU.min, op1=ALU.max), reads=[rr_.r], writes=[rr_.r])
            S.op("act", lambda e: e.activation(out=sn[:], in_=rr_[:], func=AF.Sin), reads=[rr_.r], writes=[sn.r])
            S.op("act", lambda e: e.activation(out=ab[:], in_=rr_[:], func=AF.Abs), reads=[rr_.r], writes=[ab.r])
            S.op("act", lambda e: e.activation(out=cs[:], in_=ab[:], func=AF.Sin, scale=-1.0, bias=hpi[:, 0:1]), reads=[ab.r, hpi.r], writes=[cs.r])
            tt("dve", lre[:], mag[:], cs[:], ALU.mult, [mag.r, cs.r], [lre.r])
            tt("dve", lim[:], mag[:], sn[:], ALU.mult, [mag.r, sn.r], [lim.r])
            S.op("dve", lambda e: e.tensor_scalar(out=lre[:], in0=lre[:], scalar1=-1.0, scalar2=None, op0=ALU.add), reads=[lre.r], writes=[lre.r])
            tt("dve", t1[:], a_re[:], a_re[:], ALU.mult, [a_re.r], [t1.r])
            tt("dve", t2[:], a_im[:], a_im[:], ALU.mult, [a_im.r], [t2.r])
            tt("dve", den[:], t1[:], t2[:], ALU.add, [t1.r, t2.r], [den.r])
            S.op("dve", lambda e: e.reciprocal(out=den[:], in_=den[:]), reads=[den.r], writes=[den.r])
            tt("dve", t1[:], lre[:], a_re[:], ALU.mult, [lre.r, a_re.r], [t1.r])
            tt("dve", t2[:], lim[:], a_im[:], ALU.mult, [lim.r, a_im.r], [t2.r])
            tt("dve", cfr[:], t1[:], t2[:], ALU.add, [t1.r, t2.r], [cfr.r])
            tt("dve", cfr[:], cfr[:], den[:], ALU.mult, [cfr.r, den.r], [cfr.r])
            tt("dve", t1[:], lim[:], a_re[:], ALU.mult, [lim.r, a_re.r], [t1.r])
            tt("dve", t2[:], lre[:], a_im[:], ALU.mult, [lre.r, a_im.r], [t2.r])
            tt("dve", cfi[:], t1[:], t2[:], ALU.subtract, [t1.r, t2.r], [cfi.r])
            tt("dve", cfi[:], cfi[:], den[:], ALU.mult, [cfi.r, den.r], [cfi.r])
            bc3 = lambda v: v[:].unsqueeze(2).to_broadcast([128, 16, 16])
            tt("dve", u1[:], bre[:], bc3(cfr), ALU.mult, [bre.r, cfr.r], [u1.r])
            tt("dve", u2[:], bim[:], bc3(cfi), ALU.mult, [bim.r, cfi.r], [u2.r])
            tt("dve", bbr[:], u1[:], u2[:], ALU.subtract, [u1.r, u2.r], [bbr.r])
            tt("dve", u1[:], bim[:], bc3(cfr), ALU.mult, [bim.r, cfr.r], [u1.r])
            tt("dve", u2[:], bre[:], bc3(cfi), ALU.mult, [bre.r, cfi.r], [u2.r])
            tt("dve", bbi[:], u1[:], u2[:], ALU.add, [u1.r, u2.r], [bbi.r])
            S.op("dve", lambda e: e.tensor_scalar(out=cim[:], in0=cim[:], scalar1=-1.0, scalar2=None, op0=ALU.mult), reads=[cim.r], writes=[cim.r])
            if True:
                S.op("pool", lambda e: e.memset(Cpad[:], 0.0), writes=[Cpad.r])
                S.op("pool", lambda e: e.memset(BZ[:], 0.0), writes=[BZ.r])
                for j in range(16):
                    for two in range(2):
                        c0 = 32 * (j % 4) + 16 * two
                        ps_ = slice(two * 64, (two + 1) * 64)
                        for ri, (srcC, srcB) in enumerate(((cre, bbr), (cim, bbi))):
                            S.op("pool", lambda e: e.tensor_copy(out=Cpad[ps_, j, ri, c0:c0 + 16], in_=srcC[ps_, j, :]), reads=[srcC.r], writes=[Cpad.r])
                            S.op("pool", lambda e: e.tensor_copy(out=BZ[ps_, j, ri, c0:c0 + 16], in_=srcB[ps_, j, :]), reads=[srcB.r], writes=[BZ.r])
                for j in range(16):
                    for ri in range(2):
                        bk = banks[(2 * j + ri) % 4]
                        S.op("pe", lambda e: e.transpose(out=bk[:, 0:128], in_=BZ[:, j, ri, :], identity=identf[:]), reads=[BZ.r, identf.r], writes=[bk.r])
                        S.op("act", lambda e: e.copy(out=Bpad[:, j, ri, :], in_=bk[:, 0:128]), reads=[bk.r], writes=[Bpad.r])
            S.op("dve", lambda e: e.tensor_copy(out=wr_[:], in_=cs[:]), reads=[cs.r], writes=[wr_.r])
            S.op("dve", lambda e: e.tensor_copy(out=wi_[:], in_=sn[:]), reads=[sn.r], writes=[wi_.r])
            S.op("pool", lambda e: e.memset(Tc[:, :, 0:1], 1.0), writes=[Tc.r])
            S.op("pool", lambda e: e.memset(Ts[:, :, 0:1], 0.0), writes=[Ts.r])
            if True:
                n = 1
                while n < LT:
                    bw = lambda v: v[:].unsqueeze(2).to_broadcast([128, 16, n])
                    tt("dve", p1[:, :, 0:n], Tc[:, :, 0:n], bw(wr_), ALU.mult, [Tc.r, wr_.r], [p1.r])
                    tt("dve", p2[:, :, 0:n], Ts[:, :, 0:n], bw(wi_), ALU.mult, [Ts.r, wi_.r], [p2.r])
                    tt("dve", Tc[:, :, n:2 * n], p1[:, :, 0:n], p2[:, :, 0:n], ALU.subtract, [p1.r, p2.r], [Tc.r])
                    tt("dve", p1[:, :, 0:n], Tc[:, :, 0:n], bw(wi_), ALU.mult, [Tc.r, wi_.r], [p1.r])
                    tt("dve", p2[:, :, 0:n], Ts[:, :, 0:n], bw(wr_), ALU.mult, [Ts.r, wr_.r], [p2.r])
                    tt("dve", Ts[:, :, n:2 * n], p1[:, :, 0:n], p2[:, :, 0:n], ALU.add, [p1.r, p2.r], [Ts.r])
                    tt("dve", wt1[:], wr_[:], wr_[:], ALU.mult, [wr_.r], [wt1.r])
                    tt("dve", wt2[:], wi_[:], wi_[:], ALU.mult, [wi_.r], [wt2.r])
                    tt("dve", wi_[:], wr_[:], wi_[:], ALU.mult, [wr_.r, wi_.r], [wi_.r])
                    S.op("dve", lambda e: e.tensor_scalar(out=wi_[:], in0=wi_[:], scalar1=2.0, scalar2=None, op0=ALU.mult), reads=[wi_.r], writes=[wi_.r])
                    tt("dve", wr_[:], wt1[:], wt2[:], ALU.subtract, [wt1.r, wt2.r], [wr_.r])
                    n *= 2
            wv_ = lambda buf, a, b: Buf(buf.t[:, :, a:b], buf.r)
            load_w(wu, w_in_d, C_U, 512, g1)
            for hh in range(2):
                load_w(wv_(wgs, hh * 512, (hh + 1) * 512), w_in_d, C_GS + hh * 512, 512, g1)
                load_w(wv_(wout, hh * 512, (hh + 1) * 512), wout_d, hh * 512, 512, None)
            for qq in range(4):
                load_w(wv_(wglu, qq * 512, (qq + 1) * 512), wglu_d, qq * 512, 512, None, nk=4)
            S.barrier()
            pset.close()
            hTc = [sbuf(pc, "hTc%d" % i, [128, KD, 512], BF16) for i in range(1)]
            matt = [sbuf(pc, "matt%d" % i, [128, KD, 512], BF16) for i in range(1)]
            u_sb = sbuf(pc, "u_sb", [128, 4, 512], F32)
            ssets = []
            for si in range(2):
                B_ = {}
                for nm in ("xr_sb", "xi_sb", "ta", "tb", "tcb", "td", "r_r", "r_i"):
                    B_[nm] = sbuf(pc, "%s_%d" % (nm, si), [128, 512], F32)
                B_["ctmp"] = sbuf(pc, "ctmp_%d" % si, [128, 2], F32)
                B_["pxr"] = banks[2 + 3 * si]; B_["pxi"] = banks[3 + 3 * si]
                ssets.append(B_)
            carry = sbuf(pc, "carry", [128, 16, 2], F32)
            carry_r = [S.res("carry_%d" % j) for j in range(16)]
            yv_ = sbuf(pc, "yv_", [128, 512], F32)
            yg = sbuf(pc, "yg", [128, 4, 512], BF16)
            sg1 = sbuf(pc, "sg1", [128, 512], F32); sg2 = sbuf(pc, "sg2", [128, 512], F32)
            ys = sbuf(pc, "ys", [128, 512], F32)
            merged = sbuf(pc, "merged", [128, KD, 512], BF16)
            xres = [sbuf(pc, "xres%d" % i, [128, D], F32) for i in range(2)]
            x1t = [sbuf(pc, "x1t%d" % i, [128, D], F32) for i in range(1)]
            h2b = [sbuf(pc, "h2b%d" % i, [128, D], BF16) for i in range(1)]
            h2c = [sbuf(pc, "h2c%d" % i, [128, KD, 128], BF16) for i in range(2)]
            ss2 = sbuf(pc, "ss2", [128, NT], F32); rs2 = sbuf(pc, "rs2", [128, NT], F32)
            print("SBUF remaining in phase C:", nc.sbuf_bytes_remaining)
            S.op("pool", lambda e: e.memset(carry[:], 0.0), writes=carry_r)
            v2 = lambda ap: ap.rearrange("p (a b) -> p a b", a=512 // LT)
            tb3 = lambda tab, j: tab[:, j, :].unsqueeze(1).to_broadcast([128, 512 // LT, LT])
            mT_v = mT_d.rearrange("(dc p) t -> p dc t", p=128)
            for c in range(NCH):
                hc = hTc[0]; mt = matt[0]
                S.dma("sp", "hcl", hc[:], hT_d[:, :, c * 512:(c + 1) * 512], writes=[hc.r])
                S.dma("sp", "mtl", mt[:], mT_v[:, :, c * 512:(c + 1) * 512], writes=[mt.r])
                for rc in range(4):
                    pu = banks[rc % 2]
                    for k in range(KD):
                        S.op("pe", lambda e: e.matmul(pu[:], lhsT=wu[:, k, rc * 128:(rc + 1) * 128], rhs=hc[:, k, :], start=(k == 0), stop=(k == KD - 1)),
                             reads=[wu.r, hc.r], writes=[pu.r])
                    S.op("act", lambda e: e.copy(out=u_sb[:, rc, :], in_=pu[:]), reads=[pu.r], writes=[u_sb.r])
                def ssm_gen(j, B_):
                    rc = j // 4
                    pxr = B_["pxr"]; pxi = B_["pxi"]; py = banks[4]
                    xr_sb = B_["xr_sb"]; xi_sb = B_["xi_sb"]; ta = B_["ta"]; tb = B_["tb"]; tcb = B_["tcb"]; td = B_["td"]
                    r_r = B_["r_r"]; r_i = B_["r_i"]; ctmp = B_["ctmp"]; cr = carry_r[j]
                    xtr = ta; xti = tcb; s_r = ta; s_i = tcb
                    S.op("pe", lambda e: e.matmul(pxr[:], lhsT=Bpad[:, j, 0, :], rhs=u_sb[:, rc, :], start=True, stop=True), reads=[Bpad.r, u_sb.r], writes=[pxr.r])
                    S.op("pe", lambda e: e.matmul(pxi[:], lhsT=Bpad[:, j, 1, :], rhs=u_sb[:, rc, :], start=True, stop=True), reads=[Bpad.r, u_sb.r], writes=[pxi.r])
                    yield
                    S.op("act", lambda e: e.copy(out=xr_sb[:], in_=pxr[:]), reads=[pxr.r], writes=[xr_sb.r])
                    S.op("act", lambda e: e.copy(out=xi_sb[:], in_=pxi[:]), reads=[pxi.r], writes=[xi_sb.r])
                    yield
                    tt("pool", v2(ta[:]), v2(xr_sb[:]), tb3(Tc, j), ALU.mult, [xr_sb.r, Tc.r], [ta.r])
                    tt("pool", v2(tb[:]), v2(xi_sb[:]), tb3(Ts, j), ALU.mult, [xi_sb.r, Ts.r], [tb.r])
                    yield
                    tt("pool", v2(tcb[:]), v2(xi_sb[:]), tb3(Tc, j), ALU.mult, [xi_sb.r, Tc.r], [tcb.r])
                    tt("pool", v2(td[:]), v2(xr_sb[:]), tb3(Ts, j), ALU.mult, [xr_sb.r, Ts.r], [td.r])
                    yield
                    tt("dve", xtr[:], ta[:], tb[:], ALU.add, [ta.r, tb.r], [xtr.r])
                    tt("dve", xti[:], tcb[:], td[:], ALU.subtract, [tcb.r, td.r], [xti.r])
                    yield
                    for sgi in range(512 // LT):
                        sl = slice(sgi * LT, (sgi + 1) * LT)
                        magb = mag[:, j:j + 1].to_broadcast([128, LT])
                        S.op("dve", lambda e: e.tensor_tensor_scan(out=r_r[:, sl], data0=magb, data1=xtr[:, sl], initial=carry[:, j, 0:1], op0=ALU.mult, op1=ALU.add),
                             reads=[mag.r, xtr.r, cr], writes=[r_r.r])
                        S.op("dve", lambda e: e.tensor_tensor_scan(out=r_i[:, sl], data0=magb, data1=xti[:, sl], initial=carry[:, j, 1:2], op0=ALU.mult, op1=ALU.add),
                             reads=[mag.r, xti.r, cr], writes=[r_i.r])
                        yield
                        last = sgi * LT + LT - 1
                        tt("dve", ctmp[:, 0:1], r_i[:, last:last + 1], wi_[:, j:j + 1], ALU.mult, [r_i.r, wi_.r], [ctmp.r])
                        tt("dve", ctmp[:, 1:2], r_i[:, last:last + 1], wr_[:, j:j + 1], ALU.mult, [r_i.r, wr_.r], [ctmp.r])
                        yield
                        S.op("dve", lambda e: e.scalar_tensor_tensor(out=carry[:, j, 0:1], in0=r_r[:, last:last + 1], scalar=wr_[:, j:j + 1], in1=ctmp[:, 0:1],
                                                                     op0=ALU.mult, op1=ALU.subtract), reads=[r_r.r, wr_.r, ctmp.r], writes=[cr])
                        S.op("dve", lambda e: e.scalar_tensor_tensor(out=carry[:, j, 1:2], in0=r_r[:, last:last + 1], scalar=wi_[:, j:j + 1], in1=ctmp[:, 1:2],
                                                                     op0=ALU.mult, op1=ALU.add), reads=[r_r.r, wi_.r, ctmp.r], writes=[cr])
                        yield
                    tt("pool", v2(ta[:]), v2(r_r[:]), tb3(Tc, j), ALU.mult, [r_r.r, Tc.r], [ta.r])
                    tt("pool", v2(tb[:]), v2(r_i[:]), tb3(Ts, j), ALU.mult, [r_i.r, Ts.r], [tb.r])
                    yield
                    tt("pool", v2(tcb[:]), v2(r_r[:]), tb3(Ts, j), ALU.mult, [r_r.r, Ts.r], [tcb.r])
                    tt("pool", v2(td[:]), v2(r_i[:]), tb3(Tc, j), ALU.mult, [r_i.r, Tc.r], [td.r])
                    yield
                    tt("dve", s_r[:], ta[:], tb[:], ALU.subtract, [ta.r, tb.r], [s_r.r])
                    tt("dve", s_i[:], tcb[:], td[:], ALU.add, [tcb.r, td.r], [s_i.r])
                    yield
                    S.op("pe", lambda e: e.matmul(py[:], lhsT=Cpad[:, j, 0, :], rhs=s_r[:], start=(j % 4 == 0), stop=False), reads=[Cpad.r, s_r.r], writes=[py.r])
                    S.op("pe", lambda e: e.matmul(py[:], lhsT=Cpad[:, j, 1, :], rhs=s_i[:], start=False, stop=(j % 4 == 3)), reads=[Cpad.r, s_i.r], writes=[py.r])
                    yield
                    if j % 4 == 3:
                        S.op("dve", lambda e: e.scalar_tensor_tensor(out=yv_[:], in0=u_sb[:, rc, :], scalar=dsk[:, rc:rc + 1], in1=py[:], op0=ALU.mult, op1=ALU.add),
                             reads=[u_sb.r, dsk.r, py.r], writes=[yv_.r])
                        S.op("act", lambda e: e.activation(out=yg[:, rc, :], in_=yv_[:], func=AF.Gelu_apprx_tanh), reads=[yv_.r], writes=[yg.r])

                for pair in range(8):
                    gens = [ssm_gen(2 * pair, ssets[0]), ssm_gen(2 * pair + 1, ssets[1])]
                    alive = [True, True]
                    while any(alive):
                        for gi in range(2):
                            if alive[gi]:
                                try:
                                    next(gens[gi])
                                except StopIteration:
                                    alive[gi] = False
                for dc in range(KD):
                    pvl = banks[5]; pgt = banks[6]; pgs = banks[7]
                    for rc in range(4):
                        S.op("pe", lambda e: e.matmul(pvl[:], lhsT=wglu[:, rc, dc * 128:(dc + 1) * 128], rhs=yg[:, rc, :], start=(rc == 0), stop=(rc == 3)),
                             reads=[wglu.r, yg.r], writes=[pvl.r])
                    for rc in range(4):
                        S.op("pe", lambda e: e.matmul(pgt[:], lhsT=wglu[:, rc, D + dc * 128:D + (dc + 1) * 128], rhs=yg[:, rc, :], start=(rc == 0), stop=(rc == 3)),
                             reads=[wglu.r, yg.r], writes=[pgt.r])
                    for k in range(KD):
                        S.op("pe", lambda e: e.matmul(pgs[:], lhsT=wgs[:, k, dc * 128:(dc + 1) * 128], rhs=hc[:, k, :], start=(k == 0), stop=(k == KD - 1)),
                             reads=[wgs.r, hc.r], writes=[pgs.r])
                    S.op("act", lambda e: e.activation(out=sg1[:], in_=pgt[:], func=AF.Sigmoid), reads=[pgt.r], writes=[sg1.r])
                    S.op("act", lambda e: e.activation(out=sg2[:], in_=pgs[:], func=AF.Sigmoid), reads=[pgs.r], writes=[sg2.r])
                    tt("dve", ys[:], pvl[:], sg1[:], ALU.mult, [pvl.r, sg1.r], [ys.r])
                    tt("pool", ys[:], ys[:], sg2[:], ALU.mult, [ys.r, sg2.r], [ys.r])
                    tt("pool", merged[:, dc, :], ys[:], mt[:, dc, :], ALU.add, [ys.r, mt.r], [merged.r])
                if dbg:
                    S.dma("sp", "m2st", m2T_d[:, :, c * 512:(c + 1) * 512], merged[:], reads=[merged.r])
                for ti in range(4):
                    i = c * 4 + ti
                    xr_ = xres[i % 2]; x1 = x1t[0]; hb = h2b[0]; hcp = h2c[i % 2]; junkC = hb
                    S.dma("sp", "xrl%d" % (i % 2), xr_[:], x_d[i * 128:(i + 1) * 128, :], writes=[xr_.r])
                    for nh in range(2):
                        po = banks[nh]
                        for dc in range(KD):
                            S.op("pe", lambda e: e.matmul(po[:], lhsT=merged[:, dc, ti * 128:(ti + 1) * 128], rhs=wout[:, dc, nh * 512:(nh + 1) * 512],
                                                          start=(dc == 0), stop=(dc == KD - 1)), reads=[merged.r, wout.r], writes=[po.r])
                        tt("dve", x1[:, nh * 512:(nh + 1) * 512], po[:], xr_[:, nh * 512:(nh + 1) * 512], ALU.add, [po.r, xr_.r], [x1.r])
                    S.dma("sp", "x1st", out_d[i * 128:(i + 1) * 128, :], x1[:], reads=[x1.r])
                    S.op("act", lambda e: e.activation(out=junkC[:], in_=x1[:], func=AF.Square, accum_out=ss2[:, i:i + 1]), reads=[x1.r], writes=[junkC.r, ss2.r])
                    S.op("act", lambda e: e.activation(out=rs2[:, i:i + 1], in_=ss2[:, i:i + 1], func=AF.Sqrt, bias=EPS, scale=1.0 / D), reads=[ss2.r], writes=[rs2.r])
                    S.op("dve", lambda e: e.reciprocal(out=rs2[:, i:i + 1], in_=rs2[:, i:i + 1]), reads=[rs2.r], writes=[rs2.r])
                    S.op("dve", lambda e: e.scalar_tensor_tensor(out=hb[:], in0=x1[:], scalar=rs2[:, i:i + 1], in1=g2[:], op0=ALU.mult, op1=ALU.mult), reads=[x1.r, rs2.r, g2.r], writes=[hb.r])
                    bk = banks[2 + (i % 2)]
                    bv = bk[:].bitcast(BF16)
                    for k in range(KD):
                        S.op("pe", lambda e: e.transpose(out=bv[:, k * 128:(k + 1) * 128], in_=hb[:, k * 128:(k + 1) * 128], identity=identb[:]),
                             reads=[hb.r, identb.r], writes=[bk.r])
                    S.op("act", lambda e: e.copy(out=hcp[:], in_=bv[:, 0:1024].rearrange("p (k t) -> p k t", k=KD)), reads=[bk.r], writes=[hcp.r])
                    S.dma("sp", "h2st%d" % (i % 2), h2T_d[:, :, i * 128:(i + 1) * 128], hcp[:], reads=[hcp.r])
            S.barrier()
        if stop_after == "C":
            S.wait_all_dma("sp")
            return nc

        with ExitStack() as pd:
            alloc_stg(pd)
            wq = sbuf(pd, "wqp", [128, KD, 2048], BF16)
            skT = sbuf(pd, "skT", [128, 16, 128], BF16)
            for qq in range(4):
                load_w(Buf(wq.t[:, :, qq * 512:(qq + 1) * 512], wq.r), wqr_d, qq * 512, 512, None)
            for hh in range(2):
                load_w(Buf(skT.t[:, hh * 8:(hh + 1) * 8, :], skT.r), skT_d[:, hh * 8:(hh + 1) * 8, :], 0, 128, None)
            iota16 = sbuf(pd, "iota16", [128, 16], F32)
            S.op("pool", lambda e: e.iota(iota16[:], pattern=[[1, 16]], base=0, channel_multiplier=0, allow_small_or_imprecise_dtypes=True), writes=[iota16.r])
            h2c_ = [sbuf(pd, "h2cD%d" % i, [128, KD, 512], BF16) for i in range(2)]
            qTs = sbuf(pd, "qTs", [128, 16, 512], BF16)
            sc = sbuf(pd, "sc", [128, 16, 128], F32); sc2 = sbuf(pd, "sc2", [128, 16, 128], F32)
            v16 = sbuf(pd, "v16", [128, 16, 16], F32); i16 = sbuf(pd, "i16", [128, 16, 16], U32); i16f = sbuf(pd, "i16f", [128, 16, 16], F32)
            cand = sbuf(pd, "cand", [128, 8, 256], F32); cand2 = sbuf(pd, "cand2", [128, 8, 256], F32)
            best = sbuf(pd, "best", [128, 8, 16], F32); pos = sbuf(pd, "pos", [128, 8, 16], U32)
            pa_i = sbuf(pd, "pa_i", [128, 8, 16], I32); pb_i = sbuf(pd, "pb_i", [128, 8, 16], I32)
            pa_f = sbuf(pd, "pa_f", [128, 8, 16], F32); pb_f = sbuf(pd, "pb_f", [128, 8, 16], F32)
            oh = sbuf(pd, "oh", [128, 8, 16, 16], F32); pr = sbuf(pd, "pr", [128, 8, 16, 16], F32)
            rt3 = sbuf(pd, "rt3", [128, 3, 128], F32)
            esum = sbuf(pd, "esum", [128, 8], F32)
            rtT = [sbuf(pd, "rtT%d" % i, [128, 3, 128], F32) for i in range(2)]
            print("SBUF remaining in phase D0:", nc.sbuf_bytes_remaining)
            for c in range(NCH):
                hc = h2c_[c % 2]
                S.dma("sp", "h2l%d" % (c % 2), hc[:], h2T_d[:, :, c * 512:(c + 1) * 512], writes=[hc.r])
                for b in range(16):
                    pq = banks[4 + (b % 2)]
                    for k in range(KD):
                        S.op("pe", lambda e: e.matmul(pq[:], lhsT=wq[:, k, b * 128:(b + 1) * 128], rhs=hc[:, k, :], start=(k == 0), stop=(k == KD - 1)),
                             reads=[wq.r, hc.r], writes=[pq.r])
                    S.op("act", lambda e: e.copy(out=qTs[:, b, :], in_=pq[:]), reads=[pq.r], writes=[qTs.r])
                for ti in range(4):
                    i = c * 4 + ti
                    for b in range(16):
                        bk = banks[b // 4]
                        S.op("pe", lambda e: e.matmul(bk[:, (b % 4) * 128:(b % 4 + 1) * 128], lhsT=qTs[:, b, ti * 128:(ti + 1) * 128], rhs=skT[:, b, :], start=True, stop=True),
                             reads=[qTs.r, skT.r], writes=[bk.r])
                    for q4 in range(4):
                        S.op("act", lambda e: e.copy(out=sc[:, q4 * 4:(q4 + 1) * 4, :], in_=banks[q4][:].rearrange("p (a b) -> p a b", a=4)),
                             reads=[banks[q4].r], writes=[sc.r])
                    for b in range(16):
                        S.op("dve", lambda e: e.max(out=v16[:, b, 0:8], in_=sc[:, b, :]), reads=[sc.r], writes=[v16.r])
                    for b in range(16):
                        S.op("dve", lambda e: e.max_index(out=i16[:, b, 0:8], in_max=v16[:, b, 0:8], in_values=sc[:, b, :]), reads=[sc.r, v16.r], writes=[i16.r])
                    for b in range(16):
                        S.op("dve", lambda e: e.match_replace(out=sc2[:, b, :], in_to_replace=v16[:, b, 0:8], in_values=sc[:, b, :], imm_value=-1e30),
                             reads=[sc.r, v16.r], writes=[sc2.r])
                    for b in range(16):
                        S.op("dve", lambda e: e.max(out=v16[:, b, 8:16], in_=sc2[:, b, :]), reads=[sc2.r], writes=[v16.r])
                    for b in range(16):
                        S.op("dve", lambda e: e.max_index(out=i16[:, b, 8:16], in_max=v16[:, b, 8:16], in_values=sc2[:, b, :]), reads=[sc2.r, v16.r], writes=[i16.r])
                    S.op("dve", lambda e: e.tensor_copy(out=i16f[:], in_=i16[:]), reads=[i16.r], writes=[i16f.r])
                    v4 = v16[:].rearrange("p (h c) k -> p h c k", c=2)
                    S.op("dve", lambda e: e.tensor_tensor(out=cand[:].rearrange("p h (a b) -> p h a b", a=16),
                                                          in0=v4[:, :, 0, :].unsqueeze(3).to_broadcast([128, 8, 16, 16]),
                                                          in1=v4[:, :, 1, :].unsqueeze(2).to_broadcast([128, 8, 16, 16]), op=ALU.add),
                         reads=[v16.r], writes=[cand.r])
                    for h in range(8):
                        S.op("dve", lambda e: e.max(out=best[:, h, 0:8], in_=cand[:, h, :]), reads=[cand.r], writes=[best.r])
                    for h in range(8):
                        S.op("dve", lambda e: e.max_index(out=pos[:, h, 0:8], in_max=best[:, h, 0:8], in_values=cand[:, h, :]), reads=[cand.r, best.r], writes=[pos.r])
                    for h in range(8):
                        S.op("dve", lambda e: e.match_replace(out=cand2[:, h, :], in_to_replace=best[:, h, 0:8], in_values=cand[:, h, :], imm_value=-1e30),
                             reads=[cand.r, best.r], writes=[cand2.r])
                    for h in range(8):
                        S.op("dve", lambda e: e.max(out=best[:, h, 8:16], in_=cand2[:, h, :]), reads=[cand2.r], writes=[best.r])
                    for h in range(8):
                        S.op("dve", lambda e: e.max_index(out=pos[:, h, 8:16], in_max=best[:, h, 8:16], in_values=cand2[:, h, :]), reads=[cand2.r, best.r], writes=[pos.r])
                    gat = rt3[:, 2, :].rearrange("p (h k) -> p h k", h=8)
                    S.op("dve", lambda e: e.tensor_tensor(out=gat, in0=best[:], in1=best[:, :, 0:1].to_broadcast([128, 8, 16]), op=ALU.subtract),
                         reads=[best.r], writes=[rt3.r])
                    S.op("act", lambda e: e.activation(out=gat, in_=gat, func=AF.Exp), reads=[rt3.r], writes=[rt3.r])
                    S.op("dve", lambda e: e.tensor_reduce(out=esum[:], in_=gat, axis=AX.X, op=ALU.add), reads=[rt3.r], writes=[esum.r])
                    S.op("dve", lambda e: e.reciprocal(out=esum[:], in_=esum[:]), reads=[esum.r], writes=[esum.r])
                    S.op("dve", lambda e: e.tensor_tensor(out=gat, in0=gat, in1=esum[:].unsqueeze(2).to_broadcast([128, 8, 16]), op=ALU.mult),
                         reads=[rt3.r, esum.r], writes=[rt3.r])
                    S.op("dve", lambda e: e.tensor_single_scalar(out=pa_i[:], in_=pos[:].bitcast(I32), scalar=4, op=ALU.arith_shift_right), reads=[pos.r], writes=[pa_i.r])
                    S.op("dve", lambda e: e.tensor_single_scalar(out=pb_i[:], in_=pos[:].bitcast(I32), scalar=15, op=ALU.bitwise_and), reads=[pos.r], writes=[pb_i.r])
                    S.op("dve", lambda e: e.tensor_copy(out=pa_f[:], in_=pa_i[:]), reads=[pa_i.r], writes=[pa_f.r])
                    S.op("dve", lambda e: e.tensor_copy(out=pb_f[:], in_=pb_i[:]), reads=[pb_i.r], writes=[pb_f.r])
                    i4 = i16f[:].rearrange("p (h c) k -> p h c k", c=2)
                    for which, pf_ in enumerate((pa_f, pb_f)):
                        S.op("dve", lambda e: e.tensor_tensor(out=oh[:], in0=pf_[:].unsqueeze(3).to_broadcast([128, 8, 16, 16]),
                                                              in1=iota16[:].unsqueeze(1).unsqueeze(1).to_broadcast([128, 8, 16, 16]), op=ALU.is_equal),
                             reads=[pf_.r, iota16.r], writes=[oh.r])
                        S.op("dve", lambda e: e.tensor_tensor(out=pr[:], in0=oh[:], in1=i4[:, :, which, :].unsqueeze(2).to_broadcast([128, 8, 16, 16]), op=ALU.mult),
                             reads=[oh.r, i16f.r], writes=[pr.r])
                        S.op("dve", lambda e: e.tensor_reduce(out=rt3[:, which, :].rearrange("p (h k) -> p h k", h=8), in_=pr[:], axis=AX.X, op=ALU.add),
                             reads=[pr.r], writes=[rt3.r])
                    rtt = rtT[i % 2]
                    for q3 in range(3):
                        bk = banks[6 + (q3 % 2)]
                        S.op("pe", lambda e: e.transpose(out=bk[:, 0:128], in_=rt3[:, q3, :], identity=identf[:]), reads=[rt3.r, identf.r], writes=[bk.r])
                        S.op("act", lambda e: e.copy(out=rtt[:, q3, :], in_=bk[:, 0:128]), reads=[bk.r], writes=[rtt.r])
                    S.dma("sp", "rtst%d" % (i % 2), rt_d[:, :, i * 128:(i + 1) * 128], rtt[:], reads=[rtt.r])
            S.barrier()
        if stop_after == "D0":
            S.wait_all_dma("sp")
            return nc

        G = 256
        NG = T // G
        with ExitStack() as pe_:
            iotaf = sbuf(pe_, "iotaf", [128, 128], F32)
            S.op("pool", lambda e: e.iota(iotaf[:], pattern=[[1, 128]], base=0, channel_multiplier=0, allow_small_or_imprecise_dtypes=True), writes=[iotaf.r])
            GTs = [sbuf(pe_, "GT%d" % i, [128, G, 128], BF16) for i in range(2)]
            ub = [sbuf(pe_, "ub%d" % i, [128, KD, 512], BF16) for i in range(3)]
            vb = [sbuf(pe_, "vb%d" % i, [128, 4, D], BF16) for i in range(3)]
            rtgs = [sbuf(pe_, "rtg%d" % i, [128, 3, G], F32) for i in range(2)]
            h2g = sbuf(pe_, "h2g", [128, KD, G], BF16)
            Ab = [sbuf(pe_, "Ab%d" % i, [128, G], BF16) for i in range(2)]
            WT = [sbuf(pe_, "WT%d" % i, [128, G], BF16) for i in range(2)]
            P1 = [sbuf(pe_, "P1_%d" % i, [128, 128], BF16) for i in range(4)]
            P2B = [sbuf(pe_, "P2B_%d" % i, [128, 8, 128], BF16) for i in range(2)]
            x1g = [sbuf(pe_, "x1g%d" % i, [128, D], F32) for i in range(2)]
            print("SBUF remaining in phase D1:", nc.sbuf_bytes_remaining)
            uv_loaded = [False]
            gt_cnt = [0]

            def gt_gen(g):
                GT = GTs[g % 2]; rtg = rtgs[g % 2]
                S.dma("sp", "rtl%d" % (g % 2), rtg[:], rt_d[:, :, g * G:(g + 1) * G], writes=[rtg.r])
                for t8 in range(G // 8):
                    p2b = P2B[gt_cnt[0] % 2]
                    gt_cnt[0] += 1
                    S.op("dve", lambda e: e.tensor_tensor(out=p2b[:], in0=iotaf[:].unsqueeze(1).to_broadcast([128, 8, 128]),
                                                          in1=rtg[:, 1, t8 * 8:(t8 + 1) * 8].unsqueeze(2).to_broadcast([128, 8, 128]), op=ALU.is_equal),
                         reads=[iotaf.r, rtg.r], writes=[p2b.r])
                    for t_ in range(8):
                        t = t8 * 8 + t_
                        p1 = P1[t % 4]
                        S.op("dve", lambda e: e.tensor_scalar(out=p1[:], in0=iotaf[:], scalar1=rtg[:, 0, t:t + 1], scalar2=rtg[:, 2, t:t + 1], op0=ALU.is_equal, op1=ALU.mult),
                             reads=[iotaf.r, rtg.r], writes=[p1.r])
                        gp = banks[6 + ((t // 4) % 2)]
                        S.op("pe", lambda e: e.matmul(gp[:, (t % 4) * 128:(t % 4 + 1) * 128], lhsT=p2b[:, t_, :], rhs=p1[:], start=True, stop=True),
                             reads=[p1.r, p2b.r], writes=[gp.r])
                        if t % 4 == 3:
                            S.op("act", lambda e: e.copy(out=GT[:, t - 3:t + 1, :], in_=gp[:].rearrange("p (a b) -> p a b", a=4)), reads=[gp.r], writes=[GT.r])
                        yield

            def drain(gen, n=None):
                k = 0
                while n is None or k < n:
                    try:
                        next(gen)
                    except StopIteration:
                        return
                    k += 1

            drain(gt_gen(0))
            for g in range(NG):
                GT = GTs[g % 2]
                nxt = gt_gen(g + 1) if g + 1 < NG else iter(())
                S.dma("sp", "h2gl", h2g[:], h2T_d[:, :, g * G:(g + 1) * G], writes=[h2g.r])
                def emit_S(i1):
                    blk4, bi = divmod(i1, 4)
                    u_ = ub[blk4 % 3]
                    pS = banks[4 + (i1 % 2)]
                    for k in range(KD):
                        S.op("pe", lambda e: e.matmul(pS[:, 0:G], lhsT=u_[:, k, bi * 128:(bi + 1) * 128], rhs=h2g[:, k, :], start=(k == 0), stop=(k == KD - 1)),
                             reads=[u_.r, h2g.r], writes=[pS.r])
                    ab_ = Ab[i1 % 2]; wt_ = WT[i1 % 2]
                    S.op("act", lambda e: e.activation(out=ab_[:], in_=pS[:, 0:G], func=AF.Gelu_apprx_tanh), reads=[pS.r], writes=[ab_.r])
                    eng = "dve" if i1 % 2 == 0 else "pool"
                    S.op(eng, lambda e: e.tensor_tensor(out=wt_[:], in0=ab_[:], in1=GT[:, :, i1], op=ALU.mult), reads=[ab_.r, GT.r], writes=[wt_.r])

                def emit_out(i1):
                    blk4, bi = divmod(i1, 4)
                    v_ = vb[blk4 % 3]; wt_ = WT[i1 % 2]
                    for tt_ in range(G // 128):
                        for nh in range(2):
                            acc = banks[tt_ * 2 + nh]
                            S.op("pe", lambda e: e.matmul(acc[:], lhsT=wt_[:, tt_ * 128:(tt_ + 1) * 128], rhs=v_[:, bi, nh * 512:(nh + 1) * 512],
                                                          start=(i1 == 0), stop=(i1 == 127)), reads=[wt_.r, v_.r], writes=[acc.r])

                for i1 in range(128):
                    blk4, bi = divmod(i1, 4)
                    if bi == 0:
                        if not uv_loaded[0]:
                            for tok in cast_toks:
                                S._wait("sp", tok)
                            uv_loaded[0] = True
                        u_ = ub[blk4 % 3]; v_ = vb[blk4 % 3]
                        S.dma("sp", "ul%d" % (blk4 % 3), u_[:], UTb_d[:, :, blk4 * 512:(blk4 + 1) * 512], writes=[u_.r])
                        S.dma("sp", "vl%d" % (blk4 % 3), v_[:], Vb_d[:, blk4 * 4:(blk4 + 1) * 4, :], writes=[v_.r])
                    emit_S(i1)
                    if i1 >= 1:
                        emit_out(i1 - 1)
                    drain(nxt, 2)
                emit_out(127)
                drain(nxt)
                for tt_ in range(G // 128):
                    i = g * (G // 128) + tt_
                    xg = x1g[i % 2]
                    S.dma("sp", "x1l%d" % (i % 2), xg[:], out_d[i * 128:(i + 1) * 128, :], writes=[xg.r])
                    for nh in range(2):
                        acc = banks[tt_ * 2 + nh]
                        S.op("dve", lambda e: e.tensor_tensor(out=xg[:, nh * 512:(nh + 1) * 512], in0=acc[:], in1=xg[:, nh * 512:(nh + 1) * 512], op=ALU.add),
                             reads=[acc.r, xg.r], writes=[xg.r])
                    S.dma("sp", "ost%d" % (i % 2), out_d[i * 128:(i + 1) * 128, :], xg[:], reads=[xg.r])
            S.barrier()

        S.wait_all_dma("sp")
        print("program built: ops=%d waits=%d" % (S.nops, S.nwaits))
    return nc


def make_in_maps(inputs, T, n_cores):
    f = lambda a: np.ascontiguousarray(np.asarray(a, dtype=np.float32))
    w_in = f(inputs["w_in"][0]).reshape(KD, 128, INC).transpose(1, 0, 2)
    common = {
        "w_in": f(w_in),
        "g1": f(f(inputs["mix_norm_g"][0]).reshape(KD, 128).T),
        "fb": f(np.broadcast_to(f(inputs["fox_forget_bias"][0])[None, :], (128, 8))),
        "qg": f(f(inputs["q_norm_g"][0]).reshape(128, 1)),
        "kg": f(f(inputs["k_norm_g"][0]).reshape(128, 1)),
    }
    L16 = lambda a: f(f(a).reshape(16, 2, 64).transpose(1, 2, 0).reshape(128, 16))
    common["a_re_l"] = L16(inputs["ssm_a_re"][0])
    common["a_im_l"] = L16(inputs["ssm_a_im"][0])
    common["ls_l"] = f(np.broadcast_to(f(inputs["ssm_log_step"][0]).reshape(16, 2, 1), (16, 2, 64)).transpose(1, 2, 0).reshape(128, 16))
    LB = lambda a: f(f(a).reshape(16, 2, 64, 16).transpose(1, 2, 0, 3).reshape(128, 16, 16))
    LC = lambda a: f(f(a).reshape(16, 2, 16, 64).transpose(1, 3, 0, 2).reshape(128, 16, 16))
    common["b_re_l"] = LB(inputs["ssm_b_re"][0]); common["b_im_l"] = LB(inputs["ssm_b_im"][0])
    common["c_re_l"] = LC(inputs["ssm_c_re"][0]); common["c_im_l"] = LC(inputs["ssm_c_im"][0])
    common["d_l"] = f(f(inputs["ssm_d"][0]).reshape(4, 128).T)
    common["w_glu"] = f(f(inputs["ssm_w_glu"][0]).reshape(4, 128, 2048).transpose(1, 0, 2))
    common["w_out"] = f(f(inputs["w_out"][0]).reshape(KD, 128, D).transpose(1, 0, 2))
    common["g2rep"] = f(np.broadcast_to(f(inputs["ffn_norm_g"][0])[None, :], (128, D)))
    common["w_query"] = f(f(inputs["peer_w_query"][0]).reshape(KD, 128, 2048).transpose(1, 0, 2))
    common["skT"] = f(f(inputs["peer_sub_keys"][0]).reshape(16, 128, 128).transpose(2, 0, 1))
    common["peer_uT"] = f(f(inputs["peer_u"][0]).T.reshape(KD, 128, NE).transpose(1, 0, 2))
    common["peer_v"] = f(f(inputs["peer_v"][0]).reshape(128, 128, D).transpose(1, 0, 2))
    maps = []
    for c in range(n_cores):
        m = dict(common)
        m["x"] = f(inputs["x"][c, :T])
        maps.append(m)
    return maps


def kernel(**inputs):
    T = 4096
    n = 8
    nc = build_program(T)
    in_maps = make_in_maps(inputs, T, n)
    res = run_bass_kernel_spmd(nc, in_maps, core_ids=list(range(n)))
    return np.stack([np.asarray(r["out"]) for r in res.results], axis=0).astype(np.float32)
```

```python
import math
from contextlib import ExitStack
import numpy as np
import concourse.bass as bass
import concourse.mybir as mybir
from concourse.bass_utils import run_bass_kernel_spmd

F32 = mybir.dt.float32; BF16 = mybir.dt.bfloat16; I32 = mybir.dt.int32; U32 = mybir.dt.uint32
ALU = mybir.AluOpType; AF = mybir.ActivationFunctionType; AX = mybir.AxisListType

D = 1024
KD = 8
INC = 5640
EPS = 1e-6
C_U, C_Q, C_K, C_V, C_F, C_GS, C_GA = 0, 512, 1536, 2560, 3584, 3592, 4616
NEG = -30000.0
TWO_PI = 2.0 * math.pi
NE = 16384


class Res:
    __slots__ = ("name", "w", "r")

    def __init__(self, name):
        self.name = name
        self.w = None
        self.r = {}


class Sched:
    ENGS = ("pe", "act", "dve", "pool", "sp")

    def __init__(self, nc, es, same_engine_sync=("act", "dve", "pool")):
        self.nc = nc
        self.es = es
        self.eng = {"pe": nc.tensor, "act": nc.scalar, "dve": nc.vector, "pool": nc.gpsimd, "sp": nc.sync}
        self.sem = {}
        self.cnt = {}
        for e in self.ENGS:
            self.sem[e] = es.enter_context(nc.semaphore("sem_" + e))
            self.cnt[e] = 0
        self.seen = {e: {} for e in self.ENGS}
        self.same = set(same_engine_sync)
        self.nwaits = 0
        self.nops = 0

    def res(self, name):
        return Res(name)

    def _chan(self, chan):
        if chan not in self.sem:
            self.sem[chan] = self.es.enter_context(self.nc.semaphore("semd_" + chan))
            self.cnt[chan] = 0
        return self.sem[chan]

    def _wait(self, e, tok):
        if tok is None:
            return
        key, val = tok
        if key == e and e not in self.same:
            return
        if self.seen[e].get(key, 0) >= val:
            return
        self.eng[e].wait_ge(self.sem[key], val)
        self.seen[e][key] = val
        self.nwaits += 1

    def _deps(self, e, reads, writes):
        for R in reads:
            self._wait(e, R.w)
        for W in writes:
            self._wait(e, W.w)
            for key, val in W.r.items():
                self._wait(e, (key, val))

    def _commit(self, tok, reads, writes):
        for R in reads:
            if R.r.get(tok[0], 0) < tok[1]:
                R.r[tok[0]] = tok[1]
        for W in writes:
            W.w = tok
            W.r = {}

    def op(self, e, fn, reads=(), writes=()):
        self._deps(e, reads, writes)
        ins = fn(self.eng[e])
        self.cnt[e] += 1
        ins.then_inc(self.sem[e], 1)
        tok = (e, self.cnt[e])
        self._commit(tok, reads, writes)
        self.nops += 1
        return tok

    def dma(self, e, chan, out, in_, reads=(), writes=(), **kw):
        sem = self._chan(chan)
        if self.cnt[chan] > 0:
            self._wait(e, (chan, self.cnt[chan]))
        self._deps(e, reads, writes)
        ins = self.eng[e].dma_start(out=out, in_=in_, **kw)
        self.cnt[chan] += 16
        ins.then_inc(sem, 16)
        tok = (chan, self.cnt[chan])
        self._commit(tok, reads, writes)
        return tok

    def barrier(self):
        for e in self.ENGS:
            for key in list(self.sem.keys()):
                if key != e and self.cnt[key] > 0:
                    self._wait(e, (key, self.cnt[key]))
                elif key == e and e in self.same and self.cnt[key] > 0:
                    self._wait(e, (key, self.cnt[key]))

    def wait_all_dma(self, e):
        for key in list(self.sem.keys()):
            if key not in self.ENGS and self.cnt[key] > 0:
                self._wait(e, (key, self.cnt[key]))


class Buf:
    def __init__(self, t, r):
        self.t = t
        self.r = r

    def __getitem__(self, k):
        return self.t[k]


def build_program(T, dbg=False, stop_after=None):
    NT = T // 128
    NCH = T // 512
    nc = bass.Bass("TRN2", target_bir_lowering=False)

    def din(name, shape, dt=F32):
        return nc.dram_tensor(name, shape, dt, kind="ExternalInput").ap()

    x_d = din("x", [T, D])
    w_in_d = din("w_in", [128, KD, INC])
    g1_d = din("g1", [128, KD])
    fb_d = din("fb", [128, 8])
    qg_d = din("qg", [128, 1])
    kg_d = din("kg", [128, 1])
    are_d = din("a_re_l", [128, 16]); aim_d = din("a_im_l", [128, 16]); ls_d = din("ls_l", [128, 16])
    bre_d = din("b_re_l", [128, 16, 16]); bim_d = din("b_im_l", [128, 16, 16])
    cre_d = din("c_re_l", [128, 16, 16]); cim_d = din("c_im_l", [128, 16, 16])
    dsk_d = din("d_l", [128, 4])
    wglu_d = din("w_glu", [128, 4, 2048])
    wout_d = din("w_out", [128, KD, D])
    g2_d = din("g2rep", [128, D])
    wqr_d = din("w_query", [128, KD, 2048])
    skT_d = din("skT", [128, 16, 128])
    UT_d = din("peer_uT", [128, KD, NE])
    V_d = din("peer_v", [128, 128, D])
    out_d = nc.dram_tensor("out", [T, D], F32, kind="ExternalOutput").ap()
    UTb_d = nc.dram_tensor("UTb", [128, KD, NE], BF16, kind="Internal").ap()
    Vb_d = nc.dram_tensor("Vb", [128, 128, D], BF16, kind="Internal").ap()
    rt_d = nc.dram_tensor("rt", [128, 3, T], F32, kind=("ExternalOutput" if dbg else "Internal")).ap()
    kind_scr = "ExternalOutput" if dbg else "Internal"
    mT_d = nc.dram_tensor("mT", [D, T], BF16, kind=kind_scr).ap()
    hT_d = nc.dram_tensor("hTd", [128, KD, T], BF16, kind="Internal").ap()
    h2T_d = nc.dram_tensor("h2T", [128, KD, T], BF16, kind=kind_scr).ap()
    m2T_d = nc.dram_tensor("m2T", [128, KD, T], BF16, kind=kind_scr).ap() if dbg else None

    with ExitStack() as es:
        S = Sched(nc, es)

        def sbuf(stack, name, shape, dt):
            t = stack.enter_context(nc.sbuf_tensor("sb_" + name, shape, dt))
            return Buf(t, S.res(name))

        banks = []
        for i in range(8):
            t = es.enter_context(nc.psum_tensor("bank%d" % i, [128, 512], F32))
            banks.append(Buf(t, S.res("bank%d" % i)))

        identf = sbuf(es, "identf", [128, 128], F32)
        identb = sbuf(es, "identb", [128, 128], BF16)
        onesb = sbuf(es, "onesb", [128, 128], BF16)
        onesf = sbuf(es, "onesf", [128, 128], F32)
        trif = sbuf(es, "trif", [128, 128], F32)
        maskneg = sbuf(es, "maskneg", [128, 128], BF16)
        S.op("pool", lambda e: e.memset(identf[:], 1.0), writes=[identf.r])
        S.op("pool", lambda e: e.affine_select(out=identf[:], in_=identf[:], pattern=[[-1, 128]], compare_op=ALU.is_equal,
                                               fill=0.0, base=0, channel_multiplier=1), reads=[identf.r], writes=[identf.r])
        S.op("pool", lambda e: e.tensor_copy(out=identb[:], in_=identf[:]), reads=[identf.r], writes=[identb.r])
        S.op("pool", lambda e: e.memset(onesb[:], 1.0), writes=[onesb.r])
        S.op("pool", lambda e: e.memset(onesf[:], 1.0), writes=[onesf.r])
        S.op("pool", lambda e: e.memset(trif[:], 1.0), writes=[trif.r])
        S.op("pool", lambda e: e.affine_select(out=trif[:], in_=trif[:], pattern=[[1, 128]], compare_op=ALU.is_ge,
                                               fill=0.0, base=0, channel_multiplier=-1), reads=[trif.r], writes=[trif.r])
        S.op("pool", lambda e: e.tensor_scalar(out=maskneg[:], in0=trif[:], scalar1=-1.0, scalar2=-NEG, op0=ALU.add, op1=ALU.mult),
             reads=[trif.r], writes=[maskneg.r])

        hpi = sbuf(es, "hpi", [128, 1], F32)
        S.op("pool", lambda e: e.memset(hpi[:], math.pi / 2.0), writes=[hpi.r])
        g1 = sbuf(es, "g1", [128, KD], F32)
        fb = sbuf(es, "fb", [128, 8], F32)
        qg = sbuf(es, "qg", [128, 1], F32)
        kg = sbuf(es, "kg", [128, 1], F32)
        S.dma("sp", "c_g1", g1[:], g1_d, writes=[g1.r])
        S.dma("sp", "c_fb", fb[:], fb_d, writes=[fb.r])
        S.dma("sp", "c_qg", qg[:], qg_d, writes=[qg.r])
        S.dma("sp", "c_kg", kg[:], kg_d, writes=[kg.r])
        S.op("dve", lambda e: e.tensor_scalar(out=qg[:], in0=qg[:], scalar1=128.0 ** -0.5, scalar2=None, op0=ALU.mult),
             reads=[qg.r], writes=[qg.r])

        cast_toks = []
        if stop_after is None:
            for q in range(16):
                cast_toks.append(S.dma("pool", "ucast%d" % q, UTb_d[:, :, q * 1024:(q + 1) * 1024], UT_d[:, :, q * 1024:(q + 1) * 1024]))
                cast_toks.append(S.dma("pool", "vcast%d" % q, Vb_d[:, q * 8:(q + 1) * 8, :], V_d[:, q * 8:(q + 1) * 8, :]))

        stg = [None, None]
        stg_i = [0]

        def alloc_stg(stack):
            for i in range(2):
                stg[i] = sbuf(stack, "wstg%d_%d" % (i, stg_i[0]), [128, KD, 512], F32)

        def load_w(dst, src_d, c0, ncols, gain, nk=KD):
            s = stg[stg_i[0] % 2]
            stg_i[0] += 1
            S.dma("sp", "wld%d" % (stg_i[0] % 2), s[:, 0:nk, 0:ncols], src_d[:, :, c0:c0 + ncols], writes=[s.r])
            if gain is not None:
                S.op("pool", lambda e: e.tensor_tensor(out=dst[:], in0=s[:, 0:nk, 0:ncols],
                                                       in1=gain[:, 0:nk].unsqueeze(2).to_broadcast([128, nk, ncols]), op=ALU.mult),
                     reads=[s.r, gain.r], writes=[dst.r])
            else:
                S.op("pool", lambda e: e.tensor_copy(out=dst[:], in_=s[:, 0:nk, 0:ncols]), reads=[s.r], writes=[dst.r])

        scopeAB = ExitStack()
        alloc_stg(scopeAB)
        hT = sbuf(scopeAB, "hT", [128, KD, T], BF16)
        hT_r = [S.res("hT_%d" % i) for i in range(NT)]
        cum = sbuf(scopeAB, "cum", [128, NT, 8], F32)
        cend = sbuf(scopeAB, "cend", [128, NT + 1, 8], F32)
        with ExitStack() as pa:
            xts = [sbuf(pa, "xt%d" % i, [128, D], F32) for i in range(2)]
            xs = [sbuf(pa, "xs%d" % i, [128, D], BF16) for i in range(2)]
            junk = sbuf(pa, "junkA", [128, D], BF16)
            ss = sbuf(pa, "ss", [128, NT], F32)
            rs = sbuf(pa, "rs", [128, NT], F32)
            wf = sbuf(pa, "wf", [128, KD, 8], BF16)
            zf = sbuf(pa, "zf", [128, NT, 8], F32)
            spf = sbuf(pa, "spf", [128, NT, 8], F32)
            load_w(wf, w_in_d, C_F, 8, g1)
            fbank = banks[7]
            for i in range(NT):
                xt = xts[i % 2]; xb = xs[i % 2]
                S.dma("sp", "xld%d" % (i % 2), xt[:], x_d[i * 128:(i + 1) * 128, :], writes=[xt.r])
                S.op("act", lambda e: e.activation(out=junk[:], in_=xt[:], func=AF.Square, accum_out=ss[:, i:i + 1]),
                     reads=[xt.r], writes=[junk.r, ss.r])
                S.op("act", lambda e: e.activation(out=rs[:, i:i + 1], in_=ss[:, i:i + 1], func=AF.Sqrt, bias=EPS, scale=1.0 / D),
                     reads=[ss.r], writes=[rs.r])
                S.op("dve", lambda e: e.reciprocal(out=rs[:, i:i + 1], in_=rs[:, i:i + 1]), reads=[rs.r], writes=[rs.r])
                S.op("dve", lambda e: e.tensor_scalar(out=xb[:], in0=xt[:], scalar1=rs[:, i:i + 1], scalar2=None, op0=ALU.mult),
                     reads=[xt.r, rs.r], writes=[xb.r])
                bk = banks[i % 2]
                bv = bk[:].bitcast(BF16)
                for k in range(KD):
                    S.op("pe", lambda e: e.transpose(out=bv[:, k * 128:(k + 1) * 128], in_=xb[:, k * 128:(k + 1) * 128], identity=identb[:]),
                         reads=[xb.r, identb.r], writes=[bk.r])
                S.op("act", lambda e: e.copy(out=hT[:, :, i * 128:(i + 1) * 128], in_=bv[:, 0:1024].rearrange("p (k t) -> p k t", k=KD)),
                     reads=[bk.r], writes=[hT_r[i]])
                for k in range(KD):
                    S.op("pe", lambda e: e.matmul(fbank[:, i * 8:(i + 1) * 8], lhsT=hT[:, k, i * 128:(i + 1) * 128], rhs=wf[:, k, :],
                                                  start=(k == 0), stop=(k == KD - 1)),
                         reads=[hT_r[i], wf.r], writes=[fbank.r])
            S.op("dve", lambda e: e.tensor_tensor(out=zf[:], in0=fbank[:, 0:NT * 8].rearrange("p (i h) -> p i h", h=8),
                                                  in1=fb[:].unsqueeze(1).to_broadcast([128, NT, 8]), op=ALU.add),
                 reads=[fbank.r, fb.r], writes=[zf.r])
            S.op("act", lambda e: e.activation(out=spf[:], in_=zf[:], func=AF.Exp, scale=-1.0), reads=[zf.r], writes=[spf.r])
            S.op("act", lambda e: e.activation(out=spf[:], in_=spf[:], func=AF.Ln, bias=1.0, scale=1.0), reads=[spf.r], writes=[spf.r])
            cb1 = banks[5]; cb2 = banks[6]
            for i in range(NT):
                S.op("pe", lambda e: e.matmul(cb1[:, i * 8:(i + 1) * 8], lhsT=trif[:], rhs=spf[:, i, :], start=True, stop=True),
                     reads=[trif.r, spf.r], writes=[cb1.r])
                S.op("pe", lambda e: e.matmul(cb2[:, i * 8:(i + 1) * 8], lhsT=onesf[:], rhs=spf[:, i, :], start=True, stop=True),
                     reads=[onesf.r, spf.r], writes=[cb2.r])
            S.op("dve", lambda e: e.memset(cend[:, 0, :], 0.0), writes=[cend.r])
            for i in range(NT):
                S.op("dve", lambda e: e.tensor_tensor(out=cend[:, i + 1, :], in0=cend[:, i, :], in1=cb2[:, i * 8:(i + 1) * 8], op=ALU.add),
                     reads=[cend.r, cb2.r], writes=[cend.r])
            S.op("dve", lambda e: e.tensor_tensor(out=cum[:], in0=cb1[:, 0:NT * 8].rearrange("p (i h) -> p i h", h=8),
                                                  in1=cend[:, 0:NT, :], op=ALU.add),
                 reads=[cb1.r, cend.r], writes=[cum.r])

        for c in range(NCH):
            S.dma("sp", "hsp%d" % (c % 2), hT_d[:, :, c * 512:(c + 1) * 512], hT[:, :, c * 512:(c + 1) * 512], reads=hT_r[c * 4:(c + 1) * 4])
        S.barrier()
        if stop_after == "A":
            return nc
        with ExitStack() as pb:
            wq = sbuf(pb, "wq", [128, KD, 128], BF16)
            wk = sbuf(pb, "wk", [128, KD, 128], BF16)
            wv = sbuf(pb, "wv", [128, KD, 128], BF16)
            wg = sbuf(pb, "wg", [128, KD, 128], BF16)
            qT = sbuf(pb, "qT", [128, T], BF16)
            kT = sbuf(pb, "kT", [128, T], BF16)
            vv = sbuf(pb, "vv", [128, NT, 128], BF16)
            sgT = sbuf(pb, "sgT", [128, T], BF16)
            nb = sbuf(pb, "nb", [128, NT, NT], F32)
            sq = [sbuf(pb, "sq%d" % i, [128, 512], BF16) for i in range(2)]
            rrep = [sbuf(pb, "rrep%d" % i, [128, 512], F32) for i in range(2)]
            Pt = [sbuf(pb, "Pt%d" % i, [128, 512], BF16) for i in range(3)]
            rec = sbuf(pb, "rec", [128, 512], F32)
            yv = sbuf(pb, "yv", [128, 512], F32)
            ym = [sbuf(pb, "ym%d" % i, [128, 512], BF16) for i in range(2)]
            hT_all = hT_r
            pcount = [0]
            print("SBUF remaining in phase B:", nc.sbuf_bytes_remaining)
            for h in range(8):
                load_w(wq, w_in_d, C_Q + h * 128, 128, g1)
                load_w(wk, w_in_d, C_K + h * 128, 128, g1)
                load_w(wv, w_in_d, C_V + h * 128, 128, g1)
                load_w(wg, w_in_d, C_GA + h * 128, 128, g1)
                for qs in range(NT):
                    S.op("dve", lambda e: e.tensor_scalar(out=nb[:, qs, 0:qs + 1], in0=cum[:, 0:qs + 1, h], scalar1=cend[:, qs + 1, h:h + 1],
                                                          scalar2=None, op0=ALU.subtract),
                         reads=[cum.r, cend.r], writes=[nb.r])
                for which, (wt, dstT, gvec) in enumerate(((wq, qT, qg), (wk, kT, kg))):
                    for c in range(NCH):
                        pj = banks[(2 * c) % 4]; pn = banks[(2 * c + 1) % 4]
                        for k in range(KD):
                            S.op("pe", lambda e: e.matmul(pj[:], lhsT=wt[:, k, :], rhs=hT[:, k, c * 512:(c + 1) * 512], start=(k == 0), stop=(k == KD - 1)),
                                 reads=[wt.r] + hT_all[c * 4:(c + 1) * 4], writes=[pj.r])
                        sqb = sq[c % 2]; rr = rrep[c % 2]
                        S.op("act", lambda e: e.activation(out=sqb[:], in_=pj[:], func=AF.Square), reads=[pj.r], writes=[sqb.r])
                        S.op("pe", lambda e: e.matmul(pn[:], lhsT=onesb[:], rhs=sqb[:], start=True, stop=True), reads=[onesb.r, sqb.r], writes=[pn.r])
                        S.op("act", lambda e: e.activation(out=rr[:], in_=pn[:], func=AF.Sqrt, bias=EPS, scale=1.0 / 128.0), reads=[pn.r], writes=[rr.r])
                        S.op("dve", lambda e: e.reciprocal(out=rr[:], in_=rr[:]), reads=[rr.r], writes=[rr.r])
                        S.op("dve", lambda e: e.scalar_tensor_tensor(out=dstT[:, c * 512:(c + 1) * 512], in0=pj[:], scalar=gvec[:, 0:1], in1=rr[:],
                                                                     op0=ALU.mult, op1=ALU.mult),
                             reads=[pj.r, gvec.r, rr.r], writes=[dstT.r])
                for i4 in range(NT // 4):
                    pv = banks[i4 % 2]
                    for ii in range(4):
                        i = i4 * 4 + ii
                        for k in range(KD):
                            S.op("pe", lambda e: e.matmul(pv[:, ii * 128:(ii + 1) * 128], lhsT=hT[:, k, i * 128:(i + 1) * 128], rhs=wv[:, k, :],
                                                          start=(k == 0), stop=(k == KD - 1)),
                                 reads=[wv.r, hT_all[i]], writes=[pv.r])
                    S.op("act", lambda e: e.copy(out=vv[:, i4 * 4:(i4 + 1) * 4, :], in_=pv[:].rearrange("p (a b) -> p a b", a=4)),
                         reads=[pv.r], writes=[vv.r])
                for c in range(NCH):
                    pg = banks[2 + (c % 2)]
                    for k in range(KD):
                        S.op("pe", lambda e: e.matmul(pg[:], lhsT=wg[:, k, :], rhs=hT[:, k, c * 512:(c + 1) * 512], start=(k == 0), stop=(k == KD - 1)),
                             reads=[wg.r] + hT_all[c * 4:(c + 1) * 4], writes=[pg.r])
                    S.op("act", lambda e: e.activation(out=sgT[:, c * 512:(c + 1) * 512], in_=pg[:], func=AF.Sigmoid), reads=[pg.r], writes=[sgT.r])
                for qc in range(NCH):
                    pO = banks[4 + (qc % 2)]; pL = banks[6 + (qc % 2)]
                    nkb = (qc + 1) * 4
                    def emit_S(kb):
                        q_lo = max(qc * 4, kb)
                        off = (q_lo - qc * 4) * 128
                        pS = banks[pcount[0] % 4]
                        Pb = Pt[pcount[0] % 3]
                        pcount[0] += 1
                        diag = kb >= qc * 4
                        S.op("pe", lambda e: e.matmul(pS[:, off:512], lhsT=kT[:, kb * 128:(kb + 1) * 128], rhs=qT[:, q_lo * 128:(qc + 1) * 512],
                                                      start=True, stop=not diag),
                             reads=[kT.r, qT.r], writes=[pS.r])
                        if diag:
                            S.op("pe", lambda e: e.matmul(pS[:, off:off + 128], lhsT=identb[:], rhs=maskneg[:], start=False, stop=True),
                                 reads=[identb.r, maskneg.r], writes=[pS.r])
                        for qs in range(q_lo, qc * 4 + 4):
                            o2 = (qs - qc * 4) * 128
                            S.op("act", lambda e: e.activation(out=Pb[:, o2:o2 + 128], in_=pS[:, o2:o2 + 128], func=AF.Exp,
                                                               bias=nb[:, qs, kb:kb + 1], scale=1.0),
                                 reads=[pS.r, nb.r], writes=[Pb.r])
                        return (Pb, off)

                    def emit_PV(kb, Pb, off):
                        S.op("pe", lambda e: e.matmul(pO[:, off:512], lhsT=vv[:, kb, :], rhs=Pb[:, off:512], start=(kb == 0), stop=(kb == nkb - 1)),
                             reads=[vv.r, Pb.r], writes=[pO.r])
                        S.op("pe", lambda e: e.matmul(pL[:, off:512], lhsT=onesb[:], rhs=Pb[:, off:512], start=(kb == 0), stop=(kb == nkb - 1)),
                             reads=[onesb.r, Pb.r], writes=[pL.r])

                    prev = None
                    for kb in range(nkb):
                        cur = emit_S(kb)
                        if prev is not None:
                            emit_PV(kb - 1, *prev)
                        prev = cur
                    emit_PV(nkb - 1, *prev)
                    S.op("dve", lambda e: e.reciprocal(out=rec[:], in_=pL[:]), reads=[pL.r], writes=[rec.r])
                    S.op("dve", lambda e: e.tensor_tensor(out=yv[:], in0=pO[:], in1=rec[:], op=ALU.mult), reads=[pO.r, rec.r], writes=[yv.r])
                    ymb = ym[qc % 2]
                    S.op("dve", lambda e: e.tensor_tensor(out=ymb[:], in0=yv[:], in1=sgT[:, qc * 512:(qc + 1) * 512], op=ALU.mult),
                         reads=[yv.r, sgT.r], writes=[ymb.r])
                    S.dma("sp", "mst%d" % (qc % 2), mT_d[h * 128:(h + 1) * 128, qc * 512:(qc + 1) * 512], ymb[:], reads=[ymb.r])


        S.barrier()
        scopeAB.close()
        if stop_after == "B":
            S.wait_all_dma("sp")
            return nc

        LT = 128
        with ExitStack() as pc:
            def small(name, shape=(128, 16)):
                return sbuf(pc, name, list(shape), F32)
            a_re = small("a_re"); a_im = small("a_im"); lsl = small("lsl")
            S.dma("sp", "c_are", a_re[:], are_d, writes=[a_re.r])
            S.dma("sp", "c_aim", a_im[:], aim_d, writes=[a_im.r])
            S.dma("sp", "c_ls", lsl[:], ls_d, writes=[lsl.r])
            dsk = sbuf(pc, "dsk", [128, 4], F32); g2 = sbuf(pc, "g2", [128, D], F32)
            S.dma("sp", "c_dsk", dsk[:], dsk_d, writes=[dsk.r]); S.dma("sp", "c_g2", g2[:], g2_d, writes=[g2.r])
            Cpad = sbuf(pc, "Cpad", [128, 16, 2, 128], F32)
            Bpad = sbuf(pc, "Bpad", [128, 16, 2, 128], F32)
            Tc = sbuf(pc, "Tc", [128, 16, LT], F32); Ts = sbuf(pc, "Ts", [128, 16, LT], F32)
            wu = sbuf(pc, "wu", [128, KD, 512], BF16)
            wgs = sbuf(pc, "wgs", [128, KD, D], BF16)
            wglu = sbuf(pc, "wglu", [128, 4, 2048], BF16)
            wout = sbuf(pc, "wout", [128, KD, D], BF16)
            step = small("step"); th = small("th"); mag = small("mag"); cs = small("cs"); sn = small("sn")
            ki = sbuf(pc, "ki", [128, 16], I32); kf = small("kf"); rr_ = small("rr_"); ab = small("ab")
            lre = small("lre"); lim = small("lim"); den = small("den"); t1 = small("t1"); t2 = small("t2")
            cfr = small("cfr"); cfi = small("cfi")
            wr_ = small("wr_"); wi_ = small("wi_"); wt1 = small("wt1"); wt2 = small("wt2")
            pset = ExitStack()
            alloc_stg(pset)
            bre = sbuf(pset, "bre", [128, 16, 16], F32); bim = sbuf(pset, "bim", [128, 16, 16], F32)
            cre = sbuf(pset, "cre", [128, 16, 16], F32); cim = sbuf(pset, "cim", [128, 16, 16], F32)
            S.dma("sp", "c_bre", bre[:], bre_d, writes=[bre.r]); S.dma("sp", "c_bim", bim[:], bim_d, writes=[bim.r])
            S.dma("sp", "c_cre", cre[:], cre_d, writes=[cre.r]); S.dma("sp", "c_cim", cim[:], cim_d, writes=[cim.r])
            bbr = sbuf(pset, "bbr", [128, 16, 16], F32); bbi = sbuf(pset, "bbi", [128, 16, 16], F32)
            u1 = sbuf(pset, "u1", [128, 16, 16], F32); u2 = sbuf(pset, "u2", [128, 16, 16], F32)
            BZ = sbuf(pset, "BZ", [128, 16, 2, 128], F32)
            p1 = sbuf(pset, "p1", [128, 16, LT // 2], F32); p2 = sbuf(pset, "p2", [128, 16, LT // 2], F32)

            def tt(eng, out, a, b, op, rd, wr):
                S.op(eng, lambda e: e.tensor_tensor(out=out, in0=a, in1=b, op=op), reads=rd, writes=wr)

            S.op("act", lambda e: e.activation(out=step[:], in_=lsl[:], func=AF.Exp), reads=[lsl.r], writes=[step.r])
            tt("dve", th[:], a_im[:], step[:], ALU.mult, [a_im.r, step.r], [th.r])
            tt("dve", mag[:], a_re[:], step[:], ALU.mult, [a_re.r, step.r], [mag.r])
            S.op("act", lambda e: e.activation(out=mag[:], in_=mag[:], func=AF.Exp), reads=[mag.r], writes=[mag.r])
            S.op("dve", lambda e: e.tensor_scalar(out=ki[:], in0=th[:], scalar1=1.0 / TWO_PI, scalar2=None, op0=ALU.mult), reads=[th.r], writes=[ki.r])
            S.op("dve", lambda e: e.tensor_copy(out=kf[:], in_=ki[:]), reads=[ki.r], writes=[kf.r])
            S.op("dve", lambda e: e.scalar_tensor_tensor(out=rr_[:], in0=kf[:], scalar=-TWO_PI, in1=th[:], op0=ALU.mult, op1=ALU.add),
                 reads=[kf.r, th.r], writes=[rr_.r])
            PI_LO = 3.1415925
            S.op("dve", lambda e: e.tensor_scalar(out=rr_[:], in0=rr_[:], scalar1=PI_LO, scalar2=-PI_LO, op0=ALU.min, op1=ALU.max), reads=[rr_.r], writes=[rr_.r])
            S.op("act", lambda e: e.activation(out=sn[:], in_=rr_[:], func=AF.Sin), reads=[rr_.r], writes=[sn.r])
            S.op("act", lambda e: e.activation(out=ab[:], in_=rr_[:], func=AF.Abs), reads=[rr_.r], writes=[ab.r])
            S.op("act", lambda e: e.activation(out=cs[:], in_=ab[:], func=AF.Sin, scale=-1.0, bias=hpi[:, 0:1]), reads=[ab.r, hpi.r], writes=[cs.r])
            tt("dve", lre[:], mag[:], cs[:], ALU.mult, [mag.r, cs.r], [lre.r])
            tt("dve", lim[:], mag[:], sn[:], ALU.mult, [mag.r, sn.r], [lim.r])
            S.op("dve", lambda e: e.tensor_scalar(out=lre[:], in0=lre[:], scalar1=-1.0, scalar2=None, op0=ALU.add), reads=[lre.r], writes=[lre.r])
            tt("dve", t1[:], a_re[:], a_re[:], ALU.mult, [a_re.r], [t1.r])
            tt("dve", t2[:], a_im[:], a_im[:], ALU.mult, [a_im.r], [t2.r])
            tt("dve", den[:], t1[:], t2[:], ALU.add, [t1.r, t2.r], [den.r])
            S.op("dve", lambda e: e.reciprocal(out=den[:], in_=den[:]), reads=[den.r], writes=[den.r])
            tt("dve", t1[:], lre[:], a_re[:], ALU.mult, [lre.r, a_re.r], [t1.r])
            tt("dve", t2[:], lim[:], a_im[:], ALU.mult, [lim.r, a_im.r], [t2.r])
            tt("dve", cfr[:], t1[:], t2[:], ALU.add, [t1.r, t2.r], [cfr.r])
            tt("dve", cfr[:], cfr[:], den[:], ALU.mult, [cfr.r, den.r], [cfr.r])
            tt("dve", t1[:], lim[:], a_re[:], ALU.mult, [lim.r, a_re.r], [t1.r])
            tt("dve", t2[:], lre[:], a_im[:], ALU.mult, [lre.r, a_im.r], [t2.r])
            tt("dve", cfi[:], t1[:], t2[:], ALU.subtract, [t1.r, t2.r], [cfi.r])
            tt("dve", cfi[:], cfi[:], den[:], ALU.mult, [cfi.r, den.r], [cfi.r])
            bc3 = lambda v: v[:].unsqueeze(2).to_broadcast([128, 16, 16])
            tt("dve", u1[:], bre[:], bc3(cfr), ALU.mult, [bre.r, cfr.r], [u1.r])
            tt("dve", u2[:], bim[:], bc3(cfi), ALU.mult, [bim.r, cfi.r], [u2.r])
            tt("dve", bbr[:], u1[:], u2[:], ALU.subtract, [u1.r, u2.r], [bbr.r])
            tt("dve", u1[:], bim[:], bc3(cfr), ALU.mult, [bim.r, cfr.r], [u1.r])
            tt("dve", u2[:], bre[:], bc3(cfi), ALU.mult, [bre.r, cfi.r], [u2.r])
            tt("dve", bbi[:], u1[:], u2[:], ALU.add, [u1.r, u2.r], [bbi.r])
            S.op("dve", lambda e: e.tensor_scalar(out=cim[:], in0=cim[:], scalar1=-1.0, scalar2=None, op0=ALU.mult), reads=[cim.r], writes=[cim.r])
            if True:
                S.op("pool", lambda e: e.memset(Cpad[:], 0.0), writes=[Cpad.r])
                S.op("pool", lambda e: e.memset(BZ[:], 0.0), writes=[BZ.r])
                for j in range(16):
                    for two in range(2):
                        c0 = 32 * (j % 4) + 16 * two
                        ps_ = slice(two * 64, (two + 1) * 64)
                        for ri, (srcC, srcB) in enumerate(((cre, bbr), (cim, bbi))):
                            S.op("pool", lambda e: e.tensor_copy(out=Cpad[ps_, j, ri, c0:c0 + 16], in_=srcC[ps_, j, :]), reads=[srcC.r], writes=[Cpad.r])
                            S.op("pool", lambda e: e.tensor_copy(out=BZ[ps_, j, ri, c0:c0 + 16], in_=srcB[ps_, j, :]), reads=[srcB.r], writes=[BZ.r])
                for j in range(16):
                    for ri in range(2):
                        bk = banks[(2 * j + ri) % 4]
                        S.op("pe", lambda e: e.transpose(out=bk[:, 0:128], in_=BZ[:, j, ri, :], identity=identf[:]), reads=[BZ.r, identf.r], writes=[bk.r])
                        S.op("act", lambda e: e.copy(out=Bpad[:, j, ri, :], in_=bk[:, 0:128]), reads=[bk.r], writes=[Bpad.r])
            S.op("dve", lambda e: e.tensor_copy(out=wr_[:], in_=cs[:]), reads=[cs.r], writes=[wr_.r])
            S.op("dve", lambda e: e.tensor_copy(out=wi_[:], in_=sn[:]), reads=[sn.r], writes=[wi_.r])
            S.op("pool", lambda e: e.memset(Tc[:, :, 0:1], 1.0), writes=[Tc.r])
            S.op("pool", lambda e: e.memset(Ts[:, :, 0:1], 0.0), writes=[Ts.r])
            if True:
                n = 1
                while n < LT:
                    bw = lambda v: v[:].unsqueeze(2).to_broadcast([128, 16, n])
                    tt("dve", p1[:, :, 0:n], Tc[:, :, 0:n], bw(wr_), ALU.mult, [Tc.r, wr_.r], [p1.r])
                    tt("dve", p2[:, :, 0:n], Ts[:, :, 0:n], bw(wi_), ALU.mult, [Ts.r, wi_.r], [p2.r])
                    tt("dve", Tc[:, :, n:2 * n], p1[:, :, 0:n], p2[:, :, 0:n], ALU.subtract, [p1.r, p2.r], [Tc.r])
                    tt("dve", p1[:, :, 0:n], Tc[:, :, 0:n], bw(wi_), ALU.mult, [Tc.r, wi_.r], [p1.r])
                    tt("dve", p2[:, :, 0:n], Ts[:, :, 0:n], bw(wr_), ALU.mult, [Ts.r, wr_.r], [p2.r])
                    tt("dve", Ts[:, :, n:2 * n], p1[:, :, 0:n], p2[:, :, 0:n], ALU.add, [p1.r, p2.r], [Ts.r])
                    tt("dve", wt1[:], wr_[:], wr_[:], ALU.mult, [wr_.r], [wt1.r])
                    tt("dve", wt2[:], wi_[:], wi_[:], ALU.mult, [wi_.r], [wt2.r])
                    tt("dve", wi_[:], wr_[:], wi_[:], ALU.mult, [wr_.r, wi_.r], [wi_.r])
                    S.op("dve", lambda e: e.tensor_scalar(out=wi_[:], in0=wi_[:], scalar1=2.0, scalar2=None, op0=ALU.mult), reads=[wi_.r], writes=[wi_.r])
                    tt("dve", wr_[:], wt1[:], wt2[:], ALU.subtract, [wt1.r, wt2.r], [wr_.r])
                    n *= 2
            wv_ = lambda buf, a, b: Buf(buf.t[:, :, a:b], buf.r)
            load_w(wu, w_in_d, C_U, 512, g1)
            for hh in range(2):
                load_w(wv_(wgs, hh * 512, (hh + 1) * 512), w_in_d, C_GS + hh * 512, 512, g1)
                load_w(wv_(wout, hh * 512, (hh + 1) * 512), wout_d, hh * 512, 512, None)
            for qq in range(4):
                load_w(wv_(wglu, qq * 512, (qq + 1) * 512), wglu_d, qq * 512, 512, None, nk=4)
            S.barrier()
            pset.close()
            hTc = [sbuf(pc, "hTc%d" % i, [128, KD, 512], BF16) for i in range(1)]
            matt = [sbuf(pc, "matt%d" % i, [128, KD, 512], BF16) for i in range(1)]
            u_sb = sbuf(pc, "u_sb", [128, 4, 512], F32)
            ssets = []
            for si in range(2):
                B_ = {}
                for nm in ("xr_sb", "xi_sb", "ta", "tb", "tcb", "td", "r_r", "r_i"):
                    B_[nm] = sbuf(pc, "%s_%d" % (nm, si), [128, 512], F32)
                B_["ctmp"] = sbuf(pc, "ctmp_%d" % si, [128, 2], F32)
                B_["pxr"] = banks[2 + 3 * si]; B_["pxi"] = banks[3 + 3 * si]
                ssets.append(B_)
            carry = sbuf(pc, "carry", [128, 16, 2], F32)
            carry_r = [S.res("carry_%d" % j) for j in range(16)]
            yv_ = sbuf(pc, "yv_", [128, 512], F32)
            yg = sbuf(pc, "yg", [128, 4, 512], BF16)
            sg1 = sbuf(pc, "sg1", [128, 512], F32); sg2 = sbuf(pc, "sg2", [128, 512], F32)
            ys = sbuf(pc, "ys", [128, 512], F32)
            merged = sbuf(pc, "merged", [128, KD, 512], BF16)
            xres = [sbuf(pc, "xres%d" % i, [128, D], F32) for i in range(2)]
            x1t = [sbuf(pc, "x1t%d" % i, [128, D], F32) for i in range(1)]
            h2b = [sbuf(pc, "h2b%d" % i, [128, D], BF16) for i in range(1)]
            h2c = [sbuf(pc, "h2c%d" % i, [128, KD, 128], BF16) for i in range(2)]
            ss2 = sbuf(pc, "ss2", [128, NT], F32); rs2 = sbuf(pc, "rs2", [128, NT], F32)
            print("SBUF remaining in phase C:", nc.sbuf_bytes_remaining)
            S.op("pool", lambda e: e.memset(carry[:], 0.0), writes=carry_r)
            v2 = lambda ap: ap.rearrange("p (a b) -> p a b", a=512 // LT)
            tb3 = lambda tab, j: tab[:, j, :].unsqueeze(1).to_broadcast([128, 512 // LT, LT])
            mT_v = mT_d.rearrange("(dc p) t -> p dc t", p=128)
            for c in range(NCH):
                hc = hTc[0]; mt = matt[0]
                S.dma("sp", "hcl", hc[:], hT_d[:, :, c * 512:(c + 1) * 512], writes=[hc.r])
                S.dma("sp", "mtl", mt[:], mT_v[:, :, c * 512:(c + 1) * 512], writes=[mt.r])
                for rc in range(4):
                    pu = banks[rc % 2]
                    for k in range(KD):
                        S.op("pe", lambda e: e.matmul(pu[:], lhsT=wu[:, k, rc * 128:(rc + 1) * 128], rhs=hc[:, k, :], start=(k == 0), stop=(k == KD - 1)),
                             reads=[wu.r, hc.r], writes=[pu.r])
                    S.op("act", lambda e: e.copy(out=u_sb[:, rc, :], in_=pu[:]), reads=[pu.r], writes=[u_sb.r])
                def ssm_gen(j, B_):
                    rc = j // 4
                    pxr = B_["pxr"]; pxi = B_["pxi"]; py = banks[4]
                    xr_sb = B_["xr_sb"]; xi_sb = B_["xi_sb"]; ta = B_["ta"]; tb = B_["tb"]; tcb = B_["tcb"]; td = B_["td"]
                    r_r = B_["r_r"]; r_i = B_["r_i"]; ctmp = B_["ctmp"]; cr = carry_r[j]
                    xtr = ta; xti = tcb; s_r = ta; s_i = tcb
                    S.op("pe", lambda e: e.matmul(pxr[:], lhsT=Bpad[:, j, 0, :], rhs=u_sb[:, rc, :], start=True, stop=True), reads=[Bpad.r, u_sb.r], writes=[pxr.r])
                    S.op("pe", lambda e: e.matmul(pxi[:], lhsT=Bpad[:, j, 1, :], rhs=u_sb[:, rc, :], start=True, stop=True), reads=[Bpad.r, u_sb.r], writes=[pxi.r])
                    yield
                    S.op("act", lambda e: e.copy(out=xr_sb[:], in_=pxr[:]), reads=[pxr.r], writes=[xr_sb.r])
                    S.op("act", lambda e: e.copy(out=xi_sb[:], in_=pxi[:]), reads=[pxi.r], writes=[xi_sb.r])
                    yield
                    tt("pool", v2(ta[:]), v2(xr_sb[:]), tb3(Tc, j), ALU.mult, [xr_sb.r, Tc.r], [ta.r])
                    tt("pool", v2(tb[:]), v2(xi_sb[:]), tb3(Ts, j), ALU.mult, [xi_sb.r, Ts.r], [tb.r])
                    yield
                    tt("pool", v2(tcb[:]), v2(xi_sb[:]), tb3(Tc, j), ALU.mult, [xi_sb.r, Tc.r], [tcb.r])
                    tt("pool", v2(td[:]), v2(xr_sb[:]), tb3(Ts, j), ALU.mult, [xr_sb.r, Ts.r], [td.r])
                    yield
                    tt("dve", xtr[:], ta[:], tb[:], ALU.add, [ta.r, tb.r], [xtr.r])
                    tt("dve", xti[:], tcb[:], td[:], ALU.subtract, [tcb.r, td.r], [xti.r])
                    yield
                    for sgi in range(512 // LT):
                        sl = slice(sgi * LT, (sgi + 1) * LT)
                        magb = mag[:, j:j + 1].to_broadcast([128, LT])
                        S.op("dve", lambda e: e.tensor_tensor_scan(out=r_r[:, sl], data0=magb, data1=xtr[:, sl], initial=carry[:, j, 0:1], op0=ALU.mult, op1=ALU.add),
                             reads=[mag.r, xtr.r, cr], writes=[r_r.r])
                        S.op("dve", lambda e: e.tensor_tensor_scan(out=r_i[:, sl], data0=magb, data1=xti[:, sl], initial=carry[:, j, 1:2], op0=ALU.mult, op1=ALU.add),
                             reads=[mag.r, xti.r, cr], writes=[r_i.r])
                        yield
                        last = sgi * LT + LT - 1
                        tt("dve", ctmp[:, 0:1], r_i[:, last:last + 1], wi_[:, j:j + 1], ALU.mult, [r_i.r, wi_.r], [ctmp.r])
                        tt("dve", ctmp[:, 1:2], r_i[:, last:last + 1], wr_[:, j:j + 1], ALU.mult, [r_i.r, wr_.r], [ctmp.r])
                        yield
                        S.op("dve", lambda e: e.scalar_tensor_tensor(out=carry[:, j, 0:1], in0=r_r[:, last:last + 1], scalar=wr_[:, j:j + 1], in1=ctmp[:, 0:1],
                                                                     op0=ALU.mult, op1=ALU.subtract), reads=[r_r.r, wr_.r, ctmp.r], writes=[cr])
                        S.op("dve", lambda e: e.scalar_tensor_tensor(out=carry[:, j, 1:2], in0=r_r[:, last:last + 1], scalar=wi_[:, j:j + 1], in1=ctmp[:, 1:2],
                                                                     op0=ALU.mult, op1=ALU.add), reads=[r_r.r, wi_.r, ctmp.r], writes=[cr])
                        yield
                    tt("pool", v2(ta[:]), v2(r_r[:]), tb3(Tc, j), ALU.mult, [r_r.r, Tc.r], [ta.r])
                    tt("pool", v2(tb[:]), v2(r_i[:]), tb3(Ts, j), ALU.mult, [r_i.r, Ts.r], [tb.r])
                    yield
                    tt("pool", v2(tcb[:]), v2(r_r[:]), tb3(Ts, j), ALU.mult, [r_r.r, Ts.r], [tcb.r])
                    tt("pool", v2(td[:]), v2(r_i[:]), tb3(Tc, j), ALU.mult, [r_i.r, Tc.r], [td.r])
                    yield
                    tt("dve", s_r[:], ta[:], tb[:], ALU.subtract, [ta.r, tb.r], [s_r.r])
                    tt("dve", s_i[:], tcb[:], td[:], ALU.add, [tcb.r, td.r], [s_i.r])
                    yield
                    S.op("pe", lambda e: e.matmul(py[:], lhsT=Cpad[:, j, 0, :], rhs=s_r[:], start=(j % 4 == 0), stop=False), reads=[Cpad.r, s_r.r], writes=[py.r])
                    S.op("pe", lambda e: e.matmul(py[:], lhsT=Cpad[:, j, 1, :], rhs=s_i[:], start=False, stop=(j % 4 == 3)), reads=[Cpad.r, s_i.r], writes=[py.r])
                    yield
                    if j % 4 == 3:
                        S.op("dve", lambda e: e.scalar_tensor_tensor(out=yv_[:], in0=u_sb[:, rc, :], scalar=dsk[:, rc:rc + 1], in1=py[:], op0=ALU.mult, op1=ALU.add),
                             reads=[u_sb.r, dsk.r, py.r], writes=[yv_.r])
                        S.op("act", lambda e: e.activation(out=yg[:, rc, :], in_=yv_[:], func=AF.Gelu_apprx_tanh), reads=[yv_.r], writes=[yg.r])

                SKEW = 10
                active = []
                nxt_j = 0
                while nxt_j < 16 or active:
                    if nxt_j < 16 and len(active) < 2 and (not active or active[-1][1] >= SKEW):
                        active.append([ssm_gen(nxt_j, ssets[nxt_j % 2]), 0])
                        nxt_j += 1
                    for ent in list(active):
                        try:
                            next(ent[0])
                            ent[1] += 1
                        except StopIteration:
                            active.remove(ent)
                for dc in range(KD):
                    pvl = banks[5]; pgt = banks[6]; pgs = banks[7]
                    for rc in range(4):
                        S.op("pe", lambda e: e.matmul(pvl[:], lhsT=wglu[:, rc, dc * 128:(dc + 1) * 128], rhs=yg[:, rc, :], start=(rc == 0), stop=(rc == 3)),
                             reads=[wglu.r, yg.r], writes=[pvl.r])
                    for rc in range(4):
                        S.op("pe", lambda e: e.matmul(pgt[:], lhsT=wglu[:, rc, D + dc * 128:D + (dc + 1) * 128], rhs=yg[:, rc, :], start=(rc == 0), stop=(rc == 3)),
                             reads=[wglu.r, yg.r], writes=[pgt.r])
                    for k in range(KD):
                        S.op("pe", lambda e: e.matmul(pgs[:], lhsT=wgs[:, k, dc * 128:(dc + 1) * 128], rhs=hc[:, k, :], start=(k == 0), stop=(k == KD - 1)),
                             reads=[wgs.r, hc.r], writes=[pgs.r])
                    S.op("act", lambda e: e.activation(out=sg1[:], in_=pgt[:], func=AF.Sigmoid), reads=[pgt.r], writes=[sg1.r])
                    S.op("act", lambda e: e.activation(out=sg2[:], in_=pgs[:], func=AF.Sigmoid), reads=[pgs.r], writes=[sg2.r])
                    tt("dve", ys[:], pvl[:], sg1[:], ALU.mult, [pvl.r, sg1.r], [ys.r])
                    tt("pool", ys[:], ys[:], sg2[:], ALU.mult, [ys.r, sg2.r], [ys.r])
                    tt("pool", merged[:, dc, :], ys[:], mt[:, dc, :], ALU.add, [ys.r, mt.r], [merged.r])
                if dbg:
                    S.dma("sp", "m2st", m2T_d[:, :, c * 512:(c + 1) * 512], merged[:], reads=[merged.r])
                for ti in range(4):
                    i = c * 4 + ti
                    xr_ = xres[i % 2]; x1 = x1t[0]; hb = h2b[0]; hcp = h2c[i % 2]; junkC = hb
                    S.dma("sp", "xrl%d" % (i % 2), xr_[:], x_d[i * 128:(i + 1) * 128, :], writes=[xr_.r])
                    for nh in range(2):
                        po = banks[nh]
                        for dc in range(KD):
                            S.op("pe", lambda e: e.matmul(po[:], lhsT=merged[:, dc, ti * 128:(ti + 1) * 128], rhs=wout[:, dc, nh * 512:(nh + 1) * 512],
                                                          start=(dc == 0), stop=(dc == KD - 1)), reads=[merged.r, wout.r], writes=[po.r])
                        tt("dve", x1[:, nh * 512:(nh + 1) * 512], po[:], xr_[:, nh * 512:(nh + 1) * 512], ALU.add, [po.r, xr_.r], [x1.r])
                    S.dma("sp", "x1st", out_d[i * 128:(i + 1) * 128, :], x1[:], reads=[x1.r])
                    S.op("act", lambda e: e.activation(out=junkC[:], in_=x1[:], func=AF.Square, accum_out=ss2[:, i:i + 1]), reads=[x1.r], writes=[junkC.r, ss2.r])
                    S.op("act", lambda e: e.activation(out=rs2[:, i:i + 1], in_=ss2[:, i:i + 1], func=AF.Sqrt, bias=EPS, scale=1.0 / D), reads=[ss2.r], writes=[rs2.r])
                    S.op("dve", lambda e: e.reciprocal(out=rs2[:, i:i + 1], in_=rs2[:, i:i + 1]), reads=[rs2.r], writes=[rs2.r])
                    S.op("dve", lambda e: e.scalar_tensor_tensor(out=hb[:], in0=x1[:], scalar=rs2[:, i:i + 1], in1=g2[:], op0=ALU.mult, op1=ALU.mult), reads=[x1.r, rs2.r, g2.r], writes=[hb.r])
                    bk = banks[2 + (i % 2)]
                    bv = bk[:].bitcast(BF16)
                    for k in range(KD):
                        S.op("pe", lambda e: e.transpose(out=bv[:, k * 128:(k + 1) * 128], in_=hb[:, k * 128:(k + 1) * 128], identity=identb[:]),
                             reads=[hb.r, identb.r], writes=[bk.r])
                    S.op("act", lambda e: e.copy(out=hcp[:], in_=bv[:, 0:1024].rearrange("p (k t) -> p k t", k=KD)), reads=[bk.r], writes=[hcp.r])
                    S.dma("sp", "h2st%d" % (i % 2), h2T_d[:, :, i * 128:(i + 1) * 128], hcp[:], reads=[hcp.r])
            S.barrier()
        if stop_after == "C":
            S.wait_all_dma("sp")
            return nc

        with ExitStack() as pd:
            alloc_stg(pd)
            wq = sbuf(pd, "wqp", [128, KD, 2048], BF16)
            skT = sbuf(pd, "skT", [128, 16, 128], BF16)
            for qq in range(4):
                load_w(Buf(wq.t[:, :, qq * 512:(qq + 1) * 512], wq.r), wqr_d, qq * 512, 512, None)
            for hh in range(2):
                load_w(Buf(skT.t[:, hh * 8:(hh + 1) * 8, :], skT.r), skT_d[:, hh * 8:(hh + 1) * 8, :], 0, 128, None)
            iota16 = sbuf(pd, "iota16", [128, 16], F32)
            S.op("pool", lambda e: e.iota(iota16[:], pattern=[[1, 16]], base=0, channel_multiplier=0, allow_small_or_imprecise_dtypes=True), writes=[iota16.r])
            h2c_ = [sbuf(pd, "h2cD%d" % i, [128, KD, 512], BF16) for i in range(2)]
            qTs = sbuf(pd, "qTs", [128, 16, 512], BF16)
            sc = sbuf(pd, "sc", [128, 16, 128], F32); sc2 = sbuf(pd, "sc2", [128, 16, 128], F32)
            v16 = sbuf(pd, "v16", [128, 16, 16], F32); i16 = sbuf(pd, "i16", [128, 16, 16], U32); i16f = sbuf(pd, "i16f", [128, 16, 16], F32)
            cand = sbuf(pd, "cand", [128, 8, 256], F32); cand2 = sbuf(pd, "cand2", [128, 8, 256], F32)
            best = sbuf(pd, "best", [128, 8, 16], F32); pos = sbuf(pd, "pos", [128, 8, 16], U32)
            pa_i = sbuf(pd, "pa_i", [128, 8, 16], I32); pb_i = sbuf(pd, "pb_i", [128, 8, 16], I32)
            pa_f = sbuf(pd, "pa_f", [128, 8, 16], F32); pb_f = sbuf(pd, "pb_f", [128, 8, 16], F32)
            oh = sbuf(pd, "oh", [128, 8, 16, 16], F32); pr = sbuf(pd, "pr", [128, 8, 16, 16], F32)
            rt3 = sbuf(pd, "rt3", [128, 3, 128], F32)
            esum = sbuf(pd, "esum", [128, 8], F32)
            rtT = [sbuf(pd, "rtT%d" % i, [128, 3, 128], F32) for i in range(2)]
            print("SBUF remaining in phase D0:", nc.sbuf_bytes_remaining)
            for c in range(NCH):
                hc = h2c_[c % 2]
                S.dma("sp", "h2l%d" % (c % 2), hc[:], h2T_d[:, :, c * 512:(c + 1) * 512], writes=[hc.r])
                for b in range(16):
                    pq = banks[4 + (b % 2)]
                    for k in range(KD):
                        S.op("pe", lambda e: e.matmul(pq[:], lhsT=wq[:, k, b * 128:(b + 1) * 128], rhs=hc[:, k, :], start=(k == 0), stop=(k == KD - 1)),
                             reads=[wq.r, hc.r], writes=[pq.r])
                    S.op("act", lambda e: e.copy(out=qTs[:, b, :], in_=pq[:]), reads=[pq.r], writes=[qTs.r])
                for ti in range(4):
                    i = c * 4 + ti
                    for b in range(16):
                        bk = banks[b // 4]
                        S.op("pe", lambda e: e.matmul(bk[:, (b % 4) * 128:(b % 4 + 1) * 128], lhsT=qTs[:, b, ti * 128:(ti + 1) * 128], rhs=skT[:, b, :], start=True, stop=True),
                             reads=[qTs.r, skT.r], writes=[bk.r])
                    for q4 in range(4):
                        S.op("act", lambda e: e.copy(out=sc[:, q4 * 4:(q4 + 1) * 4, :], in_=banks[q4][:].rearrange("p (a b) -> p a b", a=4)),
                             reads=[banks[q4].r], writes=[sc.r])
                    for b in range(16):
                        S.op("dve", lambda e: e.max(out=v16[:, b, 0:8], in_=sc[:, b, :]), reads=[sc.r], writes=[v16.r])
                    for b in range(16):
                        S.op("dve", lambda e: e.max_index(out=i16[:, b, 0:8], in_max=v16[:, b, 0:8], in_values=sc[:, b, :]), reads=[sc.r, v16.r], writes=[i16.r])
                    for b in range(16):
                        S.op("dve", lambda e: e.match_replace(out=sc2[:, b, :], in_to_replace=v16[:, b, 0:8], in_values=sc[:, b, :], imm_value=-1e30),
                             reads=[sc.r, v16.r], writes=[sc2.r])
                    for b in range(16):
                        S.op("dve", lambda e: e.max(out=v16[:, b, 8:16], in_=sc2[:, b, :]), reads=[sc2.r], writes=[v16.r])
                    for b in range(16):
                        S.op("dve", lambda e: e.max_index(out=i16[:, b, 8:16], in_max=v16[:, b, 8:16], in_values=sc2[:, b, :]), reads=[sc2.r, v16.r], writes=[i16.r])
                    S.op("dve", lambda e: e.tensor_copy(out=i16f[:], in_=i16[:]), reads=[i16.r], writes=[i16f.r])
                    v4 = v16[:].rearrange("p (h c) k -> p h c k", c=2)
                    S.op("dve", lambda e: e.tensor_tensor(out=cand[:].rearrange("p h (a b) -> p h a b", a=16),
                                                          in0=v4[:, :, 0, :].unsqueeze(3).to_broadcast([128, 8, 16, 16]),
                                                          in1=v4[:, :, 1, :].unsqueeze(2).to_broadcast([128, 8, 16, 16]), op=ALU.add),
                         reads=[v16.r], writes=[cand.r])
                    for h in range(8):
                        S.op("dve", lambda e: e.max(out=best[:, h, 0:8], in_=cand[:, h, :]), reads=[cand.r], writes=[best.r])
                    for h in range(8):
                        S.op("dve", lambda e: e.max_index(out=pos[:, h, 0:8], in_max=best[:, h, 0:8], in_values=cand[:, h, :]), reads=[cand.r, best.r], writes=[pos.r])
                    for h in range(8):
                        S.op("dve", lambda e: e.match_replace(out=cand2[:, h, :], in_to_replace=best[:, h, 0:8], in_values=cand[:, h, :], imm_value=-1e30),
                             reads=[cand.r, best.r], writes=[cand2.r])
                    for h in range(8):
                        S.op("dve", lambda e: e.max(out=best[:, h, 8:16], in_=cand2[:, h, :]), reads=[cand2.r], writes=[best.r])
                    for h in range(8):
                        S.op("dve", lambda e: e.max_index(out=pos[:, h, 8:16], in_max=best[:, h, 8:16], in_values=cand2[:, h, :]), reads=[cand2.r, best.r], writes=[pos.r])
                    gat = rt3[:, 2, :].rearrange("p (h k) -> p h k", h=8)
                    S.op("dve", lambda e: e.tensor_tensor(out=gat, in0=best[:], in1=best[:, :, 0:1].to_broadcast([128, 8, 16]), op=ALU.subtract),
                         reads=[best.r], writes=[rt3.r])
                    S.op("act", lambda e: e.activation(out=gat, in_=gat, func=AF.Exp), reads=[rt3.r], writes=[rt3.r])
                    S.op("dve", lambda e: e.tensor_reduce(out=esum[:], in_=gat, axis=AX.X, op=ALU.add), reads=[rt3.r], writes=[esum.r])
                    S.op("dve", lambda e: e.reciprocal(out=esum[:], in_=esum[:]), reads=[esum.r], writes=[esum.r])
                    S.op("dve", lambda e: e.tensor_tensor(out=gat, in0=gat, in1=esum[:].unsqueeze(2).to_broadcast([128, 8, 16]), op=ALU.mult),
                         reads=[rt3.r, esum.r], writes=[rt3.r])
                    S.op("dve", lambda e: e.tensor_single_scalar(out=pa_i[:], in_=pos[:].bitcast(I32), scalar=4, op=ALU.arith_shift_right), reads=[pos.r], writes=[pa_i.r])
                    S.op("dve", lambda e: e.tensor_single_scalar(out=pb_i[:], in_=pos[:].bitcast(I32), scalar=15, op=ALU.bitwise_and), reads=[pos.r], writes=[pb_i.r])
                    S.op("dve", lambda e: e.tensor_copy(out=pa_f[:], in_=pa_i[:]), reads=[pa_i.r], writes=[pa_f.r])
                    S.op("dve", lambda e: e.tensor_copy(out=pb_f[:], in_=pb_i[:]), reads=[pb_i.r], writes=[pb_f.r])
                    i4 = i16f[:].rearrange("p (h c) k -> p h c k", c=2)
                    for which, pf_ in enumerate((pa_f, pb_f)):
                        S.op("dve", lambda e: e.tensor_tensor(out=oh[:], in0=pf_[:].unsqueeze(3).to_broadcast([128, 8, 16, 16]),
                                                              in1=iota16[:].unsqueeze(1).unsqueeze(1).to_broadcast([128, 8, 16, 16]), op=ALU.is_equal),
                             reads=[pf_.r, iota16.r], writes=[oh.r])
                        S.op("dve", lambda e: e.tensor_tensor(out=pr[:], in0=oh[:], in1=i4[:, :, which, :].unsqueeze(2).to_broadcast([128, 8, 16, 16]), op=ALU.mult),
                             reads=[oh.r, i16f.r], writes=[pr.r])
                        S.op("dve", lambda e: e.tensor_reduce(out=rt3[:, which, :].rearrange("p (h k) -> p h k", h=8), in_=pr[:], axis=AX.X, op=ALU.add),
                             reads=[pr.r], writes=[rt3.r])
                    rtt = rtT[i % 2]
                    for q3 in range(3):
                        bk = banks[6 + (q3 % 2)]
                        S.op("pe", lambda e: e.transpose(out=bk[:, 0:128], in_=rt3[:, q3, :], identity=identf[:]), reads=[rt3.r, identf.r], writes=[bk.r])
                        S.op("act", lambda e: e.copy(out=rtt[:, q3, :], in_=bk[:, 0:128]), reads=[bk.r], writes=[rtt.r])
                    S.dma("sp", "rtst%d" % (i % 2), rt_d[:, :, i * 128:(i + 1) * 128], rtt[:], reads=[rtt.r])
            S.barrier()
        if stop_after == "D0":
            S.wait_all_dma("sp")
            return nc

        G = 256
        NG = T // G
        with ExitStack() as pe_:
            iotaf = sbuf(pe_, "iotaf", [128, 128], F32)
            S.op("pool", lambda e: e.iota(iotaf[:], pattern=[[1, 128]], base=0, channel_multiplier=0, allow_small_or_imprecise_dtypes=True), writes=[iotaf.r])
            GTs = [sbuf(pe_, "GT%d" % i, [128, G, 128], BF16) for i in range(2)]
            ub = [sbuf(pe_, "ub%d" % i, [128, KD, 512], BF16) for i in range(3)]
            vb = [sbuf(pe_, "vb%d" % i, [128, 4, D], BF16) for i in range(3)]
            rtgs = [sbuf(pe_, "rtg%d" % i, [128, 3, G], F32) for i in range(2)]
            h2g = sbuf(pe_, "h2g", [128, KD, G], BF16)
            Ab = [sbuf(pe_, "Ab%d" % i, [128, G], BF16) for i in range(2)]
            WT = [sbuf(pe_, "WT%d" % i, [128, G], BF16) for i in range(2)]
            P1 = [sbuf(pe_, "P1_%d" % i, [128, 128], BF16) for i in range(8)]
            P2B = [sbuf(pe_, "P2B_%d" % i, [128, 8, 128], BF16) for i in range(2)]
            x1g = [sbuf(pe_, "x1g%d" % i, [128, D], F32) for i in range(2)]
            print("SBUF remaining in phase D1:", nc.sbuf_bytes_remaining)
            uv_loaded = [False]
            gt_cnt = [0]

            def gt_gen(g):
                GT = GTs[g % 2]; rtg = rtgs[g % 2]
                S.dma("sp", "rtl%d" % (g % 2), rtg[:], rt_d[:, :, g * G:(g + 1) * G], writes=[rtg.r])
                LAG = 4
                p2bs = {}

                def gt_mm(t):
                    t8, t_ = divmod(t, 8)
                    p1 = P1[t % 8]; p2b = p2bs[t8]
                    gp = banks[6 + ((t // 4) % 2)]
                    S.op("pe", lambda e: e.matmul(gp[:, (t % 4) * 128:(t % 4 + 1) * 128], lhsT=p2b[:, t_, :], rhs=p1[:], start=True, stop=True),
                         reads=[p1.r, p2b.r], writes=[gp.r])
                    if t % 4 == 3:
                        S.op("act", lambda e: e.copy(out=GT[:, t - 3:t + 1, :], in_=gp[:].rearrange("p (a b) -> p a b", a=4)), reads=[gp.r], writes=[GT.r])

                for t in range(G):
                    t8, t_ = divmod(t, 8)
                    if t_ == 0:
                        p2b = P2B[gt_cnt[0] % 2]
                        gt_cnt[0] += 1
                        p2bs[t8] = p2b
                        S.op("dve", lambda e: e.tensor_tensor(out=p2b[:], in0=iotaf[:].unsqueeze(1).to_broadcast([128, 8, 128]),
                                                              in1=rtg[:, 1, t8 * 8:(t8 + 1) * 8].unsqueeze(2).to_broadcast([128, 8, 128]), op=ALU.is_equal),
                             reads=[iotaf.r, rtg.r], writes=[p2b.r])
                    p1 = P1[t % 8]
                    S.op("dve", lambda e: e.tensor_scalar(out=p1[:], in0=iotaf[:], scalar1=rtg[:, 0, t:t + 1], scalar2=rtg[:, 2, t:t + 1], op0=ALU.is_equal, op1=ALU.mult),
                         reads=[iotaf.r, rtg.r], writes=[p1.r])
                    if t >= LAG:
                        gt_mm(t - LAG)
                    yield
                for t in range(G - LAG, G):
                    gt_mm(t)
                yield

            def drain(gen, n=None):
                k = 0
                while n is None or k < n:
                    try:
                        next(gen)
                    except StopIteration:
                        return
                    k += 1

            drain(gt_gen(0))
            for g in range(NG):
                GT = GTs[g % 2]
                nxt = gt_gen(g + 1) if g + 1 < NG else iter(())
                S.dma("sp", "h2gl", h2g[:], h2T_d[:, :, g * G:(g + 1) * G], writes=[h2g.r])
                def emit_S(i1):
                    blk4, bi = divmod(i1, 4)
                    u_ = ub[blk4 % 3]
                    pS = banks[4 + (i1 % 2)]
                    for k in range(KD):
                        S.op("pe", lambda e: e.matmul(pS[:, 0:G], lhsT=u_[:, k, bi * 128:(bi + 1) * 128], rhs=h2g[:, k, :], start=(k == 0), stop=(k == KD - 1)),
                             reads=[u_.r, h2g.r], writes=[pS.r])
                    ab_ = Ab[i1 % 2]; wt_ = WT[i1 % 2]
                    S.op("act", lambda e: e.activation(out=ab_[:], in_=pS[:, 0:G], func=AF.Gelu_apprx_tanh), reads=[pS.r], writes=[ab_.r])
                    eng = "dve" if i1 % 2 == 0 else "pool"
                    S.op(eng, lambda e: e.tensor_tensor(out=wt_[:], in0=ab_[:], in1=GT[:, :, i1], op=ALU.mult), reads=[ab_.r, GT.r], writes=[wt_.r])

                def emit_out(i1):
                    blk4, bi = divmod(i1, 4)
                    v_ = vb[blk4 % 3]; wt_ = WT[i1 % 2]
                    for tt_ in range(G // 128):
                        for nh in range(2):
                            acc = banks[tt_ * 2 + nh]
                            S.op("pe", lambda e: e.matmul(acc[:], lhsT=wt_[:, tt_ * 128:(tt_ + 1) * 128], rhs=v_[:, bi, nh * 512:(nh + 1) * 512],
                                                          start=(i1 == 0), stop=(i1 == 127)), reads=[wt_.r, v_.r], writes=[acc.r])

                for i1 in range(128):
                    blk4, bi = divmod(i1, 4)
                    if bi == 0:
                        if not uv_loaded[0]:
                            for tok in cast_toks:
                                S._wait("sp", tok)
                            uv_loaded[0] = True
                        u_ = ub[blk4 % 3]; v_ = vb[blk4 % 3]
                        S.dma("sp", "ul%d" % (blk4 % 3), u_[:], UTb_d[:, :, blk4 * 512:(blk4 + 1) * 512], writes=[u_.r])
                        S.dma("sp", "vl%d" % (blk4 % 3), v_[:], Vb_d[:, blk4 * 4:(blk4 + 1) * 4, :], writes=[v_.r])
                    emit_S(i1)
                    if i1 >= 1:
                        emit_out(i1 - 1)
                    drain(nxt, 2)
                emit_out(127)
                drain(nxt)
                for tt_ in range(G // 128):
                    i = g * (G // 128) + tt_
                    xg = x1g[i % 2]
                    S.dma("sp", "x1l%d" % (i % 2), xg[:], out_d[i * 128:(i + 1) * 128, :], writes=[xg.r])
                    for nh in range(2):
                        acc = banks[tt_ * 2 + nh]
                        S.op("dve", lambda e: e.tensor_tensor(out=xg[:, nh * 512:(nh + 1) * 512], in0=acc[:], in1=xg[:, nh * 512:(nh + 1) * 512], op=ALU.add),
                             reads=[acc.r, xg.r], writes=[xg.r])
                    S.dma("sp", "ost%d" % (i % 2), out_d[i * 128:(i + 1) * 128, :], xg[:], reads=[xg.r])
            S.barrier()

        S.wait_all_dma("sp")
        print("program built: ops=%d waits=%d" % (S.nops, S.nwaits))
    return nc


def make_in_maps(inputs, T, n_cores):
    f = lambda a: np.ascontiguousarray(np.asarray(a, dtype=np.float32))
    w_in = f(inputs["w_in"][0]).reshape(KD, 128, INC).transpose(1, 0, 2)
    common = {
        "w_in": f(w_in),
        "g1": f(f(inputs["mix_norm_g"][0]).reshape(KD, 128).T),
        "fb": f(np.broadcast_to(f(inputs["fox_forget_bias"][0])[None, :], (128, 8))),
        "qg": f(f(inputs["q_norm_g"][0]).reshape(128, 1)),
        "kg": f(f(inputs["k_norm_g"][0]).reshape(128, 1)),
    }
    L16 = lambda a: f(f(a).reshape(16, 2, 64).transpose(1, 2, 0).reshape(128, 16))
    common["a_re_l"] = L16(inputs["ssm_a_re"][0])
    common["a_im_l"] = L16(inputs["ssm_a_im"][0])
    common["ls_l"] = f(np.broadcast_to(f(inputs["ssm_log_step"][0]).reshape(16, 2, 1), (16, 2, 64)).transpose(1, 2, 0).reshape(128, 16))
    LB = lambda a: f(f(a).reshape(16, 2, 64, 16).transpose(1, 2, 0, 3).reshape(128, 16, 16))
    LC = lambda a: f(f(a).reshape(16, 2, 16, 64).transpose(1, 3, 0, 2).reshape(128, 16, 16))
    common["b_re_l"] = LB(inputs["ssm_b_re"][0]); common["b_im_l"] = LB(inputs["ssm_b_im"][0])
    common["c_re_l"] = LC(inputs["ssm_c_re"][0]); common["c_im_l"] = LC(inputs["ssm_c_im"][0])
    common["d_l"] = f(f(inputs["ssm_d"][0]).reshape(4, 128).T)
    common["w_glu"] = f(f(inputs["ssm_w_glu"][0]).reshape(4, 128, 2048).transpose(1, 0, 2))
    common["w_out"] = f(f(inputs["w_out"][0]).reshape(KD, 128, D).transpose(1, 0, 2))
    common["g2rep"] = f(np.broadcast_to(f(inputs["ffn_norm_g"][0])[None, :], (128, D)))
    common["w_query"] = f(f(inputs["peer_w_query"][0]).reshape(KD, 128, 2048).transpose(1, 0, 2))
    common["skT"] = f(f(inputs["peer_sub_keys"][0]).reshape(16, 128, 128).transpose(2, 0, 1))
    common["peer_uT"] = f(f(inputs["peer_u"][0]).T.reshape(KD, 128, NE).transpose(1, 0, 2))
    common["peer_v"] = f(f(inputs["peer_v"][0]).reshape(128, 128, D).transpose(1, 0, 2))
    maps = []
    for c in range(n_cores):
        m = dict(common)
        m["x"] = f(inputs["x"][c, :T])
        maps.append(m)
    return maps


def kernel(**inputs):
    T = 4096
    n = 8
    nc = build_program(T)
    in_maps = make_in_maps(inputs, T, n)
    res = run_bass_kernel_spmd(nc, in_maps, core_ids=list(range(n)))
    return np.stack([np.asarray(r["out"]) for r in res.results], axis=0).astype(np.float32)
```

```python
import math
from contextlib import ExitStack
import numpy as np
import concourse.bass as bass
import concourse.mybir as mybir
from concourse.bass_utils import run_bass_kernel_spmd

F32 = mybir.dt.float32; BF16 = mybir.dt.bfloat16; I32 = mybir.dt.int32; U32 = mybir.dt.uint32
ALU = mybir.AluOpType; AF = mybir.ActivationFunctionType; AX = mybir.AxisListType

D = 1024
KD = 8
INC = 5640
EPS = 1e-6
C_U, C_Q, C_K, C_V, C_F, C_GS, C_GA = 0, 512, 1536, 2560, 3584, 3592, 4616
NEG = -30000.0
TWO_PI = 2.0 * math.pi
NE = 16384


class Res:
    __slots__ = ("name", "w", "r")

    def __init__(self, name):
        self.name = name
        self.w = None
        self.r = {}


class Sched:
    ENGS = ("pe", "act", "dve", "pool", "sp")

    def __init__(self, nc, es, same_engine_sync=("act", "dve", "pool")):
        self.nc = nc
        self.es = es
        self.eng = {"pe": nc.tensor, "act": nc.scalar, "dve": nc.vector, "pool": nc.gpsimd, "sp": nc.sync}
        self.sem = {}
        self.cnt = {}
        for e in self.ENGS:
            self.sem[e] = es.enter_context(nc.semaphore("sem_" + e))
            self.cnt[e] = 0
        self.seen = {e: {} for e in self.ENGS}
        self.same = set(same_engine_sync)
        self.nwaits = 0
        self.nops = 0

    def res(self, name):
        return Res(name)

    def _chan(self, chan):
        if chan not in self.sem:
            self.sem[chan] = self.es.enter_context(self.nc.semaphore("semd_" + chan))
            self.cnt[chan] = 0
        return self.sem[chan]

    def _wait(self, e, tok):
        if tok is None:
            return
        key, val = tok
        if key == e and e not in self.same:
            return
        if self.seen[e].get(key, 0) >= val:
            return
        self.eng[e].wait_ge(self.sem[key], val)
        self.seen[e][key] = val
        self.nwaits += 1

    def _deps(self, e, reads, writes):
        for R in reads:
            self._wait(e, R.w)
        for W in writes:
            self._wait(e, W.w)
            for key, val in W.r.items():
                self._wait(e, (key, val))

    def _commit(self, tok, reads, writes):
        for R in reads:
            if R.r.get(tok[0], 0) < tok[1]:
                R.r[tok[0]] = tok[1]
        for W in writes:
            W.w = tok
            W.r = {}

    def op(self, e, fn, reads=(), writes=()):
        self._deps(e, reads, writes)
        ins = fn(self.eng[e])
        self.cnt[e] += 1
        ins.then_inc(self.sem[e], 1)
        tok = (e, self.cnt[e])
        self._commit(tok, reads, writes)
        self.nops += 1
        return tok

    def dma(self, e, chan, out, in_, reads=(), writes=(), **kw):
        sem = self._chan(chan)
        if self.cnt[chan] > 0:
            self._wait(e, (chan, self.cnt[chan]))
        self._deps(e, reads, writes)
        ins = self.eng[e].dma_start(out=out, in_=in_, **kw)
        self.cnt[chan] += 16
        ins.then_inc(sem, 16)
        tok = (chan, self.cnt[chan])
        self._commit(tok, reads, writes)
        return tok

    def barrier(self):
        for e in self.ENGS:
            for key in list(self.sem.keys()):
                if key != e and self.cnt[key] > 0:
                    self._wait(e, (key, self.cnt[key]))
                elif key == e and e in self.same and self.cnt[key] > 0:
                    self._wait(e, (key, self.cnt[key]))

    def wait_all_dma(self, e):
        for key in list(self.sem.keys()):
            if key not in self.ENGS and self.cnt[key] > 0:
                self._wait(e, (key, self.cnt[key]))


class Buf:
    def __init__(self, t, r):
        self.t = t
        self.r = r

    def __getitem__(self, k):
        return self.t[k]


def build_program(T, dbg=False, stop_after=None):
    NT = T // 128
    NCH = T // 512
    nc = bass.Bass("TRN2", target_bir_lowering=False)

    def din(name, shape, dt=F32):
        return nc.dram_tensor(name, shape, dt, kind="ExternalInput").ap()

    x_d = din("x", [T, D])
    w_in_d = din("w_in", [128, KD, INC])
    g1_d = din("g1", [128, KD])
    fb_d = din("fb", [128, 8])
    qg_d = din("qg", [128, 1])
    kg_d = din("kg", [128, 1])
    are_d = din("a_re_l", [128, 16]); aim_d = din("a_im_l", [128, 16]); ls_d = din("ls_l", [128, 16])
    bre_d = din("b_re_l", [128, 16, 16]); bim_d = din("b_im_l", [128, 16, 16])
    cre_d = din("c_re_l", [128, 16, 16]); cim_d = din("c_im_l", [128, 16, 16])
    dsk_d = din("d_l", [128, 4])
    wglu_d = din("w_glu", [128, 4, 2048])
    wout_d = din("w_out", [128, KD, D])
    g2_d = din("g2rep", [128, D])
    wqr_d = din("w_query", [128, KD, 2048])
    skT_d = din("skT", [128, 16, 128])
    UT_d = din("peer_uT", [128, KD, NE])
    V_d = din("peer_v", [128, 128, D])
    out_d = nc.dram_tensor("out", [T, D], F32, kind="ExternalOutput").ap()
    UTb_d = nc.dram_tensor("UTb", [128, KD, NE], BF16, kind="Internal").ap()
    Vb_d = nc.dram_tensor("Vb", [128, 128, D], BF16, kind="Internal").ap()
    rt_d = nc.dram_tensor("rt", [128, 3, T], F32, kind=("ExternalOutput" if dbg else "Internal")).ap()
    kind_scr = "ExternalOutput" if dbg else "Internal"
    mT_d = nc.dram_tensor("mT", [D, T], BF16, kind=kind_scr).ap()
    hT_d = nc.dram_tensor("hTd", [128, KD, T], BF16, kind="Internal").ap()
    h2T_d = nc.dram_tensor("h2T", [128, KD, T], BF16, kind=kind_scr).ap()
    m2T_d = nc.dram_tensor("m2T", [128, KD, T], BF16, kind=kind_scr).ap() if dbg else None

    with ExitStack() as es:
        S = Sched(nc, es)

        def sbuf(stack, name, shape, dt):
            t = stack.enter_context(nc.sbuf_tensor("sb_" + name, shape, dt))
            return Buf(t, S.res(name))

        banks = []
        for i in range(8):
            t = es.enter_context(nc.psum_tensor("bank%d" % i, [128, 512], F32))
            banks.append(Buf(t, S.res("bank%d" % i)))

        identf = sbuf(es, "identf", [128, 128], F32)
        identb = sbuf(es, "identb", [128, 128], BF16)
        onesb = sbuf(es, "onesb", [128, 128], BF16)
        onesf = sbuf(es, "onesf", [128, 128], F32)
        trif = sbuf(es, "trif", [128, 128], F32)
        maskneg = sbuf(es, "maskneg", [128, 128], BF16)
        S.op("pool", lambda e: e.memset(identf[:], 1.0), writes=[identf.r])
        S.op("pool", lambda e: e.affine_select(out=identf[:], in_=identf[:], pattern=[[-1, 128]], compare_op=ALU.is_equal,
                                               fill=0.0, base=0, channel_multiplier=1), reads=[identf.r], writes=[identf.r])
        S.op("pool", lambda e: e.tensor_copy(out=identb[:], in_=identf[:]), reads=[identf.r], writes=[identb.r])
        S.op("pool", lambda e: e.memset(onesb[:], 1.0), writes=[onesb.r])
        S.op("pool", lambda e: e.memset(onesf[:], 1.0), writes=[onesf.r])
        S.op("pool", lambda e: e.memset(trif[:], 1.0), writes=[trif.r])
        S.op("pool", lambda e: e.affine_select(out=trif[:], in_=trif[:], pattern=[[1, 128]], compare_op=ALU.is_ge,
                                               fill=0.0, base=0, channel_multiplier=-1), reads=[trif.r], writes=[trif.r])
        S.op("pool", lambda e: e.tensor_scalar(out=maskneg[:], in0=trif[:], scalar1=-1.0, scalar2=-NEG, op0=ALU.add, op1=ALU.mult),
             reads=[trif.r], writes=[maskneg.r])

        hpi = sbuf(es, "hpi", [128, 1], F32)
        S.op("pool", lambda e: e.memset(hpi[:], math.pi / 2.0), writes=[hpi.r])
        g1 = sbuf(es, "g1", [128, KD], F32)
        fb = sbuf(es, "fb", [128, 8], F32)
        qg = sbuf(es, "qg", [128, 1], F32)
        kg = sbuf(es, "kg", [128, 1], F32)
        S.dma("sp", "c_g1", g1[:], g1_d, writes=[g1.r])
        S.dma("sp", "c_fb", fb[:], fb_d, writes=[fb.r])
        S.dma("sp", "c_qg", qg[:], qg_d, writes=[qg.r])
        S.dma("sp", "c_kg", kg[:], kg_d, writes=[kg.r])
        S.op("dve", lambda e: e.tensor_scalar(out=qg[:], in0=qg[:], scalar1=128.0 ** -0.5, scalar2=None, op0=ALU.mult),
             reads=[qg.r], writes=[qg.r])

        cast_toks = []
        if stop_after is None:
            for q in range(16):
                cast_toks.append(S.dma("pool", "ucast%d" % q, UTb_d[:, :, q * 1024:(q + 1) * 1024], UT_d[:, :, q * 1024:(q + 1) * 1024]))
                cast_toks.append(S.dma("pool", "vcast%d" % q, Vb_d[:, q * 8:(q + 1) * 8, :], V_d[:, q * 8:(q + 1) * 8, :]))

        stg = [None, None]
        stg_i = [0]

        def alloc_stg(stack):
            for i in range(2):
                stg[i] = sbuf(stack, "wstg%d_%d" % (i, stg_i[0]), [128, KD, 512], F32)

        def load_w(dst, src_d, c0, ncols, gain, nk=KD):
            s = stg[stg_i[0] % 2]
            stg_i[0] += 1
            S.dma("sp", "wld%d" % (stg_i[0] % 2), s[:, 0:nk, 0:ncols], src_d[:, :, c0:c0 + ncols], writes=[s.r])
            if gain is not None:
                S.op("pool", lambda e: e.tensor_tensor(out=dst[:], in0=s[:, 0:nk, 0:ncols],
                                                       in1=gain[:, 0:nk].unsqueeze(2).to_broadcast([128, nk, ncols]), op=ALU.mult),
                     reads=[s.r, gain.r], writes=[dst.r])
            else:
                S.op("pool", lambda e: e.tensor_copy(out=dst[:], in_=s[:, 0:nk, 0:ncols]), reads=[s.r], writes=[dst.r])

        scopeAB = ExitStack()
        alloc_stg(scopeAB)
        hT = sbuf(scopeAB, "hT", [128, KD, T], BF16)
        hT_r = [S.res("hT_%d" % i) for i in range(NT)]
        cum = sbuf(scopeAB, "cum", [128, NT, 8], F32)
        cend = sbuf(scopeAB, "cend", [128, NT + 1, 8], F32)
        with ExitStack() as pa:
            xts = [sbuf(pa, "xt%d" % i, [128, D], F32) for i in range(2)]
            xs = [sbuf(pa, "xs%d" % i, [128, D], BF16) for i in range(2)]
            junk = sbuf(pa, "junkA", [128, D], BF16)
            ss = sbuf(pa, "ss", [128, NT], F32)
            rs = sbuf(pa, "rs", [128, NT], F32)
            wf = sbuf(pa, "wf", [128, KD, 8], BF16)
            zf = sbuf(pa, "zf", [128, NT, 8], F32)
            spf = sbuf(pa, "spf", [128, NT, 8], F32)
            load_w(wf, w_in_d, C_F, 8, g1)
            fbank = banks[7]
            for i in range(NT):
                xt = xts[i % 2]; xb = xs[i % 2]
                S.dma("sp", "xld%d" % (i % 2), xt[:], x_d[i * 128:(i + 1) * 128, :], writes=[xt.r])
                S.op("act", lambda e: e.activation(out=junk[:], in_=xt[:], func=AF.Square, accum_out=ss[:, i:i + 1]),
                     reads=[xt.r], writes=[junk.r, ss.r])
                S.op("act", lambda e: e.activation(out=rs[:, i:i + 1], in_=ss[:, i:i + 1], func=AF.Sqrt, bias=EPS, scale=1.0 / D),
                     reads=[ss.r], writes=[rs.r])
                S.op("dve", lambda e: e.reciprocal(out=rs[:, i:i + 1], in_=rs[:, i:i + 1]), reads=[rs.r], writes=[rs.r])
                S.op("dve", lambda e: e.tensor_scalar(out=xb[:], in0=xt[:], scalar1=rs[:, i:i + 1], scalar2=None, op0=ALU.mult),
                     reads=[xt.r, rs.r], writes=[xb.r])
                bk = banks[i % 2]
                bv = bk[:].bitcast(BF16)
                for k in range(KD):
                    S.op("pe", lambda e: e.transpose(out=bv[:, k * 128:(k + 1) * 128], in_=xb[:, k * 128:(k + 1) * 128], identity=identb[:]),
                         reads=[xb.r, identb.r], writes=[bk.r])
                S.op("act", lambda e: e.copy(out=hT[:, :, i * 128:(i + 1) * 128], in_=bv[:, 0:1024].rearrange("p (k t) -> p k t", k=KD)),
                     reads=[bk.r], writes=[hT_r[i]])
                for k in range(KD):
                    S.op("pe", lambda e: e.matmul(fbank[:, i * 8:(i + 1) * 8], lhsT=hT[:, k, i * 128:(i + 1) * 128], rhs=wf[:, k, :],
                                                  start=(k == 0), stop=(k == KD - 1)),
                         reads=[hT_r[i], wf.r], writes=[fbank.r])
            S.op("dve", lambda e: e.tensor_tensor(out=zf[:], in0=fbank[:, 0:NT * 8].rearrange("p (i h) -> p i h", h=8),
                                                  in1=fb[:].unsqueeze(1).to_broadcast([128, NT, 8]), op=ALU.add),
                 reads=[fbank.r, fb.r], writes=[zf.r])
            S.op("act", lambda e: e.activation(out=spf[:], in_=zf[:], func=AF.Exp, scale=-1.0), reads=[zf.r], writes=[spf.r])
            S.op("act", lambda e: e.activation(out=spf[:], in_=spf[:], func=AF.Ln, bias=1.0, scale=1.0), reads=[spf.r], writes=[spf.r])
            cb1 = banks[5]; cb2 = banks[6]
            for i in range(NT):
                S.op("pe", lambda e: e.matmul(cb1[:, i * 8:(i + 1) * 8], lhsT=trif[:], rhs=spf[:, i, :], start=True, stop=True),
                     reads=[trif.r, spf.r], writes=[cb1.r])
                S.op("pe", lambda e: e.matmul(cb2[:, i * 8:(i + 1) * 8], lhsT=onesf[:], rhs=spf[:, i, :], start=True, stop=True),
                     reads=[onesf.r, spf.r], writes=[cb2.r])
            S.op("dve", lambda e: e.memset(cend[:, 0, :], 0.0), writes=[cend.r])
            for i in range(NT):
                S.op("dve", lambda e: e.tensor_tensor(out=cend[:, i + 1, :], in0=cend[:, i, :], in1=cb2[:, i * 8:(i + 1) * 8], op=ALU.add),
                     reads=[cend.r, cb2.r], writes=[cend.r])
            S.op("dve", lambda e: e.tensor_tensor(out=cum[:], in0=cb1[:, 0:NT * 8].rearrange("p (i h) -> p i h", h=8),
                                                  in1=cend[:, 0:NT, :], op=ALU.add),
                 reads=[cb1.r, cend.r], writes=[cum.r])

        for c in range(NCH):
            S.dma("sp", "hsp%d" % (c % 2), hT_d[:, :, c * 512:(c + 1) * 512], hT[:, :, c * 512:(c + 1) * 512], reads=hT_r[c * 4:(c + 1) * 4])
        S.barrier()
        if stop_after == "A":
            return nc
        with ExitStack() as pb:
            wq = sbuf(pb, "wq", [128, KD, 128], BF16)
            wk = sbuf(pb, "wk", [128, KD, 128], BF16)
            wv = sbuf(pb, "wv", [128, KD, 128], BF16)
            wg = sbuf(pb, "wg", [128, KD, 128], BF16)
            qT = sbuf(pb, "qT", [128, T], BF16)
            kT = sbuf(pb, "kT", [128, T], BF16)
            vv = sbuf(pb, "vv", [128, NT, 128], BF16)
            sgT = sbuf(pb, "sgT", [128, T], BF16)
            crow = sbuf(pb, "crow", [1, T], BF16)
            sq = [sbuf(pb, "sq%d" % i, [128, 512], BF16) for i in range(2)]
            rrep = [sbuf(pb, "rrep%d" % i, [128, 512], F32) for i in range(2)]
            Pt = [sbuf(pb, "Pt%d" % i, [128, 512], BF16) for i in range(3)]
            rec = sbuf(pb, "rec", [128, 512], F32)
            yv = sbuf(pb, "yv", [128, 512], F32)
            ym = [sbuf(pb, "ym%d" % i, [128, 512], BF16) for i in range(2)]
            hT_all = hT_r
            pcount = [0]
            print("SBUF remaining in phase B:", nc.sbuf_bytes_remaining)
            for h in range(8):
                load_w(wq, w_in_d, C_Q + h * 128, 128, g1)
                load_w(wk, w_in_d, C_K + h * 128, 128, g1)
                load_w(wv, w_in_d, C_V + h * 128, 128, g1)
                load_w(wg, w_in_d, C_GA + h * 128, 128, g1)
                S.op("dve", lambda e: e.tensor_scalar(out=crow[0:1, :].rearrange("p (i r) -> p i r", r=128),
                                                      in0=cend[0:1, 1:NT + 1, h:h + 1].to_broadcast([1, NT, 128]), scalar1=-1.0, scalar2=None, op0=ALU.mult),
                     reads=[cend.r], writes=[crow.r])
                for which, (wt, dstT, gvec) in enumerate(((wq, qT, qg), (wk, kT, kg))):
                    for c in range(NCH):
                        pj = banks[(2 * c) % 4]; pn = banks[(2 * c + 1) % 4]
                        for k in range(KD):
                            S.op("pe", lambda e: e.matmul(pj[:], lhsT=wt[:, k, :], rhs=hT[:, k, c * 512:(c + 1) * 512], start=(k == 0), stop=(k == KD - 1)),
                                 reads=[wt.r] + hT_all[c * 4:(c + 1) * 4], writes=[pj.r])
                        sqb = sq[c % 2]; rr = rrep[c % 2]
                        S.op("act", lambda e: e.activation(out=sqb[:], in_=pj[:], func=AF.Square), reads=[pj.r], writes=[sqb.r])
                        S.op("pe", lambda e: e.matmul(pn[:], lhsT=onesb[:], rhs=sqb[:], start=True, stop=True), reads=[onesb.r, sqb.r], writes=[pn.r])
                        S.op("act", lambda e: e.activation(out=rr[:], in_=pn[:], func=AF.Sqrt, bias=EPS, scale=1.0 / 128.0), reads=[pn.r], writes=[rr.r])
                        S.op("dve", lambda e: e.reciprocal(out=rr[:], in_=rr[:]), reads=[rr.r], writes=[rr.r])
                        S.op("dve", lambda e: e.scalar_tensor_tensor(out=dstT[:, c * 512:(c + 1) * 512], in0=pj[:], scalar=gvec[:, 0:1], in1=rr[:],
                                                                     op0=ALU.mult, op1=ALU.mult),
                             reads=[pj.r, gvec.r, rr.r], writes=[dstT.r])
                for i4 in range(NT // 4):
                    pv = banks[i4 % 2]
                    for ii in range(4):
                        i = i4 * 4 + ii
                        for k in range(KD):
                            S.op("pe", lambda e: e.matmul(pv[:, ii * 128:(ii + 1) * 128], lhsT=hT[:, k, i * 128:(i + 1) * 128], rhs=wv[:, k, :],
                                                          start=(k == 0), stop=(k == KD - 1)),
                                 reads=[wv.r, hT_all[i]], writes=[pv.r])
                    S.op("act", lambda e: e.copy(out=vv[:, i4 * 4:(i4 + 1) * 4, :], in_=pv[:].rearrange("p (a b) -> p a b", a=4)),
                         reads=[pv.r], writes=[vv.r])
                for c in range(NCH):
                    pg = banks[2 + (c % 2)]
                    for k in range(KD):
                        S.op("pe", lambda e: e.matmul(pg[:], lhsT=wg[:, k, :], rhs=hT[:, k, c * 512:(c + 1) * 512], start=(k == 0), stop=(k == KD - 1)),
                             reads=[wg.r] + hT_all[c * 4:(c + 1) * 4], writes=[pg.r])
                    S.op("act", lambda e: e.activation(out=sgT[:, c * 512:(c + 1) * 512], in_=pg[:], func=AF.Sigmoid), reads=[pg.r], writes=[sgT.r])
                for qc in range(NCH):
                    pO = banks[4 + (qc % 2)]; pL = banks[6 + (qc % 2)]
                    nkb = (qc + 1) * 4
                    def emit_S(kb):
                        q_lo = max(qc * 4, kb)
                        off = (q_lo - qc * 4) * 128
                        pS = banks[pcount[0] % 4]
                        Pb = Pt[pcount[0] % 3]
                        pcount[0] += 1
                        diag = kb >= qc * 4
                        S.op("pe", lambda e: e.matmul(pS[:, off:512], lhsT=kT[:, kb * 128:(kb + 1) * 128], rhs=qT[:, q_lo * 128:(qc + 1) * 512],
                                                      start=True, stop=False),
                             reads=[kT.r, qT.r], writes=[pS.r])
                        S.op("pe", lambda e: e.matmul(pS[:, off:512], lhsT=onesb[0:1, :], rhs=crow[0:1, q_lo * 128:(qc + 1) * 512], start=False, stop=not diag),
                             reads=[onesb.r, crow.r], writes=[pS.r])
                        if diag:
                            S.op("pe", lambda e: e.matmul(pS[:, off:off + 128], lhsT=identb[:], rhs=maskneg[:], start=False, stop=True),
                                 reads=[identb.r, maskneg.r], writes=[pS.r])
                        S.op("act", lambda e: e.activation(out=Pb[:, off:512], in_=pS[:, off:512], func=AF.Exp, bias=cum[:, kb, h:h + 1], scale=1.0),
                             reads=[pS.r, cum.r], writes=[Pb.r])
                        return (Pb, off)

                    def emit_PV(kb, Pb, off):
                        S.op("pe", lambda e: e.matmul(pO[:, off:512], lhsT=vv[:, kb, :], rhs=Pb[:, off:512], start=(kb == 0), stop=(kb == nkb - 1)),
                             reads=[vv.r, Pb.r], writes=[pO.r])
                        S.op("pe", lambda e: e.matmul(pL[:, off:512], lhsT=onesb[:], rhs=Pb[:, off:512], start=(kb == 0), stop=(kb == nkb - 1)),
                             reads=[onesb.r, Pb.r], writes=[pL.r])

                    prev = None
                    for kb in range(nkb):
                        cur = emit_S(kb)
                        if prev is not None:
                            emit_PV(kb - 1, *prev)
                        prev = cur
                    emit_PV(nkb - 1, *prev)
                    S.op("dve", lambda e: e.reciprocal(out=rec[:], in_=pL[:]), reads=[pL.r], writes=[rec.r])
                    S.op("dve", lambda e: e.tensor_tensor(out=yv[:], in0=pO[:], in1=rec[:], op=ALU.mult), reads=[pO.r, rec.r], writes=[yv.r])
                    ymb = ym[qc % 2]
                    S.op("dve", lambda e: e.tensor_tensor(out=ymb[:], in0=yv[:], in1=sgT[:, qc * 512:(qc + 1) * 512], op=ALU.mult),
                         reads=[yv.r, sgT.r], writes=[ymb.r])
                    S.dma("sp", "mst%d" % (qc % 2), mT_d[h * 128:(h + 1) * 128, qc * 512:(qc + 1) * 512], ymb[:], reads=[ymb.r])


        S.barrier()
        scopeAB.close()
        if stop_after == "B":
            S.wait_all_dma("sp")
            return nc

        LT = 128
        with ExitStack() as pc:
            def small(name, shape=(128, 16)):
                return sbuf(pc, name, list(shape), F32)
            a_re = small("a_re"); a_im = small("a_im"); lsl = small("lsl")
            S.dma("sp", "c_are", a_re[:], are_d, writes=[a_re.r])
            S.dma("sp", "c_aim", a_im[:], aim_d, writes=[a_im.r])
            S.dma("sp", "c_ls", lsl[:], ls_d, writes=[lsl.r])
            dsk = sbuf(pc, "dsk", [128, 4], F32); g2 = sbuf(pc, "g2", [128, D], F32)
            S.dma("sp", "c_dsk", dsk[:], dsk_d, writes=[dsk.r]); S.dma("sp", "c_g2", g2[:], g2_d, writes=[g2.r])
            Cpad = sbuf(pc, "Cpad", [128, 16, 2, 128], F32)
            Bpad = sbuf(pc, "Bpad", [128, 16, 2, 128], F32)
            Tc = sbuf(pc, "Tc", [128, 16, LT], F32); Ts = sbuf(pc, "Ts", [128, 16, LT], F32)
            wu = sbuf(pc, "wu", [128, KD, 512], BF16)
            wgs = sbuf(pc, "wgs", [128, KD, D], BF16)
            wglu = sbuf(pc, "wglu", [128, 4, 2048], BF16)
            wout = sbuf(pc, "wout", [128, KD, D], BF16)
            step = small("step"); th = small("th"); mag = small("mag"); cs = small("cs"); sn = small("sn")
            ki = sbuf(pc, "ki", [128, 16], I32); kf = small("kf"); rr_ = small("rr_"); ab = small("ab")
            lre = small("lre"); lim = small("lim"); den = small("den"); t1 = small("t1"); t2 = small("t2")
            cfr = small("cfr"); cfi = small("cfi")
            wr_ = small("wr_"); wi_ = small("wi_"); wt1 = small("wt1"); wt2 = small("wt2")
            pset = ExitStack()
            alloc_stg(pset)
            bre = sbuf(pset, "bre", [128, 16, 16], F32); bim = sbuf(pset, "bim", [128, 16, 16], F32)
            cre = sbuf(pset, "cre", [128, 16, 16], F32); cim = sbuf(pset, "cim", [128, 16, 16], F32)
            S.dma("sp", "c_bre", bre[:], bre_d, writes=[bre.r]); S.dma("sp", "c_bim", bim[:], bim_d, writes=[bim.r])
            S.dma("sp", "c_cre", cre[:], cre_d, writes=[cre.r]); S.dma("sp", "c_cim", cim[:], cim_d, writes=[cim.r])
            bbr = sbuf(pset, "bbr", [128, 16, 16], F32); bbi = sbuf(pset, "bbi", [128, 16, 16], F32)
            u1 = sbuf(pset, "u1", [128, 16, 16], F32); u2 = sbuf(pset, "u2", [128, 16, 16], F32)
            BZ = sbuf(pset, "BZ", [128, 16, 2, 128], F32)
            p1 = sbuf(pset, "p1", [128, 16, LT // 2], F32); p2 = sbuf(pset, "p2", [128, 16, LT // 2], F32)

            def tt(eng, out, a, b, op, rd, wr):
                S.op(eng, lambda e: e.tensor_tensor(out=out, in0=a, in1=b, op=op), reads=rd, writes=wr)

            S.op("act", lambda e: e.activation(out=step[:], in_=lsl[:], func=AF.Exp), reads=[lsl.r], writes=[step.r])
            tt("dve", th[:], a_im[:], step[:], ALU.mult, [a_im.r, step.r], [th.r])
            tt("dve", mag[:], a_re[:], step[:], ALU.mult, [a_re.r, step.r], [mag.r])
            S.op("act", lambda e: e.activation(out=mag[:], in_=mag[:], func=AF.Exp), reads=[mag.r], writes=[mag.r])
            S.op("dve", lambda e: e.tensor_scalar(out=ki[:], in0=th[:], scalar1=1.0 / TWO_PI, scalar2=None, op0=ALU.mult), reads=[th.r], writes=[ki.r])
            S.op("dve", lambda e: e.tensor_copy(out=kf[:], in_=ki[:]), reads=[ki.r], writes=[kf.r])
            S.op("dve", lambda e: e.scalar_tensor_tensor(out=rr_[:], in0=kf[:], scalar=-TWO_PI, in1=th[:], op0=ALU.mult, op1=ALU.add),
                 reads=[kf.r, th.r], writes=[rr_.r])
            PI_LO = 3.1415925
            S.op("dve", lambda e: e.tensor_scalar(out=rr_[:], in0=rr_[:], scalar1=PI_LO, scalar2=-PI_LO, op0=ALU.min, op1=ALU.max), reads=[rr_.r], writes=[rr_.r])
            S.op("act", lambda e: e.activation(out=sn[:], in_=rr_[:], func=AF.Sin), reads=[rr_.r], writes=[sn.r])
            S.op("act", lambda e: e.activation(out=ab[:], in_=rr_[:], func=AF.Abs), reads=[rr_.r], writes=[ab.r])
            S.op("act", lambda e: e.activation(out=cs[:], in_=ab[:], func=AF.Sin, scale=-1.0, bias=hpi[:, 0:1]), reads=[ab.r, hpi.r], writes=[cs.r])
            tt("dve", lre[:], mag[:], cs[:], ALU.mult, [mag.r, cs.r], [lre.r])
            tt("dve", lim[:], mag[:], sn[:], ALU.mult, [mag.r, sn.r], [lim.r])
            S.op("dve", lambda e: e.tensor_scalar(out=lre[:], in0=lre[:], scalar1=-1.0, scalar2=None, op0=ALU.add), reads=[lre.r], writes=[lre.r])
            tt("dve", t1[:], a_re[:], a_re[:], ALU.mult, [a_re.r], [t1.r])
            tt("dve", t2[:], a_im[:], a_im[:], ALU.mult, [a_im.r], [t2.r])
            tt("dve", den[:], t1[:], t2[:], ALU.add, [t1.r, t2.r], [den.r])
            S.op("dve", lambda e: e.reciprocal(out=den[:], in_=den[:]), reads=[den.r], writes=[den.r])
            tt("dve", t1[:], lre[:], a_re[:], ALU.mult, [lre.r, a_re.r], [t1.r])
            tt("dve", t2[:], lim[:], a_im[:], ALU.mult, [lim.r, a_im.r], [t2.r])
            tt("dve", cfr[:], t1[:], t2[:], ALU.add, [t1.r, t2.r], [cfr.r])
            tt("dve", cfr[:], cfr[:], den[:], ALU.mult, [cfr.r, den.r], [cfr.r])
            tt("dve", t1[:], lim[:], a_re[:], ALU.mult, [lim.r, a_re.r], [t1.r])
            tt("dve", t2[:], lre[:], a_im[:], ALU.mult, [lre.r, a_im.r], [t2.r])
            tt("dve", cfi[:], t1[:], t2[:], ALU.subtract, [t1.r, t2.r], [cfi.r])
            tt("dve", cfi[:], cfi[:], den[:], ALU.mult, [cfi.r, den.r], [cfi.r])
            bc3 = lambda v: v[:].unsqueeze(2).to_broadcast([128, 16, 16])
            tt("dve", u1[:], bre[:], bc3(cfr), ALU.mult, [bre.r, cfr.r], [u1.r])
            tt("dve", u2[:], bim[:], bc3(cfi), ALU.mult, [bim.r, cfi.r], [u2.r])
            tt("dve", bbr[:], u1[:], u2[:], ALU.subtract, [u1.r, u2.r], [bbr.r])
            tt("dve", u1[:], bim[:], bc3(cfr), ALU.mult, [bim.r, cfr.r], [u1.r])
            tt("dve", u2[:], bre[:], bc3(cfi), ALU.mult, [bre.r, cfi.r], [u2.r])
            tt("dve", bbi[:], u1[:], u2[:], ALU.add, [u1.r, u2.r], [bbi.r])
            S.op("dve", lambda e: e.tensor_scalar(out=cim[:], in0=cim[:], scalar1=-1.0, scalar2=None, op0=ALU.mult), reads=[cim.r], writes=[cim.r])
            if True:
                S.op("pool", lambda e: e.memset(Cpad[:], 0.0), writes=[Cpad.r])
                S.op("pool", lambda e: e.memset(BZ[:], 0.0), writes=[BZ.r])
                for j in range(16):
                    for two in range(2):
                        c0 = 32 * (j % 4) + 16 * two
                        ps_ = slice(two * 64, (two + 1) * 64)
                        for ri, (srcC, srcB) in enumerate(((cre, bbr), (cim, bbi))):
                            S.op("pool", lambda e: e.tensor_copy(out=Cpad[ps_, j, ri, c0:c0 + 16], in_=srcC[ps_, j, :]), reads=[srcC.r], writes=[Cpad.r])
                            S.op("pool", lambda e: e.tensor_copy(out=BZ[ps_, j, ri, c0:c0 + 16], in_=srcB[ps_, j, :]), reads=[srcB.r], writes=[BZ.r])
                for j in range(16):
                    for ri in range(2):
                        bk = banks[(2 * j + ri) % 4]
                        S.op("pe", lambda e: e.transpose(out=bk[:, 0:128], in_=BZ[:, j, ri, :], identity=identf[:]), reads=[BZ.r, identf.r], writes=[bk.r])
                        S.op("act", lambda e: e.copy(out=Bpad[:, j, ri, :], in_=bk[:, 0:128]), reads=[bk.r], writes=[Bpad.r])
            S.op("dve", lambda e: e.tensor_copy(out=wr_[:], in_=cs[:]), reads=[cs.r], writes=[wr_.r])
            S.op("dve", lambda e: e.tensor_copy(out=wi_[:], in_=sn[:]), reads=[sn.r], writes=[wi_.r])
            S.op("pool", lambda e: e.memset(Tc[:, :, 0:1], 1.0), writes=[Tc.r])
            S.op("pool", lambda e: e.memset(Ts[:, :, 0:1], 0.0), writes=[Ts.r])
            if True:
                n = 1
                while n < LT:
                    bw = lambda v: v[:].unsqueeze(2).to_broadcast([128, 16, n])
                    tt("dve", p1[:, :, 0:n], Tc[:, :, 0:n], bw(wr_), ALU.mult, [Tc.r, wr_.r], [p1.r])
                    tt("dve", p2[:, :, 0:n], Ts[:, :, 0:n], bw(wi_), ALU.mult, [Ts.r, wi_.r], [p2.r])
                    tt("dve", Tc[:, :, n:2 * n], p1[:, :, 0:n], p2[:, :, 0:n], ALU.subtract, [p1.r, p2.r], [Tc.r])
                    tt("dve", p1[:, :, 0:n], Tc[:, :, 0:n], bw(wi_), ALU.mult, [Tc.r, wi_.r], [p1.r])
                    tt("dve", p2[:, :, 0:n], Ts[:, :, 0:n], bw(wr_), ALU.mult, [Ts.r, wr_.r], [p2.r])
                    tt("dve", Ts[:, :, n:2 * n], p1[:, :, 0:n], p2[:, :, 0:n], ALU.add, [p1.r, p2.r], [Ts.r])
                    tt("dve", wt1[:], wr_[:], wr_[:], ALU.mult, [wr_.r], [wt1.r])
                    tt("dve", wt2[:], wi_[:], wi_[:], ALU.mult, [wi_.r], [wt2.r])
                    tt("dve", wi_[:], wr_[:], wi_[:], ALU.mult, [wr_.r, wi_.r], [wi_.r])
                    S.op("dve", lambda e: e.tensor_scalar(out=wi_[:], in0=wi_[:], scalar1=2.0, scalar2=None, op0=ALU.mult), reads=[wi_.r], writes=[wi_.r])
                    tt("dve", wr_[:], wt1[:], wt2[:], ALU.subtract, [wt1.r, wt2.r], [wr_.r])
                    n *= 2
            wv_ = lambda buf, a, b: Buf(buf.t[:, :, a:b], buf.r)
            load_w(wu, w_in_d, C_U, 512, g1)
            for hh in range(2):
                load_w(wv_(wgs, hh * 512, (hh + 1) * 512), w_in_d, C_GS + hh * 512, 512, g1)
                load_w(wv_(wout, hh * 512, (hh + 1) * 512), wout_d, hh * 512, 512, None)
            for qq in range(4):
                load_w(wv_(wglu, qq * 512, (qq + 1) * 512), wglu_d, qq * 512, 512, None, nk=4)
            S.barrier()
            pset.close()
            hTc = [sbuf(pc, "hTc%d" % i, [128, KD, 512], BF16) for i in range(1)]
            matt = [sbuf(pc, "matt%d" % i, [128, KD, 512], BF16) for i in range(1)]
            u_sb = sbuf(pc, "u_sb", [128, 4, 512], F32)
            ssets = []
            for si in range(2):
                B_ = {}
                for nm in ("xr_sb", "xi_sb", "ta", "tb", "tcb", "td", "r_r", "r_i"):
                    B_[nm] = sbuf(pc, "%s_%d" % (nm, si), [128, 512], F32)
                B_["ctmp"] = sbuf(pc, "ctmp_%d" % si, [128, 2], F32)
                B_["pxr"] = banks[2 + 3 * si]; B_["pxi"] = banks[3 + 3 * si]
                ssets.append(B_)
            carry = sbuf(pc, "carry", [128, 16, 2], F32)
            carry_r = [S.res("carry_%d" % j) for j in range(16)]
            yv_ = sbuf(pc, "yv_", [128, 512], F32)
            yg = sbuf(pc, "yg", [128, 4, 512], BF16)
            sg1 = sbuf(pc, "sg1", [128, 512], F32); sg2 = sbuf(pc, "sg2", [128, 512], F32)
            ys = sbuf(pc, "ys", [128, 512], F32)
            merged = sbuf(pc, "merged", [128, KD, 512], BF16)
            xres = [sbuf(pc, "xres%d" % i, [128, D], F32) for i in range(2)]
            x1t = [sbuf(pc, "x1t%d" % i, [128, D], F32) for i in range(1)]
            h2b = [sbuf(pc, "h2b%d" % i, [128, D], BF16) for i in range(1)]
            h2c = [sbuf(pc, "h2c%d" % i, [128, KD, 128], BF16) for i in range(2)]
            ss2 = sbuf(pc, "ss2", [128, NT], F32); rs2 = sbuf(pc, "rs2", [128, NT], F32)
            print("SBUF remaining in phase C:", nc.sbuf_bytes_remaining)
            S.op("pool", lambda e: e.memset(carry[:], 0.0), writes=carry_r)
            v2 = lambda ap: ap.rearrange("p (a b) -> p a b", a=512 // LT)
            tb3 = lambda tab, j: tab[:, j, :].unsqueeze(1).to_broadcast([128, 512 // LT, LT])
            mT_v = mT_d.rearrange("(dc p) t -> p dc t", p=128)
            for c in range(NCH):
                hc = hTc[0]; mt = matt[0]
                S.dma("sp", "hcl", hc[:], hT_d[:, :, c * 512:(c + 1) * 512], writes=[hc.r])
                S.dma("sp", "mtl", mt[:], mT_v[:, :, c * 512:(c + 1) * 512], writes=[mt.r])
                for rc in range(4):
                    pu = banks[rc % 2]
                    for k in range(KD):
                        S.op("pe", lambda e: e.matmul(pu[:], lhsT=wu[:, k, rc * 128:(rc + 1) * 128], rhs=hc[:, k, :], start=(k == 0), stop=(k == KD - 1)),
                             reads=[wu.r, hc.r], writes=[pu.r])
                    S.op("act", lambda e: e.copy(out=u_sb[:, rc, :], in_=pu[:]), reads=[pu.r], writes=[u_sb.r])
                def ssm_gen(j, B_):
                    rc = j // 4
                    pxr = B_["pxr"]; pxi = B_["pxi"]; py = banks[4]
                    xr_sb = B_["xr_sb"]; xi_sb = B_["xi_sb"]; ta = B_["ta"]; tb = B_["tb"]; tcb = B_["tcb"]; td = B_["td"]
                    r_r = B_["r_r"]; r_i = B_["r_i"]; ctmp = B_["ctmp"]; cr = carry_r[j]
                    xtr = ta; xti = tcb; s_r = ta; s_i = tcb
                    S.op("pe", lambda e: e.matmul(pxr[:], lhsT=Bpad[:, j, 0, :], rhs=u_sb[:, rc, :], start=True, stop=True), reads=[Bpad.r, u_sb.r], writes=[pxr.r])
                    S.op("pe", lambda e: e.matmul(pxi[:], lhsT=Bpad[:, j, 1, :], rhs=u_sb[:, rc, :], start=True, stop=True), reads=[Bpad.r, u_sb.r], writes=[pxi.r])
                    yield
                    S.op("act", lambda e: e.copy(out=xr_sb[:], in_=pxr[:]), reads=[pxr.r], writes=[xr_sb.r])
                    S.op("act", lambda e: e.copy(out=xi_sb[:], in_=pxi[:]), reads=[pxi.r], writes=[xi_sb.r])
                    yield
                    tt("dve", v2(ta[:]), v2(xr_sb[:]), tb3(Tc, j), ALU.mult, [xr_sb.r, Tc.r], [ta.r])
                    tt("pool", v2(tb[:]), v2(xi_sb[:]), tb3(Ts, j), ALU.mult, [xi_sb.r, Ts.r], [tb.r])
                    yield
                    tt("dve", v2(tcb[:]), v2(xi_sb[:]), tb3(Tc, j), ALU.mult, [xi_sb.r, Tc.r], [tcb.r])
                    tt("pool", v2(td[:]), v2(xr_sb[:]), tb3(Ts, j), ALU.mult, [xr_sb.r, Ts.r], [td.r])
                    yield
                    tt("dve", xtr[:], ta[:], tb[:], ALU.add, [ta.r, tb.r], [xtr.r])
                    tt("dve", xti[:], tcb[:], td[:], ALU.subtract, [tcb.r, td.r], [xti.r])
                    yield
                    for sgi in range(512 // LT):
                        sl = slice(sgi * LT, (sgi + 1) * LT)
                        magb = mag[:, j:j + 1].to_broadcast([128, LT])
                        S.op("dve", lambda e: e.tensor_tensor_scan(out=r_r[:, sl], data0=magb, data1=xtr[:, sl], initial=carry[:, j, 0:1], op0=ALU.mult, op1=ALU.add),
                             reads=[mag.r, xtr.r, cr], writes=[r_r.r])
                        S.op("dve", lambda e: e.tensor_tensor_scan(out=r_i[:, sl], data0=magb, data1=xti[:, sl], initial=carry[:, j, 1:2], op0=ALU.mult, op1=ALU.add),
                             reads=[mag.r, xti.r, cr], writes=[r_i.r])
                        yield
                        last = sgi * LT + LT - 1
                        tt("dve", ctmp[:, 0:1], r_i[:, last:last + 1], wi_[:, j:j + 1], ALU.mult, [r_i.r, wi_.r], [ctmp.r])
                        tt("dve", ctmp[:, 1:2], r_i[:, last:last + 1], wr_[:, j:j + 1], ALU.mult, [r_i.r, wr_.r], [ctmp.r])
                        yield
                        S.op("dve", lambda e: e.scalar_tensor_tensor(out=carry[:, j, 0:1], in0=r_r[:, last:last + 1], scalar=wr_[:, j:j + 1], in1=ctmp[:, 0:1],
                                                                     op0=ALU.mult, op1=ALU.subtract), reads=[r_r.r, wr_.r, ctmp.r], writes=[cr])
                        S.op("dve", lambda e: e.scalar_tensor_tensor(out=carry[:, j, 1:2], in0=r_r[:, last:last + 1], scalar=wi_[:, j:j + 1], in1=ctmp[:, 1:2],
                                                                     op0=ALU.mult, op1=ALU.add), reads=[r_r.r, wi_.r, ctmp.r], writes=[cr])
                        yield
                    tt("dve", v2(ta[:]), v2(r_r[:]), tb3(Tc, j), ALU.mult, [r_r.r, Tc.r], [ta.r])
                    tt("pool", v2(tb[:]), v2(r_i[:]), tb3(Ts, j), ALU.mult, [r_i.r, Ts.r], [tb.r])
                    yield
                    tt("pool", v2(tcb[:]), v2(r_r[:]), tb3(Ts, j), ALU.mult, [r_r.r, Ts.r], [tcb.r])
                    tt("dve", v2(td[:]), v2(r_i[:]), tb3(Tc, j), ALU.mult, [r_i.r, Tc.r], [td.r])
                    yield
                    tt("dve", s_r[:], ta[:], tb[:], ALU.subtract, [ta.r, tb.r], [s_r.r])
                    tt("dve", s_i[:], tcb[:], td[:], ALU.add, [tcb.r, td.r], [s_i.r])
                    yield
                    S.op("pe", lambda e: e.matmul(py[:], lhsT=Cpad[:, j, 0, :], rhs=s_r[:], start=(j % 4 == 0), stop=False), reads=[Cpad.r, s_r.r], writes=[py.r])
                    S.op("pe", lambda e: e.matmul(py[:], lhsT=Cpad[:, j, 1, :], rhs=s_i[:], start=False, stop=(j % 4 == 3)), reads=[Cpad.r, s_i.r], writes=[py.r])
                    yield
                    if j % 4 == 3:
                        S.op("dve", lambda e: e.scalar_tensor_tensor(out=yv_[:], in0=u_sb[:, rc, :], scalar=dsk[:, rc:rc + 1], in1=py[:], op0=ALU.mult, op1=ALU.add),
                             reads=[u_sb.r, dsk.r, py.r], writes=[yv_.r])
                        S.op("act", lambda e: e.activation(out=yg[:, rc, :], in_=yv_[:], func=AF.Gelu_apprx_tanh), reads=[yv_.r], writes=[yg.r])

                SKEW = 10
                active = []
                nxt_j = 0
                while nxt_j < 16 or active:
                    if nxt_j < 16 and len(active) < 2 and (not active or active[-1][1] >= SKEW):
                        active.append([ssm_gen(nxt_j, ssets[nxt_j % 2]), 0])
                        nxt_j += 1
                    for ent in list(active):
                        try:
                            next(ent[0])
                            ent[1] += 1
                        except StopIteration:
                            active.remove(ent)
                for dc in range(KD):
                    pvl = banks[5]; pgt = banks[6]; pgs = banks[7]
                    for rc in range(4):
                        S.op("pe", lambda e: e.matmul(pvl[:], lhsT=wglu[:, rc, dc * 128:(dc + 1) * 128], rhs=yg[:, rc, :], start=(rc == 0), stop=(rc == 3)),
                             reads=[wglu.r, yg.r], writes=[pvl.r])
                    for rc in range(4):
                        S.op("pe", lambda e: e.matmul(pgt[:], lhsT=wglu[:, rc, D + dc * 128:D + (dc + 1) * 128], rhs=yg[:, rc, :], start=(rc == 0), stop=(rc == 3)),
                             reads=[wglu.r, yg.r], writes=[pgt.r])
                    for k in range(KD):
                        S.op("pe", lambda e: e.matmul(pgs[:], lhsT=wgs[:, k, dc * 128:(dc + 1) * 128], rhs=hc[:, k, :], start=(k == 0), stop=(k == KD - 1)),
                             reads=[wgs.r, hc.r], writes=[pgs.r])
                    S.op("act", lambda e: e.activation(out=sg1[:], in_=pgt[:], func=AF.Sigmoid), reads=[pgt.r], writes=[sg1.r])
                    S.op("act", lambda e: e.activation(out=sg2[:], in_=pgs[:], func=AF.Sigmoid), reads=[pgs.r], writes=[sg2.r])
                    tt("dve", ys[:], pvl[:], sg1[:], ALU.mult, [pvl.r, sg1.r], [ys.r])
                    tt("pool", ys[:], ys[:], sg2[:], ALU.mult, [ys.r, sg2.r], [ys.r])
                    tt("pool", merged[:, dc, :], ys[:], mt[:, dc, :], ALU.add, [ys.r, mt.r], [merged.r])
                if dbg:
                    S.dma("sp", "m2st", m2T_d[:, :, c * 512:(c + 1) * 512], merged[:], reads=[merged.r])
                for ti in range(4):
                    i = c * 4 + ti
                    xr_ = xres[i % 2]; x1 = x1t[0]; hb = h2b[0]; hcp = h2c[i % 2]; junkC = hb
                    S.dma("sp", "xrl%d" % (i % 2), xr_[:], x_d[i * 128:(i + 1) * 128, :], writes=[xr_.r])
                    for nh in range(2):
                        po = banks[nh]
                        for dc in range(KD):
                            S.op("pe", lambda e: e.matmul(po[:], lhsT=merged[:, dc, ti * 128:(ti + 1) * 128], rhs=wout[:, dc, nh * 512:(nh + 1) * 512],
                                                          start=(dc == 0), stop=(dc == KD - 1)), reads=[merged.r, wout.r], writes=[po.r])
                        tt("dve", x1[:, nh * 512:(nh + 1) * 512], po[:], xr_[:, nh * 512:(nh + 1) * 512], ALU.add, [po.r, xr_.r], [x1.r])
                    S.dma("sp", "x1st", out_d[i * 128:(i + 1) * 128, :], x1[:], reads=[x1.r])
                    S.op("act", lambda e: e.activation(out=junkC[:], in_=x1[:], func=AF.Square, accum_out=ss2[:, i:i + 1]), reads=[x1.r], writes=[junkC.r, ss2.r])
                    S.op("act", lambda e: e.activation(out=rs2[:, i:i + 1], in_=ss2[:, i:i + 1], func=AF.Sqrt, bias=EPS, scale=1.0 / D), reads=[ss2.r], writes=[rs2.r])
                    S.op("dve", lambda e: e.reciprocal(out=rs2[:, i:i + 1], in_=rs2[:, i:i + 1]), reads=[rs2.r], writes=[rs2.r])
                    S.op("dve", lambda e: e.scalar_tensor_tensor(out=hb[:], in0=x1[:], scalar=rs2[:, i:i + 1], in1=g2[:], op0=ALU.mult, op1=ALU.mult), reads=[x1.r, rs2.r, g2.r], writes=[hb.r])
                    bk = banks[2 + (i % 2)]
                    bv = bk[:].bitcast(BF16)
                    for k in range(KD):
                        S.op("pe", lambda e: e.transpose(out=bv[:, k * 128:(k + 1) * 128], in_=hb[:, k * 128:(k + 1) * 128], identity=identb[:]),
                             reads=[hb.r, identb.r], writes=[bk.r])
                    S.op("act", lambda e: e.copy(out=hcp[:], in_=bv[:, 0:1024].rearrange("p (k t) -> p k t", k=KD)), reads=[bk.r], writes=[hcp.r])
                    S.dma("sp", "h2st%d" % (i % 2), h2T_d[:, :, i * 128:(i + 1) * 128], hcp[:], reads=[hcp.r])
            S.barrier()
        if stop_after == "C":
            S.wait_all_dma("sp")
            return nc

        with ExitStack() as pd:
            alloc_stg(pd)
            wq = sbuf(pd, "wqp", [128, KD, 2048], BF16)
            skT = sbuf(pd, "skT", [128, 16, 128], BF16)
            for qq in range(4):
                load_w(Buf(wq.t[:, :, qq * 512:(qq + 1) * 512], wq.r), wqr_d, qq * 512, 512, None)
            for hh in range(2):
                load_w(Buf(skT.t[:, hh * 8:(hh + 1) * 8, :], skT.r), skT_d[:, hh * 8:(hh + 1) * 8, :], 0, 128, None)
            iota16 = sbuf(pd, "iota16", [128, 16], F32)
            S.op("pool", lambda e: e.iota(iota16[:], pattern=[[1, 16]], base=0, channel_multiplier=0, allow_small_or_imprecise_dtypes=True), writes=[iota16.r])
            h2c_ = [sbuf(pd, "h2cD%d" % i, [128, KD, 512], BF16) for i in range(2)]
            qTs = sbuf(pd, "qTs", [128, 16, 512], BF16)
            sc = sbuf(pd, "sc", [128, 16, 128], F32); sc2 = sbuf(pd, "sc2", [128, 16, 128], F32)
            v16s = [sbuf(pd, "v16_%d" % i, [128, 16, 16], F32) for i in range(2)]
            i16s = [sbuf(pd, "i16_%d" % i, [128, 16, 16], U32) for i in range(2)]
            i16fs = [sbuf(pd, "i16f_%d" % i, [128, 16, 16], F32) for i in range(2)]
            cand = sbuf(pd, "cand", [128, 8, 256], F32); cand2 = sbuf(pd, "cand2", [128, 8, 256], F32)
            bests = [sbuf(pd, "best_%d" % i, [128, 8, 16], F32) for i in range(2)]
            poss = [sbuf(pd, "pos_%d" % i, [128, 8, 16], U32) for i in range(2)]
            pa_is = [sbuf(pd, "pa_i_%d" % i, [128, 8, 16], I32) for i in range(2)]
            pb_is = [sbuf(pd, "pb_i_%d" % i, [128, 8, 16], I32) for i in range(2)]
            pa_f = sbuf(pd, "pa_f", [128, 8, 16], F32); pb_f = sbuf(pd, "pb_f", [128, 8, 16], F32)
            ohs = [sbuf(pd, "oh%d" % i, [128, 8, 16, 16], F32) for i in range(4)]; pr = sbuf(pd, "pr", [128, 8, 16, 16], F32)
            gpre = sbuf(pd, "gpre", [128, 8, 16], F32); gjunk = sbuf(pd, "gjunk", [128, 8, 16], F32)
            rt3s = [sbuf(pd, "rt3_%d" % i, [128, 3, 128], F32) for i in range(2)]
            esum = sbuf(pd, "esum", [128, 8], F32)
            rtT = [sbuf(pd, "rtT%d" % i, [128, 3, 128], F32) for i in range(2)]
            print("SBUF remaining in phase D0:", nc.sbuf_bytes_remaining)
            def d0_chunk(c):
                hc = h2c_[c % 2]
                S.dma("sp", "h2l%d" % (c % 2), hc[:], h2T_d[:, :, c * 512:(c + 1) * 512], writes=[hc.r])
                for b in range(16):
                    pq = banks[4 + (b % 2)]
                    for k in range(KD):
                        S.op("pe", lambda e: e.matmul(pq[:], lhsT=wq[:, k, b * 128:(b + 1) * 128], rhs=hc[:, k, :], start=(k == 0), stop=(k == KD - 1)),
                             reads=[wq.r, hc.r], writes=[pq.r])
                    S.op("act", lambda e: e.copy(out=qTs[:, b, :], in_=pq[:]), reads=[pq.r], writes=[qTs.r])

            def d0_hdr(i):
                return (v16s[i % 2], i16s[i % 2], i16fs[i % 2], bests[i % 2], poss[i % 2], pa_is[i % 2], pb_is[i % 2], rt3s[i % 2])

            def d0_front(i):
                c, ti = divmod(i, 4)
                v16, i16, i16f, best, pos, pa_i, pb_i, rt3 = d0_hdr(i)
                for b in range(16):
                    bk = banks[b // 4]
                    S.op("pe", lambda e: e.matmul(bk[:, (b % 4) * 128:(b % 4 + 1) * 128], lhsT=qTs[:, b, ti * 128:(ti + 1) * 128], rhs=skT[:, b, :], start=True, stop=True),
                         reads=[qTs.r, skT.r], writes=[bk.r])
                for q4 in range(4):
                    S.op("act", lambda e: e.copy(out=sc[:, q4 * 4:(q4 + 1) * 4, :], in_=banks[q4][:].rearrange("p (a b) -> p a b", a=4)),
                         reads=[banks[q4].r], writes=[sc.r])
                for b in range(16):
                    S.op("dve", lambda e: e.max(out=v16[:, b, 0:8], in_=sc[:, b, :]), reads=[sc.r], writes=[v16.r])
                for b in range(16):
                    S.op("dve", lambda e: e.max_index(out=i16[:, b, 0:8], in_max=v16[:, b, 0:8], in_values=sc[:, b, :]), reads=[sc.r, v16.r], writes=[i16.r])
                for b in range(16):
                    S.op("dve", lambda e: e.match_replace(out=sc2[:, b, :], in_to_replace=v16[:, b, 0:8], in_values=sc[:, b, :], imm_value=-1e30),
                         reads=[sc.r, v16.r], writes=[sc2.r])
                for b in range(16):
                    S.op("dve", lambda e: e.max(out=v16[:, b, 8:16], in_=sc2[:, b, :]), reads=[sc2.r], writes=[v16.r])
                for b in range(16):
                    S.op("dve", lambda e: e.max_index(out=i16[:, b, 8:16], in_max=v16[:, b, 8:16], in_values=sc2[:, b, :]), reads=[sc2.r, v16.r], writes=[i16.r])
                v4 = v16[:].rearrange("p (h c) k -> p h c k", c=2)
                S.op("dve", lambda e: e.tensor_tensor(out=cand[:].rearrange("p h (a b) -> p h a b", a=16),
                                                      in0=v4[:, :, 0, :].unsqueeze(3).to_broadcast([128, 8, 16, 16]),
                                                      in1=v4[:, :, 1, :].unsqueeze(2).to_broadcast([128, 8, 16, 16]), op=ALU.add),
                     reads=[v16.r], writes=[cand.r])
                for h in range(8):
                    S.op("dve", lambda e: e.max(out=best[:, h, 0:8], in_=cand[:, h, :]), reads=[cand.r], writes=[best.r])
                for h in range(8):
                    S.op("dve", lambda e: e.max_index(out=pos[:, h, 0:8], in_max=best[:, h, 0:8], in_values=cand[:, h, :]), reads=[cand.r, best.r], writes=[pos.r])
                for h in range(8):
                    S.op("dve", lambda e: e.match_replace(out=cand2[:, h, :], in_to_replace=best[:, h, 0:8], in_values=cand[:, h, :], imm_value=-1e30),
                         reads=[cand.r, best.r], writes=[cand2.r])
                for h in range(8):
                    S.op("dve", lambda e: e.max(out=best[:, h, 8:16], in_=cand2[:, h, :]), reads=[cand2.r], writes=[best.r])
                for h in range(8):
                    S.op("dve", lambda e: e.max_index(out=pos[:, h, 8:16], in_max=best[:, h, 8:16], in_values=cand2[:, h, :]), reads=[cand2.r, best.r], writes=[pos.r])
                S.op("dve", lambda e: e.tensor_single_scalar(out=pa_i[:], in_=pos[:].bitcast(I32), scalar=4, op=ALU.arith_shift_right), reads=[pos.r], writes=[pa_i.r])
                S.op("dve", lambda e: e.tensor_single_scalar(out=pb_i[:], in_=pos[:].bitcast(I32), scalar=15, op=ALU.bitwise_and), reads=[pos.r], writes=[pb_i.r])
                oh_a = ohs[(i % 2) * 2]; oh_b = ohs[(i % 2) * 2 + 1]
                S.op("dve", lambda e: e.tensor_copy(out=pa_f[:], in_=pa_i[:]), reads=[pa_i.r], writes=[pa_f.r])
                S.op("dve", lambda e: e.tensor_copy(out=pb_f[:], in_=pb_i[:]), reads=[pb_i.r], writes=[pb_f.r])
                for pf_, oh_ in ((pa_f, oh_a), (pb_f, oh_b)):
                    S.op("dve", lambda e: e.tensor_tensor(out=oh_[:], in0=pf_[:].unsqueeze(3).to_broadcast([128, 8, 16, 16]),
                                                          in1=iota16[:].unsqueeze(1).unsqueeze(1).to_broadcast([128, 8, 16, 16]), op=ALU.is_equal),
                         reads=[pf_.r, iota16.r], writes=[oh_.r])

            def d0_back(i):
                v16, i16, i16f, best, pos, pa_i, pb_i, rt3 = d0_hdr(i)
                oh_a = ohs[(i % 2) * 2]; oh_b = ohs[(i % 2) * 2 + 1]
                S.op("pool", lambda e: e.tensor_copy(out=i16f[:], in_=i16[:]), reads=[i16.r], writes=[i16f.r])
                gat = rt3[:, 2, :].rearrange("p (h k) -> p h k", h=8)
                S.op("pool", lambda e: e.tensor_tensor(out=gpre[:], in0=best[:], in1=best[:, :, 0:1].to_broadcast([128, 8, 16]), op=ALU.subtract),
                     reads=[best.r], writes=[gpre.r])
                for h in range(8):
                    S.op("act", lambda e: e.activation(out=gjunk[:, h, :], in_=gpre[:, h, :], func=AF.Exp, accum_out=esum[:, h:h + 1]),
                         reads=[gpre.r], writes=[gjunk.r, esum.r])
                S.op("act", lambda e: e.activation(out=esum[:], in_=esum[:], func=AF.Ln), reads=[esum.r], writes=[esum.r])
                S.op("pool", lambda e: e.tensor_tensor(out=gpre[:], in0=gpre[:], in1=esum[:].unsqueeze(2).to_broadcast([128, 8, 16]), op=ALU.subtract),
                     reads=[gpre.r, esum.r], writes=[gpre.r])
                S.op("act", lambda e: e.activation(out=gat, in_=gpre[:], func=AF.Exp), reads=[gpre.r], writes=[rt3.r])
                i4 = i16f[:].rearrange("p (h c) k -> p h c k", c=2)
                for which, oh_ in enumerate((oh_a, oh_b)):
                    S.op("pool", lambda e: e.tensor_tensor(out=pr[:], in0=oh_[:], in1=i4[:, :, which, :].unsqueeze(2).to_broadcast([128, 8, 16, 16]), op=ALU.mult),
                         reads=[oh_.r, i16f.r], writes=[pr.r])
                    S.op("pool", lambda e: e.tensor_tensor(out=pr[:, :, :, 0:8], in0=pr[:, :, :, 0:8], in1=pr[:, :, :, 8:16], op=ALU.add), reads=[pr.r], writes=[pr.r])
                    S.op("pool", lambda e: e.tensor_tensor(out=pr[:, :, :, 0:4], in0=pr[:, :, :, 0:4], in1=pr[:, :, :, 4:8], op=ALU.add), reads=[pr.r], writes=[pr.r])
                    S.op("pool", lambda e: e.tensor_tensor(out=pr[:, :, :, 0:2], in0=pr[:, :, :, 0:2], in1=pr[:, :, :, 2:4], op=ALU.add), reads=[pr.r], writes=[pr.r])
                    S.op("pool", lambda e: e.tensor_tensor(out=rt3[:, which, :].rearrange("p (h k) -> p h k", h=8), in0=pr[:, :, :, 0], in1=pr[:, :, :, 1], op=ALU.add),
                         reads=[pr.r], writes=[rt3.r])
                rtt = rtT[i % 2]
                for q3 in range(3):
                    bk = banks[6 + (q3 % 2)]
                    S.op("pe", lambda e: e.transpose(out=bk[:, 0:128], in_=rt3[:, q3, :], identity=identf[:]), reads=[rt3.r, identf.r], writes=[bk.r])
                    S.op("act", lambda e: e.copy(out=rtt[:, q3, :], in_=bk[:, 0:128]), reads=[bk.r], writes=[rtt.r])
                S.dma("sp", "rtst%d" % (i % 2), rt_d[:, :, i * 128:(i + 1) * 128], rtt[:], reads=[rtt.r])


            for i in range(NT + 1):
                if i < NT:
                    if i % 4 == 0:
                        d0_chunk(i // 4)
                    d0_front(i)
                if i >= 1:
                    d0_back(i - 1)

            S.barrier()
        if stop_after == "D0":
            S.wait_all_dma("sp")
            return nc

        G = 256
        NG = T // G
        with ExitStack() as pe_:
            iotaf = sbuf(pe_, "iotaf", [128, 128], F32)
            S.op("pool", lambda e: e.iota(iotaf[:], pattern=[[1, 128]], base=0, channel_multiplier=0, allow_small_or_imprecise_dtypes=True), writes=[iotaf.r])
            GTs = [sbuf(pe_, "GT%d" % i, [128, G, 128], BF16) for i in range(2)]
            ub = [sbuf(pe_, "ub%d" % i, [128, KD, 512], BF16) for i in range(3)]
            vb = [sbuf(pe_, "vb%d" % i, [128, 4, D], BF16) for i in range(3)]
            rtgs = [sbuf(pe_, "rtg%d" % i, [128, 3, G], F32) for i in range(2)]
            h2g = sbuf(pe_, "h2g", [128, KD, G], BF16)
            Ab = [sbuf(pe_, "Ab%d" % i, [128, G], BF16) for i in range(2)]
            WT = [sbuf(pe_, "WT%d" % i, [128, G], BF16) for i in range(2)]
            P1 = [sbuf(pe_, "P1_%d" % i, [128, 128], BF16) for i in range(8)]
            P2B = [sbuf(pe_, "P2B_%d" % i, [128, 8, 128], BF16) for i in range(2)]
            x1g = [sbuf(pe_, "x1g%d" % i, [128, D], F32) for i in range(2)]
            print("SBUF remaining in phase D1:", nc.sbuf_bytes_remaining)
            uv_loaded = [False]
            gt_cnt = [0]

            def gt_gen(g):
                GT = GTs[g % 2]; rtg = rtgs[g % 2]
                S.dma("sp", "rtl%d" % (g % 2), rtg[:], rt_d[:, :, g * G:(g + 1) * G], writes=[rtg.r])
                LAG = 4
                p2bs = {}

                def gt_mm(t):
                    t8, t_ = divmod(t, 8)
                    p1 = P1[t % 8]; p2b = p2bs[t8]
                    gp = banks[6 + ((t // 4) % 2)]
                    S.op("pe", lambda e: e.matmul(gp[:, (t % 4) * 128:(t % 4 + 1) * 128], lhsT=p2b[:, t_, :], rhs=p1[:], start=True, stop=True),
                         reads=[p1.r, p2b.r], writes=[gp.r])
                    if t % 4 == 3:
                        S.op("act", lambda e: e.copy(out=GT[:, t - 3:t + 1, :], in_=gp[:].rearrange("p (a b) -> p a b", a=4)), reads=[gp.r], writes=[GT.r])

                for t in range(G):
                    t8, t_ = divmod(t, 8)
                    if t_ == 0:
                        p2b = P2B[gt_cnt[0] % 2]
                        gt_cnt[0] += 1
                        p2bs[t8] = p2b
                        S.op("dve", lambda e: e.tensor_tensor(out=p2b[:], in0=iotaf[:].unsqueeze(1).to_broadcast([128, 8, 128]),
                                                              in1=rtg[:, 1, t8 * 8:(t8 + 1) * 8].unsqueeze(2).to_broadcast([128, 8, 128]), op=ALU.is_equal),
                             reads=[iotaf.r, rtg.r], writes=[p2b.r])
                    p1 = P1[t % 8]
                    S.op("dve", lambda e: e.tensor_scalar(out=p1[:], in0=iotaf[:], scalar1=rtg[:, 0, t:t + 1], scalar2=rtg[:, 2, t:t + 1], op0=ALU.is_equal, op1=ALU.mult),
                         reads=[iotaf.r, rtg.r], writes=[p1.r])
                    if t >= LAG:
                        gt_mm(t - LAG)
                    yield
                for t in range(G - LAG, G):
                    gt_mm(t)
                yield

            def drain(gen, n=None):
                k = 0
                while n is None or k < n:
                    try:
                        next(gen)
                    except StopIteration:
                        return
                    k += 1

            drain(gt_gen(0))
            for g in range(NG):
                GT = GTs[g % 2]
                nxt = gt_gen(g + 1) if g + 1 < NG else iter(())
                S.dma("sp", "h2gl", h2g[:], h2T_d[:, :, g * G:(g + 1) * G], writes=[h2g.r])
                def emit_S(i1):
                    blk4, bi = divmod(i1, 4)
                    u_ = ub[blk4 % 3]
                    pS = banks[4 + (i1 % 2)]
                    for k in range(KD):
                        S.op("pe", lambda e: e.matmul(pS[:, 0:G], lhsT=u_[:, k, bi * 128:(bi + 1) * 128], rhs=h2g[:, k, :], start=(k == 0), stop=(k == KD - 1)),
                             reads=[u_.r, h2g.r], writes=[pS.r])
                    ab_ = Ab[i1 % 2]; wt_ = WT[i1 % 2]
                    S.op("act", lambda e: e.activation(out=ab_[:], in_=pS[:, 0:G], func=AF.Gelu_apprx_tanh), reads=[pS.r], writes=[ab_.r])
                    eng = "dve" if i1 % 2 == 0 else "pool"
                    S.op(eng, lambda e: e.tensor_tensor(out=wt_[:], in0=ab_[:], in1=GT[:, :, i1], op=ALU.mult), reads=[ab_.r, GT.r], writes=[wt_.r])

                def emit_out(i1):
                    blk4, bi = divmod(i1, 4)
                    v_ = vb[blk4 % 3]; wt_ = WT[i1 % 2]
                    for tt_ in range(G // 128):
                        for nh in range(2):
                            acc = banks[tt_ * 2 + nh]
                            S.op("pe", lambda e: e.matmul(acc[:], lhsT=wt_[:, tt_ * 128:(tt_ + 1) * 128], rhs=v_[:, bi, nh * 512:(nh + 1) * 512],
                                                          start=(i1 == 0), stop=(i1 == 127)), reads=[wt_.r, v_.r], writes=[acc.r])

                for i1 in range(128):
                    blk4, bi = divmod(i1, 4)
                    if bi == 0:
                        if not uv_loaded[0]:
                            for tok in cast_toks:
                                S._wait("sp", tok)
                            uv_loaded[0] = True
                        u_ = ub[blk4 % 3]; v_ = vb[blk4 % 3]
                        S.dma("sp", "ul%d" % (blk4 % 3), u_[:], UTb_d[:, :, blk4 * 512:(blk4 + 1) * 512], writes=[u_.r])
                        S.dma("sp", "vl%d" % (blk4 % 3), v_[:], Vb_d[:, blk4 * 4:(blk4 + 1) * 4, :], writes=[v_.r])
                    emit_S(i1)
                    if i1 >= 1:
                        emit_out(i1 - 1)
                    drain(nxt, 2)
                emit_out(127)
                drain(nxt)
                for tt_ in range(G // 128):
                    i = g * (G // 128) + tt_
                    xg = x1g[i % 2]
                    S.dma("sp", "x1l%d" % (i % 2), xg[:], out_d[i * 128:(i + 1) * 128, :], writes=[xg.r])
                    for nh in range(2):
                        acc = banks[tt_ * 2 + nh]
                        S.op("dve", lambda e: e.tensor_tensor(out=xg[:, nh * 512:(nh + 1) * 512], in0=acc[:], in1=xg[:, nh * 512:(nh + 1) * 512], op=ALU.add),
                             reads=[acc.r, xg.r], writes=[xg.r])
                    S.dma("sp", "ost%d" % (i % 2), out_d[i * 128:(i + 1) * 128, :], xg[:], reads=[xg.r])
            S.barrier()

        S.wait_all_dma("sp")
        print("program built: ops=%d waits=%d" % (S.nops, S.nwaits))
    return nc


def make_in_maps(inputs, T, n_cores):
    f = lambda a: np.ascontiguousarray(np.asarray(a, dtype=np.float32))
    w_in = f(inputs["w_in"][0]).reshape(KD, 128, INC).transpose(1, 0, 2)
    common = {
        "w_in": f(w_in),
        "g1": f(f(inputs["mix_norm_g"][0]).reshape(KD, 128).T),
        "fb": f(np.broadcast_to(f(inputs["fox_forget_bias"][0])[None, :], (128, 8))),
        "qg": f(f(inputs["q_norm_g"][0]).reshape(128, 1)),
        "kg": f(f(inputs["k_norm_g"][0]).reshape(128, 1)),
    }
    L16 = lambda a: f(f(a).reshape(16, 2, 64).transpose(1, 2, 0).reshape(128, 16))
    common["a_re_l"] = L16(inputs["ssm_a_re"][0])
    common["a_im_l"] = L16(inputs["ssm_a_im"][0])
    common["ls_l"] = f(np.broadcast_to(f(inputs["ssm_log_step"][0]).reshape(16, 2, 1), (16, 2, 64)).transpose(1, 2, 0).reshape(128, 16))
    LB = lambda a: f(f(a).reshape(16, 2, 64, 16).transpose(1, 2, 0, 3).reshape(128, 16, 16))
    LC = lambda a: f(f(a).reshape(16, 2, 16, 64).transpose(1, 3, 0, 2).reshape(128, 16, 16))
    common["b_re_l"] = LB(inputs["ssm_b_re"][0]); common["b_im_l"] = LB(inputs["ssm_b_im"][0])
    common["c_re_l"] = LC(inputs["ssm_c_re"][0]); common["c_im_l"] = LC(inputs["ssm_c_im"][0])
    common["d_l"] = f(f(inputs["ssm_d"][0]).reshape(4, 128).T)
    common["w_glu"] = f(f(inputs["ssm_w_glu"][0]).reshape(4, 128, 2048).transpose(1, 0, 2))
    common["w_out"] = f(f(inputs["w_out"][0]).reshape(KD, 128, D).transpose(1, 0, 2))
    common["g2rep"] = f(np.broadcast_to(f(inputs["ffn_norm_g"][0])[None, :], (128, D)))
    common["w_query"] = f(f(inputs["peer_w_query"][0]).reshape(KD, 128, 2048).transpose(1, 0, 2))
    common["skT"] = f(f(inputs["peer_sub_keys"][0]).reshape(16, 128, 128).transpose(2, 0, 1))
    common["peer_uT"] = f(f(inputs["peer_u"][0]).T.reshape(KD, 128, NE).transpose(1, 0, 2))
    common["peer_v"] = f(f(inputs["peer_v"][0]).reshape(128, 128, D).transpose(1, 0, 2))
    maps = []
    for c in range(n_cores):
        m = dict(common)
        m["x"] = f(inputs["x"][c, :T])
        maps.append(m)
    return maps


def kernel(**inputs):
    T = 4096
    n = 8
    nc = build_program(T)
    in_maps = make_in_maps(inputs, T, n)
    res = run_bass_kernel_spmd(nc, in_maps, core_ids=list(range(n)))
    return np.stack([np.asarray(r["out"]) for r in res.results], axis=0).astype(np.float32)
```

```python
import math
from contextlib import ExitStack
import numpy as np
import concourse.bass as bass
import concourse.mybir as mybir
from concourse.bass_utils import run_bass_kernel_spmd

F32 = mybir.dt.float32; BF16 = mybir.dt.bfloat16; I32 = mybir.dt.int32; U32 = mybir.dt.uint32
ALU = mybir.AluOpType; AF = mybir.ActivationFunctionType; AX = mybir.AxisListType

D = 1024
KD = 8
INC = 5640
EPS = 1e-6
C_U, C_Q, C_K, C_V, C_F, C_GS, C_GA = 0, 512, 1536, 2560, 3584, 3592, 4616
NEG = -30000.0
TWO_PI = 2.0 * math.pi
NE = 16384


class Res:
    __slots__ = ("name", "w", "r")

    def __init__(self, name):
        self.name = name
        self.w = None
        self.r = {}


class Sched:
    ENGS = ("pe", "act", "dve", "pool", "sp")

    def __init__(self, nc, es, same_engine_sync=("act", "dve", "pool")):
        self.nc = nc
        self.es = es
        self.eng = {"pe": nc.tensor, "act": nc.scalar, "dve": nc.vector, "pool": nc.gpsimd, "sp": nc.sync}
        self.sem = {}
        self.cnt = {}
        for e in self.ENGS:
            self.sem[e] = es.enter_context(nc.semaphore("sem_" + e))
            self.cnt[e] = 0
        self.seen = {e: {} for e in self.ENGS}
        self.same = set(same_engine_sync)
        self.nwaits = 0
        self.nops = 0

    def res(self, name):
        return Res(name)

    def _chan(self, chan):
        if chan not in self.sem:
            self.sem[chan] = self.es.enter_context(self.nc.semaphore("semd_" + chan))
            self.cnt[chan] = 0
        return self.sem[chan]

    def _wait(self, e, tok):
        if tok is None:
            return
        key, val = tok
        if key == e and e not in self.same:
            return
        if self.seen[e].get(key, 0) >= val:
            return
        self.eng[e].wait_ge(self.sem[key], val)
        self.seen[e][key] = val
        self.nwaits += 1

    def _deps(self, e, reads, writes):
        for R in reads:
            self._wait(e, R.w)
        for W in writes:
            self._wait(e, W.w)
            for key, val in W.r.items():
                self._wait(e, (key, val))

    def _commit(self, tok, reads, writes):
        for R in reads:
            if R.r.get(tok[0], 0) < tok[1]:
                R.r[tok[0]] = tok[1]
        for W in writes:
            W.w = tok
            W.r = {}

    def op(self, e, fn, reads=(), writes=()):
        self._deps(e, reads, writes)
        ins = fn(self.eng[e])
        self.cnt[e] += 1
        ins.then_inc(self.sem[e], 1)
        tok = (e, self.cnt[e])
        self._commit(tok, reads, writes)
        self.nops += 1
        return tok

    def dma(self, e, chan, out, in_, reads=(), writes=(), **kw):
        sem = self._chan(chan)
        if self.cnt[chan] > 0:
            self._wait(e, (chan, self.cnt[chan]))
        self._deps(e, reads, writes)
        ins = self.eng[e].dma_start(out=out, in_=in_, **kw)
        self.cnt[chan] += 16
        ins.then_inc(sem, 16)
        tok = (chan, self.cnt[chan])
        self._commit(tok, reads, writes)
        return tok

    def barrier(self):
        for e in self.ENGS:
            for key in list(self.sem.keys()):
                if key != e and self.cnt[key] > 0:
                    self._wait(e, (key, self.cnt[key]))
                elif key == e and e in self.same and self.cnt[key] > 0:
                    self._wait(e, (key, self.cnt[key]))

    def wait_all_dma(self, e):
        for key in list(self.sem.keys()):
            if key not in self.ENGS and self.cnt[key] > 0:
                self._wait(e, (key, self.cnt[key]))


class Buf:
    def __init__(self, t, r):
        self.t = t
        self.r = r

    def __getitem__(self, k):
        return self.t[k]


def build_program(T, dbg=False, stop_after=None):
    NT = T // 128
    NCH = T // 512
    nc = bass.Bass("TRN2", target_bir_lowering=False)

    def din(name, shape, dt=F32):
        return nc.dram_tensor(name, shape, dt, kind="ExternalInput").ap()

    x_d = din("x", [T, D])
    w_in_d = din("w_in", [128, KD, INC])
    g1_d = din("g1", [128, KD])
    fb_d = din("fb", [128, 8])
    qg_d = din("qg", [128, 1])
    kg_d = din("kg", [128, 1])
    are_d = din("a_re_l", [128, 16]); aim_d = din("a_im_l", [128, 16]); ls_d = din("ls_l", [128, 16])
    bre_d = din("b_re_l", [128, 16, 16]); bim_d = din("b_im_l", [128, 16, 16])
    cre_d = din("c_re_l", [128, 16, 16]); cim_d = din("c_im_l", [128, 16, 16])
    dsk_d = din("d_l", [128, 4])
    wglu_d = din("w_glu", [128, 4, 2048])
    wout_d = din("w_out", [128, KD, D])
    g2_d = din("g2rep", [128, D])
    wqr_d = din("w_query", [128, KD, 2048])
    skT_d = din("skT", [128, 16, 128])
    UT_d = din("peer_uT", [128, KD, NE])
    V_d = din("peer_v", [128, 128, D])
    out_d = nc.dram_tensor("out", [T, D], F32, kind="ExternalOutput").ap()
    UTb_d = nc.dram_tensor("UTb", [128, KD, NE], BF16, kind="Internal").ap()
    Vb_d = nc.dram_tensor("Vb", [128, 128, D], BF16, kind="Internal").ap()
    rt_d = nc.dram_tensor("rt", [128, 3, T], F32, kind=("ExternalOutput" if dbg else "Internal")).ap()
    kind_scr = "ExternalOutput" if dbg else "Internal"
    mT_d = nc.dram_tensor("mT", [D, T], BF16, kind=kind_scr).ap()
    hT_d = nc.dram_tensor("hTd", [128, KD, T], BF16, kind="Internal").ap()
    h2T_d = nc.dram_tensor("h2T", [128, KD, T], BF16, kind=kind_scr).ap()
    yg_d = nc.dram_tensor("ygd", [128, 4, T], BF16, kind="Internal").ap()
    m2T_d = nc.dram_tensor("m2T", [128, KD, T], BF16, kind=kind_scr).ap() if dbg else None

    with ExitStack() as es:
        S = Sched(nc, es)

        def sbuf(stack, name, shape, dt):
            t = stack.enter_context(nc.sbuf_tensor("sb_" + name, shape, dt))
            return Buf(t, S.res(name))

        banks = []
        for i in range(8):
            t = es.enter_context(nc.psum_tensor("bank%d" % i, [128, 512], F32))
            banks.append(Buf(t, S.res("bank%d" % i)))

        identf = sbuf(es, "identf", [128, 128], F32)
        identb = sbuf(es, "identb", [128, 128], BF16)
        onesb = sbuf(es, "onesb", [128, 128], BF16)
        onesf = sbuf(es, "onesf", [128, 128], F32)
        trif = sbuf(es, "trif", [128, 128], F32)
        maskneg = sbuf(es, "maskneg", [128, 128], BF16)
        S.op("pool", lambda e: e.memset(identf[:], 1.0), writes=[identf.r])
        S.op("pool", lambda e: e.affine_select(out=identf[:], in_=identf[:], pattern=[[-1, 128]], compare_op=ALU.is_equal,
                                               fill=0.0, base=0, channel_multiplier=1), reads=[identf.r], writes=[identf.r])
        S.op("pool", lambda e: e.tensor_copy(out=identb[:], in_=identf[:]), reads=[identf.r], writes=[identb.r])
        S.op("pool", lambda e: e.memset(onesb[:], 1.0), writes=[onesb.r])
        S.op("pool", lambda e: e.memset(onesf[:], 1.0), writes=[onesf.r])
        S.op("pool", lambda e: e.memset(trif[:], 1.0), writes=[trif.r])
        S.op("pool", lambda e: e.affine_select(out=trif[:], in_=trif[:], pattern=[[1, 128]], compare_op=ALU.is_ge,
                                               fill=0.0, base=0, channel_multiplier=-1), reads=[trif.r], writes=[trif.r])
        S.op("pool", lambda e: e.tensor_scalar(out=maskneg[:], in0=trif[:], scalar1=-1.0, scalar2=-NEG, op0=ALU.add, op1=ALU.mult),
             reads=[trif.r], writes=[maskneg.r])

        hpi = sbuf(es, "hpi", [128, 1], F32)
        S.op("pool", lambda e: e.memset(hpi[:], math.pi / 2.0), writes=[hpi.r])
        g1 = sbuf(es, "g1", [128, KD], F32)
        fb = sbuf(es, "fb", [128, 8], F32)
        qg = sbuf(es, "qg", [128, 1], F32)
        kg = sbuf(es, "kg", [128, 1], F32)
        S.dma("sp", "c_g1", g1[:], g1_d, writes=[g1.r])
        S.dma("sp", "c_fb", fb[:], fb_d, writes=[fb.r])
        S.dma("sp", "c_qg", qg[:], qg_d, writes=[qg.r])
        S.dma("sp", "c_kg", kg[:], kg_d, writes=[kg.r])
        S.op("dve", lambda e: e.tensor_scalar(out=qg[:], in0=qg[:], scalar1=128.0 ** -0.5, scalar2=None, op0=ALU.mult),
             reads=[qg.r], writes=[qg.r])

        cast_toks = []
        if stop_after is None:
            for q in range(16):
                cast_toks.append(S.dma("pool", "ucast%d" % q, UTb_d[:, :, q * 1024:(q + 1) * 1024], UT_d[:, :, q * 1024:(q + 1) * 1024]))
                cast_toks.append(S.dma("pool", "vcast%d" % q, Vb_d[:, q * 8:(q + 1) * 8, :], V_d[:, q * 8:(q + 1) * 8, :]))

        stg = [None, None]
        stg_i = [0]

        def alloc_stg(stack):
            for i in range(2):
                stg[i] = sbuf(stack, "wstg%d_%d" % (i, stg_i[0]), [128, KD, 512], F32)

        def load_w(dst, src_d, c0, ncols, gain, nk=KD):
            s = stg[stg_i[0] % 2]
            stg_i[0] += 1
            S.dma("sp", "wld%d" % (stg_i[0] % 2), s[:, 0:nk, 0:ncols], src_d[:, :, c0:c0 + ncols], writes=[s.r])
            if gain is not None:
                S.op("pool", lambda e: e.tensor_tensor(out=dst[:], in0=s[:, 0:nk, 0:ncols],
                                                       in1=gain[:, 0:nk].unsqueeze(2).to_broadcast([128, nk, ncols]), op=ALU.mult),
                     reads=[s.r, gain.r], writes=[dst.r])
            else:
                S.op("pool", lambda e: e.tensor_copy(out=dst[:], in_=s[:, 0:nk, 0:ncols]), reads=[s.r], writes=[dst.r])

        scopeAB = ExitStack()
        alloc_stg(scopeAB)
        hT = sbuf(scopeAB, "hT", [128, KD, T], BF16)
        hT_r = [S.res("hT_%d" % i) for i in range(NT)]
        cum = sbuf(scopeAB, "cum", [128, NT, 8], F32)
        cend = sbuf(scopeAB, "cend", [128, NT + 1, 8], F32)
        with ExitStack() as pa:
            xts = [sbuf(pa, "xt%d" % i, [128, D], F32) for i in range(2)]
            xs = [sbuf(pa, "xs%d" % i, [128, D], BF16) for i in range(2)]
            junk = sbuf(pa, "junkA", [128, D], BF16)
            ss = sbuf(pa, "ss", [128, NT], F32)
            rs = sbuf(pa, "rs", [128, NT], F32)
            wf = sbuf(pa, "wf", [128, KD, 8], BF16)
            zf = sbuf(pa, "zf", [128, NT, 8], F32)
            spf = sbuf(pa, "spf", [128, NT, 8], F32)
            load_w(wf, w_in_d, C_F, 8, g1)
            fbank = banks[7]
            for i in range(NT):
                xt = xts[i % 2]; xb = xs[i % 2]
                S.dma("sp", "xld%d" % (i % 2), xt[:], x_d[i * 128:(i + 1) * 128, :], writes=[xt.r])
                S.op("act", lambda e: e.activation(out=junk[:], in_=xt[:], func=AF.Square, accum_out=ss[:, i:i + 1]),
                     reads=[xt.r], writes=[junk.r, ss.r])
                S.op("act", lambda e: e.activation(out=rs[:, i:i + 1], in_=ss[:, i:i + 1], func=AF.Sqrt, bias=EPS, scale=1.0 / D),
                     reads=[ss.r], writes=[rs.r])
                S.op("dve", lambda e: e.reciprocal(out=rs[:, i:i + 1], in_=rs[:, i:i + 1]), reads=[rs.r], writes=[rs.r])
                S.op("dve", lambda e: e.tensor_scalar(out=xb[:], in0=xt[:], scalar1=rs[:, i:i + 1], scalar2=None, op0=ALU.mult),
                     reads=[xt.r, rs.r], writes=[xb.r])
                bk = banks[i % 2]
                bv = bk[:].bitcast(BF16)
                for k in range(KD):
                    S.op("pe", lambda e: e.transpose(out=bv[:, k * 128:(k + 1) * 128], in_=xb[:, k * 128:(k + 1) * 128], identity=identb[:]),
                         reads=[xb.r, identb.r], writes=[bk.r])
                S.op("act", lambda e: e.copy(out=hT[:, :, i * 128:(i + 1) * 128], in_=bv[:, 0:1024].rearrange("p (k t) -> p k t", k=KD)),
                     reads=[bk.r], writes=[hT_r[i]])
                for k in range(KD):
                    S.op("pe", lambda e: e.matmul(fbank[:, i * 8:(i + 1) * 8], lhsT=hT[:, k, i * 128:(i + 1) * 128], rhs=wf[:, k, :],
                                                  start=(k == 0), stop=(k == KD - 1)),
                         reads=[hT_r[i], wf.r], writes=[fbank.r])
            S.op("dve", lambda e: e.tensor_tensor(out=zf[:], in0=fbank[:, 0:NT * 8].rearrange("p (i h) -> p i h", h=8),
                                                  in1=fb[:].unsqueeze(1).to_broadcast([128, NT, 8]), op=ALU.add),
                 reads=[fbank.r, fb.r], writes=[zf.r])
            S.op("act", lambda e: e.activation(out=spf[:], in_=zf[:], func=AF.Exp, scale=-1.0), reads=[zf.r], writes=[spf.r])
            S.op("act", lambda e: e.activation(out=spf[:], in_=spf[:], func=AF.Ln, bias=1.0, scale=1.0), reads=[spf.r], writes=[spf.r])
            cb1 = banks[5]; cb2 = banks[6]
            for i in range(NT):
                S.op("pe", lambda e: e.matmul(cb1[:, i * 8:(i + 1) * 8], lhsT=trif[:], rhs=spf[:, i, :], start=True, stop=True),
                     reads=[trif.r, spf.r], writes=[cb1.r])
                S.op("pe", lambda e: e.matmul(cb2[:, i * 8:(i + 1) * 8], lhsT=onesf[:], rhs=spf[:, i, :], start=True, stop=True),
                     reads=[onesf.r, spf.r], writes=[cb2.r])
            S.op("dve", lambda e: e.memset(cend[:, 0, :], 0.0), writes=[cend.r])
            for i in range(NT):
                S.op("dve", lambda e: e.tensor_tensor(out=cend[:, i + 1, :], in0=cend[:, i, :], in1=cb2[:, i * 8:(i + 1) * 8], op=ALU.add),
                     reads=[cend.r, cb2.r], writes=[cend.r])
            S.op("dve", lambda e: e.tensor_tensor(out=cum[:], in0=cb1[:, 0:NT * 8].rearrange("p (i h) -> p i h", h=8),
                                                  in1=cend[:, 0:NT, :], op=ALU.add),
                 reads=[cb1.r, cend.r], writes=[cum.r])

        for c in range(NCH):
            S.dma("sp", "hsp%d" % (c % 2), hT_d[:, :, c * 512:(c + 1) * 512], hT[:, :, c * 512:(c + 1) * 512], reads=hT_r[c * 4:(c + 1) * 4])
        S.barrier()
        if stop_after == "A":
            return nc
        with ExitStack() as pb:
            wq = sbuf(pb, "wq", [128, KD, 128], BF16)
            wk = sbuf(pb, "wk", [128, KD, 128], BF16)
            wv = sbuf(pb, "wv", [128, KD, 128], BF16)
            wg = sbuf(pb, "wg", [128, KD, 128], BF16)
            qT = sbuf(pb, "qT", [128, T], BF16)
            kT = sbuf(pb, "kT", [128, T], BF16)
            vv = sbuf(pb, "vv", [128, NT, 128], BF16)
            sgT = sbuf(pb, "sgT", [128, T], BF16)
            crow = sbuf(pb, "crow", [1, T], BF16)
            sq = [sbuf(pb, "sq%d" % i, [128, 512], BF16) for i in range(2)]
            rrep = [sbuf(pb, "rrep%d" % i, [128, 512], F32) for i in range(2)]
            Pt = [sbuf(pb, "Pt%d" % i, [128, 512], BF16) for i in range(3)]
            rec = sbuf(pb, "rec", [128, 512], F32)
            yv = sbuf(pb, "yv", [128, 512], F32)
            ym = [sbuf(pb, "ym%d" % i, [128, 512], BF16) for i in range(2)]
            hT_all = hT_r
            pcount = [0]
            print("SBUF remaining in phase B:", nc.sbuf_bytes_remaining)
            for h in range(8):
                load_w(wq, w_in_d, C_Q + h * 128, 128, g1)
                load_w(wk, w_in_d, C_K + h * 128, 128, g1)
                load_w(wv, w_in_d, C_V + h * 128, 128, g1)
                load_w(wg, w_in_d, C_GA + h * 128, 128, g1)
                S.op("dve", lambda e: e.tensor_scalar(out=crow[0:1, :].rearrange("p (i r) -> p i r", r=128),
                                                      in0=cend[0:1, 1:NT + 1, h:h + 1].to_broadcast([1, NT, 128]), scalar1=-1.0, scalar2=None, op0=ALU.mult),
                     reads=[cend.r], writes=[crow.r])
                for which, (wt, dstT, gvec) in enumerate(((wq, qT, qg), (wk, kT, kg))):
                    for c in range(NCH):
                        pj = banks[(2 * c) % 4]; pn = banks[(2 * c + 1) % 4]
                        for k in range(KD):
                            S.op("pe", lambda e: e.matmul(pj[:], lhsT=wt[:, k, :], rhs=hT[:, k, c * 512:(c + 1) * 512], start=(k == 0), stop=(k == KD - 1)),
                                 reads=[wt.r] + hT_all[c * 4:(c + 1) * 4], writes=[pj.r])
                        sqb = sq[c % 2]; rr = rrep[c % 2]
                        S.op("act", lambda e: e.activation(out=sqb[:], in_=pj[:], func=AF.Square), reads=[pj.r], writes=[sqb.r])
                        S.op("pe", lambda e: e.matmul(pn[:], lhsT=onesb[:], rhs=sqb[:], start=True, stop=True), reads=[onesb.r, sqb.r], writes=[pn.r])
                        S.op("act", lambda e: e.activation(out=rr[:], in_=pn[:], func=AF.Sqrt, bias=EPS, scale=1.0 / 128.0), reads=[pn.r], writes=[rr.r])
                        S.op("dve", lambda e: e.reciprocal(out=rr[:], in_=rr[:]), reads=[rr.r], writes=[rr.r])
                        S.op("dve", lambda e: e.scalar_tensor_tensor(out=dstT[:, c * 512:(c + 1) * 512], in0=pj[:], scalar=gvec[:, 0:1], in1=rr[:],
                                                                     op0=ALU.mult, op1=ALU.mult),
                             reads=[pj.r, gvec.r, rr.r], writes=[dstT.r])
                for i4 in range(NT // 4):
                    pv = banks[i4 % 2]
                    for ii in range(4):
                        i = i4 * 4 + ii
                        for k in range(KD):
                            S.op("pe", lambda e: e.matmul(pv[:, ii * 128:(ii + 1) * 128], lhsT=hT[:, k, i * 128:(i + 1) * 128], rhs=wv[:, k, :],
                                                          start=(k == 0), stop=(k == KD - 1)),
                                 reads=[wv.r, hT_all[i]], writes=[pv.r])
                    S.op("act", lambda e: e.copy(out=vv[:, i4 * 4:(i4 + 1) * 4, :], in_=pv[:].rearrange("p (a b) -> p a b", a=4)),
                         reads=[pv.r], writes=[vv.r])
                for c in range(NCH):
                    pg = banks[2 + (c % 2)]
                    for k in range(KD):
                        S.op("pe", lambda e: e.matmul(pg[:], lhsT=wg[:, k, :], rhs=hT[:, k, c * 512:(c + 1) * 512], start=(k == 0), stop=(k == KD - 1)),
                             reads=[wg.r] + hT_all[c * 4:(c + 1) * 4], writes=[pg.r])
                    S.op("act", lambda e: e.activation(out=sgT[:, c * 512:(c + 1) * 512], in_=pg[:], func=AF.Sigmoid), reads=[pg.r], writes=[sgT.r])
                for qc in range(NCH):
                    pO = banks[4 + (qc % 2)]; pL = banks[6 + (qc % 2)]
                    nkb = (qc + 1) * 4
                    def emit_S(kb):
                        q_lo = max(qc * 4, kb)
                        off = (q_lo - qc * 4) * 128
                        pS = banks[pcount[0] % 4]
                        Pb = Pt[pcount[0] % 3]
                        pcount[0] += 1
                        diag = kb >= qc * 4
                        S.op("pe", lambda e: e.matmul(pS[:, off:512], lhsT=kT[:, kb * 128:(kb + 1) * 128], rhs=qT[:, q_lo * 128:(qc + 1) * 512],
                                                      start=True, stop=False),
                             reads=[kT.r, qT.r], writes=[pS.r])
                        S.op("pe", lambda e: e.matmul(pS[:, off:512], lhsT=onesb[0:1, :], rhs=crow[0:1, q_lo * 128:(qc + 1) * 512], start=False, stop=not diag),
                             reads=[onesb.r, crow.r], writes=[pS.r])
                        if diag:
                            S.op("pe", lambda e: e.matmul(pS[:, off:off + 128], lhsT=identb[:], rhs=maskneg[:], start=False, stop=True),
                                 reads=[identb.r, maskneg.r], writes=[pS.r])
                        S.op("act", lambda e: e.activation(out=Pb[:, off:512], in_=pS[:, off:512], func=AF.Exp, bias=cum[:, kb, h:h + 1], scale=1.0),
                             reads=[pS.r, cum.r], writes=[Pb.r])
                        return (Pb, off)

                    def emit_PV(kb, Pb, off):
                        S.op("pe", lambda e: e.matmul(pO[:, off:512], lhsT=vv[:, kb, :], rhs=Pb[:, off:512], start=(kb == 0), stop=(kb == nkb - 1)),
                             reads=[vv.r, Pb.r], writes=[pO.r])
                        S.op("pe", lambda e: e.matmul(pL[:, off:512], lhsT=onesb[:], rhs=Pb[:, off:512], start=(kb == 0), stop=(kb == nkb - 1)),
                             reads=[onesb.r, Pb.r], writes=[pL.r])

                    prev = None
                    for kb in range(nkb):
                        cur = emit_S(kb)
                        if prev is not None:
                            emit_PV(kb - 1, *prev)
                        prev = cur
                    emit_PV(nkb - 1, *prev)
                    S.op("dve", lambda e: e.reciprocal(out=rec[:], in_=pL[:]), reads=[pL.r], writes=[rec.r])
                    S.op("dve", lambda e: e.tensor_tensor(out=yv[:], in0=pO[:], in1=rec[:], op=ALU.mult), reads=[pO.r, rec.r], writes=[yv.r])
                    ymb = ym[qc % 2]
                    S.op("dve", lambda e: e.tensor_tensor(out=ymb[:], in0=yv[:], in1=sgT[:, qc * 512:(qc + 1) * 512], op=ALU.mult),
                         reads=[yv.r, sgT.r], writes=[ymb.r])
                    S.dma("sp", "mst%d" % (qc % 2), mT_d[h * 128:(h + 1) * 128, qc * 512:(qc + 1) * 512], ymb[:], reads=[ymb.r])


        S.barrier()
        scopeAB.close()
        if stop_after == "B":
            S.wait_all_dma("sp")
            return nc

        LT = 512
        with ExitStack() as pc:
            def small(name, shape=(128, 16)):
                return sbuf(pc, name, list(shape), F32)
            a_re = small("a_re"); a_im = small("a_im"); lsl = small("lsl")
            S.dma("sp", "c_are", a_re[:], are_d, writes=[a_re.r])
            S.dma("sp", "c_aim", a_im[:], aim_d, writes=[a_im.r])
            S.dma("sp", "c_ls", lsl[:], ls_d, writes=[lsl.r])
            dsk = sbuf(pc, "dsk", [128, 4], F32)
            S.dma("sp", "c_dsk", dsk[:], dsk_d, writes=[dsk.r])
            Cpad = sbuf(pc, "Cpad", [128, 16, 2, 128], F32)
            Bpad = sbuf(pc, "Bpad", [128, 16, 2, 128], F32)
            Tc = sbuf(pc, "Tc", [128, 16, LT], F32); Ts = sbuf(pc, "Ts", [128, 16, LT], F32)
            wu = sbuf(pc, "wu", [128, KD, 512], BF16)
            step = small("step"); th = small("th"); mag = small("mag"); cs = small("cs"); sn = small("sn")
            ki = sbuf(pc, "ki", [128, 16], I32); kf = small("kf"); rr_ = small("rr_"); ab = small("ab")
            lre = small("lre"); lim = small("lim"); den = small("den"); t1 = small("t1"); t2 = small("t2")
            cfr = small("cfr"); cfi = small("cfi")
            wr_ = small("wr_"); wi_ = small("wi_"); wt1 = small("wt1"); wt2 = small("wt2")
            pset = ExitStack()
            alloc_stg(pset)
            bre = sbuf(pset, "bre", [128, 16, 16], F32); bim = sbuf(pset, "bim", [128, 16, 16], F32)
            cre = sbuf(pset, "cre", [128, 16, 16], F32); cim = sbuf(pset, "cim", [128, 16, 16], F32)
            S.dma("sp", "c_bre", bre[:], bre_d, writes=[bre.r]); S.dma("sp", "c_bim", bim[:], bim_d, writes=[bim.r])
            S.dma("sp", "c_cre", cre[:], cre_d, writes=[cre.r]); S.dma("sp", "c_cim", cim[:], cim_d, writes=[cim.r])
            bbr = sbuf(pset, "bbr", [128, 16, 16], F32); bbi = sbuf(pset, "bbi", [128, 16, 16], F32)
            u1 = sbuf(pset, "u1", [128, 16, 16], F32); u2 = sbuf(pset, "u2", [128, 16, 16], F32)
            BZ = sbuf(pset, "BZ", [128, 16, 2, 128], F32)
            p1 = sbuf(pset, "p1", [128, 16, LT // 2], F32); p2 = sbuf(pset, "p2", [128, 16, LT // 2], F32)

            def tt(eng, out, a, b, op, rd, wr):
                S.op(eng, lambda e: e.tensor_tensor(out=out, in0=a, in1=b, op=op), reads=rd, writes=wr)

            S.op("act", lambda e: e.activation(out=step[:], in_=lsl[:], func=AF.Exp), reads=[lsl.r], writes=[step.r])
            tt("dve", th[:], a_im[:], step[:], ALU.mult, [a_im.r, step.r], [th.r])
            tt("dve", mag[:], a_re[:], step[:], ALU.mult, [a_re.r, step.r], [mag.r])
            S.op("act", lambda e: e.activation(out=mag[:], in_=mag[:], func=AF.Exp), reads=[mag.r], writes=[mag.r])
            S.op("dve", lambda e: e.tensor_scalar(out=ki[:], in0=th[:], scalar1=1.0 / TWO_PI, scalar2=None, op0=ALU.mult), reads=[th.r], writes=[ki.r])
            S.op("dve", lambda e: e.tensor_copy(out=kf[:], in_=ki[:]), reads=[ki.r], writes=[kf.r])
            S.op("dve", lambda e: e.scalar_tensor_tensor(out=rr_[:], in0=kf[:], scalar=-TWO_PI, in1=th[:], op0=ALU.mult, op1=ALU.add),
                 reads=[kf.r, th.r], writes=[rr_.r])
            PI_LO = 3.1415925
            S.op("dve", lambda e: e.tensor_scalar(out=rr_[:], in0=rr_[:], scalar1=PI_LO, scalar2=-PI_LO, op0=ALU.min, op1=ALU.max), reads=[rr_.r], writes=[rr_.r])
            S.op("act", lambda e: e.activation(out=sn[:], in_=rr_[:], func=AF.Sin), reads=[rr_.r], writes=[sn.r])
            S.op("act", lambda e: e.activation(out=ab[:], in_=rr_[:], func=AF.Abs), reads=[rr_.r], writes=[ab.r])
            S.op("act", lambda e: e.activation(out=cs[:], in_=ab[:], func=AF.Sin, scale=-1.0, bias=hpi[:, 0:1]), reads=[ab.r, hpi.r], writes=[cs.r])
            tt("dve", lre[:], mag[:], cs[:], ALU.mult, [mag.r, cs.r], [lre.r])
            tt("dve", lim[:], mag[:], sn[:], ALU.mult, [mag.r, sn.r], [lim.r])
            S.op("dve", lambda e: e.tensor_scalar(out=lre[:], in0=lre[:], scalar1=-1.0, scalar2=None, op0=ALU.add), reads=[lre.r], writes=[lre.r])
            tt("dve", t1[:], a_re[:], a_re[:], ALU.mult, [a_re.r], [t1.r])
            tt("dve", t2[:], a_im[:], a_im[:], ALU.mult, [a_im.r], [t2.r])
            tt("dve", den[:], t1[:], t2[:], ALU.add, [t1.r, t2.r], [den.r])
            S.op("dve", lambda e: e.reciprocal(out=den[:], in_=den[:]), reads=[den.r], writes=[den.r])
            tt("dve", t1[:], lre[:], a_re[:], ALU.mult, [lre.r, a_re.r], [t1.r])
            tt("dve", t2[:], lim[:], a_im[:], ALU.mult, [lim.r, a_im.r], [t2.r])
            tt("dve", cfr[:], t1[:], t2[:], ALU.add, [t1.r, t2.r], [cfr.r])
            tt("dve", cfr[:], cfr[:], den[:], ALU.mult, [cfr.r, den.r], [cfr.r])
            tt("dve", t1[:], lim[:], a_re[:], ALU.mult, [lim.r, a_re.r], [t1.r])
            tt("dve", t2[:], lre[:], a_im[:], ALU.mult, [lre.r, a_im.r], [t2.r])
            tt("dve", cfi[:], t1[:], t2[:], ALU.subtract, [t1.r, t2.r], [cfi.r])
            tt("dve", cfi[:], cfi[:], den[:], ALU.mult, [cfi.r, den.r], [cfi.r])
            bc3 = lambda v: v[:].unsqueeze(2).to_broadcast([128, 16, 16])
            tt("dve", u1[:], bre[:], bc3(cfr), ALU.mult, [bre.r, cfr.r], [u1.r])
            tt("dve", u2[:], bim[:], bc3(cfi), ALU.mult, [bim.r, cfi.r], [u2.r])
            tt("dve", bbr[:], u1[:], u2[:], ALU.subtract, [u1.r, u2.r], [bbr.r])
            tt("dve", u1[:], bim[:], bc3(cfr), ALU.mult, [bim.r, cfr.r], [u1.r])
            tt("dve", u2[:], bre[:], bc3(cfi), ALU.mult, [bre.r, cfi.r], [u2.r])
            tt("dve", bbi[:], u1[:], u2[:], ALU.add, [u1.r, u2.r], [bbi.r])
            S.op("dve", lambda e: e.tensor_scalar(out=cim[:], in0=cim[:], scalar1=-1.0, scalar2=None, op0=ALU.mult), reads=[cim.r], writes=[cim.r])
            if True:
                S.op("pool", lambda e: e.memset(Cpad[:], 0.0), writes=[Cpad.r])
                S.op("pool", lambda e: e.memset(BZ[:], 0.0), writes=[BZ.r])
                for j in range(16):
                    for two in range(2):
                        c0 = 32 * (j % 4) + 16 * two
                        ps_ = slice(two * 64, (two + 1) * 64)
                        for ri, (srcC, srcB) in enumerate(((cre, bbr), (cim, bbi))):
                            S.op("pool", lambda e: e.tensor_copy(out=Cpad[ps_, j, ri, c0:c0 + 16], in_=srcC[ps_, j, :]), reads=[srcC.r], writes=[Cpad.r])
                            S.op("pool", lambda e: e.tensor_copy(out=BZ[ps_, j, ri, c0:c0 + 16], in_=srcB[ps_, j, :]), reads=[srcB.r], writes=[BZ.r])
                for j in range(16):
                    for ri in range(2):
                        bk = banks[(2 * j + ri) % 4]
                        S.op("pe", lambda e: e.transpose(out=bk[:, 0:128], in_=BZ[:, j, ri, :], identity=identf[:]), reads=[BZ.r, identf.r], writes=[bk.r])
                        S.op("act", lambda e: e.copy(out=Bpad[:, j, ri, :], in_=bk[:, 0:128]), reads=[bk.r], writes=[Bpad.r])
            S.op("dve", lambda e: e.tensor_copy(out=wr_[:], in_=cs[:]), reads=[cs.r], writes=[wr_.r])
            S.op("dve", lambda e: e.tensor_copy(out=wi_[:], in_=sn[:]), reads=[sn.r], writes=[wi_.r])
            S.op("pool", lambda e: e.memset(Tc[:, :, 0:1], 1.0), writes=[Tc.r])
            S.op("pool", lambda e: e.memset(Ts[:, :, 0:1], 0.0), writes=[Ts.r])
            if True:
                n = 1
                while n < LT:
                    bw = lambda v: v[:].unsqueeze(2).to_broadcast([128, 16, n])
                    tt("dve", p1[:, :, 0:n], Tc[:, :, 0:n], bw(wr_), ALU.mult, [Tc.r, wr_.r], [p1.r])
                    tt("dve", p2[:, :, 0:n], Ts[:, :, 0:n], bw(wi_), ALU.mult, [Ts.r, wi_.r], [p2.r])
                    tt("dve", Tc[:, :, n:2 * n], p1[:, :, 0:n], p2[:, :, 0:n], ALU.subtract, [p1.r, p2.r], [Tc.r])
                    tt("dve", p1[:, :, 0:n], Tc[:, :, 0:n], bw(wi_), ALU.mult, [Tc.r, wi_.r], [p1.r])
                    tt("dve", p2[:, :, 0:n], Ts[:, :, 0:n], bw(wr_), ALU.mult, [Ts.r, wr_.r], [p2.r])
                    tt("dve", Ts[:, :, n:2 * n], p1[:, :, 0:n], p2[:, :, 0:n], ALU.add, [p1.r, p2.r], [Ts.r])
                    tt("dve", wt1[:], wr_[:], wr_[:], ALU.mult, [wr_.r], [wt1.r])
                    tt("dve", wt2[:], wi_[:], wi_[:], ALU.mult, [wi_.r], [wt2.r])
                    tt("dve", wi_[:], wr_[:], wi_[:], ALU.mult, [wr_.r, wi_.r], [wi_.r])
                    S.op("dve", lambda e: e.tensor_scalar(out=wi_[:], in0=wi_[:], scalar1=2.0, scalar2=None, op0=ALU.mult), reads=[wi_.r], writes=[wi_.r])
                    tt("dve", wr_[:], wt1[:], wt2[:], ALU.subtract, [wt1.r, wt2.r], [wr_.r])
                    n *= 2
            wv_ = lambda buf, a, b: Buf(buf.t[:, :, a:b], buf.r)
            load_w(wu, w_in_d, C_U, 512, g1)
            S.barrier()
            pset.close()
            hTc = [sbuf(pc, "hTc%d" % i, [128, KD, 512], BF16) for i in range(1)]
            u_sb = sbuf(pc, "u_sb", [128, 4, 512], F32)
            ssets = []
            set_banks = ((2, 3), (5, 6), (7, 1))
            for si in range(3):
                B_ = {}
                for nm in ("xr_sb", "xi_sb", "ta", "tb", "tcb", "td", "r_r", "r_i"):
                    B_[nm] = sbuf(pc, "%s_%d" % (nm, si), [128, 512], F32)
                B_["ctmp"] = sbuf(pc, "ctmp_%d" % si, [128, 2], F32)
                B_["pxr"] = banks[set_banks[si][0]]; B_["pxi"] = banks[set_banks[si][1]]
                ssets.append(B_)
            carry = sbuf(pc, "carry", [128, 16, 2], F32)
            carry_r = [S.res("carry_%d" % j) for j in range(16)]
            yv_ = sbuf(pc, "yv_", [128, 512], F32)
            ygs = [sbuf(pc, "yg%d" % i, [128, 4, 512], BF16) for i in range(2)]
            print("SBUF remaining in phase C:", nc.sbuf_bytes_remaining)
            S.op("pool", lambda e: e.memset(carry[:], 0.0), writes=carry_r)
            v2 = lambda ap: ap.rearrange("p (a b) -> p a b", a=512 // LT)
            tb3 = lambda tab, j: tab[:, j, :].unsqueeze(1).to_broadcast([128, 512 // LT, LT])
            mT_v = mT_d.rearrange("(dc p) t -> p dc t", p=128)
            for c in range(NCH):
                hc = hTc[0]; yg = ygs[c % 2]
                S.dma("sp", "hcl", hc[:], hT_d[:, :, c * 512:(c + 1) * 512], writes=[hc.r])
                for rc in range(4):
                    pu = banks[rc % 2]
                    for k in range(KD):
                        S.op("pe", lambda e: e.matmul(pu[:], lhsT=wu[:, k, rc * 128:(rc + 1) * 128], rhs=hc[:, k, :], start=(k == 0), stop=(k == KD - 1)),
                             reads=[wu.r, hc.r], writes=[pu.r])
                    S.op("act", lambda e: e.copy(out=u_sb[:, rc, :], in_=pu[:]), reads=[pu.r], writes=[u_sb.r])
                def ssm_gen(j, B_):
                    rc = j // 4
                    pxr = B_["pxr"]; pxi = B_["pxi"]; py = banks[4]
                    xr_sb = B_["xr_sb"]; xi_sb = B_["xi_sb"]; ta = B_["ta"]; tb = B_["tb"]; tcb = B_["tcb"]; td = B_["td"]
                    r_r = B_["r_r"]; r_i = B_["r_i"]; ctmp = B_["ctmp"]; cr = carry_r[j]
                    xtr = ta; xti = tcb; s_r = ta; s_i = tcb
                    S.op("pe", lambda e: e.matmul(pxr[:], lhsT=Bpad[:, j, 0, :], rhs=u_sb[:, rc, :], start=True, stop=True), reads=[Bpad.r, u_sb.r], writes=[pxr.r])
                    S.op("pe", lambda e: e.matmul(pxi[:], lhsT=Bpad[:, j, 1, :], rhs=u_sb[:, rc, :], start=True, stop=True), reads=[Bpad.r, u_sb.r], writes=[pxi.r])
                    yield
                    S.op("act", lambda e: e.copy(out=xr_sb[:], in_=pxr[:]), reads=[pxr.r], writes=[xr_sb.r])
                    S.op("act", lambda e: e.copy(out=xi_sb[:], in_=pxi[:]), reads=[pxi.r], writes=[xi_sb.r])
                    yield
                    tt("dve", v2(ta[:]), v2(xr_sb[:]), tb3(Tc, j), ALU.mult, [xr_sb.r, Tc.r], [ta.r])
                    tt("pool", v2(tb[:]), v2(xi_sb[:]), tb3(Ts, j), ALU.mult, [xi_sb.r, Ts.r], [tb.r])
                    yield
                    tt("dve", v2(tcb[:]), v2(xi_sb[:]), tb3(Tc, j), ALU.mult, [xi_sb.r, Tc.r], [tcb.r])
                    tt("pool", v2(td[:]), v2(xr_sb[:]), tb3(Ts, j), ALU.mult, [xr_sb.r, Ts.r], [td.r])
                    yield
                    tt("dve", xtr[:], ta[:], tb[:], ALU.add, [ta.r, tb.r], [xtr.r])
                    tt("dve", xti[:], tcb[:], td[:], ALU.subtract, [tcb.r, td.r], [xti.r])
                    yield
                    for sgi in range(512 // LT):
                        sl = slice(sgi * LT, (sgi + 1) * LT)
                        magb = mag[:, j:j + 1].to_broadcast([128, LT])
                        S.op("dve", lambda e: e.tensor_tensor_scan(out=r_r[:, sl], data0=magb, data1=xtr[:, sl], initial=carry[:, j, 0:1], op0=ALU.mult, op1=ALU.add),
                             reads=[mag.r, xtr.r, cr], writes=[r_r.r])
                        S.op("dve", lambda e: e.tensor_tensor_scan(out=r_i[:, sl], data0=magb, data1=xti[:, sl], initial=carry[:, j, 1:2], op0=ALU.mult, op1=ALU.add),
                             reads=[mag.r, xti.r, cr], writes=[r_i.r])
                        yield
                        last = sgi * LT + LT - 1
                        tt("dve", ctmp[:, 0:1], r_i[:, last:last + 1], wi_[:, j:j + 1], ALU.mult, [r_i.r, wi_.r], [ctmp.r])
                        tt("dve", ctmp[:, 1:2], r_i[:, last:last + 1], wr_[:, j:j + 1], ALU.mult, [r_i.r, wr_.r], [ctmp.r])
                        yield
                        S.op("dve", lambda e: e.scalar_tensor_tensor(out=carry[:, j, 0:1], in0=r_r[:, last:last + 1], scalar=wr_[:, j:j + 1], in1=ctmp[:, 0:1],
                                                                     op0=ALU.mult, op1=ALU.subtract), reads=[r_r.r, wr_.r, ctmp.r], writes=[cr])
                        S.op("dve", lambda e: e.scalar_tensor_tensor(out=carry[:, j, 1:2], in0=r_r[:, last:last + 1], scalar=wi_[:, j:j + 1], in1=ctmp[:, 1:2],
                                                                     op0=ALU.mult, op1=ALU.add), reads=[r_r.r, wi_.r, ctmp.r], writes=[cr])
                        yield
                    tt("dve", v2(ta[:]), v2(r_r[:]), tb3(Tc, j), ALU.mult, [r_r.r, Tc.r], [ta.r])
                    tt("pool", v2(tb[:]), v2(r_i[:]), tb3(Ts, j), ALU.mult, [r_i.r, Ts.r], [tb.r])
                    yield
                    tt("pool", v2(tcb[:]), v2(r_r[:]), tb3(Ts, j), ALU.mult, [r_r.r, Ts.r], [tcb.r])
                    tt("dve", v2(td[:]), v2(r_i[:]), tb3(Tc, j), ALU.mult, [r_i.r, Tc.r], [td.r])
                    yield
                    tt("dve", s_r[:], ta[:], tb[:], ALU.subtract, [ta.r, tb.r], [s_r.r])
                    tt("dve", s_i[:], tcb[:], td[:], ALU.add, [tcb.r, td.r], [s_i.r])
                    yield
                    S.op("pe", lambda e: e.matmul(py[:], lhsT=Cpad[:, j, 0, :], rhs=s_r[:], start=(j % 4 == 0), stop=False), reads=[Cpad.r, s_r.r], writes=[py.r])
                    S.op("pe", lambda e: e.matmul(py[:], lhsT=Cpad[:, j, 1, :], rhs=s_i[:], start=False, stop=(j % 4 == 3)), reads=[Cpad.r, s_i.r], writes=[py.r])
                    yield
                    if j % 4 == 3:
                        S.op("dve", lambda e: e.scalar_tensor_tensor(out=yv_[:], in0=u_sb[:, rc, :], scalar=dsk[:, rc:rc + 1], in1=py[:], op0=ALU.mult, op1=ALU.add),
                             reads=[u_sb.r, dsk.r, py.r], writes=[yv_.r])
                        S.op("act", lambda e: e.activation(out=yg[:, rc, :], in_=yv_[:], func=AF.Gelu_apprx_tanh), reads=[yv_.r], writes=[yg.r])

                SKEW = 4
                active = []
                nxt_j = 0
                while nxt_j < 16 or active:
                    if nxt_j < 16 and len(active) < 3 and (not active or active[-1][1] >= SKEW):
                        active.append([ssm_gen(nxt_j, ssets[nxt_j % 3]), 0])
                        nxt_j += 1
                    for ent in list(active):
                        try:
                            next(ent[0])
                            ent[1] += 1
                        except StopIteration:
                            active.remove(ent)
                S.dma("sp", "ygst%d" % (c % 2), yg_d[:, :, c * 512:(c + 1) * 512], yg[:], reads=[yg.r])
            S.barrier()

        with ExitStack() as pc2:
            g2 = sbuf(pc2, "g2", [128, D], F32)
            S.dma("sp", "c_g2", g2[:], g2_d, writes=[g2.r])
            wgs = sbuf(pc2, "wgs", [128, KD, D], BF16)
            wglu = sbuf(pc2, "wglu", [128, 4, 2048], BF16)
            wout = sbuf(pc2, "wout", [128, KD, D], BF16)
            with ExitStack() as pw2:
                alloc_stg(pw2)
                for hh in range(2):
                    load_w(wv_(wgs, hh * 512, (hh + 1) * 512), w_in_d, C_GS + hh * 512, 512, g1)
                    load_w(wv_(wout, hh * 512, (hh + 1) * 512), wout_d, hh * 512, 512, None)
                for qq in range(4):
                    load_w(wv_(wglu, qq * 512, (qq + 1) * 512), wglu_d, qq * 512, 512, None, nk=4)
                S.barrier()
            hTc2 = [sbuf(pc2, "hTc2_%d" % i, [128, KD, 512], BF16) for i in range(2)]
            matt = [sbuf(pc2, "matt%d" % i, [128, KD, 512], BF16) for i in range(2)]
            ygc = [sbuf(pc2, "ygc%d" % i, [128, 4, 512], BF16) for i in range(2)]
            sg1 = sbuf(pc2, "sg1", [128, 512], F32); sg2 = sbuf(pc2, "sg2", [128, 512], F32)
            ys = sbuf(pc2, "ys", [128, 512], F32)
            merged = sbuf(pc2, "merged", [128, KD, 512], BF16)
            xres = [sbuf(pc2, "xres%d" % i, [128, D], F32) for i in range(2)]
            x1t = [sbuf(pc2, "x1t%d" % i, [128, D], F32) for i in range(2)]
            h2b = [sbuf(pc2, "h2b%d" % i, [128, D], BF16) for i in range(2)]
            h2c = [sbuf(pc2, "h2c%d" % i, [128, KD, 128], BF16) for i in range(2)]
            ss2 = sbuf(pc2, "ss2", [128, NT], F32); rs2 = sbuf(pc2, "rs2", [128, NT], F32)
            for c in range(NCH):
                hc = hTc2[c % 2]; mt = matt[c % 2]; yg = ygc[c % 2]
                S.dma("sp", "hcl2_%d" % (c % 2), hc[:], hT_d[:, :, c * 512:(c + 1) * 512], writes=[hc.r])
                S.dma("sp", "mtl%d" % (c % 2), mt[:], mT_v[:, :, c * 512:(c + 1) * 512], writes=[mt.r])
                S.dma("sp", "ygl%d" % (c % 2), yg[:], yg_d[:, :, c * 512:(c + 1) * 512], writes=[yg.r])
                for dc in range(KD):
                    pvl = banks[5]; pgt = banks[6]; pgs = banks[7]
                    for rc in range(4):
                        S.op("pe", lambda e: e.matmul(pvl[:], lhsT=wglu[:, rc, dc * 128:(dc + 1) * 128], rhs=yg[:, rc, :], start=(rc == 0), stop=(rc == 3)),
                             reads=[wglu.r, yg.r], writes=[pvl.r])
                    for rc in range(4):
                        S.op("pe", lambda e: e.matmul(pgt[:], lhsT=wglu[:, rc, D + dc * 128:D + (dc + 1) * 128], rhs=yg[:, rc, :], start=(rc == 0), stop=(rc == 3)),
                             reads=[wglu.r, yg.r], writes=[pgt.r])
                    for k in range(KD):
                        S.op("pe", lambda e: e.matmul(pgs[:], lhsT=wgs[:, k, dc * 128:(dc + 1) * 128], rhs=hc[:, k, :], start=(k == 0), stop=(k == KD - 1)),
                             reads=[wgs.r, hc.r], writes=[pgs.r])
                    S.op("act", lambda e: e.activation(out=sg1[:], in_=pgt[:], func=AF.Sigmoid), reads=[pgt.r], writes=[sg1.r])
                    S.op("act", lambda e: e.activation(out=sg2[:], in_=pgs[:], func=AF.Sigmoid), reads=[pgs.r], writes=[sg2.r])
                    tt("dve", ys[:], pvl[:], sg1[:], ALU.mult, [pvl.r, sg1.r], [ys.r])
                    tt("pool", ys[:], ys[:], sg2[:], ALU.mult, [ys.r, sg2.r], [ys.r])
                    tt("pool", merged[:, dc, :], ys[:], mt[:, dc, :], ALU.add, [ys.r, mt.r], [merged.r])
                if dbg:
                    S.dma("sp", "m2st", m2T_d[:, :, c * 512:(c + 1) * 512], merged[:], reads=[merged.r])
                for ti in range(4):
                    i = c * 4 + ti
                    xr_ = xres[i % 2]; x1 = x1t[i % 2]; hb = h2b[i % 2]; hcp = h2c[i % 2]; junkC = hb
                    S.dma("sp", "xrl%d" % (i % 2), xr_[:], x_d[i * 128:(i + 1) * 128, :], writes=[xr_.r])
                    for nh in range(2):
                        po = banks[nh]
                        for dc in range(KD):
                            S.op("pe", lambda e: e.matmul(po[:], lhsT=merged[:, dc, ti * 128:(ti + 1) * 128], rhs=wout[:, dc, nh * 512:(nh + 1) * 512],
                                                          start=(dc == 0), stop=(dc == KD - 1)), reads=[merged.r, wout.r], writes=[po.r])
                        tt("dve", x1[:, nh * 512:(nh + 1) * 512], po[:], xr_[:, nh * 512:(nh + 1) * 512], ALU.add, [po.r, xr_.r], [x1.r])
                    S.dma("sp", "x1st%d" % (i % 2), out_d[i * 128:(i + 1) * 128, :], x1[:], reads=[x1.r])
                    S.op("act", lambda e: e.activation(out=junkC[:], in_=x1[:], func=AF.Square, accum_out=ss2[:, i:i + 1]), reads=[x1.r], writes=[junkC.r, ss2.r])
                    S.op("act", lambda e: e.activation(out=rs2[:, i:i + 1], in_=ss2[:, i:i + 1], func=AF.Sqrt, bias=EPS, scale=1.0 / D), reads=[ss2.r], writes=[rs2.r])
                    S.op("dve", lambda e: e.reciprocal(out=rs2[:, i:i + 1], in_=rs2[:, i:i + 1]), reads=[rs2.r], writes=[rs2.r])
                    S.op("dve", lambda e: e.scalar_tensor_tensor(out=hb[:], in0=x1[:], scalar=rs2[:, i:i + 1], in1=g2[:], op0=ALU.mult, op1=ALU.mult), reads=[x1.r, rs2.r, g2.r], writes=[hb.r])
                    bk = banks[2 + (i % 2)]
                    bv = bk[:].bitcast(BF16)
                    for k in range(KD):
                        S.op("pe", lambda e: e.transpose(out=bv[:, k * 128:(k + 1) * 128], in_=hb[:, k * 128:(k + 1) * 128], identity=identb[:]),
                             reads=[hb.r, identb.r], writes=[bk.r])
                    S.op("act", lambda e: e.copy(out=hcp[:], in_=bv[:, 0:1024].rearrange("p (k t) -> p k t", k=KD)), reads=[bk.r], writes=[hcp.r])
                    S.dma("sp", "h2st%d" % (i % 2), h2T_d[:, :, i * 128:(i + 1) * 128], hcp[:], reads=[hcp.r])
            S.barrier()
        if stop_after == "C":
            S.wait_all_dma("sp")
            return nc

        with ExitStack() as pd:
            alloc_stg(pd)
            wq = sbuf(pd, "wqp", [128, KD, 2048], BF16)
            skT = sbuf(pd, "skT", [128, 16, 128], BF16)
            for qq in range(4):
                load_w(Buf(wq.t[:, :, qq * 512:(qq + 1) * 512], wq.r), wqr_d, qq * 512, 512, None)
            for hh in range(2):
                load_w(Buf(skT.t[:, hh * 8:(hh + 1) * 8, :], skT.r), skT_d[:, hh * 8:(hh + 1) * 8, :], 0, 128, None)
            iota16 = sbuf(pd, "iota16", [128, 16], F32)
            S.op("pool", lambda e: e.iota(iota16[:], pattern=[[1, 16]], base=0, channel_multiplier=0, allow_small_or_imprecise_dtypes=True), writes=[iota16.r])
            h2c_ = [sbuf(pd, "h2cD%d" % i, [128, KD, 512], BF16) for i in range(2)]
            qTs = sbuf(pd, "qTs", [128, 16, 512], BF16)
            sc = sbuf(pd, "sc", [128, 16, 128], F32); sc2 = sbuf(pd, "sc2", [128, 16, 128], F32)
            v16s = [sbuf(pd, "v16_%d" % i, [128, 16, 16], F32) for i in range(2)]
            i16s = [sbuf(pd, "i16_%d" % i, [128, 16, 16], U32) for i in range(2)]
            i16fs = [sbuf(pd, "i16f_%d" % i, [128, 16, 16], F32) for i in range(2)]
            cand = sbuf(pd, "cand", [128, 8, 256], F32); cand2 = sbuf(pd, "cand2", [128, 8, 256], F32)
            bests = [sbuf(pd, "best_%d" % i, [128, 8, 16], F32) for i in range(2)]
            poss = [sbuf(pd, "pos_%d" % i, [128, 8, 16], U32) for i in range(2)]
            pa_is = [sbuf(pd, "pa_i_%d" % i, [128, 8, 16], I32) for i in range(2)]
            pb_is = [sbuf(pd, "pb_i_%d" % i, [128, 8, 16], I32) for i in range(2)]
            pa_f = sbuf(pd, "pa_f", [128, 8, 16], F32); pb_f = sbuf(pd, "pb_f", [128, 8, 16], F32)
            ohs = [sbuf(pd, "oh%d" % i, [128, 8, 16, 16], F32) for i in range(4)]; pr = sbuf(pd, "pr", [128, 8, 16, 16], F32)
            gpre = sbuf(pd, "gpre", [128, 8, 16], F32); gjunk = sbuf(pd, "gjunk", [128, 8, 16], F32)
            rt3s = [sbuf(pd, "rt3_%d" % i, [128, 3, 128], F32) for i in range(2)]
            esum = sbuf(pd, "esum", [128, 8], F32)
            rtT = [sbuf(pd, "rtT%d" % i, [128, 3, 128], F32) for i in range(2)]
            print("SBUF remaining in phase D0:", nc.sbuf_bytes_remaining)
            def d0_chunk(c):
                hc = h2c_[c % 2]
                S.dma("sp", "h2l%d" % (c % 2), hc[:], h2T_d[:, :, c * 512:(c + 1) * 512], writes=[hc.r])
                for b in range(16):
                    pq = banks[4 + (b % 2)]
                    for k in range(KD):
                        S.op("pe", lambda e: e.matmul(pq[:], lhsT=wq[:, k, b * 128:(b + 1) * 128], rhs=hc[:, k, :], start=(k == 0), stop=(k == KD - 1)),
                             reads=[wq.r, hc.r], writes=[pq.r])
                    S.op("act", lambda e: e.copy(out=qTs[:, b, :], in_=pq[:]), reads=[pq.r], writes=[qTs.r])

            def d0_hdr(i):
                return (v16s[i % 2], i16s[i % 2], i16fs[i % 2], bests[i % 2], poss[i % 2], pa_is[i % 2], pb_is[i % 2], rt3s[i % 2])

            def d0_front(i):
                c, ti = divmod(i, 4)
                v16, i16, i16f, best, pos, pa_i, pb_i, rt3 = d0_hdr(i)
                for b in range(16):
                    bk = banks[b // 4]
                    S.op("pe", lambda e: e.matmul(bk[:, (b % 4) * 128:(b % 4 + 1) * 128], lhsT=qTs[:, b, ti * 128:(ti + 1) * 128], rhs=skT[:, b, :], start=True, stop=True),
                         reads=[qTs.r, skT.r], writes=[bk.r])
                for q4 in range(4):
                    S.op("act", lambda e: e.copy(out=sc[:, q4 * 4:(q4 + 1) * 4, :], in_=banks[q4][:].rearrange("p (a b) -> p a b", a=4)),
                         reads=[banks[q4].r], writes=[sc.r])
                for b in range(16):
                    S.op("dve", lambda e: e.max(out=v16[:, b, 0:8], in_=sc[:, b, :]), reads=[sc.r], writes=[v16.r])
                for b in range(16):
                    S.op("dve", lambda e: e.max_index(out=i16[:, b, 0:8], in_max=v16[:, b, 0:8], in_values=sc[:, b, :]), reads=[sc.r, v16.r], writes=[i16.r])
                for b in range(16):
                    S.op("dve", lambda e: e.match_replace(out=sc2[:, b, :], in_to_replace=v16[:, b, 0:8], in_values=sc[:, b, :], imm_value=-1e30),
                         reads=[sc.r, v16.r], writes=[sc2.r])
                for b in range(16):
                    S.op("dve", lambda e: e.max(out=v16[:, b, 8:16], in_=sc2[:, b, :]), reads=[sc2.r], writes=[v16.r])
                for b in range(16):
                    S.op("dve", lambda e: e.max_index(out=i16[:, b, 8:16], in_max=v16[:, b, 8:16], in_values=sc2[:, b, :]), reads=[sc2.r, v16.r], writes=[i16.r])
                v4 = v16[:].rearrange("p (h c) k -> p h c k", c=2)
                S.op("dve", lambda e: e.tensor_tensor(out=cand[:].rearrange("p h (a b) -> p h a b", a=16),
                                                      in0=v4[:, :, 0, :].unsqueeze(3).to_broadcast([128, 8, 16, 16]),
                                                      in1=v4[:, :, 1, :].unsqueeze(2).to_broadcast([128, 8, 16, 16]), op=ALU.add),
                     reads=[v16.r], writes=[cand.r])
                for h in range(8):
                    S.op("dve", lambda e: e.max(out=best[:, h, 0:8], in_=cand[:, h, :]), reads=[cand.r], writes=[best.r])
                for h in range(8):
                    S.op("dve", lambda e: e.max_index(out=pos[:, h, 0:8], in_max=best[:, h, 0:8], in_values=cand[:, h, :]), reads=[cand.r, best.r], writes=[pos.r])
                for h in range(8):
                    S.op("dve", lambda e: e.match_replace(out=cand2[:, h, :], in_to_replace=best[:, h, 0:8], in_values=cand[:, h, :], imm_value=-1e30),
                         reads=[cand.r, best.r], writes=[cand2.r])
                for h in range(8):
                    S.op("dve", lambda e: e.max(out=best[:, h, 8:16], in_=cand2[:, h, :]), reads=[cand2.r], writes=[best.r])
                for h in range(8):
                    S.op("dve", lambda e: e.max_index(out=pos[:, h, 8:16], in_max=best[:, h, 8:16], in_values=cand2[:, h, :]), reads=[cand2.r, best.r], writes=[pos.r])
                S.op("dve", lambda e: e.tensor_single_scalar(out=pa_i[:], in_=pos[:].bitcast(I32), scalar=4, op=ALU.arith_shift_right), reads=[pos.r], writes=[pa_i.r])
                S.op("dve", lambda e: e.tensor_single_scalar(out=pb_i[:], in_=pos[:].bitcast(I32), scalar=15, op=ALU.bitwise_and), reads=[pos.r], writes=[pb_i.r])
                oh_a = ohs[(i % 2) * 2]; oh_b = ohs[(i % 2) * 2 + 1]
                S.op("dve", lambda e: e.tensor_copy(out=pa_f[:], in_=pa_i[:]), reads=[pa_i.r], writes=[pa_f.r])
                S.op("dve", lambda e: e.tensor_copy(out=pb_f[:], in_=pb_i[:]), reads=[pb_i.r], writes=[pb_f.r])
                for pf_, oh_ in ((pa_f, oh_a), (pb_f, oh_b)):
                    S.op("dve", lambda e: e.tensor_tensor(out=oh_[:], in0=pf_[:].unsqueeze(3).to_broadcast([128, 8, 16, 16]),
                                                          in1=iota16[:].unsqueeze(1).unsqueeze(1).to_broadcast([128, 8, 16, 16]), op=ALU.is_equal),
                         reads=[pf_.r, iota16.r], writes=[oh_.r])

            def d0_back(i):
                v16, i16, i16f, best, pos, pa_i, pb_i, rt3 = d0_hdr(i)
                oh_a = ohs[(i % 2) * 2]; oh_b = ohs[(i % 2) * 2 + 1]
                S.op("pool", lambda e: e.tensor_copy(out=i16f[:], in_=i16[:]), reads=[i16.r], writes=[i16f.r])
                gat = rt3[:, 2, :].rearrange("p (h k) -> p h k", h=8)
                S.op("pool", lambda e: e.tensor_tensor(out=gpre[:], in0=best[:], in1=best[:, :, 0:1].to_broadcast([128, 8, 16]), op=ALU.subtract),
                     reads=[best.r], writes=[gpre.r])
                for h in range(8):
                    S.op("act", lambda e: e.activation(out=gjunk[:, h, :], in_=gpre[:, h, :], func=AF.Exp, accum_out=esum[:, h:h + 1]),
                         reads=[gpre.r], writes=[gjunk.r, esum.r])
                S.op("act", lambda e: e.activation(out=esum[:], in_=esum[:], func=AF.Ln), reads=[esum.r], writes=[esum.r])
                S.op("pool", lambda e: e.tensor_tensor(out=gpre[:], in0=gpre[:], in1=esum[:].unsqueeze(2).to_broadcast([128, 8, 16]), op=ALU.subtract),
                     reads=[gpre.r, esum.r], writes=[gpre.r])
                S.op("act", lambda e: e.activation(out=gat, in_=gpre[:], func=AF.Exp), reads=[gpre.r], writes=[rt3.r])
                i4 = i16f[:].rearrange("p (h c) k -> p h c k", c=2)
                for which, oh_ in enumerate((oh_a, oh_b)):
                    S.op("pool", lambda e: e.tensor_tensor(out=pr[:], in0=oh_[:], in1=i4[:, :, which, :].unsqueeze(2).to_broadcast([128, 8, 16, 16]), op=ALU.mult),
                         reads=[oh_.r, i16f.r], writes=[pr.r])
                    S.op("pool", lambda e: e.tensor_tensor(out=pr[:, :, :, 0:8], in0=pr[:, :, :, 0:8], in1=pr[:, :, :, 8:16], op=ALU.add), reads=[pr.r], writes=[pr.r])
                    S.op("pool", lambda e: e.tensor_tensor(out=pr[:, :, :, 0:4], in0=pr[:, :, :, 0:4], in1=pr[:, :, :, 4:8], op=ALU.add), reads=[pr.r], writes=[pr.r])
                    S.op("pool", lambda e: e.tensor_tensor(out=pr[:, :, :, 0:2], in0=pr[:, :, :, 0:2], in1=pr[:, :, :, 2:4], op=ALU.add), reads=[pr.r], writes=[pr.r])
                    S.op("pool", lambda e: e.tensor_tensor(out=rt3[:, which, :].rearrange("p (h k) -> p h k", h=8), in0=pr[:, :, :, 0], in1=pr[:, :, :, 1], op=ALU.add),
                         reads=[pr.r], writes=[rt3.r])
                rtt = rtT[i % 2]
                for q3 in range(3):
                    bk = banks[6 + (q3 % 2)]
                    S.op("pe", lambda e: e.transpose(out=bk[:, 0:128], in_=rt3[:, q3, :], identity=identf[:]), reads=[rt3.r, identf.r], writes=[bk.r])
                    S.op("act", lambda e: e.copy(out=rtt[:, q3, :], in_=bk[:, 0:128]), reads=[bk.r], writes=[rtt.r])
                S.dma("sp", "rtst%d" % (i % 2), rt_d[:, :, i * 128:(i + 1) * 128], rtt[:], reads=[rtt.r])


            for i in range(NT + 1):
                if i < NT:
                    if i % 4 == 0:
                        d0_chunk(i // 4)
                    d0_front(i)
                if i >= 1:
                    d0_back(i - 1)

            S.barrier()
        if stop_after == "D0":
            S.wait_all_dma("sp")
            return nc

        G = 256
        NG = T // G
        with ExitStack() as pe_:
            iotaf = sbuf(pe_, "iotaf", [128, 128], F32)
            S.op("pool", lambda e: e.iota(iotaf[:], pattern=[[1, 128]], base=0, channel_multiplier=0, allow_small_or_imprecise_dtypes=True), writes=[iotaf.r])
            GTs = [sbuf(pe_, "GT%d" % i, [128, G, 128], BF16) for i in range(2)]
            ub = [sbuf(pe_, "ub%d" % i, [128, KD, 512], BF16) for i in range(3)]
            vb = [sbuf(pe_, "vb%d" % i, [128, 4, D], BF16) for i in range(3)]
            rtgs = [sbuf(pe_, "rtg%d" % i, [128, 3, G], F32) for i in range(2)]
            h2g = sbuf(pe_, "h2g", [128, KD, G], BF16)
            Ab = [sbuf(pe_, "Ab%d" % i, [128, G], BF16) for i in range(2)]
            WT = [sbuf(pe_, "WT%d" % i, [128, G], BF16) for i in range(2)]
            P1 = [sbuf(pe_, "P1_%d" % i, [128, 128], BF16) for i in range(8)]
            P2B = [sbuf(pe_, "P2B_%d" % i, [128, 8, 128], BF16) for i in range(2)]
            x1g = [sbuf(pe_, "x1g%d" % i, [128, D], F32) for i in range(2)]
            print("SBUF remaining in phase D1:", nc.sbuf_bytes_remaining)
            uv_loaded = [False]
            gt_cnt = [0]

            def gt_gen(g):
                GT = GTs[g % 2]; rtg = rtgs[g % 2]
                S.dma("sp", "rtl%d" % (g % 2), rtg[:], rt_d[:, :, g * G:(g + 1) * G], writes=[rtg.r])
                LAG = 4
                p2bs = {}

                def gt_mm(t):
                    t8, t_ = divmod(t, 8)
                    p1 = P1[t % 8]; p2b = p2bs[t8]
                    gp = banks[6 + ((t // 4) % 2)]
                    S.op("pe", lambda e: e.matmul(gp[:, (t % 4) * 128:(t % 4 + 1) * 128], lhsT=p2b[:, t_, :], rhs=p1[:], start=True, stop=True),
                         reads=[p1.r, p2b.r], writes=[gp.r])
                    if t % 4 == 3:
                        S.op("act", lambda e: e.copy(out=GT[:, t - 3:t + 1, :], in_=gp[:].rearrange("p (a b) -> p a b", a=4)), reads=[gp.r], writes=[GT.r])

                for t in range(G):
                    t8, t_ = divmod(t, 8)
                    if t_ == 0:
                        p2b = P2B[gt_cnt[0] % 2]
                        gt_cnt[0] += 1
                        p2bs[t8] = p2b
                        S.op("dve", lambda e: e.tensor_tensor(out=p2b[:], in0=iotaf[:].unsqueeze(1).to_broadcast([128, 8, 128]),
                                                              in1=rtg[:, 1, t8 * 8:(t8 + 1) * 8].unsqueeze(2).to_broadcast([128, 8, 128]), op=ALU.is_equal),
                             reads=[iotaf.r, rtg.r], writes=[p2b.r])
                    p1 = P1[t % 8]
                    S.op("dve", lambda e: e.tensor_scalar(out=p1[:], in0=iotaf[:], scalar1=rtg[:, 0, t:t + 1], scalar2=rtg[:, 2, t:t + 1], op0=ALU.is_equal, op1=ALU.mult),
                         reads=[iotaf.r, rtg.r], writes=[p1.r])
                    if t >= LAG:
                        gt_mm(t - LAG)
                    yield
                for t in range(G - LAG, G):
                    gt_mm(t)
                yield

            def drain(gen, n=None):
                k = 0
                while n is None or k < n:
                    try:
                        next(gen)
                    except StopIteration:
                        return
                    k += 1

            drain(gt_gen(0))
            for g in range(NG):
                GT = GTs[g % 2]
                nxt = gt_gen(g + 1) if g + 1 < NG else iter(())
                S.dma("sp", "h2gl", h2g[:], h2T_d[:, :, g * G:(g + 1) * G], writes=[h2g.r])
                def emit_S(i1):
                    blk4, bi = divmod(i1, 4)
                    u_ = ub[blk4 % 3]
                    pS = banks[4 + (i1 % 2)]
                    for k in range(KD):
                        S.op("pe", lambda e: e.matmul(pS[:, 0:G], lhsT=u_[:, k, bi * 128:(bi + 1) * 128], rhs=h2g[:, k, :], start=(k == 0), stop=(k == KD - 1)),
                             reads=[u_.r, h2g.r], writes=[pS.r])
                    ab_ = Ab[i1 % 2]; wt_ = WT[i1 % 2]
                    S.op("act", lambda e: e.activation(out=ab_[:], in_=pS[:, 0:G], func=AF.Gelu_apprx_tanh), reads=[pS.r], writes=[ab_.r])
                    eng = "dve" if i1 % 2 == 0 else "pool"
                    S.op(eng, lambda e: e.tensor_tensor(out=wt_[:], in0=ab_[:], in1=GT[:, :, i1], op=ALU.mult), reads=[ab_.r, GT.r], writes=[wt_.r])

                def emit_out(i1):
                    blk4, bi = divmod(i1, 4)
                    v_ = vb[blk4 % 3]; wt_ = WT[i1 % 2]
                    for tt_ in range(G // 128):
                        for nh in range(2):
                            acc = banks[tt_ * 2 + nh]
                            S.op("pe", lambda e: e.matmul(acc[:], lhsT=wt_[:, tt_ * 128:(tt_ + 1) * 128], rhs=v_[:, bi, nh * 512:(nh + 1) * 512],
                                                          start=(i1 == 0), stop=(i1 == 127)), reads=[wt_.r, v_.r], writes=[acc.r])

                for i1 in range(128):
                    blk4, bi = divmod(i1, 4)
                    if bi == 0:
                        if not uv_loaded[0]:
                            for tok in cast_toks:
                                S._wait("sp", tok)
                            uv_loaded[0] = True
                        u_ = ub[blk4 % 3]; v_ = vb[blk4 % 3]
                        S.dma("sp", "ul%d" % (blk4 % 3), u_[:], UTb_d[:, :, blk4 * 512:(blk4 + 1) * 512], writes=[u_.r])
                        S.dma("sp", "vl%d" % (blk4 % 3), v_[:], Vb_d[:, blk4 * 4:(blk4 + 1) * 4, :], writes=[v_.r])
                    emit_S(i1)
                    if i1 >= 1:
                        emit_out(i1 - 1)
                    drain(nxt, 2)
                emit_out(127)
                drain(nxt)
                for tt_ in range(G // 128):
                    i = g * (G // 128) + tt_
                    xg = x1g[i % 2]
                    S.dma("sp", "x1l%d" % (i % 2), xg[:], out_d[i * 128:(i + 1) * 128, :], writes=[xg.r])
                    for nh in range(2):
                        acc = banks[tt_ * 2 + nh]
                        S.op("dve", lambda e: e.tensor_tensor(out=xg[:, nh * 512:(nh + 1) * 512], in0=acc[:], in1=xg[:, nh * 512:(nh + 1) * 512], op=ALU.add),
                             reads=[acc.r, xg.r], writes=[xg.r])
                    S.dma("sp", "ost%d" % (i % 2), out_d[i * 128:(i + 1) * 128, :], xg[:], reads=[xg.r])
            S.barrier()

        S.wait_all_dma("sp")
        print("program built: ops=%d waits=%d" % (S.nops, S.nwaits))
    return nc


def make_in_maps(inputs, T, n_cores):
    f = lambda a: np.ascontiguousarray(np.asarray(a, dtype=np.float32))
    w_in = f(inputs["w_in"][0]).reshape(KD, 128, INC).transpose(1, 0, 2)
    common = {
        "w_in": f(w_in),
        "g1": f(f(inputs["mix_norm_g"][0]).reshape(KD, 128).T),
        "fb": f(np.broadcast_to(f(inputs["fox_forget_bias"][0])[None, :], (128, 8))),
        "qg": f(f(inputs["q_norm_g"][0]).reshape(128, 1)),
        "kg": f(f(inputs["k_norm_g"][0]).reshape(128, 1)),
    }
    L16 = lambda a: f(f(a).reshape(16, 2, 64).transpose(1, 2, 0).reshape(128, 16))
    common["a_re_l"] = L16(inputs["ssm_a_re"][0])
    common["a_im_l"] = L16(inputs["ssm_a_im"][0])
    common["ls_l"] = f(np.broadcast_to(f(inputs["ssm_log_step"][0]).reshape(16, 2, 1), (16, 2, 64)).transpose(1, 2, 0).reshape(128, 16))
    LB = lambda a: f(f(a).reshape(16, 2, 64, 16).transpose(1, 2, 0, 3).reshape(128, 16, 16))
    LC = lambda a: f(f(a).reshape(16, 2, 16, 64).transpose(1, 3, 0, 2).reshape(128, 16, 16))
    common["b_re_l"] = LB(inputs["ssm_b_re"][0]); common["b_im_l"] = LB(inputs["ssm_b_im"][0])
    common["c_re_l"] = LC(inputs["ssm_c_re"][0]); common["c_im_l"] = LC(inputs["ssm_c_im"][0])
    common["d_l"] = f(f(inputs["ssm_d"][0]).reshape(4, 128).T)
    common["w_glu"] = f(f(inputs["ssm_w_glu"][0]).reshape(4, 128, 2048).transpose(1, 0, 2))
    common["w_out"] = f(f(inputs["w_out"][0]).reshape(KD, 128, D).transpose(1, 0, 2))
    common["g2rep"] = f(np.broadcast_to(f(inputs["ffn_norm_g"][0])[None, :], (128, D)))
    common["w_query"] = f(f(inputs["peer_w_query"][0]).reshape(KD, 128, 2048).transpose(1, 0, 2))
    common["skT"] = f(f(inputs["peer_sub_keys"][0]).reshape(16, 128, 128).transpose(2, 0, 1))
    common["peer_uT"] = f(f(inputs["peer_u"][0]).T.reshape(KD, 128, NE).transpose(1, 0, 2))
    common["peer_v"] = f(f(inputs["peer_v"][0]).reshape(128, 128, D).transpose(1, 0, 2))
    maps = []
    for c in range(n_cores):
        m = dict(common)
        m["x"] = f(inputs["x"][c, :T])
        maps.append(m)
    return maps


def kernel(**inputs):
    T = 4096
    n = 8
    nc = build_program(T)
    in_maps = make_in_maps(inputs, T, n)
    res = run_bass_kernel_spmd(nc, in_maps, core_ids=list(range(n)))
    return np.stack([np.asarray(r["out"]) for r in res.results], axis=0).astype(np.float32)
```

```python
import math
from contextlib import ExitStack
import numpy as np
import concourse.bass as bass
import concourse.mybir as mybir
from concourse.bass_utils import run_bass_kernel_spmd

F32 = mybir.dt.float32; BF16 = mybir.dt.bfloat16; I32 = mybir.dt.int32; U32 = mybir.dt.uint32
ALU = mybir.AluOpType; AF = mybir.ActivationFunctionType; AX = mybir.AxisListType

D = 1024
KD = 8
INC = 5640
EPS = 1e-6
C_U, C_Q, C_K, C_V, C_F, C_GS, C_GA = 0, 512, 1536, 2560, 3584, 3592, 4616
NEG = -30000.0
TWO_PI = 2.0 * math.pi
NE = 16384


class Res:
    __slots__ = ("name", "w", "r")

    def __init__(self, name):
        self.name = name
        self.w = None
        self.r = {}


class Sched:
    ENGS = ("pe", "act", "dve", "pool", "sp")

    def __init__(self, nc, es, same_engine_sync=("act", "dve", "pool")):
        self.nc = nc
        self.es = es
        self.eng = {"pe": nc.tensor, "act": nc.scalar, "dve": nc.vector, "pool": nc.gpsimd, "sp": nc.sync}
        self.sem = {}
        self.cnt = {}
        for e in self.ENGS:
            self.sem[e] = es.enter_context(nc.semaphore("sem_" + e))
            self.cnt[e] = 0
        self.seen = {e: {} for e in self.ENGS}
        self.same = set(same_engine_sync)
        self.nwaits = 0
        self.nops = 0

    def res(self, name):
        return Res(name)

    def _chan(self, chan):
        if chan not in self.sem:
            self.sem[chan] = self.es.enter_context(self.nc.semaphore("semd_" + chan))
            self.cnt[chan] = 0
        return self.sem[chan]

    def _wait(self, e, tok):
        if tok is None:
            return
        key, val = tok
        if key == e and e not in self.same:
            return
        if self.seen[e].get(key, 0) >= val:
            return
        self.eng[e].wait_ge(self.sem[key], val)
        self.seen[e][key] = val
        self.nwaits += 1

    def _deps(self, e, reads, writes):
        for R in reads:
            self._wait(e, R.w)
        for W in writes:
            self._wait(e, W.w)
            for key, val in W.r.items():
                self._wait(e, (key, val))

    def _commit(self, tok, reads, writes):
        for R in reads:
            if R.r.get(tok[0], 0) < tok[1]:
                R.r[tok[0]] = tok[1]
        for W in writes:
            W.w = tok
            W.r = {}

    def op(self, e, fn, reads=(), writes=()):
        self._deps(e, reads, writes)
        ins = fn(self.eng[e])
        self.cnt[e] += 1
        ins.then_inc(self.sem[e], 1)
        tok = (e, self.cnt[e])
        self._commit(tok, reads, writes)
        self.nops += 1
        return tok

    def dma(self, e, chan, out, in_, reads=(), writes=(), **kw):
        sem = self._chan(chan)
        if self.cnt[chan] > 0:
            self._wait(e, (chan, self.cnt[chan]))
        self._deps(e, reads, writes)
        ins = self.eng[e].dma_start(out=out, in_=in_, **kw)
        self.cnt[chan] += 16
        ins.then_inc(sem, 16)
        tok = (chan, self.cnt[chan])
        self._commit(tok, reads, writes)
        return tok

    def barrier(self):
        for e in self.ENGS:
            for key in list(self.sem.keys()):
                if key != e and self.cnt[key] > 0:
                    self._wait(e, (key, self.cnt[key]))
                elif key == e and e in self.same and self.cnt[key] > 0:
                    self._wait(e, (key, self.cnt[key]))

    def wait_all_dma(self, e):
        for key in list(self.sem.keys()):
            if key not in self.ENGS and self.cnt[key] > 0:
                self._wait(e, (key, self.cnt[key]))


class Buf:
    def __init__(self, t, r):
        self.t = t
        self.r = r

    def __getitem__(self, k):
        return self.t[k]


def build_program(T, dbg=False, stop_after=None):
    NT = T // 128
    NCH = T // 512
    nc = bass.Bass("TRN2", target_bir_lowering=False)

    def din(name, shape, dt=F32):
        return nc.dram_tensor(name, shape, dt, kind="ExternalInput").ap()

    x_d = din("x", [T, D])
    w_in_d = din("w_in", [128, KD, INC])
    g1_d = din("g1", [128, KD])
    fb_d = din("fb", [128, 8])
    qg_d = din("qg", [128, 1])
    kg_d = din("kg", [128, 1])
    are_d = din("a_re_l", [128, 16]); aim_d = din("a_im_l", [128, 16]); ls_d = din("ls_l", [128, 16])
    bre_d = din("b_re_l", [128, 16, 16]); bim_d = din("b_im_l", [128, 16, 16])
    cre_d = din("c_re_l", [128, 16, 16]); cim_d = din("c_im_l", [128, 16, 16])
    dsk_d = din("d_l", [128, 4])
    wglu_d = din("w_glu", [128, 4, 2048])
    wout_d = din("w_out", [128, KD, D])
    g2_d = din("g2rep", [128, D])
    wqr_d = din("w_query", [128, KD, 2048])
    skT_d = din("skT", [128, 16, 128])
    UT_d = din("peer_uT", [128, KD, NE])
    V_d = din("peer_v", [128, 128, D])
    out_d = nc.dram_tensor("out", [T, D], F32, kind="ExternalOutput").ap()
    UTb_d = nc.dram_tensor("UTb", [128, KD, NE], BF16, kind="Internal").ap()
    Vb_d = nc.dram_tensor("Vb", [128, 128, D], BF16, kind="Internal").ap()
    rt_d = nc.dram_tensor("rt", [128, 3, T], F32, kind=("ExternalOutput" if dbg else "Internal")).ap()
    kind_scr = "ExternalOutput" if dbg else "Internal"
    mT_d = nc.dram_tensor("mT", [D, T], BF16, kind=kind_scr).ap()
    hT_d = nc.dram_tensor("hTd", [128, KD, T], BF16, kind="Internal").ap()
    h2T_d = nc.dram_tensor("h2T", [128, KD, T], BF16, kind=kind_scr).ap()
    yg_d = nc.dram_tensor("ygd", [128, 4, T], BF16, kind="Internal").ap()
    m2T_d = nc.dram_tensor("m2T", [128, KD, T], BF16, kind=kind_scr).ap() if dbg else None

    with ExitStack() as es:
        S = Sched(nc, es)

        def sbuf(stack, name, shape, dt):
            t = stack.enter_context(nc.sbuf_tensor("sb_" + name, shape, dt))
            return Buf(t, S.res(name))

        banks = []
        for i in range(8):
            t = es.enter_context(nc.psum_tensor("bank%d" % i, [128, 512], F32))
            banks.append(Buf(t, S.res("bank%d" % i)))

        identf = sbuf(es, "identf", [128, 128], F32)
        identb = sbuf(es, "identb", [128, 128], BF16)
        onesb = sbuf(es, "onesb", [128, 128], BF16)
        onesf = sbuf(es, "onesf", [128, 128], F32)
        trif = sbuf(es, "trif", [128, 128], F32)
        maskneg = sbuf(es, "maskneg", [128, 128], BF16)
        S.op("pool", lambda e: e.memset(identf[:], 1.0), writes=[identf.r])
        S.op("pool", lambda e: e.affine_select(out=identf[:], in_=identf[:], pattern=[[-1, 128]], compare_op=ALU.is_equal,
                                               fill=0.0, base=0, channel_multiplier=1), reads=[identf.r], writes=[identf.r])
        S.op("pool", lambda e: e.tensor_copy(out=identb[:], in_=identf[:]), reads=[identf.r], writes=[identb.r])
        S.op("pool", lambda e: e.memset(onesb[:], 1.0), writes=[onesb.r])
        S.op("pool", lambda e: e.memset(onesf[:], 1.0), writes=[onesf.r])
        S.op("pool", lambda e: e.memset(trif[:], 1.0), writes=[trif.r])
        S.op("pool", lambda e: e.affine_select(out=trif[:], in_=trif[:], pattern=[[1, 128]], compare_op=ALU.is_ge,
                                               fill=0.0, base=0, channel_multiplier=-1), reads=[trif.r], writes=[trif.r])
        S.op("pool", lambda e: e.tensor_scalar(out=maskneg[:], in0=trif[:], scalar1=-1.0, scalar2=-NEG, op0=ALU.add, op1=ALU.mult),
             reads=[trif.r], writes=[maskneg.r])

        hpi = sbuf(es, "hpi", [128, 1], F32)
        S.op("pool", lambda e: e.memset(hpi[:], math.pi / 2.0), writes=[hpi.r])
        epst = sbuf(es, "epst", [128, 1], F32)
        S.op("pool", lambda e: e.memset(epst[:], EPS), writes=[epst.r])
        g1 = sbuf(es, "g1", [128, KD], F32)
        fb = sbuf(es, "fb", [128, 8], F32)
        qg = sbuf(es, "qg", [128, 1], F32)
        kg = sbuf(es, "kg", [128, 1], F32)
        S.dma("sp", "c_g1", g1[:], g1_d, writes=[g1.r])
        S.dma("sp", "c_fb", fb[:], fb_d, writes=[fb.r])
        S.dma("sp", "c_qg", qg[:], qg_d, writes=[qg.r])
        S.dma("sp", "c_kg", kg[:], kg_d, writes=[kg.r])
        S.op("dve", lambda e: e.tensor_scalar(out=qg[:], in0=qg[:], scalar1=128.0 ** -0.5, scalar2=None, op0=ALU.mult),
             reads=[qg.r], writes=[qg.r])

        cast_toks = []
        if stop_after is None:
            for q in range(16):
                cast_toks.append(S.dma("pool", "ucast%d" % q, UTb_d[:, :, q * 1024:(q + 1) * 1024], UT_d[:, :, q * 1024:(q + 1) * 1024]))
                cast_toks.append(S.dma("pool", "vcast%d" % q, Vb_d[:, q * 8:(q + 1) * 8, :], V_d[:, q * 8:(q + 1) * 8, :]))

        stg = [None, None]
        stg_i = [0]

        def alloc_stg(stack):
            for i in range(2):
                stg[i] = sbuf(stack, "wstg%d_%d" % (i, stg_i[0]), [128, KD, 512], F32)

        def load_w(dst, src_d, c0, ncols, gain, nk=KD):
            s = stg[stg_i[0] % 2]
            stg_i[0] += 1
            S.dma("sp", "wld%d" % (stg_i[0] % 2), s[:, 0:nk, 0:ncols], src_d[:, :, c0:c0 + ncols], writes=[s.r])
            if gain is not None:
                S.op("pool", lambda e: e.tensor_tensor(out=dst[:], in0=s[:, 0:nk, 0:ncols],
                                                       in1=gain[:, 0:nk].unsqueeze(2).to_broadcast([128, nk, ncols]), op=ALU.mult),
                     reads=[s.r, gain.r], writes=[dst.r])
            else:
                S.op("pool", lambda e: e.tensor_copy(out=dst[:], in_=s[:, 0:nk, 0:ncols]), reads=[s.r], writes=[dst.r])

        scopeAB = ExitStack()
        alloc_stg(scopeAB)
        hT = sbuf(scopeAB, "hT", [128, KD, T], BF16)
        hT_r = [S.res("hT_%d" % i) for i in range(NT)]
        cum = sbuf(scopeAB, "cum", [128, NT, 8], F32)
        cend = sbuf(scopeAB, "cend", [128, NT + 1, 8], F32)
        with ExitStack() as pa:
            xts = [sbuf(pa, "xt%d" % i, [128, D], F32) for i in range(2)]
            xs = [sbuf(pa, "xs%d" % i, [128, D], BF16) for i in range(2)]
            junk = sbuf(pa, "junkA", [128, D], BF16)
            ss = sbuf(pa, "ss", [128, NT], F32)
            rs = sbuf(pa, "rs", [128, NT], F32)
            wf = sbuf(pa, "wf", [128, KD, 8], BF16)
            zf = sbuf(pa, "zf", [128, NT, 8], F32)
            spf = sbuf(pa, "spf", [128, NT, 8], F32)
            load_w(wf, w_in_d, C_F, 8, g1)
            fbank = banks[7]
            for i in range(NT):
                xt = xts[i % 2]; xb = xs[i % 2]
                S.dma("sp", "xld%d" % (i % 2), xt[:], x_d[i * 128:(i + 1) * 128, :], writes=[xt.r])
                S.op("act", lambda e: e.activation(out=junk[:], in_=xt[:], func=AF.Square, accum_out=ss[:, i:i + 1]),
                     reads=[xt.r], writes=[junk.r, ss.r])
                S.op("act", lambda e: e.activation(out=rs[:, i:i + 1], in_=ss[:, i:i + 1], func=AF.Sqrt, bias=EPS, scale=1.0 / D),
                     reads=[ss.r], writes=[rs.r])
                S.op("dve", lambda e: e.reciprocal(out=rs[:, i:i + 1], in_=rs[:, i:i + 1]), reads=[rs.r], writes=[rs.r])
                S.op("dve", lambda e: e.tensor_scalar(out=xb[:], in0=xt[:], scalar1=rs[:, i:i + 1], scalar2=None, op0=ALU.mult),
                     reads=[xt.r, rs.r], writes=[xb.r])
                bk = banks[i % 2]
                bv = bk[:].bitcast(BF16)
                for k in range(KD):
                    S.op("pe", lambda e: e.transpose(out=bv[:, k * 128:(k + 1) * 128], in_=xb[:, k * 128:(k + 1) * 128], identity=identb[:]),
                         reads=[xb.r, identb.r], writes=[bk.r])
                S.op("act", lambda e: e.copy(out=hT[:, :, i * 128:(i + 1) * 128], in_=bv[:, 0:1024].rearrange("p (k t) -> p k t", k=KD)),
                     reads=[bk.r], writes=[hT_r[i]])
                for k in range(KD):
                    S.op("pe", lambda e: e.matmul(fbank[:, i * 8:(i + 1) * 8], lhsT=hT[:, k, i * 128:(i + 1) * 128], rhs=wf[:, k, :],
                                                  start=(k == 0), stop=(k == KD - 1)),
                         reads=[hT_r[i], wf.r], writes=[fbank.r])
            S.op("dve", lambda e: e.tensor_tensor(out=zf[:], in0=fbank[:, 0:NT * 8].rearrange("p (i h) -> p i h", h=8),
                                                  in1=fb[:].unsqueeze(1).to_broadcast([128, NT, 8]), op=ALU.add),
                 reads=[fbank.r, fb.r], writes=[zf.r])
            S.op("act", lambda e: e.activation(out=spf[:], in_=zf[:], func=AF.Exp, scale=-1.0), reads=[zf.r], writes=[spf.r])
            S.op("act", lambda e: e.activation(out=spf[:], in_=spf[:], func=AF.Ln, bias=1.0, scale=1.0), reads=[spf.r], writes=[spf.r])
            cb1 = banks[5]; cb2 = banks[6]
            for i in range(NT):
                S.op("pe", lambda e: e.matmul(cb1[:, i * 8:(i + 1) * 8], lhsT=trif[:], rhs=spf[:, i, :], start=True, stop=True),
                     reads=[trif.r, spf.r], writes=[cb1.r])
                S.op("pe", lambda e: e.matmul(cb2[:, i * 8:(i + 1) * 8], lhsT=onesf[:], rhs=spf[:, i, :], start=True, stop=True),
                     reads=[onesf.r, spf.r], writes=[cb2.r])
            S.op("dve", lambda e: e.memset(cend[:, 0, :], 0.0), writes=[cend.r])
            for i in range(NT):
                S.op("dve", lambda e: e.tensor_tensor(out=cend[:, i + 1, :], in0=cend[:, i, :], in1=cb2[:, i * 8:(i + 1) * 8], op=ALU.add),
                     reads=[cend.r, cb2.r], writes=[cend.r])
            S.op("dve", lambda e: e.tensor_tensor(out=cum[:], in0=cb1[:, 0:NT * 8].rearrange("p (i h) -> p i h", h=8),
                                                  in1=cend[:, 0:NT, :], op=ALU.add),
                 reads=[cb1.r, cend.r], writes=[cum.r])

        for c in range(NCH):
            S.dma("sp", "hsp%d" % (c % 2), hT_d[:, :, c * 512:(c + 1) * 512], hT[:, :, c * 512:(c + 1) * 512], reads=hT_r[c * 4:(c + 1) * 4])
        S.barrier()
        if stop_after == "A":
            return nc
        with ExitStack() as pb:
            wq = sbuf(pb, "wq", [128, KD, 128], BF16)
            wk = sbuf(pb, "wk", [128, KD, 128], BF16)
            wv = sbuf(pb, "wv", [128, KD, 128], BF16)
            wg = sbuf(pb, "wg", [128, KD, 128], BF16)
            qT = sbuf(pb, "qT", [128, T], BF16)
            kT = sbuf(pb, "kT", [128, T], BF16)
            vv = sbuf(pb, "vv", [128, NT, 128], BF16)
            sgT = sbuf(pb, "sgT", [128, T], BF16)
            crow = sbuf(pb, "crow", [1, T], BF16)
            sq = [sbuf(pb, "sq%d" % i, [128, 512], BF16) for i in range(2)]
            rrep = [sbuf(pb, "rrep%d" % i, [128, 512], F32) for i in range(2)]
            Pt = [sbuf(pb, "Pt%d" % i, [128, 512], BF16) for i in range(3)]
            rec = sbuf(pb, "rec", [128, 512], F32)
            yv = sbuf(pb, "yv", [128, 512], F32)
            ym = [sbuf(pb, "ym%d" % i, [128, 512], BF16) for i in range(2)]
            hT_all = hT_r
            pcount = [0]
            print("SBUF remaining in phase B:", nc.sbuf_bytes_remaining)
            for h in range(8):
                load_w(wq, w_in_d, C_Q + h * 128, 128, g1)
                load_w(wk, w_in_d, C_K + h * 128, 128, g1)
                load_w(wv, w_in_d, C_V + h * 128, 128, g1)
                load_w(wg, w_in_d, C_GA + h * 128, 128, g1)
                S.op("dve", lambda e: e.tensor_scalar(out=crow[0:1, :].rearrange("p (i r) -> p i r", r=128),
                                                      in0=cend[0:1, 1:NT + 1, h:h + 1].to_broadcast([1, NT, 128]), scalar1=-1.0, scalar2=None, op0=ALU.mult),
                     reads=[cend.r], writes=[crow.r])
                for which, (wt, dstT, gvec) in enumerate(((wq, qT, qg), (wk, kT, kg))):
                    for c in range(NCH):
                        pj = banks[(2 * c) % 4]; pn = banks[(2 * c + 1) % 4]
                        for k in range(KD):
                            S.op("pe", lambda e: e.matmul(pj[:], lhsT=wt[:, k, :], rhs=hT[:, k, c * 512:(c + 1) * 512], start=(k == 0), stop=(k == KD - 1)),
                                 reads=[wt.r] + hT_all[c * 4:(c + 1) * 4], writes=[pj.r])
                        sqb = sq[c % 2]; rr = rrep[c % 2]
                        S.op("act", lambda e: e.activation(out=sqb[:], in_=pj[:], func=AF.Square), reads=[pj.r], writes=[sqb.r])
                        S.op("pe", lambda e: e.matmul(pn[:], lhsT=onesb[:], rhs=sqb[:], start=True, stop=True), reads=[onesb.r, sqb.r], writes=[pn.r])
                        S.op("act", lambda e: e.activation(out=rr[:], in_=pn[:], func=AF.Ln, bias=epst[:, 0:1], scale=1.0 / 128.0), reads=[pn.r, epst.r], writes=[rr.r])
                        S.op("act", lambda e: e.activation(out=rr[:], in_=rr[:], func=AF.Exp, scale=-0.5), reads=[rr.r], writes=[rr.r])
                        S.op("dve", lambda e: e.scalar_tensor_tensor(out=dstT[:, c * 512:(c + 1) * 512], in0=pj[:], scalar=gvec[:, 0:1], in1=rr[:],
                                                                     op0=ALU.mult, op1=ALU.mult),
                             reads=[pj.r, gvec.r, rr.r], writes=[dstT.r])
                for i4 in range(NT // 4):
                    pv = banks[i4 % 2]
                    for ii in range(4):
                        i = i4 * 4 + ii
                        for k in range(KD):
                            S.op("pe", lambda e: e.matmul(pv[:, ii * 128:(ii + 1) * 128], lhsT=hT[:, k, i * 128:(i + 1) * 128], rhs=wv[:, k, :],
                                                          start=(k == 0), stop=(k == KD - 1)),
                                 reads=[wv.r, hT_all[i]], writes=[pv.r])
                    S.op("act", lambda e: e.copy(out=vv[:, i4 * 4:(i4 + 1) * 4, :], in_=pv[:].rearrange("p (a b) -> p a b", a=4)),
                         reads=[pv.r], writes=[vv.r])
                for c in range(NCH):
                    pg = banks[2 + (c % 2)]
                    for k in range(KD):
                        S.op("pe", lambda e: e.matmul(pg[:], lhsT=wg[:, k, :], rhs=hT[:, k, c * 512:(c + 1) * 512], start=(k == 0), stop=(k == KD - 1)),
                             reads=[wg.r] + hT_all[c * 4:(c + 1) * 4], writes=[pg.r])
                    S.op("act", lambda e: e.activation(out=sgT[:, c * 512:(c + 1) * 512], in_=pg[:], func=AF.Sigmoid), reads=[pg.r], writes=[sgT.r])
                for qc in range(NCH):
                    pO = banks[4 + (qc % 2)]; pL = banks[6 + (qc % 2)]
                    nkb = (qc + 1) * 4
                    def emit_S(kb):
                        q_lo = max(qc * 4, kb)
                        off = (q_lo - qc * 4) * 128
                        pS = banks[pcount[0] % 4]
                        Pb = Pt[pcount[0] % 3]
                        pcount[0] += 1
                        diag = kb >= qc * 4
                        S.op("pe", lambda e: e.matmul(pS[:, off:512], lhsT=kT[:, kb * 128:(kb + 1) * 128], rhs=qT[:, q_lo * 128:(qc + 1) * 512],
                                                      start=True, stop=False),
                             reads=[kT.r, qT.r], writes=[pS.r])
                        S.op("pe", lambda e: e.matmul(pS[:, off:512], lhsT=onesb[0:1, :], rhs=crow[0:1, q_lo * 128:(qc + 1) * 512], start=False, stop=not diag),
                             reads=[onesb.r, crow.r], writes=[pS.r])
                        if diag:
                            S.op("pe", lambda e: e.matmul(pS[:, off:off + 128], lhsT=identb[:], rhs=maskneg[:], start=False, stop=True),
                                 reads=[identb.r, maskneg.r], writes=[pS.r])
                        S.op("act", lambda e: e.activation(out=Pb[:, off:512], in_=pS[:, off:512], func=AF.Exp, bias=cum[:, kb, h:h + 1], scale=1.0),
                             reads=[pS.r, cum.r], writes=[Pb.r])
                        return (Pb, off)

                    def emit_PV(kb, Pb, off):
                        S.op("pe", lambda e: e.matmul(pO[:, off:512], lhsT=vv[:, kb, :], rhs=Pb[:, off:512], start=(kb == 0), stop=(kb == nkb - 1)),
                             reads=[vv.r, Pb.r], writes=[pO.r])
                        S.op("pe", lambda e: e.matmul(pL[:, off:512], lhsT=onesb[:], rhs=Pb[:, off:512], start=(kb == 0), stop=(kb == nkb - 1)),
                             reads=[onesb.r, Pb.r], writes=[pL.r])

                    pend = []
                    for kb in range(nkb):
                        pend.append((kb,) + emit_S(kb))
                        if len(pend) > 2:
                            emit_PV(*pend.pop(0))
                    while pend:
                        emit_PV(*pend.pop(0))
                    S.op("dve", lambda e: e.reciprocal(out=rec[:], in_=pL[:]), reads=[pL.r], writes=[rec.r])
                    S.op("dve", lambda e: e.tensor_tensor(out=yv[:], in0=pO[:], in1=rec[:], op=ALU.mult), reads=[pO.r, rec.r], writes=[yv.r])
                    ymb = ym[qc % 2]
                    S.op("dve", lambda e: e.tensor_tensor(out=ymb[:], in0=yv[:], in1=sgT[:, qc * 512:(qc + 1) * 512], op=ALU.mult),
                         reads=[yv.r, sgT.r], writes=[ymb.r])
                    S.dma("sp", "mst%d" % (qc % 2), mT_d[h * 128:(h + 1) * 128, qc * 512:(qc + 1) * 512], ymb[:], reads=[ymb.r])


        S.barrier()
        scopeAB.close()
        if stop_after == "B":
            S.wait_all_dma("sp")
            return nc

        LT = 512
        with ExitStack() as pc:
            def small(name, shape=(128, 16)):
                return sbuf(pc, name, list(shape), F32)
            a_re = small("a_re"); a_im = small("a_im"); lsl = small("lsl")
            S.dma("sp", "c_are", a_re[:], are_d, writes=[a_re.r])
            S.dma("sp", "c_aim", a_im[:], aim_d, writes=[a_im.r])
            S.dma("sp", "c_ls", lsl[:], ls_d, writes=[lsl.r])
            dsk = sbuf(pc, "dsk", [128, 4], F32)
            S.dma("sp", "c_dsk", dsk[:], dsk_d, writes=[dsk.r])
            Cpad = sbuf(pc, "Cpad", [128, 16, 2, 128], F32)
            Bpad = sbuf(pc, "Bpad", [128, 16, 2, 128], F32)
            Tc = sbuf(pc, "Tc", [128, 16, LT], F32); Ts = sbuf(pc, "Ts", [128, 16, LT], F32)
            wu = sbuf(pc, "wu", [128, KD, 512], BF16)
            step = small("step"); th = small("th"); mag = small("mag"); cs = small("cs"); sn = small("sn")
            ki = sbuf(pc, "ki", [128, 16], I32); kf = small("kf"); rr_ = small("rr_"); ab = small("ab")
            lre = small("lre"); lim = small("lim"); den = small("den"); t1 = small("t1"); t2 = small("t2")
            cfr = small("cfr"); cfi = small("cfi")
            wr_ = small("wr_"); wi_ = small("wi_"); wt1 = small("wt1"); wt2 = small("wt2")
            pset = ExitStack()
            alloc_stg(pset)
            bre = sbuf(pset, "bre", [128, 16, 16], F32); bim = sbuf(pset, "bim", [128, 16, 16], F32)
            cre = sbuf(pset, "cre", [128, 16, 16], F32); cim = sbuf(pset, "cim", [128, 16, 16], F32)
            S.dma("sp", "c_bre", bre[:], bre_d, writes=[bre.r]); S.dma("sp", "c_bim", bim[:], bim_d, writes=[bim.r])
            S.dma("sp", "c_cre", cre[:], cre_d, writes=[cre.r]); S.dma("sp", "c_cim", cim[:], cim_d, writes=[cim.r])
            bbr = sbuf(pset, "bbr", [128, 16, 16], F32); bbi = sbuf(pset, "bbi", [128, 16, 16], F32)
            u1 = sbuf(pset, "u1", [128, 16, 16], F32); u2 = sbuf(pset, "u2", [128, 16, 16], F32)
            BZ = sbuf(pset, "BZ", [128, 16, 2, 128], F32)
            p1 = sbuf(pset, "p1", [128, 16, LT // 2], F32); p2 = sbuf(pset, "p2", [128, 16, LT // 2], F32)

            def tt(eng, out, a, b, op, rd, wr):
                S.op(eng, lambda e: e.tensor_tensor(out=out, in0=a, in1=b, op=op), reads=rd, writes=wr)

            S.op("act", lambda e: e.activation(out=step[:], in_=lsl[:], func=AF.Exp), reads=[lsl.r], writes=[step.r])
            tt("dve", th[:], a_im[:], step[:], ALU.mult, [a_im.r, step.r], [th.r])
            tt("dve", mag[:], a_re[:], step[:], ALU.mult, [a_re.r, step.r], [mag.r])
            S.op("act", lambda e: e.activation(out=mag[:], in_=mag[:], func=AF.Exp), reads=[mag.r], writes=[mag.r])
            S.op("dve", lambda e: e.tensor_scalar(out=ki[:], in0=th[:], scalar1=1.0 / TWO_PI, scalar2=None, op0=ALU.mult), reads=[th.r], writes=[ki.r])
            S.op("dve", lambda e: e.tensor_copy(out=kf[:], in_=ki[:]), reads=[ki.r], writes=[kf.r])
            S.op("dve", lambda e: e.scalar_tensor_tensor(out=rr_[:], in0=kf[:], scalar=-TWO_PI, in1=th[:], op0=ALU.mult, op1=ALU.add),
                 reads=[kf.r, th.r], writes=[rr_.r])
            PI_LO = 3.1415925
            S.op("dve", lambda e: e.tensor_scalar(out=rr_[:], in0=rr_[:], scalar1=PI_LO, scalar2=-PI_LO, op0=ALU.min, op1=ALU.max), reads=[rr_.r], writes=[rr_.r])
            S.op("act", lambda e: e.activation(out=sn[:], in_=rr_[:], func=AF.Sin), reads=[rr_.r], writes=[sn.r])
            S.op("act", lambda e: e.activation(out=ab[:], in_=rr_[:], func=AF.Abs), reads=[rr_.r], writes=[ab.r])
            S.op("act", lambda e: e.activation(out=cs[:], in_=ab[:], func=AF.Sin, scale=-1.0, bias=hpi[:, 0:1]), reads=[ab.r, hpi.r], writes=[cs.r])
            tt("dve", lre[:], mag[:], cs[:], ALU.mult, [mag.r, cs.r], [lre.r])
            tt("dve", lim[:], mag[:], sn[:], ALU.mult, [mag.r, sn.r], [lim.r])
            S.op("dve", lambda e: e.tensor_scalar(out=lre[:], in0=lre[:], scalar1=-1.0, scalar2=None, op0=ALU.add), reads=[lre.r], writes=[lre.r])
            tt("dve", t1[:], a_re[:], a_re[:], ALU.mult, [a_re.r], [t1.r])
            tt("dve", t2[:], a_im[:], a_im[:], ALU.mult, [a_im.r], [t2.r])
            tt("dve", den[:], t1[:], t2[:], ALU.add, [t1.r, t2.r], [den.r])
            S.op("dve", lambda e: e.reciprocal(out=den[:], in_=den[:]), reads=[den.r], writes=[den.r])
            tt("dve", t1[:], lre[:], a_re[:], ALU.mult, [lre.r, a_re.r], [t1.r])
            tt("dve", t2[:], lim[:], a_im[:], ALU.mult, [lim.r, a_im.r], [t2.r])
            tt("dve", cfr[:], t1[:], t2[:], ALU.add, [t1.r, t2.r], [cfr.r])
            tt("dve", cfr[:], cfr[:], den[:], ALU.mult, [cfr.r, den.r], [cfr.r])
            tt("dve", t1[:], lim[:], a_re[:], ALU.mult, [lim.r, a_re.r], [t1.r])
            tt("dve", t2[:], lre[:], a_im[:], ALU.mult, [lre.r, a_im.r], [t2.r])
            tt("dve", cfi[:], t1[:], t2[:], ALU.subtract, [t1.r, t2.r], [cfi.r])
            tt("dve", cfi[:], cfi[:], den[:], ALU.mult, [cfi.r, den.r], [cfi.r])
            bc3 = lambda v: v[:].unsqueeze(2).to_broadcast([128, 16, 16])
            tt("dve", u1[:], bre[:], bc3(cfr), ALU.mult, [bre.r, cfr.r], [u1.r])
            tt("dve", u2[:], bim[:], bc3(cfi), ALU.mult, [bim.r, cfi.r], [u2.r])
            tt("dve", bbr[:], u1[:], u2[:], ALU.subtract, [u1.r, u2.r], [bbr.r])
            tt("dve", u1[:], bim[:], bc3(cfr), ALU.mult, [bim.r, cfr.r], [u1.r])
            tt("dve", u2[:], bre[:], bc3(cfi), ALU.mult, [bre.r, cfi.r], [u2.r])
            tt("dve", bbi[:], u1[:], u2[:], ALU.add, [u1.r, u2.r], [bbi.r])
            S.op("dve", lambda e: e.tensor_scalar(out=cim[:], in0=cim[:], scalar1=-1.0, scalar2=None, op0=ALU.mult), reads=[cim.r], writes=[cim.r])
            if True:
                S.op("pool", lambda e: e.memset(Cpad[:], 0.0), writes=[Cpad.r])
                S.op("pool", lambda e: e.memset(BZ[:], 0.0), writes=[BZ.r])
                for j in range(16):
                    for two in range(2):
                        c0 = 32 * (j % 4) + 16 * two
                        ps_ = slice(two * 64, (two + 1) * 64)
                        for ri, (srcC, srcB) in enumerate(((cre, bbr), (cim, bbi))):
                            S.op("pool", lambda e: e.tensor_copy(out=Cpad[ps_, j, ri, c0:c0 + 16], in_=srcC[ps_, j, :]), reads=[srcC.r], writes=[Cpad.r])
                            S.op("pool", lambda e: e.tensor_copy(out=BZ[ps_, j, ri, c0:c0 + 16], in_=srcB[ps_, j, :]), reads=[srcB.r], writes=[BZ.r])
                for j in range(16):
                    for ri in range(2):
                        bk = banks[(2 * j + ri) % 4]
                        S.op("pe", lambda e: e.transpose(out=bk[:, 0:128], in_=BZ[:, j, ri, :], identity=identf[:]), reads=[BZ.r, identf.r], writes=[bk.r])
                        S.op("act", lambda e: e.copy(out=Bpad[:, j, ri, :], in_=bk[:, 0:128]), reads=[bk.r], writes=[Bpad.r])
            S.op("dve", lambda e: e.tensor_copy(out=wr_[:], in_=cs[:]), reads=[cs.r], writes=[wr_.r])
            S.op("dve", lambda e: e.tensor_copy(out=wi_[:], in_=sn[:]), reads=[sn.r], writes=[wi_.r])
            S.op("pool", lambda e: e.memset(Tc[:, :, 0:1], 1.0), writes=[Tc.r])
            S.op("pool", lambda e: e.memset(Ts[:, :, 0:1], 0.0), writes=[Ts.r])
            if True:
                n = 1
                while n < LT:
                    bw = lambda v: v[:].unsqueeze(2).to_broadcast([128, 16, n])
                    tt("dve", p1[:, :, 0:n], Tc[:, :, 0:n], bw(wr_), ALU.mult, [Tc.r, wr_.r], [p1.r])
                    tt("dve", p2[:, :, 0:n], Ts[:, :, 0:n], bw(wi_), ALU.mult, [Ts.r, wi_.r], [p2.r])
                    tt("dve", Tc[:, :, n:2 * n], p1[:, :, 0:n], p2[:, :, 0:n], ALU.subtract, [p1.r, p2.r], [Tc.r])
                    tt("dve", p1[:, :, 0:n], Tc[:, :, 0:n], bw(wi_), ALU.mult, [Tc.r, wi_.r], [p1.r])
                    tt("dve", p2[:, :, 0:n], Ts[:, :, 0:n], bw(wr_), ALU.mult, [Ts.r, wr_.r], [p2.r])
                    tt("dve", Ts[:, :, n:2 * n], p1[:, :, 0:n], p2[:, :, 0:n], ALU.add, [p1.r, p2.r], [Ts.r])
                    tt("dve", wt1[:], wr_[:], wr_[:], ALU.mult, [wr_.r], [wt1.r])
                    tt("dve", wt2[:], wi_[:], wi_[:], ALU.mult, [wi_.r], [wt2.r])
                    tt("dve", wi_[:], wr_[:], wi_[:], ALU.mult, [wr_.r, wi_.r], [wi_.r])
                    S.op("dve", lambda e: e.tensor_scalar(out=wi_[:], in0=wi_[:], scalar1=2.0, scalar2=None, op0=ALU.mult), reads=[wi_.r], writes=[wi_.r])
                    tt("dve", wr_[:], wt1[:], wt2[:], ALU.subtract, [wt1.r, wt2.r], [wr_.r])
                    n *= 2
            wv_ = lambda buf, a, b: Buf(buf.t[:, :, a:b], buf.r)
            load_w(wu, w_in_d, C_U, 512, g1)
            S.barrier()
            pset.close()
            hTc = [sbuf(pc, "hTc%d" % i, [128, KD, 512], BF16) for i in range(1)]
            u_sb = sbuf(pc, "u_sb", [128, 4, 512], F32)
            ssets = []
            set_banks = ((2, 3), (5, 6), (7, 1))
            for si in range(3):
                B_ = {}
                for nm in ("xr_sb", "xi_sb", "ta", "tb", "tcb", "td", "r_r", "r_i"):
                    B_[nm] = sbuf(pc, "%s_%d" % (nm, si), [128, 512], F32)
                B_["ctmp"] = sbuf(pc, "ctmp_%d" % si, [128, 2], F32)
                B_["pxr"] = banks[set_banks[si][0]]; B_["pxi"] = banks[set_banks[si][1]]
                ssets.append(B_)
            carry = sbuf(pc, "carry", [128, 16, 2], F32)
            carry_r = [S.res("carry_%d" % j) for j in range(16)]
            yv_ = sbuf(pc, "yv_", [128, 512], F32)
            ygs = [sbuf(pc, "yg%d" % i, [128, 4, 512], BF16) for i in range(2)]
            print("SBUF remaining in phase C:", nc.sbuf_bytes_remaining)
            S.op("pool", lambda e: e.memset(carry[:], 0.0), writes=carry_r)
            v2 = lambda ap: ap.rearrange("p (a b) -> p a b", a=512 // LT)
            tb3 = lambda tab, j: tab[:, j, :].unsqueeze(1).to_broadcast([128, 512 // LT, LT])
            mT_v = mT_d.rearrange("(dc p) t -> p dc t", p=128)
            for c in range(NCH):
                hc = hTc[0]; yg = ygs[c % 2]
                S.dma("sp", "hcl", hc[:], hT_d[:, :, c * 512:(c + 1) * 512], writes=[hc.r])
                for rc in range(4):
                    pu = banks[rc % 2]
                    for k in range(KD):
                        S.op("pe", lambda e: e.matmul(pu[:], lhsT=wu[:, k, rc * 128:(rc + 1) * 128], rhs=hc[:, k, :], start=(k == 0), stop=(k == KD - 1)),
                             reads=[wu.r, hc.r], writes=[pu.r])
                    S.op("act", lambda e: e.copy(out=u_sb[:, rc, :], in_=pu[:]), reads=[pu.r], writes=[u_sb.r])
                def ssm_gen(j, B_):
                    rc = j // 4
                    pxr = B_["pxr"]; pxi = B_["pxi"]; py = banks[4]
                    xr_sb = B_["xr_sb"]; xi_sb = B_["xi_sb"]; ta = B_["ta"]; tb = B_["tb"]; tcb = B_["tcb"]; td = B_["td"]
                    r_r = B_["r_r"]; r_i = B_["r_i"]; ctmp = B_["ctmp"]; cr = carry_r[j]
                    xtr = ta; xti = tcb; s_r = ta; s_i = tcb
                    S.op("pe", lambda e: e.matmul(pxr[:], lhsT=Bpad[:, j, 0, :], rhs=u_sb[:, rc, :], start=True, stop=True), reads=[Bpad.r, u_sb.r], writes=[pxr.r])
                    S.op("pe", lambda e: e.matmul(pxi[:], lhsT=Bpad[:, j, 1, :], rhs=u_sb[:, rc, :], start=True, stop=True), reads=[Bpad.r, u_sb.r], writes=[pxi.r])
                    yield
                    S.op("act", lambda e: e.copy(out=xr_sb[:], in_=pxr[:]), reads=[pxr.r], writes=[xr_sb.r])
                    S.op("act", lambda e: e.copy(out=xi_sb[:], in_=pxi[:]), reads=[pxi.r], writes=[xi_sb.r])
                    yield
                    tt("dve", v2(ta[:]), v2(xr_sb[:]), tb3(Tc, j), ALU.mult, [xr_sb.r, Tc.r], [ta.r])
                    tt("pool", v2(tb[:]), v2(xi_sb[:]), tb3(Ts, j), ALU.mult, [xi_sb.r, Ts.r], [tb.r])
                    yield
                    tt("dve", v2(tcb[:]), v2(xi_sb[:]), tb3(Tc, j), ALU.mult, [xi_sb.r, Tc.r], [tcb.r])
                    tt("pool", v2(td[:]), v2(xr_sb[:]), tb3(Ts, j), ALU.mult, [xr_sb.r, Ts.r], [td.r])
                    yield
                    tt("dve", xtr[:], ta[:], tb[:], ALU.add, [ta.r, tb.r], [xtr.r])
                    tt("pool", xti[:], tcb[:], td[:], ALU.subtract, [tcb.r, td.r], [xti.r])
                    yield
                    for sgi in range(512 // LT):
                        sl = slice(sgi * LT, (sgi + 1) * LT)
                        magb = mag[:, j:j + 1].to_broadcast([128, LT])
                        S.op("dve", lambda e: e.tensor_tensor_scan(out=r_r[:, sl], data0=magb, data1=xtr[:, sl], initial=carry[:, j, 0:1], op0=ALU.mult, op1=ALU.add),
                             reads=[mag.r, xtr.r, cr], writes=[r_r.r])
                        S.op("dve", lambda e: e.tensor_tensor_scan(out=r_i[:, sl], data0=magb, data1=xti[:, sl], initial=carry[:, j, 1:2], op0=ALU.mult, op1=ALU.add),
                             reads=[mag.r, xti.r, cr], writes=[r_i.r])
                        yield
                        last = sgi * LT + LT - 1
                        tt("dve", ctmp[:, 0:1], r_i[:, last:last + 1], wi_[:, j:j + 1], ALU.mult, [r_i.r, wi_.r], [ctmp.r])
                        tt("dve", ctmp[:, 1:2], r_i[:, last:last + 1], wr_[:, j:j + 1], ALU.mult, [r_i.r, wr_.r], [ctmp.r])
                        yield
                        S.op("dve", lambda e: e.scalar_tensor_tensor(out=carry[:, j, 0:1], in0=r_r[:, last:last + 1], scalar=wr_[:, j:j + 1], in1=ctmp[:, 0:1],
                                                                     op0=ALU.mult, op1=ALU.subtract), reads=[r_r.r, wr_.r, ctmp.r], writes=[cr])
                        S.op("dve", lambda e: e.scalar_tensor_tensor(out=carry[:, j, 1:2], in0=r_r[:, last:last + 1], scalar=wi_[:, j:j + 1], in1=ctmp[:, 1:2],
                                                                     op0=ALU.mult, op1=ALU.add), reads=[r_r.r, wi_.r, ctmp.r], writes=[cr])
                        yield
                    tt("dve", v2(ta[:]), v2(r_r[:]), tb3(Tc, j), ALU.mult, [r_r.r, Tc.r], [ta.r])
                    tt("pool", v2(tb[:]), v2(r_i[:]), tb3(Ts, j), ALU.mult, [r_i.r, Ts.r], [tb.r])
                    yield
                    tt("pool", v2(tcb[:]), v2(r_r[:]), tb3(Ts, j), ALU.mult, [r_r.r, Ts.r], [tcb.r])
                    tt("dve", v2(td[:]), v2(r_i[:]), tb3(Tc, j), ALU.mult, [r_i.r, Tc.r], [td.r])
                    yield
                    tt("dve", s_r[:], ta[:], tb[:], ALU.subtract, [ta.r, tb.r], [s_r.r])
                    tt("pool", s_i[:], tcb[:], td[:], ALU.add, [tcb.r, td.r], [s_i.r])
                    yield
                    S.op("pe", lambda e: e.matmul(py[:], lhsT=Cpad[:, j, 0, :], rhs=s_r[:], start=(j % 4 == 0), stop=False), reads=[Cpad.r, s_r.r], writes=[py.r])
                    S.op("pe", lambda e: e.matmul(py[:], lhsT=Cpad[:, j, 1, :], rhs=s_i[:], start=False, stop=(j % 4 == 3)), reads=[Cpad.r, s_i.r], writes=[py.r])
                    yield
                    if j % 4 == 3:
                        S.op("dve", lambda e: e.scalar_tensor_tensor(out=yv_[:], in0=u_sb[:, rc, :], scalar=dsk[:, rc:rc + 1], in1=py[:], op0=ALU.mult, op1=ALU.add),
                             reads=[u_sb.r, dsk.r, py.r], writes=[yv_.r])
                        S.op("act", lambda e: e.activation(out=yg[:, rc, :], in_=yv_[:], func=AF.Gelu_apprx_tanh), reads=[yv_.r], writes=[yg.r])

                SKEW = 4
                active = []
                nxt_j = 0
                while nxt_j < 16 or active:
                    if nxt_j < 16 and len(active) < 3 and (not active or active[-1][1] >= SKEW):
                        active.append([ssm_gen(nxt_j, ssets[nxt_j % 3]), 0])
                        nxt_j += 1
                    for ent in list(active):
                        try:
                            next(ent[0])
                            ent[1] += 1
                        except StopIteration:
                            active.remove(ent)
                S.dma("sp", "ygst%d" % (c % 2), yg_d[:, :, c * 512:(c + 1) * 512], yg[:], reads=[yg.r])
            S.barrier()

        with ExitStack() as pc2:
            g2 = sbuf(pc2, "g2", [128, D], F32)
            S.dma("sp", "c_g2", g2[:], g2_d, writes=[g2.r])
            wgs = sbuf(pc2, "wgs", [128, KD, D], BF16)
            wglu = sbuf(pc2, "wglu", [128, 4, 2048], BF16)
            wout = sbuf(pc2, "wout", [128, KD, D], BF16)
            with ExitStack() as pw2:
                alloc_stg(pw2)
                for hh in range(2):
                    load_w(wv_(wgs, hh * 512, (hh + 1) * 512), w_in_d, C_GS + hh * 512, 512, g1)
                    load_w(wv_(wout, hh * 512, (hh + 1) * 512), wout_d, hh * 512, 512, None)
                for qq in range(4):
                    load_w(wv_(wglu, qq * 512, (qq + 1) * 512), wglu_d, qq * 512, 512, None, nk=4)
                S.barrier()
            hTc2 = [sbuf(pc2, "hTc2_%d" % i, [128, KD, 512], BF16) for i in range(2)]
            matt = [sbuf(pc2, "matt%d" % i, [128, KD, 512], BF16) for i in range(2)]
            ygc = [sbuf(pc2, "ygc%d" % i, [128, 4, 512], BF16) for i in range(2)]
            sg1 = sbuf(pc2, "sg1", [128, 512], F32); sg2 = sbuf(pc2, "sg2", [128, 512], F32)
            ys = sbuf(pc2, "ys", [128, 512], F32)
            merged = sbuf(pc2, "merged", [128, KD, 512], BF16)
            xres = [sbuf(pc2, "xres%d" % i, [128, D], F32) for i in range(2)]
            x1t = [sbuf(pc2, "x1t%d" % i, [128, D], F32) for i in range(2)]
            h2b = [sbuf(pc2, "h2b%d" % i, [128, D], BF16) for i in range(2)]
            h2c = [sbuf(pc2, "h2c%d" % i, [128, KD, 128], BF16) for i in range(2)]
            ss2 = sbuf(pc2, "ss2", [128, NT], F32); rs2 = sbuf(pc2, "rs2", [128, NT], F32)
            for c in range(NCH):
                hc = hTc2[c % 2]; mt = matt[c % 2]; yg = ygc[c % 2]
                S.dma("sp", "hcl2_%d" % (c % 2), hc[:], hT_d[:, :, c * 512:(c + 1) * 512], writes=[hc.r])
                S.dma("sp", "mtl%d" % (c % 2), mt[:], mT_v[:, :, c * 512:(c + 1) * 512], writes=[mt.r])
                S.dma("sp", "ygl%d" % (c % 2), yg[:], yg_d[:, :, c * 512:(c + 1) * 512], writes=[yg.r])
                for dc in range(KD):
                    pvl = banks[5]; pgt = banks[6]; pgs = banks[7]
                    for rc in range(4):
                        S.op("pe", lambda e: e.matmul(pvl[:], lhsT=wglu[:, rc, dc * 128:(dc + 1) * 128], rhs=yg[:, rc, :], start=(rc == 0), stop=(rc == 3)),
                             reads=[wglu.r, yg.r], writes=[pvl.r])
                    for rc in range(4):
                        S.op("pe", lambda e: e.matmul(pgt[:], lhsT=wglu[:, rc, D + dc * 128:D + (dc + 1) * 128], rhs=yg[:, rc, :], start=(rc == 0), stop=(rc == 3)),
                             reads=[wglu.r, yg.r], writes=[pgt.r])
                    for k in range(KD):
                        S.op("pe", lambda e: e.matmul(pgs[:], lhsT=wgs[:, k, dc * 128:(dc + 1) * 128], rhs=hc[:, k, :], start=(k == 0), stop=(k == KD - 1)),
                             reads=[wgs.r, hc.r], writes=[pgs.r])
                    S.op("act", lambda e: e.activation(out=sg1[:], in_=pgt[:], func=AF.Sigmoid), reads=[pgt.r], writes=[sg1.r])
                    S.op("act", lambda e: e.activation(out=sg2[:], in_=pgs[:], func=AF.Sigmoid), reads=[pgs.r], writes=[sg2.r])
                    tt("dve", ys[:], pvl[:], sg1[:], ALU.mult, [pvl.r, sg1.r], [ys.r])
                    tt("pool", ys[:], ys[:], sg2[:], ALU.mult, [ys.r, sg2.r], [ys.r])
                    tt("pool", merged[:, dc, :], ys[:], mt[:, dc, :], ALU.add, [ys.r, mt.r], [merged.r])
                if dbg:
                    S.dma("sp", "m2st", m2T_d[:, :, c * 512:(c + 1) * 512], merged[:], reads=[merged.r])
                for ti in range(4):
                    i = c * 4 + ti
                    xr_ = xres[i % 2]; x1 = x1t[i % 2]; hb = h2b[i % 2]; hcp = h2c[i % 2]; junkC = hb
                    S.dma("sp", "xrl%d" % (i % 2), xr_[:], x_d[i * 128:(i + 1) * 128, :], writes=[xr_.r])
                    for nh in range(2):
                        po = banks[nh]
                        for dc in range(KD):
                            S.op("pe", lambda e: e.matmul(po[:], lhsT=merged[:, dc, ti * 128:(ti + 1) * 128], rhs=wout[:, dc, nh * 512:(nh + 1) * 512],
                                                          start=(dc == 0), stop=(dc == KD - 1)), reads=[merged.r, wout.r], writes=[po.r])
                        tt("dve", x1[:, nh * 512:(nh + 1) * 512], po[:], xr_[:, nh * 512:(nh + 1) * 512], ALU.add, [po.r, xr_.r], [x1.r])
                    S.dma("sp", "x1st%d" % (i % 2), out_d[i * 128:(i + 1) * 128, :], x1[:], reads=[x1.r])
                    S.op("act", lambda e: e.activation(out=junkC[:], in_=x1[:], func=AF.Square, accum_out=ss2[:, i:i + 1]), reads=[x1.r], writes=[junkC.r, ss2.r])
                    S.op("act", lambda e: e.activation(out=rs2[:, i:i + 1], in_=ss2[:, i:i + 1], func=AF.Sqrt, bias=EPS, scale=1.0 / D), reads=[ss2.r], writes=[rs2.r])
                    S.op("dve", lambda e: e.reciprocal(out=rs2[:, i:i + 1], in_=rs2[:, i:i + 1]), reads=[rs2.r], writes=[rs2.r])
                    S.op("dve", lambda e: e.scalar_tensor_tensor(out=hb[:], in0=x1[:], scalar=rs2[:, i:i + 1], in1=g2[:], op0=ALU.mult, op1=ALU.mult), reads=[x1.r, rs2.r, g2.r], writes=[hb.r])
                    bk = banks[2 + (i % 2)]
                    bv = bk[:].bitcast(BF16)
                    for k in range(KD):
                        S.op("pe", lambda e: e.transpose(out=bv[:, k * 128:(k + 1) * 128], in_=hb[:, k * 128:(k + 1) * 128], identity=identb[:]),
                             reads=[hb.r, identb.r], writes=[bk.r])
                    S.op("act", lambda e: e.copy(out=hcp[:], in_=bv[:, 0:1024].rearrange("p (k t) -> p k t", k=KD)), reads=[bk.r], writes=[hcp.r])
                    S.dma("sp", "h2st%d" % (i % 2), h2T_d[:, :, i * 128:(i + 1) * 128], hcp[:], reads=[hcp.r])
            S.barrier()
        if stop_after == "C":
            S.wait_all_dma("sp")
            return nc

        with ExitStack() as pd:
            alloc_stg(pd)
            wq = sbuf(pd, "wqp", [128, KD, 2048], BF16)
            skT = sbuf(pd, "skT", [128, 16, 128], BF16)
            for qq in range(4):
                load_w(Buf(wq.t[:, :, qq * 512:(qq + 1) * 512], wq.r), wqr_d, qq * 512, 512, None)
            for hh in range(2):
                load_w(Buf(skT.t[:, hh * 8:(hh + 1) * 8, :], skT.r), skT_d[:, hh * 8:(hh + 1) * 8, :], 0, 128, None)
            iota16 = sbuf(pd, "iota16", [128, 16], F32)
            S.op("pool", lambda e: e.iota(iota16[:], pattern=[[1, 16]], base=0, channel_multiplier=0, allow_small_or_imprecise_dtypes=True), writes=[iota16.r])
            h2c_ = [sbuf(pd, "h2cD%d" % i, [128, KD, 512], BF16) for i in range(2)]
            qTs = sbuf(pd, "qTs", [128, 16, 512], BF16)
            sc = sbuf(pd, "sc", [128, 16, 128], F32); sc2 = sbuf(pd, "sc2", [128, 16, 128], F32)
            v16s = [sbuf(pd, "v16_%d" % i, [128, 16, 16], F32) for i in range(2)]
            i16s = [sbuf(pd, "i16_%d" % i, [128, 16, 16], U32) for i in range(2)]
            i16fs = [sbuf(pd, "i16f_%d" % i, [128, 16, 16], F32) for i in range(2)]
            cand = sbuf(pd, "cand", [128, 8, 256], F32); cand2 = sbuf(pd, "cand2", [128, 8, 256], F32)
            bests = [sbuf(pd, "best_%d" % i, [128, 8, 16], F32) for i in range(2)]
            poss = [sbuf(pd, "pos_%d" % i, [128, 8, 16], U32) for i in range(2)]
            pa_is = [sbuf(pd, "pa_i_%d" % i, [128, 8, 16], I32) for i in range(2)]
            pb_is = [sbuf(pd, "pb_i_%d" % i, [128, 8, 16], I32) for i in range(2)]
            pa_f = sbuf(pd, "pa_f", [128, 8, 16], F32); pb_f = sbuf(pd, "pb_f", [128, 8, 16], F32)
            ohs = [sbuf(pd, "oh%d" % i, [128, 8, 16, 16], F32) for i in range(4)]; pr = sbuf(pd, "pr", [128, 8, 16, 16], F32)
            gpre = sbuf(pd, "gpre", [128, 8, 16], F32); gjunk = sbuf(pd, "gjunk", [128, 8, 16], F32)
            rt3s = [sbuf(pd, "rt3_%d" % i, [128, 3, 128], F32) for i in range(2)]
            esum = sbuf(pd, "esum", [128, 8], F32)
            rtT = [sbuf(pd, "rtT%d" % i, [128, 3, 128], F32) for i in range(2)]
            print("SBUF remaining in phase D0:", nc.sbuf_bytes_remaining)
            def d0_chunk(c):
                hc = h2c_[c % 2]
                S.dma("sp", "h2l%d" % (c % 2), hc[:], h2T_d[:, :, c * 512:(c + 1) * 512], writes=[hc.r])
                for b in range(16):
                    pq = banks[4 + (b % 2)]
                    for k in range(KD):
                        S.op("pe", lambda e: e.matmul(pq[:], lhsT=wq[:, k, b * 128:(b + 1) * 128], rhs=hc[:, k, :], start=(k == 0), stop=(k == KD - 1)),
                             reads=[wq.r, hc.r], writes=[pq.r])
                    S.op("act", lambda e: e.copy(out=qTs[:, b, :], in_=pq[:]), reads=[pq.r], writes=[qTs.r])

            def d0_hdr(i):
                return (v16s[i % 2], i16s[i % 2], i16fs[i % 2], bests[i % 2], poss[i % 2], pa_is[i % 2], pb_is[i % 2], rt3s[i % 2])

            def d0_front(i):
                c, ti = divmod(i, 4)
                v16, i16, i16f, best, pos, pa_i, pb_i, rt3 = d0_hdr(i)
                for b in range(16):
                    bk = banks[b // 4]
                    S.op("pe", lambda e: e.matmul(bk[:, (b % 4) * 128:(b % 4 + 1) * 128], lhsT=qTs[:, b, ti * 128:(ti + 1) * 128], rhs=skT[:, b, :], start=True, stop=True),
                         reads=[qTs.r, skT.r], writes=[bk.r])
                for q4 in range(4):
                    S.op("act", lambda e: e.copy(out=sc[:, q4 * 4:(q4 + 1) * 4, :], in_=banks[q4][:].rearrange("p (a b) -> p a b", a=4)),
                         reads=[banks[q4].r], writes=[sc.r])
                for b in range(16):
                    S.op("dve", lambda e: e.max(out=v16[:, b, 0:8], in_=sc[:, b, :]), reads=[sc.r], writes=[v16.r])
                for b in range(16):
                    S.op("dve", lambda e: e.max_index(out=i16[:, b, 0:8], in_max=v16[:, b, 0:8], in_values=sc[:, b, :]), reads=[sc.r, v16.r], writes=[i16.r])
                for b in range(16):
                    S.op("dve", lambda e: e.match_replace(out=sc2[:, b, :], in_to_replace=v16[:, b, 0:8], in_values=sc[:, b, :], imm_value=-1e30),
                         reads=[sc.r, v16.r], writes=[sc2.r])
                for b in range(16):
                    S.op("dve", lambda e: e.max(out=v16[:, b, 8:16], in_=sc2[:, b, :]), reads=[sc2.r], writes=[v16.r])
                for b in range(16):
                    S.op("dve", lambda e: e.max_index(out=i16[:, b, 8:16], in_max=v16[:, b, 8:16], in_values=sc2[:, b, :]), reads=[sc2.r, v16.r], writes=[i16.r])
                v4 = v16[:].rearrange("p (h c) k -> p h c k", c=2)
                S.op("dve", lambda e: e.tensor_tensor(out=cand[:].rearrange("p h (a b) -> p h a b", a=16),
                                                      in0=v4[:, :, 0, :].unsqueeze(3).to_broadcast([128, 8, 16, 16]),
                                                      in1=v4[:, :, 1, :].unsqueeze(2).to_broadcast([128, 8, 16, 16]), op=ALU.add),
                     reads=[v16.r], writes=[cand.r])
                for h in range(8):
                    S.op("dve", lambda e: e.max(out=best[:, h, 0:8], in_=cand[:, h, :]), reads=[cand.r], writes=[best.r])
                for h in range(8):
                    S.op("dve", lambda e: e.max_index(out=pos[:, h, 0:8], in_max=best[:, h, 0:8], in_values=cand[:, h, :]), reads=[cand.r, best.r], writes=[pos.r])
                for h in range(8):
                    S.op("dve", lambda e: e.match_replace(out=cand2[:, h, :], in_to_replace=best[:, h, 0:8], in_values=cand[:, h, :], imm_value=-1e30),
                         reads=[cand.r, best.r], writes=[cand2.r])
                for h in range(8):
                    S.op("dve", lambda e: e.max(out=best[:, h, 8:16], in_=cand2[:, h, :]), reads=[cand2.r], writes=[best.r])
                for h in range(8):
                    S.op("dve", lambda e: e.max_index(out=pos[:, h, 8:16], in_max=best[:, h, 8:16], in_values=cand2[:, h, :]), reads=[cand2.r, best.r], writes=[pos.r])
                S.op("dve", lambda e: e.tensor_single_scalar(out=pa_i[:], in_=pos[:].bitcast(I32), scalar=4, op=ALU.arith_shift_right), reads=[pos.r], writes=[pa_i.r])
                S.op("dve", lambda e: e.tensor_single_scalar(out=pb_i[:], in_=pos[:].bitcast(I32), scalar=15, op=ALU.bitwise_and), reads=[pos.r], writes=[pb_i.r])
                oh_a = ohs[(i % 2) * 2]; oh_b = ohs[(i % 2) * 2 + 1]
                S.op("dve", lambda e: e.tensor_copy(out=pa_f[:], in_=pa_i[:]), reads=[pa_i.r], writes=[pa_f.r])
                S.op("dve", lambda e: e.tensor_copy(out=pb_f[:], in_=pb_i[:]), reads=[pb_i.r], writes=[pb_f.r])
                for pf_, oh_ in ((pa_f, oh_a), (pb_f, oh_b)):
                    S.op("dve", lambda e: e.tensor_tensor(out=oh_[:], in0=pf_[:].unsqueeze(3).to_broadcast([128, 8, 16, 16]),
                                                          in1=iota16[:].unsqueeze(1).unsqueeze(1).to_broadcast([128, 8, 16, 16]), op=ALU.is_equal),
                         reads=[pf_.r, iota16.r], writes=[oh_.r])

            def d0_back(i):
                v16, i16, i16f, best, pos, pa_i, pb_i, rt3 = d0_hdr(i)
                oh_a = ohs[(i % 2) * 2]; oh_b = ohs[(i % 2) * 2 + 1]
                S.op("pool", lambda e: e.tensor_copy(out=i16f[:], in_=i16[:]), reads=[i16.r], writes=[i16f.r])
                gat = rt3[:, 2, :].rearrange("p (h k) -> p h k", h=8)
                S.op("pool", lambda e: e.tensor_tensor(out=gpre[:], in0=best[:], in1=best[:, :, 0:1].to_broadcast([128, 8, 16]), op=ALU.subtract),
                     reads=[best.r], writes=[gpre.r])
                for h in range(8):
                    S.op("act", lambda e: e.activation(out=gjunk[:, h, :], in_=gpre[:, h, :], func=AF.Exp, accum_out=esum[:, h:h + 1]),
                         reads=[gpre.r], writes=[gjunk.r, esum.r])
                S.op("act", lambda e: e.activation(out=esum[:], in_=esum[:], func=AF.Ln), reads=[esum.r], writes=[esum.r])
                S.op("pool", lambda e: e.tensor_tensor(out=gpre[:], in0=gpre[:], in1=esum[:].unsqueeze(2).to_broadcast([128, 8, 16]), op=ALU.subtract),
                     reads=[gpre.r, esum.r], writes=[gpre.r])
                S.op("act", lambda e: e.activation(out=gat, in_=gpre[:], func=AF.Exp), reads=[gpre.r], writes=[rt3.r])
                i4 = i16f[:].rearrange("p (h c) k -> p h c k", c=2)
                for which, oh_ in enumerate((oh_a, oh_b)):
                    S.op("pool", lambda e: e.tensor_tensor(out=pr[:], in0=oh_[:], in1=i4[:, :, which, :].unsqueeze(2).to_broadcast([128, 8, 16, 16]), op=ALU.mult),
                         reads=[oh_.r, i16f.r], writes=[pr.r])
                    S.op("pool", lambda e: e.tensor_tensor(out=pr[:, :, :, 0:8], in0=pr[:, :, :, 0:8], in1=pr[:, :, :, 8:16], op=ALU.add), reads=[pr.r], writes=[pr.r])
                    S.op("pool", lambda e: e.tensor_tensor(out=pr[:, :, :, 0:4], in0=pr[:, :, :, 0:4], in1=pr[:, :, :, 4:8], op=ALU.add), reads=[pr.r], writes=[pr.r])
                    S.op("pool", lambda e: e.tensor_tensor(out=pr[:, :, :, 0:2], in0=pr[:, :, :, 0:2], in1=pr[:, :, :, 2:4], op=ALU.add), reads=[pr.r], writes=[pr.r])
                    S.op("pool", lambda e: e.tensor_tensor(out=rt3[:, which, :].rearrange("p (h k) -> p h k", h=8), in0=pr[:, :, :, 0], in1=pr[:, :, :, 1], op=ALU.add),
                         reads=[pr.r], writes=[rt3.r])
                rtt = rtT[i % 2]
                for q3 in range(3):
                    bk = banks[6 + (q3 % 2)]
                    S.op("pe", lambda e: e.transpose(out=bk[:, 0:128], in_=rt3[:, q3, :], identity=identf[:]), reads=[rt3.r, identf.r], writes=[bk.r])
                    S.op("act", lambda e: e.copy(out=rtt[:, q3, :], in_=bk[:, 0:128]), reads=[bk.r], writes=[rtt.r])
                S.dma("sp", "rtst%d" % (i % 2), rt_d[:, :, i * 128:(i + 1) * 128], rtt[:], reads=[rtt.r])


            for i in range(NT + 1):
                if i < NT:
                    if i % 4 == 0:
                        d0_chunk(i // 4)
                    d0_front(i)
                if i >= 1:
                    d0_back(i - 1)

            S.barrier()
        if stop_after == "D0":
            S.wait_all_dma("sp")
            return nc

        G = 256
        NG = T // G
        with ExitStack() as pe_:
            iotaf = sbuf(pe_, "iotaf", [128, 128], F32)
            S.op("pool", lambda e: e.iota(iotaf[:], pattern=[[1, 128]], base=0, channel_multiplier=0, allow_small_or_imprecise_dtypes=True), writes=[iotaf.r])
            GTs = [sbuf(pe_, "GT%d" % i, [128, G, 128], BF16) for i in range(2)]
            ub = [sbuf(pe_, "ub%d" % i, [128, KD, 512], BF16) for i in range(3)]
            vb = [sbuf(pe_, "vb%d" % i, [128, 4, D], BF16) for i in range(3)]
            rtgs = [sbuf(pe_, "rtg%d" % i, [128, 3, G], F32) for i in range(2)]
            h2g = sbuf(pe_, "h2g", [128, KD, G], BF16)
            Ab = [sbuf(pe_, "Ab%d" % i, [128, G], BF16) for i in range(2)]
            WT = [sbuf(pe_, "WT%d" % i, [128, G], BF16) for i in range(2)]
            P1 = [sbuf(pe_, "P1_%d" % i, [128, 128], BF16) for i in range(8)]
            P2B = [sbuf(pe_, "P2B_%d" % i, [128, 8, 128], BF16) for i in range(2)]
            x1g = [sbuf(pe_, "x1g%d" % i, [128, D], F32) for i in range(2)]
            print("SBUF remaining in phase D1:", nc.sbuf_bytes_remaining)
            uv_loaded = [False]
            gt_cnt = [0]

            def gt_gen(g):
                GT = GTs[g % 2]; rtg = rtgs[g % 2]
                S.dma("sp", "rtl%d" % (g % 2), rtg[:], rt_d[:, :, g * G:(g + 1) * G], writes=[rtg.r])
                LAG = 4
                p2bs = {}

                def gt_mm(t):
                    t8, t_ = divmod(t, 8)
                    p1 = P1[t % 8]; p2b = p2bs[t8]
                    gp = banks[6 + ((t // 4) % 2)]
                    S.op("pe", lambda e: e.matmul(gp[:, (t % 4) * 128:(t % 4 + 1) * 128], lhsT=p2b[:, t_, :], rhs=p1[:], start=True, stop=True),
                         reads=[p1.r, p2b.r], writes=[gp.r])
                    if t % 4 == 3:
                        S.op("act", lambda e: e.copy(out=GT[:, t - 3:t + 1, :], in_=gp[:].rearrange("p (a b) -> p a b", a=4)), reads=[gp.r], writes=[GT.r])

                for t in range(G):
                    t8, t_ = divmod(t, 8)
                    if t_ == 0:
                        p2b = P2B[gt_cnt[0] % 2]
                        gt_cnt[0] += 1
                        p2bs[t8] = p2b
                        S.op("dve", lambda e: e.tensor_tensor(out=p2b[:], in0=iotaf[:].unsqueeze(1).to_broadcast([128, 8, 128]),
                                                              in1=rtg[:, 1, t8 * 8:(t8 + 1) * 8].unsqueeze(2).to_broadcast([128, 8, 128]), op=ALU.is_equal),
                             reads=[iotaf.r, rtg.r], writes=[p2b.r])
                    p1 = P1[t % 8]
                    S.op("dve", lambda e: e.tensor_scalar(out=p1[:], in0=iotaf[:], scalar1=rtg[:, 0, t:t + 1], scalar2=rtg[:, 2, t:t + 1], op0=ALU.is_equal, op1=ALU.mult),
                         reads=[iotaf.r, rtg.r], writes=[p1.r])
                    if t >= LAG:
                        gt_mm(t - LAG)
                    yield
                for t in range(G - LAG, G):
                    gt_mm(t)
                yield

            def drain(gen, n=None):
                k = 0
                while n is None or k < n:
                    try:
                        next(gen)
                    except StopIteration:
                        return
                    k += 1

            drain(gt_gen(0))
            for g in range(NG):
                GT = GTs[g % 2]
                nxt = gt_gen(g + 1) if g + 1 < NG else iter(())
                S.dma("sp", "h2gl", h2g[:], h2T_d[:, :, g * G:(g + 1) * G], writes=[h2g.r])
                def emit_S(i1):
                    blk4, bi = divmod(i1, 4)
                    u_ = ub[blk4 % 3]
                    pS = banks[4 + (i1 % 2)]
                    for k in range(KD):
                        S.op("pe", lambda e: e.matmul(pS[:, 0:G], lhsT=u_[:, k, bi * 128:(bi + 1) * 128], rhs=h2g[:, k, :], start=(k == 0), stop=(k == KD - 1)),
                             reads=[u_.r, h2g.r], writes=[pS.r])
                    ab_ = Ab[i1 % 2]; wt_ = WT[i1 % 2]
                    S.op("act", lambda e: e.activation(out=ab_[:], in_=pS[:, 0:G], func=AF.Gelu_apprx_tanh), reads=[pS.r], writes=[ab_.r])
                    eng = "dve" if i1 % 2 == 0 else "pool"
                    S.op(eng, lambda e: e.tensor_tensor(out=wt_[:], in0=ab_[:], in1=GT[:, :, i1], op=ALU.mult), reads=[ab_.r, GT.r], writes=[wt_.r])

                def emit_out(i1):
                    blk4, bi = divmod(i1, 4)
                    v_ = vb[blk4 % 3]; wt_ = WT[i1 % 2]
                    for tt_ in range(G // 128):
                        for nh in range(2):
                            acc = banks[tt_ * 2 + nh]
                            S.op("pe", lambda e: e.matmul(acc[:], lhsT=wt_[:, tt_ * 128:(tt_ + 1) * 128], rhs=v_[:, bi, nh * 512:(nh + 1) * 512],
                                                          start=(i1 == 0), stop=(i1 == 127)), reads=[wt_.r, v_.r], writes=[acc.r])

                for i1 in range(128):
                    blk4, bi = divmod(i1, 4)
                    if bi == 0:
                        if not uv_loaded[0]:
                            for tok in cast_toks:
                                S._wait("sp", tok)
                            uv_loaded[0] = True
                        u_ = ub[blk4 % 3]; v_ = vb[blk4 % 3]
                        S.dma("sp", "ul%d" % (blk4 % 3), u_[:], UTb_d[:, :, blk4 * 512:(blk4 + 1) * 512], writes=[u_.r])
                        S.dma("sp", "vl%d" % (blk4 % 3), v_[:], Vb_d[:, blk4 * 4:(blk4 + 1) * 4, :], writes=[v_.r])
                    emit_S(i1)
                    if i1 >= 1:
                        emit_out(i1 - 1)
                    drain(nxt, 2)
                emit_out(127)
                drain(nxt)
                for tt_ in range(G // 128):
                    i = g * (G // 128) + tt_
                    xg = x1g[i % 2]
                    S.dma("sp", "x1l%d" % (i % 2), xg[:], out_d[i * 128:(i + 1) * 128, :], writes=[xg.r])
                    for nh in range(2):
                        acc = banks[tt_ * 2 + nh]
                        S.op("dve", lambda e: e.tensor_tensor(out=xg[:, nh * 512:(nh + 1) * 512], in0=acc[:], in1=xg[:, nh * 512:(nh + 1) * 512], op=ALU.add),
                             reads=[acc.r, xg.r], writes=[xg.r])
                    S.dma("sp", "ost%d" % (i % 2), out_d[i * 128:(i + 1) * 128, :], xg[:], reads=[xg.r])
            S.barrier()

        S.wait_all_dma("sp")
        print("program built: ops=%d waits=%d" % (S.nops, S.nwaits))
    return nc


def make_in_maps(inputs, T, n_cores):
    f = lambda a: np.ascontiguousarray(np.asarray(a, dtype=np.float32))
    w_in = f(inputs["w_in"][0]).reshape(KD, 128, INC).transpose(1, 0, 2)
    common = {
        "w_in": f(w_in),
        "g1": f(f(inputs["mix_norm_g"][0]).reshape(KD, 128).T),
        "fb": f(np.broadcast_to(f(inputs["fox_forget_bias"][0])[None, :], (128, 8))),
        "qg": f(f(inputs["q_norm_g"][0]).reshape(128, 1)),
        "kg": f(f(inputs["k_norm_g"][0]).reshape(128, 1)),
    }
    L16 = lambda a: f(f(a).reshape(16, 2, 64).transpose(1, 2, 0).reshape(128, 16))
    common["a_re_l"] = L16(inputs["ssm_a_re"][0])
    common["a_im_l"] = L16(inputs["ssm_a_im"][0])
    common["ls_l"] = f(np.broadcast_to(f(inputs["ssm_log_step"][0]).reshape(16, 2, 1), (16, 2, 64)).transpose(1, 2, 0).reshape(128, 16))
    LB = lambda a: f(f(a).reshape(16, 2, 64, 16).transpose(1, 2, 0, 3).reshape(128, 16, 16))
    LC = lambda a: f(f(a).reshape(16, 2, 16, 64).transpose(1, 3, 0, 2).reshape(128, 16, 16))
    common["b_re_l"] = LB(inputs["ssm_b_re"][0]); common["b_im_l"] = LB(inputs["ssm_b_im"][0])
    common["c_re_l"] = LC(inputs["ssm_c_re"][0]); common["c_im_l"] = LC(inputs["ssm_c_im"][0])
    common["d_l"] = f(f(inputs["ssm_d"][0]).reshape(4, 128).T)
    common["w_glu"] = f(f(inputs["ssm_w_glu"][0]).reshape(4, 128, 2048).transpose(1, 0, 2))
    common["w_out"] = f(f(inputs["w_out"][0]).reshape(KD, 128, D).transpose(1, 0, 2))
    common["g2rep"] = f(np.broadcast_to(f(inputs["ffn_norm_g"][0])[None, :], (128, D)))
    common["w_query"] = f(f(inputs["peer_w_query"][0]).reshape(KD, 128, 2048).transpose(1, 0, 2))
    common["skT"] = f(f(inputs["peer_sub_keys"][0]).reshape(16, 128, 128).transpose(2, 0, 1))
    common["peer_uT"] = f(f(inputs["peer_u"][0]).T.reshape(KD, 128, NE).transpose(1, 0, 2))
    common["peer_v"] = f(f(inputs["peer_v"][0]).reshape(128, 128, D).transpose(1, 0, 2))
    maps = []
    for c in range(n_cores):
        m = dict(common)
        m["x"] = f(inputs["x"][c, :T])
        maps.append(m)
    return maps


def kernel(**inputs):
    T = 4096
    n = 8
    nc = build_program(T)
    in_maps = make_in_maps(inputs, T, n)
    res = run_bass_kernel_spmd(nc, in_maps, core_ids=list(range(n)))
    return np.stack([np.asarray(r["out"]) for r in res.results], axis=0).astype(np.float32)
```

```python
import math
from contextlib import ExitStack
import numpy as np
import concourse.bass as bass
import concourse.mybir as mybir
from concourse.bass_utils import run_bass_kernel_spmd

F32 = mybir.dt.float32; BF16 = mybir.dt.bfloat16; I32 = mybir.dt.int32; U32 = mybir.dt.uint32
ALU = mybir.AluOpType; AF = mybir.ActivationFunctionType; AX = mybir.AxisListType

D = 1024
KD = 8
INC = 5640
EPS = 1e-6
C_U, C_Q, C_K, C_V, C_F, C_GS, C_GA = 0, 512, 1536, 2560, 3584, 3592, 4616
NEG = -30000.0
TWO_PI = 2.0 * math.pi
NE = 16384


class Res:
    __slots__ = ("name", "w", "r")

    def __init__(self, name):
        self.name = name
        self.w = None
        self.r = {}


class Sched:
    ENGS = ("pe", "act", "dve", "pool", "sp")

    def __init__(self, nc, es, same_engine_sync=("act", "dve", "pool")):
        self.nc = nc
        self.es = es
        self.eng = {"pe": nc.tensor, "act": nc.scalar, "dve": nc.vector, "pool": nc.gpsimd, "sp": nc.sync}
        self.sem = {}
        self.cnt = {}
        for e in self.ENGS:
            self.sem[e] = es.enter_context(nc.semaphore("sem_" + e))
            self.cnt[e] = 0
        self.seen = {e: {} for e in self.ENGS}
        self.same = set(same_engine_sync)
        self.nwaits = 0
        self.nops = 0

    def res(self, name):
        return Res(name)

    def _chan(self, chan):
        if chan not in self.sem:
            self.sem[chan] = self.es.enter_context(self.nc.semaphore("semd_" + chan))
            self.cnt[chan] = 0
        return self.sem[chan]

    def _wait(self, e, tok):
        if tok is None:
            return
        key, val = tok
        if key == e and e not in self.same:
            return
        if self.seen[e].get(key, 0) >= val:
            return
        self.eng[e].wait_ge(self.sem[key], val)
        self.seen[e][key] = val
        self.nwaits += 1

    def _deps(self, e, reads, writes):
        for R in reads:
            self._wait(e, R.w)
        for W in writes:
            self._wait(e, W.w)
            for key, val in W.r.items():
                self._wait(e, (key, val))

    def _commit(self, tok, reads, writes):
        for R in reads:
            if R.r.get(tok[0], 0) < tok[1]:
                R.r[tok[0]] = tok[1]
        for W in writes:
            W.w = tok
            W.r = {}

    def op(self, e, fn, reads=(), writes=()):
        self._deps(e, reads, writes)
        ins = fn(self.eng[e])
        self.cnt[e] += 1
        ins.then_inc(self.sem[e], 1)
        tok = (e, self.cnt[e])
        self._commit(tok, reads, writes)
        self.nops += 1
        return tok

    def dma(self, e, chan, out, in_, reads=(), writes=(), **kw):
        sem = self._chan(chan)
        if self.cnt[chan] > 0:
            self._wait(e, (chan, self.cnt[chan]))
        self._deps(e, reads, writes)
        ins = self.eng[e].dma_start(out=out, in_=in_, **kw)
        self.cnt[chan] += 16
        ins.then_inc(sem, 16)
        tok = (chan, self.cnt[chan])
        self._commit(tok, reads, writes)
        return tok

    def barrier(self):
        for e in self.ENGS:
            for key in list(self.sem.keys()):
                if key != e and self.cnt[key] > 0:
                    self._wait(e, (key, self.cnt[key]))
                elif key == e and e in self.same and self.cnt[key] > 0:
                    self._wait(e, (key, self.cnt[key]))

    def wait_all_dma(self, e):
        for key in list(self.sem.keys()):
            if key not in self.ENGS and self.cnt[key] > 0:
                self._wait(e, (key, self.cnt[key]))


class Buf:
    def __init__(self, t, r):
        self.t = t
        self.r = r

    def __getitem__(self, k):
        return self.t[k]


def build_program(T, dbg=False, stop_after=None):
    NT = T // 128
    NCH = T // 512
    nc = bass.Bass("TRN2", target_bir_lowering=False)

    def din(name, shape, dt=F32):
        return nc.dram_tensor(name, shape, dt, kind="ExternalInput").ap()

    x_d = din("x", [T, D])
    w_in_d = din("w_in", [128, KD, INC])
    g1_d = din("g1", [128, KD])
    fb_d = din("fb", [128, 8])
    qg_d = din("qg", [128, 1])
    kg_d = din("kg", [128, 1])
    are_d = din("a_re_l", [128, 16]); aim_d = din("a_im_l", [128, 16]); ls_d = din("ls_l", [128, 16])
    bre_d = din("b_re_l", [128, 16, 16]); bim_d = din("b_im_l", [128, 16, 16])
    cre_d = din("c_re_l", [128, 16, 16]); cim_d = din("c_im_l", [128, 16, 16])
    dsk_d = din("d_l", [128, 4])
    wglu_d = din("w_glu", [128, 4, 2048])
    wout_d = din("w_out", [128, KD, D])
    g2_d = din("g2rep", [128, D])
    wqr_d = din("w_query", [128, KD, 2048])
    skT_d = din("skT", [128, 16, 128])
    UT_d = din("peer_uT", [128, KD, NE])
    V_d = din("peer_v", [128, 128, D])
    out_d = nc.dram_tensor("out", [T, D], F32, kind="ExternalOutput").ap()
    UTb_d = nc.dram_tensor("UTb", [128, KD, NE], BF16, kind="Internal").ap()
    Vb_d = nc.dram_tensor("Vb", [128, 128, D], BF16, kind="Internal").ap()
    rt_d = nc.dram_tensor("rt", [128, 3, T], F32, kind=("ExternalOutput" if dbg else "Internal")).ap()
    kind_scr = "ExternalOutput" if dbg else "Internal"
    mT_d = nc.dram_tensor("mT", [D, T], BF16, kind=kind_scr).ap()
    hT_d = nc.dram_tensor("hTd", [128, KD, T], BF16, kind="Internal").ap()
    h2T_d = nc.dram_tensor("h2T", [128, KD, T], BF16, kind=kind_scr).ap()
    yg_d = nc.dram_tensor("ygd", [128, 4, T], BF16, kind="Internal").ap()
    m2T_d = nc.dram_tensor("m2T", [128, KD, T], BF16, kind=kind_scr).ap() if dbg else None

    with ExitStack() as es:
        S = Sched(nc, es)

        def sbuf(stack, name, shape, dt):
            t = stack.enter_context(nc.sbuf_tensor("sb_" + name, shape, dt))
            return Buf(t, S.res(name))

        banks = []
        for i in range(8):
            t = es.enter_context(nc.psum_tensor("bank%d" % i, [128, 512], F32))
            banks.append(Buf(t, S.res("bank%d" % i)))

        identf = sbuf(es, "identf", [128, 128], F32)
        identb = sbuf(es, "identb", [128, 128], BF16)
        onesb = sbuf(es, "onesb", [128, 128], BF16)
        onesf = sbuf(es, "onesf", [128, 128], F32)
        trif = sbuf(es, "trif", [128, 128], F32)
        maskneg = sbuf(es, "maskneg", [128, 128], BF16)
        S.op("pool", lambda e: e.memset(identf[:], 1.0), writes=[identf.r])
        S.op("pool", lambda e: e.affine_select(out=identf[:], in_=identf[:], pattern=[[-1, 128]], compare_op=ALU.is_equal,
                                               fill=0.0, base=0, channel_multiplier=1), reads=[identf.r], writes=[identf.r])
        S.op("pool", lambda e: e.tensor_copy(out=identb[:], in_=identf[:]), reads=[identf.r], writes=[identb.r])
        S.op("pool", lambda e: e.memset(onesb[:], 1.0), writes=[onesb.r])
        S.op("pool", lambda e: e.memset(onesf[:], 1.0), writes=[onesf.r])
        S.op("pool", lambda e: e.memset(trif[:], 1.0), writes=[trif.r])
        S.op("pool", lambda e: e.affine_select(out=trif[:], in_=trif[:], pattern=[[1, 128]], compare_op=ALU.is_ge,
                                               fill=0.0, base=0, channel_multiplier=-1), reads=[trif.r], writes=[trif.r])
        S.op("pool", lambda e: e.tensor_scalar(out=maskneg[:], in0=trif[:], scalar1=-1.0, scalar2=-NEG, op0=ALU.add, op1=ALU.mult),
             reads=[trif.r], writes=[maskneg.r])

        hpi = sbuf(es, "hpi", [128, 1], F32)
        S.op("pool", lambda e: e.memset(hpi[:], math.pi / 2.0), writes=[hpi.r])
        epst = sbuf(es, "epst", [128, 1], F32)
        S.op("pool", lambda e: e.memset(epst[:], EPS), writes=[epst.r])
        g1 = sbuf(es, "g1", [128, KD], F32)
        fb = sbuf(es, "fb", [128, 8], F32)
        qg = sbuf(es, "qg", [128, 1], F32)
        kg = sbuf(es, "kg", [128, 1], F32)
        S.dma("sp", "c_g1", g1[:], g1_d, writes=[g1.r])
        S.dma("sp", "c_fb", fb[:], fb_d, writes=[fb.r])
        S.dma("sp", "c_qg", qg[:], qg_d, writes=[qg.r])
        S.dma("sp", "c_kg", kg[:], kg_d, writes=[kg.r])
        S.op("dve", lambda e: e.tensor_scalar(out=qg[:], in0=qg[:], scalar1=128.0 ** -0.5, scalar2=None, op0=ALU.mult),
             reads=[qg.r], writes=[qg.r])

        cast_toks = []
        if stop_after is None:
            for q in range(16):
                cast_toks.append(S.dma("pool", "ucast%d" % q, UTb_d[:, :, q * 1024:(q + 1) * 1024], UT_d[:, :, q * 1024:(q + 1) * 1024]))
                cast_toks.append(S.dma("pool", "vcast%d" % q, Vb_d[:, q * 8:(q + 1) * 8, :], V_d[:, q * 8:(q + 1) * 8, :]))

        stg = [None, None]
        stg_i = [0]

        def alloc_stg(stack):
            for i in range(2):
                stg[i] = sbuf(stack, "wstg%d_%d" % (i, stg_i[0]), [128, KD, 512], F32)

        def load_w(dst, src_d, c0, ncols, gain, nk=KD):
            s = stg[stg_i[0] % 2]
            stg_i[0] += 1
            S.dma("sp", "wld%d" % (stg_i[0] % 2), s[:, 0:nk, 0:ncols], src_d[:, :, c0:c0 + ncols], writes=[s.r])
            if gain is not None:
                S.op("pool", lambda e: e.tensor_tensor(out=dst[:], in0=s[:, 0:nk, 0:ncols],
                                                       in1=gain[:, 0:nk].unsqueeze(2).to_broadcast([128, nk, ncols]), op=ALU.mult),
                     reads=[s.r, gain.r], writes=[dst.r])
            else:
                S.op("pool", lambda e: e.tensor_copy(out=dst[:], in_=s[:, 0:nk, 0:ncols]), reads=[s.r], writes=[dst.r])

        scopeAB = ExitStack()
        alloc_stg(scopeAB)
        hT = sbuf(scopeAB, "hT", [128, KD, T], BF16)
        hT_r = [S.res("hT_%d" % i) for i in range(NT)]
        cum = sbuf(scopeAB, "cum", [128, NT, 8], F32)
        cend = sbuf(scopeAB, "cend", [128, NT + 1, 8], F32)
        with ExitStack() as pa:
            xts = [sbuf(pa, "xt%d" % i, [128, D], F32) for i in range(2)]
            xs = [sbuf(pa, "xs%d" % i, [128, D], BF16) for i in range(2)]
            junk = sbuf(pa, "junkA", [128, D], BF16)
            ss = sbuf(pa, "ss", [128, NT], F32)
            rs = sbuf(pa, "rs", [128, NT], F32)
            wf = sbuf(pa, "wf", [128, KD, 8], BF16)
            zf = sbuf(pa, "zf", [128, NT, 8], F32)
            spf = sbuf(pa, "spf", [128, NT, 8], F32)
            load_w(wf, w_in_d, C_F, 8, g1)
            fbank = banks[7]
            for i in range(NT):
                xt = xts[i % 2]; xb = xs[i % 2]
                S.dma("sp", "xld%d" % (i % 2), xt[:], x_d[i * 128:(i + 1) * 128, :], writes=[xt.r])
                S.op("act", lambda e: e.activation(out=junk[:], in_=xt[:], func=AF.Square, accum_out=ss[:, i:i + 1]),
                     reads=[xt.r], writes=[junk.r, ss.r])
                S.op("act", lambda e: e.activation(out=rs[:, i:i + 1], in_=ss[:, i:i + 1], func=AF.Sqrt, bias=EPS, scale=1.0 / D),
                     reads=[ss.r], writes=[rs.r])
                S.op("dve", lambda e: e.reciprocal(out=rs[:, i:i + 1], in_=rs[:, i:i + 1]), reads=[rs.r], writes=[rs.r])
                S.op("dve", lambda e: e.tensor_scalar(out=xb[:], in0=xt[:], scalar1=rs[:, i:i + 1], scalar2=None, op0=ALU.mult),
                     reads=[xt.r, rs.r], writes=[xb.r])
                bk = banks[i % 2]
                bv = bk[:].bitcast(BF16)
                for k in range(KD):
                    S.op("pe", lambda e: e.transpose(out=bv[:, k * 128:(k + 1) * 128], in_=xb[:, k * 128:(k + 1) * 128], identity=identb[:]),
                         reads=[xb.r, identb.r], writes=[bk.r])
                S.op("act", lambda e: e.copy(out=hT[:, :, i * 128:(i + 1) * 128], in_=bv[:, 0:1024].rearrange("p (k t) -> p k t", k=KD)),
                     reads=[bk.r], writes=[hT_r[i]])
                for k in range(KD):
                    S.op("pe", lambda e: e.matmul(fbank[:, i * 8:(i + 1) * 8], lhsT=hT[:, k, i * 128:(i + 1) * 128], rhs=wf[:, k, :],
                                                  start=(k == 0), stop=(k == KD - 1)),
                         reads=[hT_r[i], wf.r], writes=[fbank.r])
            S.op("dve", lambda e: e.tensor_tensor(out=zf[:], in0=fbank[:, 0:NT * 8].rearrange("p (i h) -> p i h", h=8),
                                                  in1=fb[:].unsqueeze(1).to_broadcast([128, NT, 8]), op=ALU.add),
                 reads=[fbank.r, fb.r], writes=[zf.r])
            S.op("act", lambda e: e.activation(out=spf[:], in_=zf[:], func=AF.Exp, scale=-1.0), reads=[zf.r], writes=[spf.r])
            S.op("act", lambda e: e.activation(out=spf[:], in_=spf[:], func=AF.Ln, bias=1.0, scale=1.0), reads=[spf.r], writes=[spf.r])
            cb1 = banks[5]; cb2 = banks[6]
            for i in range(NT):
                S.op("pe", lambda e: e.matmul(cb1[:, i * 8:(i + 1) * 8], lhsT=trif[:], rhs=spf[:, i, :], start=True, stop=True),
                     reads=[trif.r, spf.r], writes=[cb1.r])
                S.op("pe", lambda e: e.matmul(cb2[:, i * 8:(i + 1) * 8], lhsT=onesf[:], rhs=spf[:, i, :], start=True, stop=True),
                     reads=[onesf.r, spf.r], writes=[cb2.r])
            S.op("dve", lambda e: e.memset(cend[:, 0, :], 0.0), writes=[cend.r])
            for i in range(NT):
                S.op("dve", lambda e: e.tensor_tensor(out=cend[:, i + 1, :], in0=cend[:, i, :], in1=cb2[:, i * 8:(i + 1) * 8], op=ALU.add),
                     reads=[cend.r, cb2.r], writes=[cend.r])
            S.op("dve", lambda e: e.tensor_tensor(out=cum[:], in0=cb1[:, 0:NT * 8].rearrange("p (i h) -> p i h", h=8),
                                                  in1=cend[:, 0:NT, :], op=ALU.add),
                 reads=[cb1.r, cend.r], writes=[cum.r])

        for c in range(NCH):
            S.dma("sp", "hsp%d" % (c % 2), hT_d[:, :, c * 512:(c + 1) * 512], hT[:, :, c * 512:(c + 1) * 512], reads=hT_r[c * 4:(c + 1) * 4])
        S.barrier()
        if stop_after == "A":
            return nc
        with ExitStack() as pb:
            wq = sbuf(pb, "wq", [128, KD, 128], BF16)
            wk = sbuf(pb, "wk", [128, KD, 128], BF16)
            wv = sbuf(pb, "wv", [128, KD, 128], BF16)
            wg = sbuf(pb, "wg", [128, KD, 128], BF16)
            qT = sbuf(pb, "qT", [128, T], BF16)
            kT = sbuf(pb, "kT", [128, T], BF16)
            vv = sbuf(pb, "vv", [128, NT, 128], BF16)
            sgT = sbuf(pb, "sgT", [128, T], BF16)
            crow = sbuf(pb, "crow", [1, T], BF16)
            sq = [sbuf(pb, "sq%d" % i, [128, 512], BF16) for i in range(2)]
            rrep = [sbuf(pb, "rrep%d" % i, [128, 512], F32) for i in range(2)]
            Pt = [sbuf(pb, "Pt%d" % i, [128, 512], BF16) for i in range(3)]
            rec = sbuf(pb, "rec", [128, 512], F32)
            yv = sbuf(pb, "yv", [128, 512], F32)
            ym = [sbuf(pb, "ym%d" % i, [128, 512], BF16) for i in range(2)]
            hT_all = hT_r
            pcount = [0]
            print("SBUF remaining in phase B:", nc.sbuf_bytes_remaining)
            for h in range(8):
                load_w(wq, w_in_d, C_Q + h * 128, 128, g1)
                load_w(wk, w_in_d, C_K + h * 128, 128, g1)
                load_w(wv, w_in_d, C_V + h * 128, 128, g1)
                load_w(wg, w_in_d, C_GA + h * 128, 128, g1)
                S.op("dve", lambda e: e.tensor_scalar(out=crow[0:1, :].rearrange("p (i r) -> p i r", r=128),
                                                      in0=cend[0:1, 1:NT + 1, h:h + 1].to_broadcast([1, NT, 128]), scalar1=-1.0, scalar2=None, op0=ALU.mult),
                     reads=[cend.r], writes=[crow.r])
                for which, (wt, dstT, gvec) in enumerate(((wq, qT, qg), (wk, kT, kg))):
                    for c in range(NCH):
                        pj = banks[(2 * c) % 4]; pn = banks[(2 * c + 1) % 4]
                        for k in range(KD):
                            S.op("pe", lambda e: e.matmul(pj[:], lhsT=wt[:, k, :], rhs=hT[:, k, c * 512:(c + 1) * 512], start=(k == 0), stop=(k == KD - 1)),
                                 reads=[wt.r] + hT_all[c * 4:(c + 1) * 4], writes=[pj.r])
                        sqb = sq[c % 2]; rr = rrep[c % 2]
                        S.op("act", lambda e: e.activation(out=sqb[:], in_=pj[:], func=AF.Square), reads=[pj.r], writes=[sqb.r])
                        S.op("pe", lambda e: e.matmul(pn[:], lhsT=onesb[:], rhs=sqb[:], start=True, stop=True), reads=[onesb.r, sqb.r], writes=[pn.r])
                        S.op("act", lambda e: e.activation(out=rr[:], in_=pn[:], func=AF.Ln, bias=epst[:, 0:1], scale=1.0 / 128.0), reads=[pn.r, epst.r], writes=[rr.r])
                        S.op("act", lambda e: e.activation(out=rr[:], in_=rr[:], func=AF.Exp, scale=-0.5), reads=[rr.r], writes=[rr.r])
                        S.op("dve", lambda e: e.scalar_tensor_tensor(out=dstT[:, c * 512:(c + 1) * 512], in0=pj[:], scalar=gvec[:, 0:1], in1=rr[:],
                                                                     op0=ALU.mult, op1=ALU.mult),
                             reads=[pj.r, gvec.r, rr.r], writes=[dstT.r])
                for i4 in range(NT // 4):
                    pv = banks[i4 % 2]
                    for ii in range(4):
                        i = i4 * 4 + ii
                        for k in range(KD):
                            S.op("pe", lambda e: e.matmul(pv[:, ii * 128:(ii + 1) * 128], lhsT=hT[:, k, i * 128:(i + 1) * 128], rhs=wv[:, k, :],
                                                          start=(k == 0), stop=(k == KD - 1)),
                                 reads=[wv.r, hT_all[i]], writes=[pv.r])
                    S.op("act", lambda e: e.copy(out=vv[:, i4 * 4:(i4 + 1) * 4, :], in_=pv[:].rearrange("p (a b) -> p a b", a=4)),
                         reads=[pv.r], writes=[vv.r])
                for c in range(NCH):
                    pg = banks[2 + (c % 2)]
                    for k in range(KD):
                        S.op("pe", lambda e: e.matmul(pg[:], lhsT=wg[:, k, :], rhs=hT[:, k, c * 512:(c + 1) * 512], start=(k == 0), stop=(k == KD - 1)),
                             reads=[wg.r] + hT_all[c * 4:(c + 1) * 4], writes=[pg.r])
                    S.op("act", lambda e: e.activation(out=sgT[:, c * 512:(c + 1) * 512], in_=pg[:], func=AF.Sigmoid), reads=[pg.r], writes=[sgT.r])
                for qc in range(NCH):
                    pO = banks[4 + (qc % 2)]; pL = banks[6 + (qc % 2)]
                    nkb = (qc + 1) * 4
                    def emit_S(kb):
                        q_lo = max(qc * 4, kb)
                        off = (q_lo - qc * 4) * 128
                        pS = banks[pcount[0] % 4]
                        Pb = Pt[pcount[0] % 3]
                        pcount[0] += 1
                        diag = kb >= qc * 4
                        S.op("pe", lambda e: e.matmul(pS[:, off:512], lhsT=kT[:, kb * 128:(kb + 1) * 128], rhs=qT[:, q_lo * 128:(qc + 1) * 512],
                                                      start=True, stop=False),
                             reads=[kT.r, qT.r], writes=[pS.r])
                        S.op("pe", lambda e: e.matmul(pS[:, off:512], lhsT=onesb[0:1, :], rhs=crow[0:1, q_lo * 128:(qc + 1) * 512], start=False, stop=not diag),
                             reads=[onesb.r, crow.r], writes=[pS.r])
                        if diag:
                            S.op("pe", lambda e: e.matmul(pS[:, off:off + 128], lhsT=identb[:], rhs=maskneg[:], start=False, stop=True),
                                 reads=[identb.r, maskneg.r], writes=[pS.r])
                        S.op("act", lambda e: e.activation(out=Pb[:, off:512], in_=pS[:, off:512], func=AF.Exp, bias=cum[:, kb, h:h + 1], scale=1.0),
                             reads=[pS.r, cum.r], writes=[Pb.r])
                        return (Pb, off)

                    def emit_PV(kb, Pb, off):
                        S.op("pe", lambda e: e.matmul(pO[:, off:512], lhsT=vv[:, kb, :], rhs=Pb[:, off:512], start=(kb == 0), stop=(kb == nkb - 1)),
                             reads=[vv.r, Pb.r], writes=[pO.r])
                        S.op("pe", lambda e: e.matmul(pL[:, off:512], lhsT=onesb[:], rhs=Pb[:, off:512], start=(kb == 0), stop=(kb == nkb - 1)),
                             reads=[onesb.r, Pb.r], writes=[pL.r])

                    pend = []
                    for kb in range(nkb):
                        pend.append((kb,) + emit_S(kb))
                        if len(pend) > 2:
                            emit_PV(*pend.pop(0))
                    while pend:
                        emit_PV(*pend.pop(0))
                    S.op("dve", lambda e: e.reciprocal(out=rec[:], in_=pL[:]), reads=[pL.r], writes=[rec.r])
                    S.op("dve", lambda e: e.tensor_tensor(out=yv[:], in0=pO[:], in1=rec[:], op=ALU.mult), reads=[pO.r, rec.r], writes=[yv.r])
                    ymb = ym[qc % 2]
                    S.op("dve", lambda e: e.tensor_tensor(out=ymb[:], in0=yv[:], in1=sgT[:, qc * 512:(qc + 1) * 512], op=ALU.mult),
                         reads=[yv.r, sgT.r], writes=[ymb.r])
                    S.dma("sp", "mst%d" % (qc % 2), mT_d[h * 128:(h + 1) * 128, qc * 512:(qc + 1) * 512], ymb[:], reads=[ymb.r])


        S.barrier()
        scopeAB.close()
        if stop_after == "B":
            S.wait_all_dma("sp")
            return nc

        LT = 512
        with ExitStack() as pc:
            def small(name, shape=(128, 16)):
                return sbuf(pc, name, list(shape), F32)
            a_re = small("a_re"); a_im = small("a_im"); lsl = small("lsl")
            S.dma("sp", "c_are", a_re[:], are_d, writes=[a_re.r])
            S.dma("sp", "c_aim", a_im[:], aim_d, writes=[a_im.r])
            S.dma("sp", "c_ls", lsl[:], ls_d, writes=[lsl.r])
            dsk = sbuf(pc, "dsk", [128, 4], F32)
            S.dma("sp", "c_dsk", dsk[:], dsk_d, writes=[dsk.r])
            Cpad = sbuf(pc, "Cpad", [128, 16, 2, 128], F32)
            Bpad = sbuf(pc, "Bpad", [128, 16, 2, 128], F32)
            Tc = sbuf(pc, "Tc", [128, 16, LT], F32); Ts = sbuf(pc, "Ts", [128, 16, LT], F32)
            wu = sbuf(pc, "wu", [128, KD, 512], BF16)
            step = small("step"); th = small("th"); mag = small("mag"); cs = small("cs"); sn = small("sn")
            ki = sbuf(pc, "ki", [128, 16], I32); kf = small("kf"); rr_ = small("rr_"); ab = small("ab")
            lre = small("lre"); lim = small("lim"); den = small("den"); t1 = small("t1"); t2 = small("t2")
            cfr = small("cfr"); cfi = small("cfi")
            wr_ = small("wr_"); wi_ = small("wi_"); wt1 = small("wt1"); wt2 = small("wt2"); nwi_ = small("nwi_")
            pset = ExitStack()
            alloc_stg(pset)
            bre = sbuf(pset, "bre", [128, 16, 16], F32); bim = sbuf(pset, "bim", [128, 16, 16], F32)
            cre = sbuf(pset, "cre", [128, 16, 16], F32); cim = sbuf(pset, "cim", [128, 16, 16], F32)
            S.dma("sp", "c_bre", bre[:], bre_d, writes=[bre.r]); S.dma("sp", "c_bim", bim[:], bim_d, writes=[bim.r])
            S.dma("sp", "c_cre", cre[:], cre_d, writes=[cre.r]); S.dma("sp", "c_cim", cim[:], cim_d, writes=[cim.r])
            bbr = sbuf(pset, "bbr", [128, 16, 16], F32); bbi = sbuf(pset, "bbi", [128, 16, 16], F32)
            u1 = sbuf(pset, "u1", [128, 16, 16], F32); u2 = sbuf(pset, "u2", [128, 16, 16], F32)
            BZ = sbuf(pset, "BZ", [128, 16, 2, 128], F32)
            p1 = sbuf(pset, "p1", [128, 16, LT // 2], F32); p2 = sbuf(pset, "p2", [128, 16, LT // 2], F32)

            def tt(eng, out, a, b, op, rd, wr):
                S.op(eng, lambda e: e.tensor_tensor(out=out, in0=a, in1=b, op=op), reads=rd, writes=wr)

            S.op("act", lambda e: e.activation(out=step[:], in_=lsl[:], func=AF.Exp), reads=[lsl.r], writes=[step.r])
            tt("dve", th[:], a_im[:], step[:], ALU.mult, [a_im.r, step.r], [th.r])
            tt("dve", mag[:], a_re[:], step[:], ALU.mult, [a_re.r, step.r], [mag.r])
            S.op("act", lambda e: e.activation(out=mag[:], in_=mag[:], func=AF.Exp), reads=[mag.r], writes=[mag.r])
            S.op("dve", lambda e: e.tensor_scalar(out=ki[:], in0=th[:], scalar1=1.0 / TWO_PI, scalar2=None, op0=ALU.mult), reads=[th.r], writes=[ki.r])
            S.op("dve", lambda e: e.tensor_copy(out=kf[:], in_=ki[:]), reads=[ki.r], writes=[kf.r])
            S.op("dve", lambda e: e.scalar_tensor_tensor(out=rr_[:], in0=kf[:], scalar=-TWO_PI, in1=th[:], op0=ALU.mult, op1=ALU.add),
                 reads=[kf.r, th.r], writes=[rr_.r])
            PI_LO = 3.1415925
            S.op("dve", lambda e: e.tensor_scalar(out=rr_[:], in0=rr_[:], scalar1=PI_LO, scalar2=-PI_LO, op0=ALU.min, op1=ALU.max), reads=[rr_.r], writes=[rr_.r])
            S.op("act", lambda e: e.activation(out=sn[:], in_=rr_[:], func=AF.Sin), reads=[rr_.r], writes=[sn.r])
            S.op("act", lambda e: e.activation(out=ab[:], in_=rr_[:], func=AF.Abs), reads=[rr_.r], writes=[ab.r])
            S.op("act", lambda e: e.activation(out=cs[:], in_=ab[:], func=AF.Sin, scale=-1.0, bias=hpi[:, 0:1]), reads=[ab.r, hpi.r], writes=[cs.r])
            tt("dve", lre[:], mag[:], cs[:], ALU.mult, [mag.r, cs.r], [lre.r])
            tt("dve", lim[:], mag[:], sn[:], ALU.mult, [mag.r, sn.r], [lim.r])
            S.op("dve", lambda e: e.tensor_scalar(out=lre[:], in0=lre[:], scalar1=-1.0, scalar2=None, op0=ALU.add), reads=[lre.r], writes=[lre.r])
            tt("dve", t1[:], a_re[:], a_re[:], ALU.mult, [a_re.r], [t1.r])
            tt("dve", t2[:], a_im[:], a_im[:], ALU.mult, [a_im.r], [t2.r])
            tt("dve", den[:], t1[:], t2[:], ALU.add, [t1.r, t2.r], [den.r])
            S.op("dve", lambda e: e.reciprocal(out=den[:], in_=den[:]), reads=[den.r], writes=[den.r])
            tt("dve", t1[:], lre[:], a_re[:], ALU.mult, [lre.r, a_re.r], [t1.r])
            tt("dve", t2[:], lim[:], a_im[:], ALU.mult, [lim.r, a_im.r], [t2.r])
            tt("dve", cfr[:], t1[:], t2[:], ALU.add, [t1.r, t2.r], [cfr.r])
            tt("dve", cfr[:], cfr[:], den[:], ALU.mult, [cfr.r, den.r], [cfr.r])
            tt("dve", t1[:], lim[:], a_re[:], ALU.mult, [lim.r, a_re.r], [t1.r])
            tt("dve", t2[:], lre[:], a_im[:], ALU.mult, [lre.r, a_im.r], [t2.r])
            tt("dve", cfi[:], t1[:], t2[:], ALU.subtract, [t1.r, t2.r], [cfi.r])
            tt("dve", cfi[:], cfi[:], den[:], ALU.mult, [cfi.r, den.r], [cfi.r])
            bc3 = lambda v: v[:].unsqueeze(2).to_broadcast([128, 16, 16])
            tt("dve", u1[:], bre[:], bc3(cfr), ALU.mult, [bre.r, cfr.r], [u1.r])
            tt("dve", u2[:], bim[:], bc3(cfi), ALU.mult, [bim.r, cfi.r], [u2.r])
            tt("dve", bbr[:], u1[:], u2[:], ALU.subtract, [u1.r, u2.r], [bbr.r])
            tt("dve", u1[:], bim[:], bc3(cfr), ALU.mult, [bim.r, cfr.r], [u1.r])
            tt("dve", u2[:], bre[:], bc3(cfi), ALU.mult, [bre.r, cfi.r], [u2.r])
            tt("dve", bbi[:], u1[:], u2[:], ALU.add, [u1.r, u2.r], [bbi.r])
            S.op("dve", lambda e: e.tensor_scalar(out=cim[:], in0=cim[:], scalar1=-1.0, scalar2=None, op0=ALU.mult), reads=[cim.r], writes=[cim.r])
            if True:
                S.op("pool", lambda e: e.memset(Cpad[:], 0.0), writes=[Cpad.r])
                S.op("pool", lambda e: e.memset(BZ[:], 0.0), writes=[BZ.r])
                for j in range(16):
                    for two in range(2):
                        c0 = 32 * (j % 4) + 16 * two
                        ps_ = slice(two * 64, (two + 1) * 64)
                        for ri, (srcC, srcB) in enumerate(((cre, bbr), (cim, bbi))):
                            S.op("pool", lambda e: e.tensor_copy(out=Cpad[ps_, j, ri, c0:c0 + 16], in_=srcC[ps_, j, :]), reads=[srcC.r], writes=[Cpad.r])
                            S.op("pool", lambda e: e.tensor_copy(out=BZ[ps_, j, ri, c0:c0 + 16], in_=srcB[ps_, j, :]), reads=[srcB.r], writes=[BZ.r])
                for j in range(16):
                    for ri in range(2):
                        bk = banks[(2 * j + ri) % 4]
                        S.op("pe", lambda e: e.transpose(out=bk[:, 0:128], in_=BZ[:, j, ri, :], identity=identf[:]), reads=[BZ.r, identf.r], writes=[bk.r])
                        S.op("act", lambda e: e.copy(out=Bpad[:, j, ri, :], in_=bk[:, 0:128]), reads=[bk.r], writes=[Bpad.r])
            S.op("dve", lambda e: e.tensor_copy(out=wr_[:], in_=cs[:]), reads=[cs.r], writes=[wr_.r])
            S.op("dve", lambda e: e.tensor_copy(out=wi_[:], in_=sn[:]), reads=[sn.r], writes=[wi_.r])
            S.op("pool", lambda e: e.memset(Tc[:, :, 0:1], 1.0), writes=[Tc.r])
            S.op("pool", lambda e: e.memset(Ts[:, :, 0:1], 0.0), writes=[Ts.r])
            if True:
                n = 1
                while n < LT:
                    bw = lambda v: v[:].unsqueeze(2).to_broadcast([128, 16, n])
                    tt("dve", p1[:, :, 0:n], Tc[:, :, 0:n], bw(wr_), ALU.mult, [Tc.r, wr_.r], [p1.r])
                    tt("dve", p2[:, :, 0:n], Ts[:, :, 0:n], bw(wi_), ALU.mult, [Ts.r, wi_.r], [p2.r])
                    tt("dve", Tc[:, :, n:2 * n], p1[:, :, 0:n], p2[:, :, 0:n], ALU.subtract, [p1.r, p2.r], [Tc.r])
                    tt("dve", p1[:, :, 0:n], Tc[:, :, 0:n], bw(wi_), ALU.mult, [Tc.r, wi_.r], [p1.r])
                    tt("dve", p2[:, :, 0:n], Ts[:, :, 0:n], bw(wr_), ALU.mult, [Ts.r, wr_.r], [p2.r])
                    tt("dve", Ts[:, :, n:2 * n], p1[:, :, 0:n], p2[:, :, 0:n], ALU.add, [p1.r, p2.r], [Ts.r])
                    tt("dve", wt1[:], wr_[:], wr_[:], ALU.mult, [wr_.r], [wt1.r])
                    tt("dve", wt2[:], wi_[:], wi_[:], ALU.mult, [wi_.r], [wt2.r])
                    tt("dve", wi_[:], wr_[:], wi_[:], ALU.mult, [wr_.r, wi_.r], [wi_.r])
                    S.op("dve", lambda e: e.tensor_scalar(out=wi_[:], in0=wi_[:], scalar1=2.0, scalar2=None, op0=ALU.mult), reads=[wi_.r], writes=[wi_.r])
                    tt("dve", wr_[:], wt1[:], wt2[:], ALU.subtract, [wt1.r, wt2.r], [wr_.r])
                    n *= 2
            S.op("dve", lambda e: e.tensor_scalar(out=nwi_[:], in0=wi_[:], scalar1=-1.0, scalar2=None, op0=ALU.mult), reads=[wi_.r], writes=[nwi_.r])
            wv_ = lambda buf, a, b: Buf(buf.t[:, :, a:b], buf.r)
            load_w(wu, w_in_d, C_U, 512, g1)
            S.barrier()
            pset.close()
            hTc = [sbuf(pc, "hTc%d" % i, [128, KD, 512], BF16) for i in range(1)]
            u_sb = sbuf(pc, "u_sb", [128, 4, 512], F32)
            ssets = []
            set_banks = ((2, 3), (5, 6), (7, 1))
            for si in range(3):
                B_ = {}
                for nm in ("xr_sb", "xi_sb", "ta", "tb", "tcb", "td", "r_r", "r_i"):
                    B_[nm] = sbuf(pc, "%s_%d" % (nm, si), [128, 512], F32)
                B_["ctmp"] = sbuf(pc, "ctmp_%d" % si, [128, 2], F32)
                B_["pxr"] = banks[set_banks[si][0]]; B_["pxi"] = banks[set_banks[si][1]]
                ssets.append(B_)
            carry = sbuf(pc, "carry", [128, 16, 2], F32)
            carry_r = [S.res("carry_%d" % j) for j in range(16)]
            yv_ = sbuf(pc, "yv_", [128, 512], F32)
            ygs = [sbuf(pc, "yg%d" % i, [128, 4, 512], BF16) for i in range(2)]
            print("SBUF remaining in phase C:", nc.sbuf_bytes_remaining)
            S.op("pool", lambda e: e.memset(carry[:], 0.0), writes=carry_r)
            v2 = lambda ap: ap.rearrange("p (a b) -> p a b", a=512 // LT)
            tb3 = lambda tab, j: tab[:, j, :].unsqueeze(1).to_broadcast([128, 512 // LT, LT])
            mT_v = mT_d.rearrange("(dc p) t -> p dc t", p=128)
            for c in range(NCH):
                hc = hTc[0]; yg = ygs[c % 2]
                S.dma("sp", "hcl", hc[:], hT_d[:, :, c * 512:(c + 1) * 512], writes=[hc.r])
                for rc in range(4):
                    pu = banks[rc % 2]
                    for k in range(KD):
                        S.op("pe", lambda e: e.matmul(pu[:], lhsT=wu[:, k, rc * 128:(rc + 1) * 128], rhs=hc[:, k, :], start=(k == 0), stop=(k == KD - 1)),
                             reads=[wu.r, hc.r], writes=[pu.r])
                    S.op("act", lambda e: e.copy(out=u_sb[:, rc, :], in_=pu[:]), reads=[pu.r], writes=[u_sb.r])
                def ssm_gen(j, B_):
                    rc = j // 4
                    pxr = B_["pxr"]; pxi = B_["pxi"]; py = banks[4]
                    xr_sb = B_["xr_sb"]; xi_sb = B_["xi_sb"]; ta = B_["ta"]; tb = B_["tb"]; tcb = B_["tcb"]; td = B_["td"]
                    r_r = B_["r_r"]; r_i = B_["r_i"]; ctmp = B_["ctmp"]; cr = carry_r[j]
                    xtr = ta; xti = tcb; s_r = ta; s_i = tcb
                    S.op("pe", lambda e: e.matmul(pxr[:], lhsT=Bpad[:, j, 0, :], rhs=u_sb[:, rc, :], start=True, stop=True), reads=[Bpad.r, u_sb.r], writes=[pxr.r])
                    S.op("pe", lambda e: e.matmul(pxi[:], lhsT=Bpad[:, j, 1, :], rhs=u_sb[:, rc, :], start=True, stop=True), reads=[Bpad.r, u_sb.r], writes=[pxi.r])
                    yield
                    S.op("act", lambda e: e.copy(out=xr_sb[:], in_=pxr[:]), reads=[pxr.r], writes=[xr_sb.r])
                    S.op("act", lambda e: e.copy(out=xi_sb[:], in_=pxi[:]), reads=[pxi.r], writes=[xi_sb.r])
                    yield
                    tt("dve", v2(ta[:]), v2(xr_sb[:]), tb3(Tc, j), ALU.mult, [xr_sb.r, Tc.r], [ta.r])
                    tt("pool", v2(tb[:]), v2(xi_sb[:]), tb3(Ts, j), ALU.mult, [xi_sb.r, Ts.r], [tb.r])
                    yield
                    tt("dve", v2(tcb[:]), v2(xi_sb[:]), tb3(Tc, j), ALU.mult, [xi_sb.r, Tc.r], [tcb.r])
                    tt("pool", v2(td[:]), v2(xr_sb[:]), tb3(Ts, j), ALU.mult, [xr_sb.r, Ts.r], [td.r])
                    yield
                    tt("dve", xtr[:], ta[:], tb[:], ALU.add, [ta.r, tb.r], [xtr.r])
                    tt("pool", xti[:], tcb[:], td[:], ALU.subtract, [tcb.r, td.r], [xti.r])
                    yield
                    for sgi in range(512 // LT):
                        sl = slice(sgi * LT, (sgi + 1) * LT)
                        magb = mag[:, j:j + 1].to_broadcast([128, LT])
                        S.op("dve", lambda e: e.tensor_tensor_scan(out=r_r[:, sl], data0=magb, data1=xtr[:, sl], initial=carry[:, j, 0:1], op0=ALU.mult, op1=ALU.add),
                             reads=[mag.r, xtr.r, cr], writes=[r_r.r])
                        S.op("dve", lambda e: e.tensor_tensor_scan(out=r_i[:, sl], data0=magb, data1=xti[:, sl], initial=carry[:, j, 1:2], op0=ALU.mult, op1=ALU.add),
                             reads=[mag.r, xti.r, cr], writes=[r_i.r])
                        yield
                        last = sgi * LT + LT - 1
                        S.op("act", lambda e: e.activation(out=ctmp[:, 0:1], in_=r_i[:, last:last + 1], func=AF.Identity, scale=nwi_[:, j:j + 1]),
                             reads=[r_i.r, nwi_.r], writes=[ctmp.r])
                        S.op("act", lambda e: e.activation(out=ctmp[:, 1:2], in_=r_i[:, last:last + 1], func=AF.Identity, scale=wr_[:, j:j + 1]),
                             reads=[r_i.r, wr_.r], writes=[ctmp.r])
                        yield
                        S.op("act", lambda e: e.activation(out=carry[:, j, 0:1], in_=r_r[:, last:last + 1], func=AF.Identity, scale=wr_[:, j:j + 1], bias=ctmp[:, 0:1]),
                             reads=[r_r.r, wr_.r, ctmp.r], writes=[cr])
                        S.op("act", lambda e: e.activation(out=carry[:, j, 1:2], in_=r_r[:, last:last + 1], func=AF.Identity, scale=wi_[:, j:j + 1], bias=ctmp[:, 1:2]),
                             reads=[r_r.r, wi_.r, ctmp.r], writes=[cr])
                        yield
                    tt("dve", v2(ta[:]), v2(r_r[:]), tb3(Tc, j), ALU.mult, [r_r.r, Tc.r], [ta.r])
                    tt("pool", v2(tb[:]), v2(r_i[:]), tb3(Ts, j), ALU.mult, [r_i.r, Ts.r], [tb.r])
                    yield
                    tt("pool", v2(tcb[:]), v2(r_r[:]), tb3(Ts, j), ALU.mult, [r_r.r, Ts.r], [tcb.r])
                    tt("dve", v2(td[:]), v2(r_i[:]), tb3(Tc, j), ALU.mult, [r_i.r, Tc.r], [td.r])
                    yield
                    tt("dve", s_r[:], ta[:], tb[:], ALU.subtract, [ta.r, tb.r], [s_r.r])
                    tt("pool", s_i[:], tcb[:], td[:], ALU.add, [tcb.r, td.r], [s_i.r])
                    yield
                    S.op("pe", lambda e: e.matmul(py[:], lhsT=Cpad[:, j, 0, :], rhs=s_r[:], start=(j % 4 == 0), stop=False), reads=[Cpad.r, s_r.r], writes=[py.r])
                    S.op("pe", lambda e: e.matmul(py[:], lhsT=Cpad[:, j, 1, :], rhs=s_i[:], start=False, stop=(j % 4 == 3)), reads=[Cpad.r, s_i.r], writes=[py.r])
                    yield
                    if j % 4 == 3:
                        S.op("dve", lambda e: e.scalar_tensor_tensor(out=yv_[:], in0=u_sb[:, rc, :], scalar=dsk[:, rc:rc + 1], in1=py[:], op0=ALU.mult, op1=ALU.add),
                             reads=[u_sb.r, dsk.r, py.r], writes=[yv_.r])
                        S.op("act", lambda e: e.activation(out=yg[:, rc, :], in_=yv_[:], func=AF.Gelu_apprx_tanh), reads=[yv_.r], writes=[yg.r])

                SKEW = 4
                active = []
                nxt_j = 0
                while nxt_j < 16 or active:
                    if nxt_j < 16 and len(active) < 3 and (not active or active[-1][1] >= SKEW):
                        active.append([ssm_gen(nxt_j, ssets[nxt_j % 3]), 0])
                        nxt_j += 1
                    for ent in list(active):
                        try:
                            next(ent[0])
                            ent[1] += 1
                        except StopIteration:
                            active.remove(ent)
                S.dma("sp", "ygst%d" % (c % 2), yg_d[:, :, c * 512:(c + 1) * 512], yg[:], reads=[yg.r])
            S.barrier()

        with ExitStack() as pc2:
            g2 = sbuf(pc2, "g2", [128, D], F32)
            S.dma("sp", "c_g2", g2[:], g2_d, writes=[g2.r])
            wgs = sbuf(pc2, "wgs", [128, KD, D], BF16)
            wglu = sbuf(pc2, "wglu", [128, 4, 2048], BF16)
            wout = sbuf(pc2, "wout", [128, KD, D], BF16)
            with ExitStack() as pw2:
                alloc_stg(pw2)
                for hh in range(2):
                    load_w(wv_(wgs, hh * 512, (hh + 1) * 512), w_in_d, C_GS + hh * 512, 512, g1)
                    load_w(wv_(wout, hh * 512, (hh + 1) * 512), wout_d, hh * 512, 512, None)
                for qq in range(4):
                    load_w(wv_(wglu, qq * 512, (qq + 1) * 512), wglu_d, qq * 512, 512, None, nk=4)
                S.barrier()
            hTc2 = [sbuf(pc2, "hTc2_%d" % i, [128, KD, 512], BF16) for i in range(2)]
            matt = [sbuf(pc2, "matt%d" % i, [128, KD, 512], BF16) for i in range(2)]
            ygc = [sbuf(pc2, "ygc%d" % i, [128, 4, 512], BF16) for i in range(2)]
            sg1 = sbuf(pc2, "sg1", [128, 512], F32); sg2 = sbuf(pc2, "sg2", [128, 512], F32)
            ys = sbuf(pc2, "ys", [128, 512], F32)
            merged = sbuf(pc2, "merged", [128, KD, 512], BF16)
            xres = [sbuf(pc2, "xres%d" % i, [128, D], F32) for i in range(2)]
            x1t = [sbuf(pc2, "x1t%d" % i, [128, D], F32) for i in range(2)]
            h2b = [sbuf(pc2, "h2b%d" % i, [128, D], BF16) for i in range(2)]
            h2c = [sbuf(pc2, "h2c%d" % i, [128, KD, 128], BF16) for i in range(2)]
            ss2 = sbuf(pc2, "ss2", [128, NT], F32); rs2 = sbuf(pc2, "rs2", [128, NT], F32)
            for c in range(NCH):
                hc = hTc2[c % 2]; mt = matt[c % 2]; yg = ygc[c % 2]
                S.dma("sp", "hcl2_%d" % (c % 2), hc[:], hT_d[:, :, c * 512:(c + 1) * 512], writes=[hc.r])
                S.dma("sp", "mtl%d" % (c % 2), mt[:], mT_v[:, :, c * 512:(c + 1) * 512], writes=[mt.r])
                S.dma("sp", "ygl%d" % (c % 2), yg[:], yg_d[:, :, c * 512:(c + 1) * 512], writes=[yg.r])
                for dc in range(KD):
                    pvl = banks[5]; pgt = banks[6]; pgs = banks[7]
                    for rc in range(4):
                        S.op("pe", lambda e: e.matmul(pvl[:], lhsT=wglu[:, rc, dc * 128:(dc + 1) * 128], rhs=yg[:, rc, :], start=(rc == 0), stop=(rc == 3)),
                             reads=[wglu.r, yg.r], writes=[pvl.r])
                    for rc in range(4):
                        S.op("pe", lambda e: e.matmul(pgt[:], lhsT=wglu[:, rc, D + dc * 128:D + (dc + 1) * 128], rhs=yg[:, rc, :], start=(rc == 0), stop=(rc == 3)),
                             reads=[wglu.r, yg.r], writes=[pgt.r])
                    for k in range(KD):
                        S.op("pe", lambda e: e.matmul(pgs[:], lhsT=wgs[:, k, dc * 128:(dc + 1) * 128], rhs=hc[:, k, :], start=(k == 0), stop=(k == KD - 1)),
                             reads=[wgs.r, hc.r], writes=[pgs.r])
                    S.op("act", lambda e: e.activation(out=sg1[:], in_=pgt[:], func=AF.Sigmoid), reads=[pgt.r], writes=[sg1.r])
                    S.op("act", lambda e: e.activation(out=sg2[:], in_=pgs[:], func=AF.Sigmoid), reads=[pgs.r], writes=[sg2.r])
                    tt("dve", ys[:], pvl[:], sg1[:], ALU.mult, [pvl.r, sg1.r], [ys.r])
                    tt("pool", ys[:], ys[:], sg2[:], ALU.mult, [ys.r, sg2.r], [ys.r])
                    tt("pool", merged[:, dc, :], ys[:], mt[:, dc, :], ALU.add, [ys.r, mt.r], [merged.r])
                if dbg:
                    S.dma("sp", "m2st", m2T_d[:, :, c * 512:(c + 1) * 512], merged[:], reads=[merged.r])
                for ti in range(4):
                    i = c * 4 + ti
                    xr_ = xres[i % 2]; x1 = x1t[i % 2]; hb = h2b[i % 2]; hcp = h2c[i % 2]; junkC = hb
                    S.dma("sp", "xrl%d" % (i % 2), xr_[:], x_d[i * 128:(i + 1) * 128, :], writes=[xr_.r])
                    for nh in range(2):
                        po = banks[nh]
                        for dc in range(KD):
                            S.op("pe", lambda e: e.matmul(po[:], lhsT=merged[:, dc, ti * 128:(ti + 1) * 128], rhs=wout[:, dc, nh * 512:(nh + 1) * 512],
                                                          start=(dc == 0), stop=(dc == KD - 1)), reads=[merged.r, wout.r], writes=[po.r])
                        tt("dve", x1[:, nh * 512:(nh + 1) * 512], po[:], xr_[:, nh * 512:(nh + 1) * 512], ALU.add, [po.r, xr_.r], [x1.r])
                    S.dma("sp", "x1st%d" % (i % 2), out_d[i * 128:(i + 1) * 128, :], x1[:], reads=[x1.r])
                    S.op("act", lambda e: e.activation(out=junkC[:], in_=x1[:], func=AF.Square, accum_out=ss2[:, i:i + 1]), reads=[x1.r], writes=[junkC.r, ss2.r])
                    S.op("act", lambda e: e.activation(out=rs2[:, i:i + 1], in_=ss2[:, i:i + 1], func=AF.Sqrt, bias=EPS, scale=1.0 / D), reads=[ss2.r], writes=[rs2.r])
                    S.op("dve", lambda e: e.reciprocal(out=rs2[:, i:i + 1], in_=rs2[:, i:i + 1]), reads=[rs2.r], writes=[rs2.r])
                    S.op("dve", lambda e: e.scalar_tensor_tensor(out=hb[:], in0=x1[:], scalar=rs2[:, i:i + 1], in1=g2[:], op0=ALU.mult, op1=ALU.mult), reads=[x1.r, rs2.r, g2.r], writes=[hb.r])
                    bk = banks[2 + (i % 2)]
                    bv = bk[:].bitcast(BF16)
                    for k in range(KD):
                        S.op("pe", lambda e: e.transpose(out=bv[:, k * 128:(k + 1) * 128], in_=hb[:, k * 128:(k + 1) * 128], identity=identb[:]),
                             reads=[hb.r, identb.r], writes=[bk.r])
                    S.op("act", lambda e: e.copy(out=hcp[:], in_=bv[:, 0:1024].rearrange("p (k t) -> p k t", k=KD)), reads=[bk.r], writes=[hcp.r])
                    S.dma("sp", "h2st%d" % (i % 2), h2T_d[:, :, i * 128:(i + 1) * 128], hcp[:], reads=[hcp.r])
            S.barrier()
        if stop_after == "C":
            S.wait_all_dma("sp")
            return nc

        with ExitStack() as pd:
            alloc_stg(pd)
            wq = sbuf(pd, "wqp", [128, KD, 2048], BF16)
            skT = sbuf(pd, "skT", [128, 16, 128], BF16)
            for qq in range(4):
                load_w(Buf(wq.t[:, :, qq * 512:(qq + 1) * 512], wq.r), wqr_d, qq * 512, 512, None)
            for hh in range(2):
                load_w(Buf(skT.t[:, hh * 8:(hh + 1) * 8, :], skT.r), skT_d[:, hh * 8:(hh + 1) * 8, :], 0, 128, None)
            iota16 = sbuf(pd, "iota16", [128, 16], F32)
            S.op("pool", lambda e: e.iota(iota16[:], pattern=[[1, 16]], base=0, channel_multiplier=0, allow_small_or_imprecise_dtypes=True), writes=[iota16.r])
            h2c_ = [sbuf(pd, "h2cD%d" % i, [128, KD, 512], BF16) for i in range(2)]
            qTs = sbuf(pd, "qTs", [128, 16, 512], BF16)
            sc = sbuf(pd, "sc", [128, 16, 128], F32); sc2 = sbuf(pd, "sc2", [128, 16, 128], F32)
            v16s = [sbuf(pd, "v16_%d" % i, [128, 16, 16], F32) for i in range(2)]
            i16s = [sbuf(pd, "i16_%d" % i, [128, 16, 16], U32) for i in range(2)]
            i16fs = [sbuf(pd, "i16f_%d" % i, [128, 16, 16], F32) for i in range(2)]
            cand = sbuf(pd, "cand", [128, 8, 256], F32); cand2 = sbuf(pd, "cand2", [128, 8, 256], F32)
            bests = [sbuf(pd, "best_%d" % i, [128, 8, 16], F32) for i in range(2)]
            poss = [sbuf(pd, "pos_%d" % i, [128, 8, 16], U32) for i in range(2)]
            pa_is = [sbuf(pd, "pa_i_%d" % i, [128, 8, 16], I32) for i in range(2)]
            pb_is = [sbuf(pd, "pb_i_%d" % i, [128, 8, 16], I32) for i in range(2)]
            pa_f = sbuf(pd, "pa_f", [128, 8, 16], F32); pb_f = sbuf(pd, "pb_f", [128, 8, 16], F32)
            ohs = [sbuf(pd, "oh%d" % i, [128, 8, 16, 16], F32) for i in range(4)]; pr = sbuf(pd, "pr", [128, 8, 16, 16], F32)
            gpre = sbuf(pd, "gpre", [128, 8, 16], F32); gjunk = sbuf(pd, "gjunk", [128, 8, 16], F32)
            rt3s = [sbuf(pd, "rt3_%d" % i, [128, 3, 128], F32) for i in range(2)]
            esum = sbuf(pd, "esum", [128, 8], F32)
            rtT = [sbuf(pd, "rtT%d" % i, [128, 3, 128], F32) for i in range(2)]
            print("SBUF remaining in phase D0:", nc.sbuf_bytes_remaining)
            def d0_chunk(c):
                hc = h2c_[c % 2]
                S.dma("sp", "h2l%d" % (c % 2), hc[:], h2T_d[:, :, c * 512:(c + 1) * 512], writes=[hc.r])
                for b in range(16):
                    pq = banks[4 + (b % 2)]
                    for k in range(KD):
                        S.op("pe", lambda e: e.matmul(pq[:], lhsT=wq[:, k, b * 128:(b + 1) * 128], rhs=hc[:, k, :], start=(k == 0), stop=(k == KD - 1)),
                             reads=[wq.r, hc.r], writes=[pq.r])
                    S.op("act", lambda e: e.copy(out=qTs[:, b, :], in_=pq[:]), reads=[pq.r], writes=[qTs.r])

            def d0_hdr(i):
                return (v16s[i % 2], i16s[i % 2], i16fs[i % 2], bests[i % 2], poss[i % 2], pa_is[i % 2], pb_is[i % 2], rt3s[i % 2])

            def d0_front(i):
                c, ti = divmod(i, 4)
                v16, i16, i16f, best, pos, pa_i, pb_i, rt3 = d0_hdr(i)
                for b in range(16):
                    bk = banks[b // 4]
                    S.op("pe", lambda e: e.matmul(bk[:, (b % 4) * 128:(b % 4 + 1) * 128], lhsT=qTs[:, b, ti * 128:(ti + 1) * 128], rhs=skT[:, b, :], start=True, stop=True),
                         reads=[qTs.r, skT.r], writes=[bk.r])
                for q4 in range(4):
                    S.op("act", lambda e: e.copy(out=sc[:, q4 * 4:(q4 + 1) * 4, :], in_=banks[q4][:].rearrange("p (a b) -> p a b", a=4)),
                         reads=[banks[q4].r], writes=[sc.r])
                for b in range(16):
                    S.op("dve", lambda e: e.max(out=v16[:, b, 0:8], in_=sc[:, b, :]), reads=[sc.r], writes=[v16.r])
                for b in range(16):
                    S.op("dve", lambda e: e.max_index(out=i16[:, b, 0:8], in_max=v16[:, b, 0:8], in_values=sc[:, b, :]), reads=[sc.r, v16.r], writes=[i16.r])
                for b in range(16):
                    S.op("dve", lambda e: e.match_replace(out=sc2[:, b, :], in_to_replace=v16[:, b, 0:8], in_values=sc[:, b, :], imm_value=-1e30),
                         reads=[sc.r, v16.r], writes=[sc2.r])
                for b in range(16):
                    S.op("dve", lambda e: e.max(out=v16[:, b, 8:16], in_=sc2[:, b, :]), reads=[sc2.r], writes=[v16.r])
                for b in range(16):
                    S.op("dve", lambda e: e.max_index(out=i16[:, b, 8:16], in_max=v16[:, b, 8:16], in_values=sc2[:, b, :]), reads=[sc2.r, v16.r], writes=[i16.r])
                v4 = v16[:].rearrange("p (h c) k -> p h c k", c=2)
                S.op("dve", lambda e: e.tensor_tensor(out=cand[:].rearrange("p h (a b) -> p h a b", a=16),
                                                      in0=v4[:, :, 0, :].unsqueeze(3).to_broadcast([128, 8, 16, 16]),
                                                      in1=v4[:, :, 1, :].unsqueeze(2).to_broadcast([128, 8, 16, 16]), op=ALU.add),
                     reads=[v16.r], writes=[cand.r])
                for h in range(8):
                    S.op("dve", lambda e: e.max(out=best[:, h, 0:8], in_=cand[:, h, :]), reads=[cand.r], writes=[best.r])
                for h in range(8):
                    S.op("dve", lambda e: e.max_index(out=pos[:, h, 0:8], in_max=best[:, h, 0:8], in_values=cand[:, h, :]), reads=[cand.r, best.r], writes=[pos.r])
                for h in range(8):
                    S.op("dve", lambda e: e.match_replace(out=cand2[:, h, :], in_to_replace=best[:, h, 0:8], in_values=cand[:, h, :], imm_value=-1e30),
                         reads=[cand.r, best.r], writes=[cand2.r])
                for h in range(8):
                    S.op("dve", lambda e: e.max(out=best[:, h, 8:16], in_=cand2[:, h, :]), reads=[cand2.r], writes=[best.r])
                for h in range(8):
                    S.op("dve", lambda e: e.max_index(out=pos[:, h, 8:16], in_max=best[:, h, 8:16], in_values=cand2[:, h, :]), reads=[cand2.r, best.r], writes=[pos.r])
                S.op("dve", lambda e: e.tensor_single_scalar(out=pa_i[:], in_=pos[:].bitcast(I32), scalar=4, op=ALU.arith_shift_right), reads=[pos.r], writes=[pa_i.r])
                S.op("dve", lambda e: e.tensor_single_scalar(out=pb_i[:], in_=pos[:].bitcast(I32), scalar=15, op=ALU.bitwise_and), reads=[pos.r], writes=[pb_i.r])
                oh_a = ohs[(i % 2) * 2]; oh_b = ohs[(i % 2) * 2 + 1]
                S.op("dve", lambda e: e.tensor_copy(out=pa_f[:], in_=pa_i[:]), reads=[pa_i.r], writes=[pa_f.r])
                S.op("dve", lambda e: e.tensor_copy(out=pb_f[:], in_=pb_i[:]), reads=[pb_i.r], writes=[pb_f.r])
                for pf_, oh_ in ((pa_f, oh_a), (pb_f, oh_b)):
                    S.op("dve", lambda e: e.tensor_tensor(out=oh_[:], in0=pf_[:].unsqueeze(3).to_broadcast([128, 8, 16, 16]),
                                                          in1=iota16[:].unsqueeze(1).unsqueeze(1).to_broadcast([128, 8, 16, 16]), op=ALU.is_equal),
                         reads=[pf_.r, iota16.r], writes=[oh_.r])

            def d0_back(i):
                v16, i16, i16f, best, pos, pa_i, pb_i, rt3 = d0_hdr(i)
                oh_a = ohs[(i % 2) * 2]; oh_b = ohs[(i % 2) * 2 + 1]
                S.op("pool", lambda e: e.tensor_copy(out=i16f[:], in_=i16[:]), reads=[i16.r], writes=[i16f.r])
                gat = rt3[:, 2, :].rearrange("p (h k) -> p h k", h=8)
                S.op("pool", lambda e: e.tensor_tensor(out=gpre[:], in0=best[:], in1=best[:, :, 0:1].to_broadcast([128, 8, 16]), op=ALU.subtract),
                     reads=[best.r], writes=[gpre.r])
                for h in range(8):
                    S.op("act", lambda e: e.activation(out=gjunk[:, h, :], in_=gpre[:, h, :], func=AF.Exp, accum_out=esum[:, h:h + 1]),
                         reads=[gpre.r], writes=[gjunk.r, esum.r])
                S.op("act", lambda e: e.activation(out=esum[:], in_=esum[:], func=AF.Ln), reads=[esum.r], writes=[esum.r])
                S.op("pool", lambda e: e.tensor_tensor(out=gpre[:], in0=gpre[:], in1=esum[:].unsqueeze(2).to_broadcast([128, 8, 16]), op=ALU.subtract),
                     reads=[gpre.r, esum.r], writes=[gpre.r])
                S.op("act", lambda e: e.activation(out=gat, in_=gpre[:], func=AF.Exp), reads=[gpre.r], writes=[rt3.r])
                i4 = i16f[:].rearrange("p (h c) k -> p h c k", c=2)
                for which, oh_ in enumerate((oh_a, oh_b)):
                    S.op("pool", lambda e: e.tensor_tensor(out=pr[:], in0=oh_[:], in1=i4[:, :, which, :].unsqueeze(2).to_broadcast([128, 8, 16, 16]), op=ALU.mult),
                         reads=[oh_.r, i16f.r], writes=[pr.r])
                    S.op("pool", lambda e: e.tensor_tensor(out=pr[:, :, :, 0:8], in0=pr[:, :, :, 0:8], in1=pr[:, :, :, 8:16], op=ALU.add), reads=[pr.r], writes=[pr.r])
                    S.op("pool", lambda e: e.tensor_tensor(out=pr[:, :, :, 0:4], in0=pr[:, :, :, 0:4], in1=pr[:, :, :, 4:8], op=ALU.add), reads=[pr.r], writes=[pr.r])
                    S.op("pool", lambda e: e.tensor_tensor(out=pr[:, :, :, 0:2], in0=pr[:, :, :, 0:2], in1=pr[:, :, :, 2:4], op=ALU.add), reads=[pr.r], writes=[pr.r])
                    S.op("pool", lambda e: e.tensor_tensor(out=rt3[:, which, :].rearrange("p (h k) -> p h k", h=8), in0=pr[:, :, :, 0], in1=pr[:, :, :, 1], op=ALU.add),
                         reads=[pr.r], writes=[rt3.r])
                rtt = rtT[i % 2]
                for q3 in range(3):
                    bk = banks[6 + (q3 % 2)]
                    S.op("pe", lambda e: e.transpose(out=bk[:, 0:128], in_=rt3[:, q3, :], identity=identf[:]), reads=[rt3.r, identf.r], writes=[bk.r])
                    S.op("act", lambda e: e.copy(out=rtt[:, q3, :], in_=bk[:, 0:128]), reads=[bk.r], writes=[rtt.r])
                S.dma("sp", "rtst%d" % (i % 2), rt_d[:, :, i * 128:(i + 1) * 128], rtt[:], reads=[rtt.r])


            for i in range(NT + 1):
                if i < NT:
                    if i % 4 == 0:
                        d0_chunk(i // 4)
                    d0_front(i)
                if i >= 1:
                    d0_back(i - 1)

            S.barrier()
        if stop_after == "D0":
            S.wait_all_dma("sp")
            return nc

        G = 256
        NG = T // G
        with ExitStack() as pe_:
            iotaf = sbuf(pe_, "iotaf", [128, 128], F32)
            S.op("pool", lambda e: e.iota(iotaf[:], pattern=[[1, 128]], base=0, channel_multiplier=0, allow_small_or_imprecise_dtypes=True), writes=[iotaf.r])
            GTs = [sbuf(pe_, "GT%d" % i, [128, G, 128], BF16) for i in range(2)]
            ub = [sbuf(pe_, "ub%d" % i, [128, KD, 512], BF16) for i in range(3)]
            vb = [sbuf(pe_, "vb%d" % i, [128, 4, D], BF16) for i in range(3)]
            rtgs = [sbuf(pe_, "rtg%d" % i, [128, 3, G], F32) for i in range(2)]
            h2g = sbuf(pe_, "h2g", [128, KD, G], BF16)
            Ab = [sbuf(pe_, "Ab%d" % i, [128, G], BF16) for i in range(2)]
            WT = [sbuf(pe_, "WT%d" % i, [128, G], BF16) for i in range(2)]
            P1 = [sbuf(pe_, "P1_%d" % i, [128, 128], BF16) for i in range(8)]
            P2B = [sbuf(pe_, "P2B_%d" % i, [128, 8, 128], BF16) for i in range(2)]
            x1g = [sbuf(pe_, "x1g%d" % i, [128, D], F32) for i in range(2)]
            print("SBUF remaining in phase D1:", nc.sbuf_bytes_remaining)
            uv_loaded = [False]
            gt_cnt = [0]

            def gt_gen(g):
                GT = GTs[g % 2]; rtg = rtgs[g % 2]
                S.dma("sp", "rtl%d" % (g % 2), rtg[:], rt_d[:, :, g * G:(g + 1) * G], writes=[rtg.r])
                LAG = 4
                p2bs = {}

                def gt_mm(t):
                    t8, t_ = divmod(t, 8)
                    p1 = P1[t % 8]; p2b = p2bs[t8]
                    gp = banks[6 + ((t // 4) % 2)]
                    S.op("pe", lambda e: e.matmul(gp[:, (t % 4) * 128:(t % 4 + 1) * 128], lhsT=p2b[:, t_, :], rhs=p1[:], start=True, stop=True),
                         reads=[p1.r, p2b.r], writes=[gp.r])
                    if t % 4 == 3:
                        S.op("act", lambda e: e.copy(out=GT[:, t - 3:t + 1, :], in_=gp[:].rearrange("p (a b) -> p a b", a=4)), reads=[gp.r], writes=[GT.r])

                for t in range(G):
                    t8, t_ = divmod(t, 8)
                    if t_ == 0:
                        p2b = P2B[gt_cnt[0] % 2]
                        gt_cnt[0] += 1
                        p2bs[t8] = p2b
                        S.op("dve", lambda e: e.tensor_tensor(out=p2b[:], in0=iotaf[:].unsqueeze(1).to_broadcast([128, 8, 128]),
                                                              in1=rtg[:, 1, t8 * 8:(t8 + 1) * 8].unsqueeze(2).to_broadcast([128, 8, 128]), op=ALU.is_equal),
                             reads=[iotaf.r, rtg.r], writes=[p2b.r])
                    p1 = P1[t % 8]
                    S.op("dve", lambda e: e.tensor_scalar(out=p1[:], in0=iotaf[:], scalar1=rtg[:, 0, t:t + 1], scalar2=rtg[:, 2, t:t + 1], op0=ALU.is_equal, op1=ALU.mult),
                         reads=[iotaf.r, rtg.r], writes=[p1.r])
                    if t >= LAG:
                        gt_mm(t - LAG)
                    yield
                for t in range(G - LAG, G):
                    gt_mm(t)
                yield

            def drain(gen, n=None):
                k = 0
                while n is None or k < n:
                    try:
                        next(gen)
                    except StopIteration:
                        return
                    k += 1

            drain(gt_gen(0))
            for g in range(NG):
                GT = GTs[g % 2]
                nxt = gt_gen(g + 1) if g + 1 < NG else iter(())
                S.dma("sp", "h2gl", h2g[:], h2T_d[:, :, g * G:(g + 1) * G], writes=[h2g.r])
                def emit_S(i1):
                    blk4, bi = divmod(i1, 4)
                    u_ = ub[blk4 % 3]
                    pS = banks[4 + (i1 % 2)]
                    for k in range(KD):
                        S.op("pe", lambda e: e.matmul(pS[:, 0:G], lhsT=u_[:, k, bi * 128:(bi + 1) * 128], rhs=h2g[:, k, :], start=(k == 0), stop=(k == KD - 1)),
                             reads=[u_.r, h2g.r], writes=[pS.r])
                    ab_ = Ab[i1 % 2]; wt_ = WT[i1 % 2]
                    S.op("act", lambda e: e.activation(out=ab_[:], in_=pS[:, 0:G], func=AF.Gelu_apprx_tanh), reads=[pS.r], writes=[ab_.r])
                    eng = "dve" if i1 % 2 == 0 else "pool"
                    S.op(eng, lambda e: e.tensor_tensor(out=wt_[:], in0=ab_[:], in1=GT[:, :, i1], op=ALU.mult), reads=[ab_.r, GT.r], writes=[wt_.r])

                def emit_out(i1):
                    blk4, bi = divmod(i1, 4)
                    v_ = vb[blk4 % 3]; wt_ = WT[i1 % 2]
                    for tt_ in range(G // 128):
                        for nh in range(2):
                            acc = banks[tt_ * 2 + nh]
                            S.op("pe", lambda e: e.matmul(acc[:], lhsT=wt_[:, tt_ * 128:(tt_ + 1) * 128], rhs=v_[:, bi, nh * 512:(nh + 1) * 512],
                                                          start=(i1 == 0), stop=(i1 == 127)), reads=[wt_.r, v_.r], writes=[acc.r])

                for i1 in range(128):
                    blk4, bi = divmod(i1, 4)
                    if bi == 0:
                        if not uv_loaded[0]:
                            for tok in cast_toks:
                                S._wait("sp", tok)
                            uv_loaded[0] = True
                        u_ = ub[blk4 % 3]; v_ = vb[blk4 % 3]
                        S.dma("sp", "ul%d" % (blk4 % 3), u_[:], UTb_d[:, :, blk4 * 512:(blk4 + 1) * 512], writes=[u_.r])
                        S.dma("sp", "vl%d" % (blk4 % 3), v_[:], Vb_d[:, blk4 * 4:(blk4 + 1) * 4, :], writes=[v_.r])
                    emit_S(i1)
                    if i1 >= 1:
                        emit_out(i1 - 1)
                    drain(nxt, 2)
                emit_out(127)
                drain(nxt)
                for tt_ in range(G // 128):
                    i = g * (G // 128) + tt_
                    xg = x1g[i % 2]
                    S.dma("sp", "x1l%d" % (i % 2), xg[:], out_d[i * 128:(i + 1) * 128, :], writes=[xg.r])
                    for nh in range(2):
                        acc = banks[tt_ * 2 + nh]
                        S.op("dve", lambda e: e.tensor_tensor(out=xg[:, nh * 512:(nh + 1) * 512], in0=acc[:], in1=xg[:, nh * 512:(nh + 1) * 512], op=ALU.add),
                             reads=[acc.r, xg.r], writes=[xg.r])
                    S.dma("sp", "ost%d" % (i % 2), out_d[i * 128:(i + 1) * 128, :], xg[:], reads=[xg.r])
            S.barrier()

        S.wait_all_dma("sp")
        print("program built: ops=%d waits=%d" % (S.nops, S.nwaits))
    return nc


def make_in_maps(inputs, T, n_cores):
    f = lambda a: np.ascontiguousarray(np.asarray(a, dtype=np.float32))
    w_in = f(inputs["w_in"][0]).reshape(KD, 128, INC).transpose(1, 0, 2)
    common = {
        "w_in": f(w_in),
        "g1": f(f(inputs["mix_norm_g"][0]).reshape(KD, 128).T),
        "fb": f(np.broadcast_to(f(inputs["fox_forget_bias"][0])[None, :], (128, 8))),
        "qg": f(f(inputs["q_norm_g"][0]).reshape(128, 1)),
        "kg": f(f(inputs["k_norm_g"][0]).reshape(128, 1)),
    }
    L16 = lambda a: f(f(a).reshape(16, 2, 64).transpose(1, 2, 0).reshape(128, 16))
    common["a_re_l"] = L16(inputs["ssm_a_re"][0])
    common["a_im_l"] = L16(inputs["ssm_a_im"][0])
    common["ls_l"] = f(np.broadcast_to(f(inputs["ssm_log_step"][0]).reshape(16, 2, 1), (16, 2, 64)).transpose(1, 2, 0).reshape(128, 16))
    LB = lambda a: f(f(a).reshape(16, 2, 64, 16).transpose(1, 2, 0, 3).reshape(128, 16, 16))
    LC = lambda a: f(f(a).reshape(16, 2, 16, 64).transpose(1, 3, 0, 2).reshape(128, 16, 16))
    common["b_re_l"] = LB(inputs["ssm_b_re"][0]); common["b_im_l"] = LB(inputs["ssm_b_im"][0])
    common["c_re_l"] = LC(inputs["ssm_c_re"][0]); common["c_im_l"] = LC(inputs["ssm_c_im"][0])
    common["d_l"] = f(f(inputs["ssm_d"][0]).reshape(4, 128).T)
    common["w_glu"] = f(f(inputs["ssm_w_glu"][0]).reshape(4, 128, 2048).transpose(1, 0, 2))
    common["w_out"] = f(f(inputs["w_out"][0]).reshape(KD, 128, D).transpose(1, 0, 2))
    common["g2rep"] = f(np.broadcast_to(f(inputs["ffn_norm_g"][0])[None, :], (128, D)))
    common["w_query"] = f(f(inputs["peer_w_query"][0]).reshape(KD, 128, 2048).transpose(1, 0, 2))
    common["skT"] = f(f(inputs["peer_sub_keys"][0]).reshape(16, 128, 128).transpose(2, 0, 1))
    common["peer_uT"] = f(f(inputs["peer_u"][0]).T.reshape(KD, 128, NE).transpose(1, 0, 2))
    common["peer_v"] = f(f(inputs["peer_v"][0]).reshape(128, 128, D).transpose(1, 0, 2))
    maps = []
    for c in range(n_cores):
        m = dict(common)
        m["x"] = f(inputs["x"][c, :T])
        maps.append(m)
    return maps


def kernel(**inputs):
    T = 4096
    n = 8
    nc = build_program(T)
    in_maps = make_in_maps(inputs, T, n)
    res = run_bass_kernel_spmd(nc, in_maps, core_ids=list(range(n)))
    return np.stack([np.asarray(r["out"]) for r in res.results], axis=0).astype(np.float32)
```
